# Optimizing a Trainium2 kernel written in Bass

```python
import jax
import jax.numpy as jnp
from jax import lax
import numpy as np

D_MODEL = 1024
BATCH = 2
SEQ = 8192
DEPTH = 2

EPS = 1e-6
ROPE_THETA = 10000.0
NEG_INF = -1e30
D_FF = 2816
N_BRANCH = 3
MIX_WIDTH = 512

MLA_HEADS = 8
MLA_Q_RANK = 256
MLA_KV_RANK = 128
MLA_NOPE = 64
MLA_ROPE = 32
MLA_V = 64
MLA_QK = MLA_NOPE + MLA_ROPE
ATTN_BLOCK_Q = 128

GDN_HEADS = 4
GDN_DK = 128
GDN_DV = 128
GDN_CONV = 4
GDN_CHUNK = 64

MOBA_HEADS = 8
MOBA_DH = 64
MOBA_BLOCK = 256
MOBA_TOPK = 3
MOBA_QCHUNK = 64

IN_SPLITS = (
    MLA_Q_RANK,
    MLA_KV_RANK,
    MLA_ROPE,
    GDN_HEADS * GDN_DK,
    GDN_HEADS * GDN_DK,
    GDN_HEADS * GDN_DV,
    GDN_HEADS,
    GDN_HEADS,
    GDN_HEADS * GDN_DV,
    3 * MOBA_HEADS * MOBA_DH,
    N_BRANCH * D_MODEL,
)
D_IN = sum(IN_SPLITS)

kernel_name = 'hybrid_mla_gdn_moba_macaron'


def split_cols(t, sizes):
    offs = np.cumsum(sizes)[:-1].tolist()
    return jnp.split(t, offs, axis=-1)


def rms_norm(x, g):
    xf = x.astype(jnp.float32)
    y = xf * lax.rsqrt(jnp.mean(xf * xf, axis=-1, keepdims=True) + EPS)
    return (y * g.astype(jnp.float32)).astype(x.dtype)


def l2norm(x):
    return x * lax.rsqrt(jnp.sum(x * x, axis=-1, keepdims=True) + EPS)


def rope(x, pos):
    d = x.shape[-1]
    half = d // 2
    inv_freq = ROPE_THETA ** (-jnp.arange(half, dtype=jnp.float32) * 2.0 / d)
    ang = pos.astype(jnp.float32)[:, None] * inv_freq[None, :]
    cos = jnp.cos(ang)[None, :, None, :]
    sin = jnp.sin(ang)[None, :, None, :]
    xf = x.astype(jnp.float32)
    x1, x2 = xf[..., :half], xf[..., half:]
    return jnp.concatenate([x1 * cos - x2 * sin, x2 * cos + x1 * sin], axis=-1).astype(x.dtype)


def swiglu(h, w_in, w_out):
    gate, up = jnp.split(h @ w_in, 2, axis=-1)
    return (jax.nn.silu(gate) * up) @ w_out


def causal_dwconv(x, w):
    c = x.shape[-1]
    k = w.shape[0]
    return lax.conv_general_dilated(
        x, w[:, None, :].astype(x.dtype), window_strides=(1,), padding=[(k - 1, 0)],
        dimension_numbers=('NWC', 'WIO', 'NWC'), feature_group_count=c)


def causal_attention_blocked(q, k, v, scale):
    b, h, s, dk = q.shape
    nb = s // ATTN_BLOCK_Q
    qb = jnp.moveaxis(q.reshape(b, h, nb, ATTN_BLOCK_Q, dk), 2, 0)
    k_pos = jnp.arange(s)

    def one_block(args):
        i, q_i = args
        logits = jnp.einsum('bhqd,bhkd->bhqk', q_i, k, preferred_element_type=jnp.float32) * scale
        q_pos = i * ATTN_BLOCK_Q + jnp.arange(ATTN_BLOCK_Q)
        logits = jnp.where(k_pos[None, :] <= q_pos[:, None], logits, NEG_INF)
        p = jax.nn.softmax(logits, axis=-1).astype(v.dtype)
        return jnp.einsum('bhqk,bhkd->bhqd', p, v)

    o = lax.map(one_block, (jnp.arange(nb), qb))
    return jnp.moveaxis(o, 0, 2).reshape(b, h, s, v.shape[-1])


def mla_branch(c_q, c_kv, k_rope, pos, cq_norm, ckv_norm, w_uq, w_ukv, q_norm, k_norm):
    b, s, _ = c_q.shape
    h = MLA_HEADS
    q = (rms_norm(c_q, cq_norm) @ w_uq).reshape(b, s, h, MLA_QK)
    kv = (rms_norm(c_kv, ckv_norm) @ w_ukv).reshape(b, s, h, MLA_NOPE + MLA_V)
    k_nope, v = kv[..., :MLA_NOPE], kv[..., MLA_NOPE:]
    k = jnp.concatenate([k_nope, jnp.broadcast_to(k_rope[:, :, None, :], (b, s, h, MLA_ROPE))], axis=-1)
    q = rms_norm(q, q_norm)
    k = rms_norm(k, k_norm)
    q = jnp.concatenate([q[..., :MLA_NOPE], rope(q[..., MLA_NOPE:], pos)], axis=-1)
    k = jnp.concatenate([k[..., :MLA_NOPE], rope(k[..., MLA_NOPE:], pos)], axis=-1)
    to_bhsd = lambda t: jnp.transpose(t, (0, 2, 1, 3))
    o = causal_attention_blocked(to_bhsd(q), to_bhsd(k), to_bhsd(v), MLA_QK ** -0.5)
    return jnp.transpose(o, (0, 2, 1, 3)).reshape(b, s, h * MLA_V)


def gated_delta_rule_chunked(q, k, v, beta, g):
    b, s, h, dk = q.shape
    dv = v.shape[-1]
    c = GDN_CHUNK
    n = s // c

    def chunks(t):
        return jnp.moveaxis(t.reshape((b, n, c, h) + t.shape[3:]), 3, 1)

    q, k, v, beta, g = (chunks(t) for t in (q, k, v, beta, g))
    gcum = jnp.cumsum(g, axis=-1)
    tril = jnp.tril(jnp.ones((c, c), dtype=bool))
    strict = jnp.tril(jnp.ones((c, c), dtype=bool), -1)
    decay = jnp.exp(jnp.where(tril, gcum[..., :, None] - gcum[..., None, :], NEG_INF))
    k_beta = k * beta[..., None]
    a = jnp.where(strict, jnp.einsum('bhnid,bhnjd->bhnij', k_beta, k) * decay, 0.0)
    eye = jnp.eye(c, dtype=q.dtype)
    rhs = jnp.concatenate([v * beta[..., None], k_beta * jnp.exp(gcum)[..., None]], axis=-1)
    sol = lax.linalg.triangular_solve(eye + a, rhs, left_side=True, lower=True, unit_diagonal=True)
    u, w = sol[..., :dv], sol[..., dv:]
    qk = jnp.einsum('bhnid,bhnjd->bhnij', q, k) * decay
    q_dec = q * jnp.exp(gcum)[..., None]
    g_last = gcum[..., -1]
    k_dec = k * jnp.exp(g_last[..., None] - gcum)[..., None]

    def step(state, xs):
        q_i, w_i, u_i, qk_i, k_i, gl_i = xs
        v_new = u_i - jnp.einsum('bhcd,bhde->bhce', w_i, state)
        o_i = jnp.einsum('bhcd,bhde->bhce', q_i, state) + jnp.einsum('bhcj,bhje->bhce', qk_i, v_new)
        state = state * jnp.exp(gl_i)[..., None, None] + jnp.einsum('bhcd,bhce->bhde', k_i, v_new)
        return state, o_i

    xs = tuple(jnp.moveaxis(t, 2, 0) for t in (q_dec, w, u, qk, k_dec, g_last))
    state0 = jnp.zeros((b, h, dk, dv), q.dtype)
    _, o = lax.scan(step, state0, xs)
    return jnp.transpose(o, (1, 0, 3, 2, 4)).reshape(b, s, h, dv)


def gdn_branch(q, k, v, b_logit, a_logit, z, conv_w, a_log, dt_bias, out_norm):
    b, s, _ = q.shape
    f32 = jnp.float32
    qkv = jax.nn.silu(causal_dwconv(jnp.concatenate([q, k, v], axis=-1), conv_w)).astype(f32)
    q, k, v = jnp.split(qkv, [GDN_HEADS * GDN_DK, 2 * GDN_HEADS * GDN_DK], axis=-1)
    q = l2norm(q.reshape(b, s, GDN_HEADS, GDN_DK)) * (GDN_DK ** -0.5)
    k = l2norm(k.reshape(b, s, GDN_HEADS, GDN_DK))
    v = v.reshape(b, s, GDN_HEADS, GDN_DV)
    beta = jax.nn.sigmoid(b_logit.astype(f32))
    g = -jnp.exp(a_log.astype(f32)) * jax.nn.softplus(a_logit.astype(f32) + dt_bias.astype(f32))
    o = gated_delta_rule_chunked(q, k, v, beta, g)
    o = rms_norm(o, out_norm).astype(z.dtype) * jax.nn.silu(z.reshape(b, s, GDN_HEADS, GDN_DV))
    return o.reshape(b, s, GDN_HEADS * GDN_DV)


def moba_branch(qkv, pos, q_norm, k_norm):
    b, s, _ = qkv.shape
    h, dh, bs = MOBA_HEADS, MOBA_DH, MOBA_BLOCK
    q, k, v = (t.reshape(b, s, h, dh) for t in jnp.split(qkv, 3, axis=-1))
    q = rope(rms_norm(q, q_norm), pos)
    k = rope(rms_norm(k, k_norm), pos)
    nb = -(-s // bs)
    s_pad = nb * bs

    def prep(t):
        return jnp.pad(jnp.transpose(t, (0, 2, 1, 3)), ((0, 0), (0, 0), (0, s_pad - s), (0, 0)))

    q, k, v = prep(q), prep(k), prep(v)
    kb = k.reshape(b, h, nb, bs, dh)
    vb = v.reshape(b, h, nb, bs, dh)
    k_mean = jnp.mean(kb.astype(jnp.float32), axis=3)
    gate = jnp.einsum('bhsd,bhnd->bhsn', q.astype(jnp.float32), k_mean)
    q_blk = jnp.arange(s_pad) // bs
    gate = jnp.where(jnp.arange(nb)[None, :] < q_blk[:, None], gate, NEG_INF)
    n_sel = min(MOBA_TOPK, nb)
    _, sel = lax.top_k(gate, n_sel)
    sel_valid = sel < q_blk[:, None]
    scale = dh ** -0.5
    b_idx = jnp.arange(b)[:, None, None, None]
    h_idx = jnp.arange(h)[None, :, None, None]
    key_off = jnp.arange(bs)

    def one_chunk(ci):
        start = ci * MOBA_QCHUNK
        q_c = lax.dynamic_slice_in_dim(q, start, MOBA_QCHUNK, axis=2)
        sel_c = lax.dynamic_slice_in_dim(sel, start, MOBA_QCHUNK, axis=2)
        valid_c = lax.dynamic_slice_in_dim(sel_valid, start, MOBA_QCHUNK, axis=2)
        k_sel = kb[b_idx, h_idx, sel_c]
        v_sel = vb[b_idx, h_idx, sel_c]
        s_sel = jnp.einsum('bhqd,bhqnkd->bhqnk', q_c, k_sel, preferred_element_type=jnp.float32) * scale
        s_sel = jnp.where(valid_c[..., None], s_sel, NEG_INF)
        j = start // bs
        k_own = lax.dynamic_index_in_dim(kb, j, axis=2, keepdims=False)
        v_own = lax.dynamic_index_in_dim(vb, j, axis=2, keepdims=False)
        s_own = jnp.einsum('bhqd,bhkd->bhqk', q_c, k_own, preferred_element_type=jnp.float32) * scale
        q_pos = start + jnp.arange(MOBA_QCHUNK)
        k_pos = j * bs + key_off
        s_own = jnp.where(k_pos[None, :] <= q_pos[:, None], s_own, NEG_INF)
        logits = jnp.concatenate([s_sel.reshape(b, h, MOBA_QCHUNK, n_sel * bs), s_own], axis=-1)
        p = jax.nn.softmax(logits, axis=-1).astype(v.dtype)
        p_sel = p[..., :n_sel * bs].reshape(b, h, MOBA_QCHUNK, n_sel, bs)
        p_own = p[..., n_sel * bs:]
        return (jnp.einsum('bhqnk,bhqnkd->bhqd', p_sel, v_sel)
                + jnp.einsum('bhqk,bhkd->bhqd', p_own, v_own))

    o = lax.map(one_chunk, jnp.arange(s_pad // MOBA_QCHUNK))
    o = jnp.transpose(o, (1, 0, 3, 2, 4)).reshape(b, s_pad, h * dh)
    return o[:, :s]


def setup_inputs(seed: int = 0) -> dict:
    key = jax.random.key(seed)
    ks = jax.random.split(key, 24)
    L = DEPTH
    f32 = jnp.float32

    def nrm(k, shape, fan_in):
        return jax.random.normal(k, shape, f32) * (fan_in ** -0.5)

    def gain(k, shape):
        return 1.0 + 0.02 * jax.random.normal(k, shape, f32)

    dt = jnp.exp(jax.random.uniform(ks[14], (L, GDN_HEADS), f32, np.log(1e-3), np.log(1e-1)))
    return {
        'x': jax.random.normal(ks[0], (BATCH, SEQ, D_MODEL), f32),
        'ffa_norm': gain(ks[1], (L, D_MODEL)),
        'ffa_w_in': nrm(ks[2], (L, D_MODEL, 2 * D_FF), D_MODEL),
        'ffa_w_out': nrm(ks[3], (L, D_FF, D_MODEL), D_FF),
        'mix_norm': gain(ks[4], (L, D_MODEL)),
        'w_in': nrm(ks[5], (L, D_MODEL, D_IN), D_MODEL),
        'mla_cq_norm': gain(ks[6], (L, MLA_Q_RANK)),
        'mla_ckv_norm': gain(ks[7], (L, MLA_KV_RANK)),
        'mla_w_uq': nrm(ks[8], (L, MLA_Q_RANK, MLA_HEADS * MLA_QK), MLA_Q_RANK),
        'mla_w_ukv': nrm(ks[9], (L, MLA_KV_RANK, MLA_HEADS * (MLA_NOPE + MLA_V)), MLA_KV_RANK),
        'mla_q_norm': gain(ks[10], (L, MLA_QK)),
        'mla_k_norm': gain(ks[11], (L, MLA_QK)),
        'gdn_conv': nrm(ks[12], (L, GDN_CONV, GDN_HEADS * (2 * GDN_DK + GDN_DV)), GDN_CONV),
        'gdn_a_log': jnp.log(jax.random.uniform(ks[13], (L, GDN_HEADS), f32, 1.0, 16.0)),
        'gdn_dt_bias': dt + jnp.log(-jnp.expm1(-dt)),
        'gdn_out_norm': gain(ks[15], (L, GDN_DV)),
        'moba_q_norm': gain(ks[16], (L, MOBA_DH)),
        'moba_k_norm': gain(ks[17], (L, MOBA_DH)),
        'w_branch': nrm(ks[18], (L, N_BRANCH, MIX_WIDTH, D_MODEL), MIX_WIDTH),
        'w_out': nrm(ks[19], (L, D_MODEL, D_MODEL), D_MODEL),
        'ffb_norm': gain(ks[20], (L, D_MODEL)),
        'ffb_w_in': nrm(ks[21], (L, D_MODEL, 2 * D_FF), D_MODEL),
        'ffb_w_out': nrm(ks[22], (L, D_FF, D_MODEL), D_FF),
    }


def reference(x, ffa_norm, ffa_w_in, ffa_w_out, mix_norm, w_in, mla_cq_norm, mla_ckv_norm,
              mla_w_uq, mla_w_ukv, mla_q_norm, mla_k_norm, gdn_conv, gdn_a_log, gdn_dt_bias,
              gdn_out_norm, moba_q_norm, moba_k_norm, w_branch, w_out, ffb_norm, ffb_w_in,
              ffb_w_out):
    b, s, _ = x.shape
    pos = jnp.arange(s, dtype=jnp.int32)
    for l in range(DEPTH):
        x = x + 0.5 * swiglu(rms_norm(x, ffa_norm[l]), ffa_w_in[l], ffa_w_out[l])
        h = rms_norm(x, mix_norm[l])
        (c_q, c_kv, k_rope, g_q, g_k, g_v, g_b, g_a, g_z, m_qkv, gate_logits) = split_cols(h @ w_in[l], IN_SPLITS)
        o_mla = mla_branch(c_q, c_kv, k_rope, pos, mla_cq_norm[l], mla_ckv_norm[l], mla_w_uq[l],
                           mla_w_ukv[l], mla_q_norm[l], mla_k_norm[l])
        o_gdn = gdn_branch(g_q, g_k, g_v, g_b, g_a, g_z, gdn_conv[l], gdn_a_log[l], gdn_dt_bias[l],
                           gdn_out_norm[l])
        o_moba = moba_branch(m_qkv, pos, moba_q_norm[l], moba_k_norm[l])
        branches = jnp.stack([o_mla, o_gdn, o_moba], axis=2)
        up = jnp.einsum('bsnw,nwd->bsnd', branches, w_branch[l])
        gates = jax.nn.sigmoid(gate_logits.reshape(b, s, N_BRANCH, D_MODEL))
        x = x + jnp.sum(gates * up, axis=2) @ w_out[l]
        x = x + 0.5 * swiglu(rms_norm(x, ffb_norm[l]), ffb_w_in[l], ffb_w_out[l])
    return x
```

```python
import ml_dtypes
from concourse.bass_utils import run_bass_kernel_spmd


import numpy as np
import concourse.bass as bass
import concourse.mybir as mybir

F32 = mybir.dt.float32
BF16 = mybir.dt.bfloat16
AF = mybir.ActivationFunctionType
ALU = mybir.AluOpType
AX = mybir.AxisListType

ENGS = ("tensor", "vector", "scalar", "gpsimd", "sync")


class Buf:
    __slots__ = ("name", "last_w", "readers", "excl")

    def __init__(self, name, excl=False):
        self.name = name
        self.last_w = None
        self.readers = []
        self.excl = excl


class Op:
    __slots__ = ("eng", "fn", "idx", "deps", "signal", "dsem", "dord", "count")

    def __init__(self, eng, fn, idx):
        self.eng = eng
        self.fn = fn
        self.idx = idx
        self.deps = []
        self.signal = False
        self.dsem = None
        self.dord = 0
        self.count = 0


class DSem:
    def __init__(self, name):
        self.name = name
        self.n = 0
        self.handle = None


class Prog:
    def __init__(self, nc):
        self.nc = nc
        self.ops = {e: [] for e in ENGS}
        self.dsems = []
        self.nbuf = 0

    def buf(self, name=None, excl=False):
        self.nbuf += 1
        return Buf(name or f"b{self.nbuf}", excl)

    def dsem(self, name=None):
        d = DSem(name or f"d{len(self.dsems)}")
        self.dsems.append(d)
        return d

    def _deps(self, op, reads, writes):
        deps = op.deps
        for b in reads:
            if b.excl:
                writes = list(writes) + [b]
                continue
            if b.last_w is not None:
                deps.append(b.last_w)
            b.readers.append(op)
        for b in writes:
            if b.last_w is not None:
                deps.append(b.last_w)
            deps.extend(r for r in b.readers if r is not op)
            b.readers = []
            b.last_w = op

    def op(self, eng, fn, reads=(), writes=()):
        o = Op(eng, fn, len(self.ops[eng]))
        self.ops[eng].append(o)
        self._deps(o, reads, writes)
        return o

    def dma(self, eng, dsem, out, in_, reads=(), writes=()):
        o = Op(eng, ("dma", out, in_), len(self.ops[eng]))
        dsem.n += 1
        o.dsem = dsem
        o.dord = dsem.n
        self.ops[eng].append(o)
        self._deps(o, reads, writes)
        return o

    def mm(self, out, lhsT, rhs, start, stop, reads=(), writes=()):
        return self.op("tensor", lambda e: e.matmul(out, lhsT, rhs, start=start, stop=stop),
                       reads, writes)

    def emit(self, final_waits=()):
        nc = self.nc
        for e in ENGS:
            for o in self.ops[e]:
                for d in o.deps:
                    if d.dsem is None:
                        if d.eng == "tensor" and o.eng == "tensor":
                            continue
                        d.signal = True
        esem = {e: nc.alloc_semaphore(f"sem_{e}") for e in ENGS}
        for d in self.dsems:
            if d.n:
                d.handle = nc.alloc_semaphore(f"dsem_{d.name}")
        for e in ENGS:
            c = 0
            for o in self.ops[e]:
                if o.dsem is None and o.signal:
                    c += 1
                    o.count = c
        stats = {}
        with nc.Block() as block:
            def run(ename, eng):
                waited = {}
                nwait = 0
                for o in self.ops[ename]:
                    need = {}
                    for d in o.deps:
                        if d.dsem is not None:
                            key = ("d", id(d.dsem)); sem = d.dsem.handle; val = 16 * d.dord
                        else:
                            if d.eng == "tensor" and ename == "tensor":
                                continue
                            key = ("e", d.eng); sem = esem[d.eng]; val = d.count
                        if need.get(key, (None, -1))[1] < val:
                            need[key] = (sem, val)
                    for key, (sem, val) in need.items():
                        if waited.get(key, -1) >= val:
                            continue
                        eng.wait_ge(sem, val)
                        waited[key] = val
                        nwait += 1
                    if o.dsem is not None:
                        _, out, in_ = o.fn
                        eng.dma_start(out=out, in_=in_).then_inc(o.dsem.handle, 16)
                    else:
                        ins = o.fn(eng)
                        if o.signal:
                            ins.then_inc(esem[ename], 1)
                for (kind, obj) in final_waits:
                    if ename != kind:
                        continue
                    eng.wait_ge(obj.handle, 16 * obj.n)
                stats[ename] = (len(self.ops[ename]), nwait)

            @block.tensor
            def _(eng):
                run("tensor", eng)

            @block.vector
            def _(eng):
                run("vector", eng)

            @block.scalar
            def _(eng):
                run("scalar", eng)

            @block.gpsimd
            def _(eng):
                run("gpsimd", eng)

            @block.sync
            def _(eng):
                run("sync", eng)
        return stats


T = 2048
D = 1024
DFF = 2816
NJ = DFF // 128
EPS = 1e-6


def build_ffn():
    nc = bass.Bass("TRN2", target_bir_lowering=False)
    xT = nc.dram_tensor("xT", [D, T], F32, kind="ExternalInput").ap()
    gd = nc.dram_tensor("g", [128, 8], F32, kind="ExternalInput").ap()
    w1d = nc.dram_tensor("w1", [NJ, 128, 2048], F32, kind="ExternalInput").ap()
    w2d = nc.dram_tensor("w2", [8, 128, NJ * 128], F32, kind="ExternalInput").ap()
    onesd = nc.dram_tensor("ones", [128, 128], F32, kind="ExternalInput").ap()
    yT = nc.dram_tensor("yT", [D, T], F32, kind="ExternalOutput").ap()
    P = Prog(nc)
    A = nc.alloc_sbuf_tensor
    ones = A("ones_sb", [128, 128], BF16); b_ones = P.buf()
    g = A("g_sb", [128, 8], F32); b_g = P.buf()
    xin = [A(f"xin{i}", [128, 8, 512], F32) for i in range(2)]; b_xin = [P.buf() for _ in range(2)]
    sq = [A(f"sq{i}", [128, 512], BF16) for i in range(2)]; b_sq = [P.buf() for _ in range(2)]
    lnb = A("lnb", [128, 512], F32); b_ln = P.buf()
    rstd = A("rstd", [128, 512], F32); b_rstd = P.buf()
    hT = A("hT", [128, 8, 1024], BF16); b_h = [P.buf() for _ in range(2)]
    actT = A("actT", [128, NJ, 1024], BF16); b_act = [P.buf() for _ in range(2)]
    w1 = [A(f"w1_{i}", [128, 2048], BF16) for i in range(2)]; b_w1 = [P.buf() for _ in range(2)]
    w2 = [A(f"w2_{i}", [128, NJ * 128], BF16) for i in range(2)]; b_w2 = [P.buf() for _ in range(2)]
    sg = [A(f"sg{i}", [128, 512], F32) for i in range(2)]; b_sg = [P.buf() for _ in range(2)]
    xres = [A(f"xres{i}", [128, 512], F32) for i in range(2)]; b_xres = [P.buf() for _ in range(2)]
    yo = [A(f"yo{i}", [128, 512], F32) for i in range(2)]; b_yo = [P.buf() for _ in range(2)]
    PS = nc.alloc_psum_tensor
    pS = PS("pS", [128, 512], F32); b_pS = P.buf(excl=True)
    pG = [PS(f"pG{i}", [128, 512], F32) for i in range(2)]; b_pG = [P.buf(excl=True) for _ in range(2)]
    pU = [PS(f"pU{i}", [128, 512], F32) for i in range(2)]; b_pU = [P.buf(excl=True) for _ in range(2)]
    pO = [PS(f"pO{i}", [128, 512], F32) for i in range(2)]; b_pO = [P.buf(excl=True) for _ in range(2)]
    d_c = P.dsem(); d_g = P.dsem()
    d_xin = [P.dsem() for _ in range(2)]
    d_w1 = [P.dsem() for _ in range(2)]; d_w2 = [P.dsem() for _ in range(2)]
    d_xres = [P.dsem() for _ in range(2)]; d_out = [P.dsem() for _ in range(2)]
    b_ydram = P.buf()

    P.dma("gpsimd", d_c, ones[:], onesd[:, :], writes=[b_ones])
    P.dma("sync", d_g, g[:], gd[:, :], writes=[b_g])
    xT_v = xT.rearrange("(kc p) n -> p kc n", p=128)
    for hh in range(2):
        t0 = hh * 1024
        for tt in range(2):
            tok = t0 + tt * 512
            xi = xin[tt]; bxi = b_xin[tt]
            P.dma("sync", d_xin[tt], xi[:], xT_v[:, :, tok:tok + 512], writes=[bxi])
            for kc in range(8):
                s = sq[kc % 2]; bs = b_sq[kc % 2]
                P.op("scalar", lambda e, s=s, xi=xi, kc=kc: e.activation(out=s[:], in_=xi[:, kc, :], func=AF.Square),
                     reads=[bxi], writes=[bs])
                P.mm(pS[:], ones[:], s[:], kc == 0, kc == 7, reads=[b_ones, bs], writes=[b_pS])
            P.op("scalar", lambda e: e.activation(out=lnb[:], in_=pS[:], func=AF.Ln, scale=1.0 / D, bias=EPS),
                 reads=[b_pS], writes=[b_ln])
            P.op("scalar", lambda e: e.activation(out=rstd[:], in_=lnb[:], func=AF.Exp, scale=-0.5),
                 reads=[b_ln], writes=[b_rstd])
            for kc in range(8):
                P.op("vector", lambda e, xi=xi, kc=kc, tt=tt: e.scalar_tensor_tensor(
                    out=hT[:, kc, tt * 512:(tt + 1) * 512], in0=xi[:, kc, :], scalar=g[:, kc:kc + 1], in1=rstd[:],
                    op0=ALU.mult, op1=ALU.mult), reads=[bxi, b_g, b_rstd], writes=[b_h[tt]])
        for j in range(NJ):
            w = w1[j % 2]; bw = b_w1[j % 2]
            P.dma("gpsimd", d_w1[j % 2], w[:], w1d[j, :, :], writes=[bw])
            for tt in range(2):
                hs = lambda kc, tt=tt: hT[:, kc, tt * 512:(tt + 1) * 512]
                for kc in range(8):
                    P.mm(pG[tt][:], w[:, kc * 128:(kc + 1) * 128], hs(kc), kc == 0, kc == 7,
                         reads=[bw, b_h[tt]], writes=[b_pG[tt]])
                for kc in range(8):
                    P.mm(pU[tt][:], w[:, 1024 + kc * 128:1024 + (kc + 1) * 128], hs(kc), kc == 0, kc == 7,
                         reads=[bw, b_h[tt]], writes=[b_pU[tt]])
                P.op("scalar", lambda e, tt=tt: e.activation(out=sg[tt][:], in_=pG[tt][:], func=AF.Silu),
                     reads=[b_pG[tt]], writes=[b_sg[tt]])
                P.op("vector", lambda e, tt=tt, j=j: e.tensor_tensor(
                    out=actT[:, j, tt * 512:(tt + 1) * 512], in0=sg[tt][:], in1=pU[tt][:], op=ALU.mult),
                    reads=[b_sg[tt], b_pU[tt]], writes=[b_act[tt]])
        for c in range(8):
            w = w2[c % 2]; bw = b_w2[c % 2]
            P.dma("gpsimd", d_w2[c % 2], w[:], w2d[c, :, :], writes=[bw])
            for tt in range(2):
                tok = t0 + tt * 512
                for j in range(NJ):
                    P.mm(pO[tt][:], w[:, j * 128:(j + 1) * 128], actT[:, j, tt * 512:(tt + 1) * 512], j == 0, j == NJ - 1,
                         reads=[bw, b_act[tt]], writes=[b_pO[tt]])
                P.dma("sync", d_xres[tt], xres[tt][:], xT[c * 128:(c + 1) * 128, tok:tok + 512], writes=[b_xres[tt]])
                P.op("vector", lambda e, tt=tt: e.scalar_tensor_tensor(
                    out=yo[tt][:], in0=pO[tt][:], scalar=0.5, in1=xres[tt][:], op0=ALU.mult, op1=ALU.add),
                    reads=[b_pO[tt], b_xres[tt]], writes=[b_yo[tt]])
                P.dma("sync", d_out[tt], yT[c * 128:(c + 1) * 128, tok:tok + 512], yo[tt][:], reads=[b_yo[tt]])
    st = P.emit(final_waits=[("sync", d_out[0]), ("sync", d_out[1])])
    return nc


def ffn_host_inputs(x_flat, norm, w_in, w_out):
    g = np.ascontiguousarray(norm.reshape(8, 128).T)
    wi = w_in.reshape(8, 128, 2, NJ, 128)
    w1 = np.ascontiguousarray(wi.transpose(3, 1, 2, 0, 4)).reshape(NJ, 128, 2048)
    wo = w_out.reshape(NJ, 128, 8, 128)
    w2 = np.ascontiguousarray(wo.transpose(2, 1, 0, 3)).reshape(8, 128, NJ * 128)
    ones = np.ones((128, 128), np.float32)
    maps = []
    for c in range(8):
        xT = np.ascontiguousarray(x_flat[c * T:(c + 1) * T].T)
        maps.append({"xT": xT, "g": g, "w1": w1, "w2": w2, "ones": ones})
    return maps


T = 2048
D = 1024
EPS = 1e-6
NCH = 25
ROPE_THETA = 10000.0
NTILES = 4

O_CQ, O_CKV, O_KR, O_GQ, O_GK, O_GV, O_GB, O_GA, O_GZ, O_MQ, O_MK, O_MV, O_GATE = 0, 256, 384, 416, 928, 1440, 1952, 1956, 1960, 2472, 2984, 3496, 4008
CH_COLS = ([(O_CQ, 128), (O_CQ + 128, 128), (O_CKV, 128), (O_KR, 32)] +
           [(O_GQ + h * 128, 128) for h in range(4)] + [(O_GK + h * 128, 128) for h in range(4)] +
           [(O_GV + h * 128, 128) for h in range(4)] + [(O_GB, 8)] +
           [(O_MQ + h * 128, 128) for h in range(4)] + [(O_MK + h * 128, 128) for h in range(4)])


def build_projb():
    nc = bass.Bass("TRN2", target_bir_lowering=False)
    DI = lambda n, s, dt=F32: nc.dram_tensor(n, s, dt, kind="ExternalInput").ap()
    DO = lambda n, s, dt=F32: nc.dram_tensor(n, s, dt, kind="ExternalOutput").ap()
    xT = DI("xT", [D, T]); gains_d = DI("gains", [128, 16])
    wb1 = DI("wb1", [NCH, 128, 1024]); wz_d = DI("wz", [128, 4096]); wmv_d = DI("wmv", [128, 4096])
    wuq_d = DI("wuq", [128, 1536]); wuk_d = DI("wuk", [128, 768]); wuv_d = DI("wuv", [128, 512])
    cmat = DI("cmat", [5, 128, 128])
    rope_d = DI("rope", [4, 128, T])
    o_mq = DO("mla_qT", [8, 96, T], BF16); o_mk = DO("mla_kT", [8, 96, T], BF16); o_mv = DO("mla_v", [T, 512], BF16)
    o_oq = DO("mo_qT", [4, 128, T], BF16); o_ok = DO("mo_kT", [4, 128, T], BF16); o_ov = DO("mo_v", [T, 512], BF16)
    o_gr = DO("graw", [12, 128, T]); o_gba = DO("gba", [8, T]); o_z = DO("z", [T, 512])
    P = Prog(nc)
    A = nc.alloc_sbuf_tensor
    PSA = nc.alloc_psum_tensor

    def sb(name, shape, dt=F32):
        return A("s_" + name, shape, dt), P.buf(name)

    def load(eng, t, b, src):
        P.dma(eng, P.dsem(), t, src, writes=[b])

    gains, b_gains = sb("gains", [128, 16]); load("sync", gains[:], b_gains, gains_d[:, :])
    CM, b_CM = sb("CM", [128, 5, 128], BF16); load("gpsimd", CM[:], b_CM, cmat.rearrange("k p n -> p k n"))
    ONES = CM[:, 0, :]; BLK64 = CM[:, 1, :]; RM_MLA = CM[0:96, 2, 0:96]; RM_MO = CM[:, 3, :]; SEL = CM[0:32, 4, 0:96]
    rope, b_rope = sb("rope", [128, 4, T]); load("sync", rope[:], b_rope, rope_d.rearrange("k p n -> p k n"))
    wz, b_wz = sb("wz", [128, 4096], BF16); load("gpsimd", wz[:], b_wz, wz_d[:, :])
    wmv, b_wmv = sb("wmv", [128, 4096], BF16); load("gpsimd", wmv[:], b_wmv, wmv_d[:, :])
    wuq, b_wuq = sb("wuq", [128, 1536], BF16); load("gpsimd", wuq[:], b_wuq, wuq_d[:, :])
    wuk, b_wuk = sb("wuk", [128, 768], BF16); load("gpsimd", wuk[:], b_wuk, wuk_d[:, :])
    wuv, b_wuv = sb("wuv", [128, 512], BF16); load("gpsimd", wuv[:], b_wuv, wuv_d[:, :])
    xins = [sb(f"xin{i}", [128, 8, 512]) for i in range(2)]; d_xins = [P.dsem() for _ in range(2)]
    sq = [sb(f"sq{i}", [128, 512], BF16) for i in range(2)]
    lnb, b_ln = sb("lnb", [128, 512]); rstd, b_rstd = sb("rstd", [128, 512])
    hT, b_h = sb("hT", [128, 8, T], BF16)
    wch = [sb(f"wch{i}", [128, 1024], BF16) for i in range(3)]; d_wch = [P.dsem() for _ in range(3)]
    cq = [sb(f"cq{i}", [128, T]) for i in range(3)]
    cqn = [sb(f"cqn{i}", [128, T], BF16) for i in range(3)]
    krope, b_krope = sb("krope", [32, T], BF16)
    NF = 4; NB = 6
    stf = [sb(f"stf{i}", [128, 512]) for i in range(NF)]; d_stf = [P.dsem() for _ in range(NF)]
    stb = [sb(f"stb{i}", [128, 512], BF16) for i in range(NB)]; d_stb = [P.dsem() for _ in range(NB)]
    cf = [0]; cb = [0]
    sqv = [sb(f"sqv{i}", [128, 512], BF16) for i in range(2)]
    lnv = [sb(f"lnv{i}", [128, 512]) for i in range(2)]
    rsv = [sb(f"rsv{i}", [128, 512]) for i in range(2)]
    qn = [sb(f"qn{i}", [128, 512], BF16) for i in range(2)]
    t1 = [sb(f"t1{i}", [128, 512]) for i in range(2)]
    t2 = [sb(f"t2{i}", [128, 512]) for i in range(2)]
    pS = (PSA("pS", [128, 512], F32), P.buf(excl=True))
    pP = [(PSA(f"pP{i}", [128, 512], F32), P.buf(excl=True)) for i in range(2)]
    pN = [(PSA(f"pN{i}", [128, 512], F32), P.buf(excl=True)) for i in range(2)]
    pR = [(PSA(f"pR{i}", [128, 512], F32), P.buf(excl=True)) for i in range(2)]
    pT = (PSA("pT", [128, 512], F32), P.buf(excl=True))
    cp = [0]; cr = [0]
    all_out = []

    def out_f32(src_ps, b_ps, R, dst, eng="scalar"):
        i = cf[0] % NF; cf[0] += 1
        s, b_s = stf[i]
        if eng == "scalar":
            P.op("scalar", lambda e: e.copy(out=s[0:R, :], in_=src_ps), reads=[b_ps], writes=[b_s])
        else:
            P.op("vector", lambda e: e.tensor_copy(out=s[0:R, :], in_=src_ps), reads=[b_ps], writes=[b_s])
        P.dma("sync", d_stf[i], dst, s[0:R, :], reads=[b_s])

    def out_bf(src_ps, b_ps, R, dst, eng="vector"):
        i = cb[0] % NB; cb[0] += 1
        s, b_s = stb[i]
        if eng == "scalar":
            P.op("scalar", lambda e: e.copy(out=s[0:R, :], in_=src_ps), reads=[b_ps], writes=[b_s])
        else:
            P.op("vector", lambda e: e.tensor_copy(out=s[0:R, :], in_=src_ps), reads=[b_ps], writes=[b_s])
        P.dma("sync", d_stb[i], dst, s[0:R, :], reads=[b_s])

    def norm_rope(ps, b_ps, R, gcol, onesm, rm, kcos, dim, tok, dst):
        k = cr[0] % 2; cr[0] += 1
        s_, b_s = sqv[k]; l_, b_l = lnv[k]; r_, b_r = rsv[k]; q_, b_q = qn[k]; a_, b_a = t1[k]; c_, b_c = t2[k]
        pn, b_pn = pN[k]; pr, b_pr = pR[k]
        P.op("scalar", lambda e: e.activation(out=s_[0:R, :], in_=ps, func=AF.Square), reads=[b_ps], writes=[b_s])
        P.mm(pn[0:R, :], onesm, s_[0:R, :], True, True, reads=[b_CM, b_s], writes=[b_pn])
        P.op("scalar", lambda e: e.activation(out=l_[0:R, :], in_=pn[0:R, :], func=AF.Ln, scale=1.0 / dim, bias=EPS), reads=[b_pn], writes=[b_l])
        P.op("scalar", lambda e: e.activation(out=r_[0:R, :], in_=l_[0:R, :], func=AF.Exp, scale=-0.5), reads=[b_l], writes=[b_r])
        P.op("vector", lambda e: e.scalar_tensor_tensor(out=q_[0:R, :], in0=ps, scalar=gains[0:R, gcol:gcol + 1], in1=r_[0:R, :], op0=ALU.mult, op1=ALU.mult),
             reads=[b_ps, b_gains, b_r], writes=[b_q])
        P.mm(pr[0:R, :], rm, q_[0:R, :], True, True, reads=[b_CM, b_q], writes=[b_pr])
        P.op("gpsimd", lambda e: e.tensor_tensor(out=a_[0:R, :], in0=q_[0:R, :], in1=rope[0:R, kcos, tok:tok + 512], op=ALU.mult),
             reads=[b_q, b_rope], writes=[b_a])
        P.op("vector", lambda e: e.tensor_tensor(out=c_[0:R, :], in0=pr[0:R, :], in1=rope[0:R, kcos + 1, tok:tok + 512], op=ALU.mult),
             reads=[b_pr, b_rope], writes=[b_c])
        i = cb[0] % NB; cb[0] += 1
        s, b_sb = stb[i]
        P.op("gpsimd", lambda e: e.tensor_tensor(out=s[0:R, :], in0=a_[0:R, :], in1=c_[0:R, :], op=ALU.add), reads=[b_a, b_c], writes=[b_sb])
        P.dma("sync", d_stb[i], dst, s[0:R, :], reads=[b_sb])

    xT_v = xT.rearrange("(kc p) n -> p kc n", p=128)
    for tt in range(NTILES):
        tok = tt * 512
        ts = slice(tok, tok + 512)
        xin, b_xin = xins[tt % 2]
        P.dma("sync", d_xins[tt % 2], xin[:], xT_v[:, :, ts], writes=[b_xin])
        for kc in range(8):
            s, bs = sq[kc % 2]
            P.op("scalar", lambda e, s=s, kc=kc, xin=xin: e.activation(out=s[:], in_=xin[:, kc, :], func=AF.Square), reads=[b_xin], writes=[bs])
            P.mm(pS[0][:], ONES, s[:], kc == 0, kc == 7, reads=[b_CM, bs], writes=[pS[1]])
        P.op("scalar", lambda e: e.activation(out=lnb[:], in_=pS[0][:], func=AF.Ln, scale=1.0 / D, bias=EPS), reads=[pS[1]], writes=[b_ln])
        P.op("scalar", lambda e: e.activation(out=rstd[:], in_=lnb[:], func=AF.Exp, scale=-0.5), reads=[b_ln], writes=[b_rstd])
        for kc in range(8):
            P.op("vector", lambda e, kc=kc, xin=xin, ts=ts: e.scalar_tensor_tensor(out=hT[:, kc, ts], in0=xin[:, kc, :], scalar=gains[:, kc:kc + 1], in1=rstd[:], op0=ALU.mult, op1=ALU.mult),
                 reads=[b_xin, b_gains, b_rstd], writes=[b_h])
    for ch in range(NCH):
        w, bw = wch[ch % 3]
        P.dma("gpsimd", d_wch[ch % 3], w[:], wb1[ch, :, :], writes=[bw])
        M = CH_COLS[ch][1]
        for tt in range(NTILES):
            tok = tt * 512
            ts = slice(tok, tok + 512)
            pp, b_pp = pP[cp[0] % 2]; cp[0] += 1
            for kc in range(8):
                P.mm(pp[0:M, :], w[:, kc * 128:kc * 128 + M], hT[:, kc, ts], kc == 0, kc == 7, reads=[bw, b_h], writes=[b_pp])
            if ch < 3:
                c_, b_c = cq[ch]
                P.op("scalar", lambda e, c_=c_, pp=pp, ts=ts: e.copy(out=c_[:, ts], in_=pp[:]), reads=[b_pp], writes=[b_c])
            elif ch == 3:
                P.op("vector", lambda e, pp=pp, ts=ts: e.tensor_copy(out=krope[:, ts], in_=pp[0:32, :]), reads=[b_pp], writes=[b_krope])
            elif ch < 16:
                out_f32(pp[:], b_pp, 128, o_gr[ch - 4, :, ts], eng="scalar" if (ch + tt) % 2 else "vector")
            elif ch == 16:
                out_f32(pp[0:8, :], b_pp, 8, o_gba[:, ts])
            elif ch < 21:
                norm_rope(pp[:], b_pp, 128, 13, BLK64, RM_MO, 2, 64.0, tok, o_oq[ch - 17, :, ts])
            else:
                norm_rope(pp[:], b_pp, 128, 14, BLK64, RM_MO, 2, 64.0, tok, o_ok[ch - 21, :, ts])
    for tt in range(NTILES):
        ts = slice(tt * 512, (tt + 1) * 512)
        for grp, (idxs, dim, gc) in enumerate([((0, 1), 256.0, 8), ((2,), 128.0, 10)]):
            pn, b_pn = pN[cr[0] % 2]; k = cr[0] % 2; cr[0] += 1
            for n_, ci in enumerate(idxs):
                s, bs = sq[n_ % 2]
                P.op("scalar", lambda e, s=s, ci=ci, ts=ts: e.activation(out=s[:], in_=cq[ci][0][:, ts], func=AF.Square), reads=[cq[ci][1]], writes=[bs])
                P.mm(pn[:], ONES, s[:], n_ == 0, n_ == len(idxs) - 1, reads=[b_CM, bs], writes=[b_pn])
            l_, b_l = lnv[k]; r_, b_r = rsv[k]
            P.op("scalar", lambda e, l_=l_, pn=pn, dim=dim: e.activation(out=l_[:], in_=pn[:], func=AF.Ln, scale=1.0 / dim, bias=EPS), reads=[b_pn], writes=[b_l])
            P.op("scalar", lambda e, l_=l_, r_=r_: e.activation(out=r_[:], in_=l_[:], func=AF.Exp, scale=-0.5), reads=[b_l], writes=[b_r])
            for n_, ci in enumerate(idxs):
                P.op("vector", lambda e, ci=ci, r_=r_, gc=gc, n_=n_, ts=ts: e.scalar_tensor_tensor(out=cqn[ci][0][:, ts], in0=cq[ci][0][:, ts], scalar=gains[:, gc + n_:gc + n_ + 1], in1=r_[:], op0=ALU.mult, op1=ALU.mult),
                     reads=[cq[ci][1], b_gains, b_r], writes=[cqn[ci][1]])
    for h in range(8):
        for tt in range(NTILES):
            tok = tt * 512
            ts = slice(tok, tok + 512)
            pp, b_pp = pP[cp[0] % 2]; cp[0] += 1
            for kc in range(2):
                P.mm(pp[0:96, :], wuq[:, kc * 768 + h * 96:kc * 768 + (h + 1) * 96], cqn[kc][0][:, ts], kc == 0, kc == 1, reads=[b_wuq, cqn[kc][1]], writes=[b_pp])
            norm_rope(pp[0:96, :], b_pp, 96, 11, ONES[0:96, 0:96], RM_MLA, 0, 96.0, tok, o_mq[h, :, ts])
            pp, b_pp = pP[cp[0] % 2]; cp[0] += 1
            P.mm(pp[0:96, :], wuk[:, h * 96:(h + 1) * 96], cqn[2][0][:, ts], True, False, reads=[b_wuk, cqn[2][1]], writes=[b_pp])
            P.mm(pp[0:96, :], SEL, krope[:, ts], False, True, reads=[b_CM, b_krope], writes=[b_pp])
            norm_rope(pp[0:96, :], b_pp, 96, 12, ONES[0:96, 0:96], RM_MLA, 0, 96.0, tok, o_mk[h, :, ts])
    for grp in range(4 * NTILES):
        gs = slice(grp * 128, (grp + 1) * 128)
        rows = gs
        P.mm(pT[0][:], cqn[2][0][:, gs], wuv[:], True, True, reads=[cqn[2][1], b_wuv], writes=[pT[1]])
        out_bf(pT[0][:], pT[1], 128, o_mv[rows, :], eng="scalar")
        for kc in range(8):
            P.mm(pT[0][:], hT[:, kc, gs], wz[:, kc * 512:(kc + 1) * 512], kc == 0, kc == 7, reads=[b_h, b_wz], writes=[pT[1]])
        out_f32(pT[0][:], pT[1], 128, o_z[rows, :], eng="vector")
        for kc in range(8):
            P.mm(pT[0][:], hT[:, kc, gs], wmv[:, kc * 512:(kc + 1) * 512], kc == 0, kc == 7, reads=[b_h, b_wmv], writes=[pT[1]])
        out_bf(pT[0][:], pT[1], 128, o_ov[rows, :], eng="scalar")
    st = P.emit(final_waits=[("sync", d) for d in d_stf + d_stb])
    return nc


def projb_consts():
    ones = np.ones((128, 128), np.float32)
    t = np.arange(128)
    blk64 = ((t[:, None] // 64) == (t[None, :] // 64)).astype(np.float32)
    rm_mla = np.zeros((128, 128), np.float32)
    for m in range(64, 80):
        rm_mla[m + 16, m] = -1.0
    for m in range(80, 96):
        rm_mla[m - 16, m] = 1.0
    rm_mo = np.zeros((128, 128), np.float32)
    for base in (0, 64):
        for m in range(base, base + 32):
            rm_mo[m + 32, m] = -1.0
        for m in range(base + 32, base + 64):
            rm_mo[m - 32, m] = 1.0
    sel = np.zeros((128, 128), np.float32)
    for i in range(32):
        sel[i, 64 + i] = 1.0
    return np.stack([ones, blk64, rm_mla, rm_mo, sel], 0)


def rope_tables(pos):
    pos = pos.astype(np.float32)
    out = np.zeros((4, 128, len(pos)), np.float32)
    out[0] = 1.0; out[2] = 1.0
    inv = (ROPE_THETA ** (-np.arange(16, dtype=np.float32) * 2.0 / 32)).astype(np.float32)
    ang = pos[None, :] * inv[:, None]
    for r in range(64, 96):
        i = (r - 64) % 16
        out[0, r] = np.cos(ang[i]); out[1, r] = np.sin(ang[i])
    out[0, 96:] = 0
    inv = (ROPE_THETA ** (-np.arange(32, dtype=np.float32) * 2.0 / 64)).astype(np.float32)
    ang = pos[None, :] * inv[:, None]
    for r in range(128):
        i = (r % 64) % 32
        out[2, r] = np.cos(ang[i]); out[3, r] = np.sin(ang[i])
    return out


def projb_weights(mix_norm, w_in, cq_norm, ckv_norm, w_uq, w_ukv, q_norm, k_norm, mq_norm, mk_norm):
    gains = np.zeros((128, 16), np.float32)
    gains[:, 0:8] = mix_norm.reshape(8, 128).T
    gains[:, 8:10] = cq_norm.reshape(2, 128).T
    gains[:, 10] = ckv_norm
    gains[0:96, 11] = q_norm; gains[0:96, 12] = k_norm
    gains[:, 13] = np.tile(mq_norm, 2); gains[:, 14] = np.tile(mk_norm, 2)
    wb1 = np.zeros((NCH, 128, 8, 128), np.float32)
    wr = w_in.reshape(8, 128, -1)
    for ch, (c0, m) in enumerate(CH_COLS):
        wb1[ch, :, :, 0:m] = wr[:, :, c0:c0 + m].transpose(1, 0, 2)
    wb1 = wb1.reshape(NCH, 128, 1024)
    wz = np.ascontiguousarray(wr[:, :, O_GZ:O_GZ + 512].transpose(1, 0, 2)).reshape(128, 4096)
    wmv = np.ascontiguousarray(wr[:, :, O_MV:O_MV + 512].transpose(1, 0, 2)).reshape(128, 4096)
    wuq = np.ascontiguousarray(w_uq.reshape(2, 128, 768).transpose(1, 0, 2)).reshape(128, 1536)
    kv = w_ukv.reshape(128, 8, 128)
    wuk = np.zeros((128, 8, 96), np.float32); wuk[:, :, 0:64] = kv[:, :, 0:64]
    wuv = np.ascontiguousarray(kv[:, :, 64:128]).reshape(128, 512)
    return {"gains": gains, "wb1": wb1, "wz": wz, "wmv": wmv, "wuq": wuq, "wuk": wuk.reshape(128, 768), "wuv": wuv, "cmat": projb_consts()}

S = 8192
NQT = 16
HEADS = (0, 1, 2, 3)


def attn_consts():
    keys = np.arange(S)
    blkoh = (keys[None, :] // 256 == np.arange(32)[:, None]).astype(np.float32)
    p = np.arange(128)[:, None]; j = np.arange(512)[None, :]
    cmask = np.stack([((128 * d + p) <= j).astype(np.float32) for d in range(4)], 0)
    return blkoh, cmask, np.eye(128, dtype=np.float32), np.ones((128, 64), np.float32)


def build_attn():
    nc = bass.Bass("TRN2", target_bir_lowering=False)
    DI = lambda n, s, dt=F32: nc.dram_tensor(n, s, dt, kind="ExternalInput").ap()
    mq = DI("mq", [2, 96, S], BF16); mk = DI("mk", [2, 96, S], BF16); mv = DI("mv", [S, 128], BF16)
    oq = DI("oq", [128, S], BF16); ok = DI("ok", [128, S], BF16); ov = DI("ov", [S, 128], BF16)
    blkoh_d = DI("blkoh", [32, S]); cmask_d = DI("cmask", [4, 128, 512]); ident_d = DI("ident", [128, 128]); onesf_d = DI("onesf", [128, 64])
    oT = nc.dram_tensor("oT", [4, 64, S], BF16, kind="ExternalOutput").ap()
    P = Prog(nc)
    A = nc.alloc_sbuf_tensor
    PSA = nc.alloc_psum_tensor

    def sb(name, shape, dt=F32):
        return A("s_" + name, shape, dt), P.buf(name)

    cmask, b_cm = sb("cmask", [128, 4, 512], BF16); P.dma("gpsimd", P.dsem(), cmask[:], cmask_d.rearrange("k p n -> p k n"), writes=[b_cm])
    ident, b_id = sb("ident", [128, 128], BF16); P.dma("gpsimd", P.dsem(), ident[:], ident_d[:, :], writes=[b_id])
    onesf, b_of = sb("onesf", [128, 64]); P.dma("sync", P.dsem(), onesf[:], onesf_d[:, :], writes=[b_of])
    Ka = [sb(f"Ka{i}", [128, S], BF16) for i in range(2)]
    Qa = [sb(f"Qa{i}", [128, S], BF16) for i in range(2)]
    Va = [sb(f"Va{i}", [128, 64, 65], BF16) for i in range(2)]
    d_K = [P.dsem() for _ in range(2)]; d_Q = [P.dsem() for _ in range(2)]; d_V = [P.dsem() for _ in range(2)]
    d_K2 = [P.dsem() for _ in range(2)]
    kmf, b_kmf = sb("kmf", [128, 32]); kmT, b_kmT = sb("kmT", [128, 32], BF16)
    gm = [sb(f"gm{i}", [128, 32]) for i in range(4)]
    top8 = [sb(f"top8{i}", [128, 8]) for i in range(4)]
    sel = [sb(f"sel{i}", [128, 32]) for i in range(4)]
    negpad = [sb(f"negpad{i}", [128, 128], BF16) for i in range(4)]
    PT = [sb(f"PT{i}", [128, 512], BF16) for i in range(3)]
    osb = [sb(f"osb{i}", [128, 512]) for i in range(2)]
    rec = [sb(f"rec{i}", [128, 512]) for i in range(2)]
    onb = [sb(f"onb{i}", [64, 512], BF16) for i in range(2)]; d_on = [P.dsem() for _ in range(2)]
    pSc = [(PSA(f"pSc{i}", [128, 512], F32), P.buf(excl=True)) for i in range(2)]
    pO = [(PSA(f"pO{i}", [128, 512], F32), P.buf(excl=True)) for i in range(2)]
    pBC = (PSA("pBC", [128, 512], F32), P.buf(excl=True))
    pG = (PSA("pG", [128, 512], F32), P.buf(excl=True))
    pTr = (PSA("pTr", [128, 512], F32), P.buf(excl=True))
    for i in range(4):
        P.op("vector", lambda e, i=i: e.memset(negpad[i][0][:], 0.0), writes=[negpad[i][1]])

    for n_, hi in enumerate(HEADS):
        s2 = n_ % 2
        K, b_K = Ka[s2]; Q, b_Q = Qa[s2]; V, b_V = Va[s2]
        moba = hi >= 2
        if not moba:
            rows = slice(0, 96); scale = 96.0 ** -0.5
            P.dma("sync", d_K[s2], K[0:96, :], mk[hi, :, :], writes=[b_K])
            P.dma("sync", d_Q[s2], Q[0:96, :], mq[hi, :, :], writes=[b_Q])
            vsrc = mv[:, hi * 64:(hi + 1) * 64]
        else:
            scale = 0.125
            hb = hi - 2
            rows = slice(0, 96); krows = slice(0, 64); off = 64
            srows = slice(hb * 64, (hb + 1) * 64)
            P.dma("sync", d_K[s2], K[krows, :], ok[srows, :], writes=[b_K])
            P.dma("gpsimd", d_K2[s2], K[off:off + 32, :], blkoh_d[:, :], writes=[b_K])
            P.dma("sync", d_Q[s2], Q[krows, :], oq[srows, :], writes=[b_Q])
            vsrc = ov[:, hb * 64:(hb + 1) * 64]
        P.dma("sync", d_V[s2], V[:, :, 0:64], vsrc.rearrange("(kc p) d -> p kc d", p=128), writes=[b_V])
        P.op("gpsimd", lambda e, V=V: e.memset(V[:, :, 64:65], 1.0), writes=[b_V])
        if moba:
            P.op("vector", lambda e, K=K, krows=krows: e.tensor_reduce(out=kmf[krows, :], in_=K[krows, :].rearrange("p (n k) -> p n k", k=256), axis=AX.X, op=ALU.add),
                 reads=[b_K], writes=[b_kmf])
            P.op("vector", lambda e, krows=krows: e.tensor_scalar(out=kmT[krows, :], in0=kmf[krows, :], scalar1=1.0 / 256, scalar2=None, op0=ALU.mult),
                 reads=[b_kmf], writes=[b_kmT])
            for qt in range(NQT):
                for j in range(4):
                    qc = qt * 4 + j
                    P.mm(pG[0][:, j * 32:(j + 1) * 32], Q[krows, qc * 128:(qc + 1) * 128], kmT[krows, :], True, True, reads=[b_Q, b_kmT], writes=[pG[1]])
                for j in range(4):
                    qc = qt * 4 + j; qb = qc // 2
                    g_, b_g = gm[j]; t8, b_t8 = top8[j]; sl_, b_sl = sel[j]; npd, b_np = negpad[j]
                    P.op("gpsimd", lambda e, g_=g_: e.memset(g_[:], -1e30), writes=[b_g])
                    if qb > 0:
                        P.op("vector", lambda e, g_=g_, j=j, qb=qb: e.tensor_copy(out=g_[:, 0:qb], in_=pG[0][:, j * 32:j * 32 + qb]), reads=[pG[1]], writes=[b_g])
                    P.op("vector", lambda e, g_=g_, t8=t8: e.max(out=t8[:], in_=g_[:]), reads=[b_g], writes=[b_t8])
                    P.op("vector", lambda e, g_=g_, t8=t8, sl_=sl_: e.tensor_scalar(out=sl_[:], in0=g_[:], scalar1=t8[:, 2:3], scalar2=None, op0=ALU.is_ge),
                         reads=[b_g, b_t8], writes=[b_sl])
                    P.op("vector", lambda e, sl_=sl_, npd=npd, off=off: e.tensor_scalar(out=npd[:, off:off + 32], in0=sl_[:], scalar1=-1.0, scalar2=30000.0, op0=ALU.add, op1=ALU.mult),
                         reads=[b_sl], writes=[b_np])
                    P.op("vector", lambda e, npd=npd, off=off, qb=qb: e.memset(npd[:, off + qb:off + qb + 1], 0.0), writes=[b_np])
                    P.mm(pTr[0][:, j * 128:(j + 1) * 128], npd[:], ident[:], True, True, reads=[b_np, b_id], writes=[pTr[1]])
                P.op("scalar", lambda e, Q=Q, off=off, qt=qt: e.copy(out=Q[off:off + 32, qt * 512:(qt + 1) * 512], in_=pTr[0][off:off + 32, :]),
                     reads=[pTr[1]], writes=[b_Q])
        for qt in range(NQT):
            nkc = 4 * qt + 4
            qs = slice(qt * 512, (qt + 1) * 512)
            po, b_po = pO[qt % 2]

            def score(kc):
                ps, b_ps = pSc[kc % 2]
                P.mm(ps[:], K[rows, kc * 128:(kc + 1) * 128], Q[rows, qs], True, True, reads=[b_K, b_Q], writes=[b_ps])
            score(0)
            for kc in range(nkc):
                if kc + 1 < nkc:
                    score(kc + 1)
                ps, b_ps = pSc[kc % 2]
                pt_, b_pt = PT[kc % 3]
                P.op("scalar", lambda e, ps=ps, pt_=pt_, scale=scale: e.activation(out=pt_[:], in_=ps[:], func=AF.Exp, scale=scale), reads=[b_ps], writes=[b_pt])
                if kc >= 4 * qt:
                    dd = kc - 4 * qt
                    P.op("gpsimd", lambda e, pt_=pt_, dd=dd: e.tensor_tensor(out=pt_[:], in0=pt_[:], in1=cmask[:, dd, :], op=ALU.mult), reads=[b_pt, b_cm], writes=[b_pt])
                P.mm(po[0:65, :], V[:, kc, :], pt_[:], kc == 0, kc == nkc - 1, reads=[b_V, b_pt], writes=[b_po])
            o_, b_o = osb[qt % 2]; r_, b_r = rec[qt % 2]; on_, b_on = onb[qt % 2]
            P.op("vector", lambda e, o_=o_, po=po: e.tensor_copy(out=o_[0:65, :], in_=po[0:65, :]), reads=[b_po], writes=[b_o])
            P.op("vector", lambda e, o_=o_, r_=r_: e.reciprocal(out=r_[64:65, :], in_=o_[64:65, :]), reads=[b_o], writes=[b_r])
            P.mm(pBC[0][0:64, :], onesf[64:65, :], r_[64:65, :], True, True, reads=[b_of, b_r], writes=[pBC[1]])
            P.op("vector", lambda e, o_=o_, on_=on_: e.tensor_tensor(out=on_[:], in0=o_[0:64, :], in1=pBC[0][0:64, :], op=ALU.mult), reads=[b_o, pBC[1]], writes=[b_on])
            P.dma("sync", d_on[qt % 2], oT[hi, :, qs], on_[:], reads=[b_on])
    st = P.emit(final_waits=[("sync", d) for d in d_on])
    return nc


S = 8192
NSEG = 4
SEG = 2048
NT = 16
GT = 4
EPS = 1e-6
NLV = 6


def gdn_consts():
    t = np.arange(128)
    M = (t[:, None] <= t[None, :]).astype(np.float32)
    NEGM = np.where(t[:, None] >= t[None, :], 0.0, -1e30).astype(np.float32)
    STRICT = (t[:, None] > t[None, :]).astype(np.float32)
    ident = np.eye(128, dtype=np.float32)
    ones = np.ones((128, 128), np.float32)
    return np.stack([M, ones, NEGM, STRICT, ident], 0)


def build_gdn():
    nc = bass.Bass("TRN2", target_bir_lowering=False)
    DI = lambda n, s, dt=F32: nc.dram_tensor(n, s, dt, kind="ExternalInput").ap()
    rq = DI("rq", [128, S]); rk = DI("rk", [128, S]); rv = DI("rv", [128, S])
    zd = DI("z", [S, 128])
    bl = DI("bl", [128, 64]); al = DI("al", [128, 64])
    cw = DI("cw", [128, 12]); sc = DI("sc", [128, 2]); gn = DI("gn", [128, 128])
    cst = DI("cst", [5, 128, 128])
    od = nc.dram_tensor("o", [S, 128], BF16, kind="ExternalOutput").ap()
    P = Prog(nc)
    A = nc.alloc_sbuf_tensor
    PS = nc.alloc_psum_tensor

    def sb(name, shape, dt=F32):
        return A("s_" + name, shape, dt), P.buf(name)

    C, b_C = sb("C", [128, 5, 128])
    cwt, b_cw = sb("cwt", [128, 12]); sct, b_sc = sb("sct", [128, 2]); gnt, b_gn = sb("gnt", [128, 128])
    blt, b_bl = sb("blt", [128, 64]); alt, b_al = sb("alt", [128, 64])
    beta, b_beta = sb("beta", [128, 64]); gg, b_gg = sb("gg", [128, 64])
    tmp64, b_tmp64 = sb("tmp64", [128, 64]); ea, b_ea = sb("ea", [128, 1])
    P.dma("sync", P.dsem(), C[:], cst.rearrange("k p n -> p k n"), writes=[b_C])
    for (t_, d_, b_) in [(cwt, cw, b_cw), (sct, sc, b_sc), (gnt, gn, b_gn), (blt, bl, b_bl), (alt, al, b_al)]:
        P.dma("sync", P.dsem(), t_[:], d_[:, :], writes=[b_])
    Mm, ONES, NEGM, STRICT, IDENT = [C[:, i, :] for i in range(5)]
    NEG4, b_N4 = sb("NEG4", [128, GT, 128]); STR4, b_S4 = sb("STR4", [128, GT, 128]); ID4, b_I4 = sb("ID4", [128, GT, 128])
    for t in range(GT):
        P.op("gpsimd", lambda e, t=t: e.tensor_copy(out=NEG4[:, t, :], in_=NEGM), reads=[b_C], writes=[b_N4])
        P.op("gpsimd", lambda e, t=t: e.tensor_copy(out=STR4[:, t, :], in_=STRICT), reads=[b_C], writes=[b_S4])
        P.op("gpsimd", lambda e, t=t: e.tensor_copy(out=ID4[:, t, :], in_=IDENT), reads=[b_C], writes=[b_I4])
    P.op("scalar", lambda e: e.activation(out=beta[:], in_=blt[:], func=AF.Sigmoid), reads=[b_bl], writes=[b_beta])
    P.op("scalar", lambda e: e.activation(out=tmp64[:], in_=alt[:], func=AF.Exp, bias=sct[:, 1:2]), reads=[b_al, b_sc], writes=[b_tmp64])
    P.op("scalar", lambda e: e.activation(out=tmp64[:], in_=tmp64[:], func=AF.Ln, bias=1.0), reads=[b_tmp64], writes=[b_tmp64])
    P.op("scalar", lambda e: e.activation(out=ea[:], in_=sct[:, 0:1], func=AF.Exp), reads=[b_sc], writes=[b_ea])
    P.op("vector", lambda e: e.tensor_scalar(out=gg[:], in0=tmp64[:], scalar1=ea[:, 0:1], scalar2=-1.0, op0=ALU.mult, op1=ALU.mult),
         reads=[b_tmp64, b_ea], writes=[b_gg])

    raw = [sb(f"raw{i}", [128, SEG + 3]) for i in range(3)]
    d_raw = [P.dsem() for _ in range(3)]
    cvs = [[sb(f"cv{s}_{i}", [128, SEG]) for i in range(3)] for s in range(2)]
    sqb, b_sq = sb("sqb", [128, 512]); lnb, b_ln = sb("lnb", [128, 512]); rsb, b_rs = sb("rsb", [128, 512])
    stat = [[sb(f"{n}{s}", [128, NT]) for n in ("gcum", "egc", "edec", "dec", "begc")] for s in range(2)]

    def g4(name, n=1):
        return [sb(f"{name}{i}", [128, GT, 128]) for i in range(n)]
    ktm = g4("ktm")[0]; vb = g4("vb")[0]; rw = g4("rw")[0]
    Gm = g4("Gm")[0]; nGm = g4("nGm")[0]; dmin = g4("dmin")[0]; Dm = g4("Dm")[0]; Dms = g4("Dms")[0]
    Am = g4("Am")[0]; Bm = g4("Bm")[0]; qkd = g4("qkd")[0]
    Qm = g4("Qm", 2); Ym = g4("Ym", 2); YTm = g4("YTm", 2)
    kdec = g4("kdec", 2); qkdT = g4("qkdT", 2); uu = g4("uu", 2); wT = g4("wT", 2)
    zt = g4("zt", 2); d_z = [P.dsem() for _ in range(2)]
    szt = g4("szt", 2)
    ofb = [sb(f"ofb{i}", [128, GT, 128], BF16) for i in range(2)]; d_o = [P.dsem() for _ in range(2)]
    NB = 2
    vnew = [sb(f"vnew{i}", [128, 128]) for i in range(NB)]
    o1 = [sb(f"o1{i}", [128, 128]) for i in range(NB)]
    ot = [sb(f"ot{i}", [128, 128]) for i in range(NB)]
    osq = [sb(f"osq{i}", [128, 128]) for i in range(NB)]
    ss = [sb(f"ss{i}", [128, 1]) for i in range(NB)]
    lss = [sb(f"lss{i}", [128, 1]) for i in range(NB)]
    rss = [sb(f"rss{i}", [128, 1]) for i in range(NB)]
    og = [sb(f"og{i}", [128, 128]) for i in range(NB)]
    St = [sb(f"St{i}", [128, 128]) for i in range(2)]
    pb = [(PS(f"pb{i}", [128, 512], F32), P.buf(f"pb{i}", excl=True)) for i in range(8)]
    pcount = [0]

    def bank():
        i = pcount[0] % 6
        pcount[0] += 1
        return pb[i]
    pV, b_pV = pb[6]
    pSt, b_pSt = pb[7]
    P.op("vector", lambda e: e.memset(St[0][0][:], 0.0), writes=[St[0][1]])
    state = {"scur": 0}

    def seg_prep(seg):
        s0 = seg * SEG
        cv = cvs[seg % 2]
        for qi, rd in enumerate((rq, rk, rv)):
            r_, b_r = raw[qi]
            if seg == 0:
                P.op("gpsimd", lambda e, r_=r_: e.memset(r_[:, 0:3], 0.0), writes=[b_r])
                P.dma("sync", d_raw[qi], r_[:, 3:], rd[:, 0:SEG], writes=[b_r])
            else:
                P.dma("sync", d_raw[qi], r_[:, :], rd[:, s0 - 3:s0 + SEG], writes=[b_r])
            c_, b_c = cv[qi]
            for hf in range(2):
                lo = hf * 1024
                sl = slice(lo, lo + 1024)
                P.op("gpsimd", lambda e, c_=c_, r_=r_, lo=lo, qi=qi, sl=sl: e.tensor_scalar(
                    out=c_[:, sl], in0=r_[:, lo:lo + 1024], scalar1=cwt[:, qi * 4:qi * 4 + 1], scalar2=None, op0=ALU.mult),
                    reads=[b_r, b_cw], writes=[b_c])
                for tap in range(1, 4):
                    P.op("vector", lambda e, c_=c_, r_=r_, lo=lo, qi=qi, sl=sl, tap=tap: e.scalar_tensor_tensor(
                        out=c_[:, sl], in0=r_[:, lo + tap:lo + tap + 1024], scalar=cwt[:, qi * 4 + tap:qi * 4 + tap + 1],
                        in1=c_[:, sl], op0=ALU.mult, op1=ALU.add), reads=[b_r, b_cw, b_c], writes=[b_c])
                P.op("scalar", lambda e, c_=c_, sl=sl: e.activation(out=c_[:, sl], in_=c_[:, sl], func=AF.Silu),
                     reads=[b_c], writes=[b_c])
        for qi in range(2):
            c_, b_c = cv[qi]
            for t4 in range(4):
                sl = slice(t4 * 512, (t4 + 1) * 512)
                pt, b_pt = bank()
                P.op("scalar", lambda e, c_=c_, sl=sl: e.activation(out=sqb[:], in_=c_[:, sl], func=AF.Square), reads=[b_c], writes=[b_sq])
                P.mm(pt[:], ONES, sqb[:], True, True, reads=[b_C, b_sq], writes=[b_pt])
                P.op("scalar", lambda e, pt=pt: e.activation(out=lnb[:], in_=pt[:], func=AF.Ln, bias=EPS), reads=[b_pt], writes=[b_ln])
                P.op("scalar", lambda e: e.activation(out=rsb[:], in_=lnb[:], func=AF.Exp, scale=-0.5), reads=[b_ln], writes=[b_rs])
                scl = (128.0 ** -0.5) if qi == 0 else 1.0
                P.op("vector", lambda e, c_=c_, sl=sl, scl=scl: e.scalar_tensor_tensor(
                    out=c_[:, sl], in0=c_[:, sl], scalar=scl, in1=rsb[:], op0=ALU.mult, op1=ALU.mult),
                    reads=[b_c, b_rs], writes=[b_c])
        (gcum, b_gcum), (egc, b_egc), (edec, b_edec), (dec, b_dec), (begc, b_begc) = stat[seg % 2]
        gsl = slice(seg * NT, (seg + 1) * NT)
        pt, b_pt = bank()
        P.mm(pt[:, 0:NT], Mm, gg[:, gsl], True, True, reads=[b_C, b_gg], writes=[b_pt])
        P.mm(pt[:, 16:16 + NT], ONES, gg[:, gsl], True, True, reads=[b_C, b_gg], writes=[b_pt])
        P.op("vector", lambda e, pt=pt: e.tensor_copy(out=gcum[:], in_=pt[:, 0:NT]), reads=[b_pt], writes=[b_gcum])
        P.op("scalar", lambda e, pt=pt: e.activation(out=egc[:], in_=pt[:, 0:NT], func=AF.Exp), reads=[b_pt], writes=[b_egc])
        P.op("vector", lambda e, pt=pt: e.tensor_tensor(out=edec[:], in0=pt[:, 16:16 + NT], in1=gcum[:], op=ALU.subtract),
             reads=[b_pt, b_gcum], writes=[b_edec])
        P.op("scalar", lambda e: e.activation(out=edec[:], in_=edec[:], func=AF.Exp), reads=[b_edec], writes=[b_edec])
        P.op("scalar", lambda e, pt=pt: e.activation(out=dec[:], in_=pt[:, 16:16 + NT], func=AF.Exp), reads=[b_pt], writes=[b_dec])
        P.op("vector", lambda e, gsl=gsl: e.tensor_tensor(out=begc[:], in0=beta[:, gsl], in1=egc[:], op=ALU.mult),
             reads=[b_beta, b_egc], writes=[b_begc])

    def prepass_stages(seg, grp):
        cv = cvs[seg % 2]
        qT_, b_qT = cv[0]; kT_, b_kT = cv[1]; vT_, b_vT = cv[2]
        (gcum, b_gcum), (egc, b_egc), (edec, b_edec), (dec, b_dec), (begc, b_begc) = stat[seg % 2]
        gi = (seg * (NT // GT) + grp) % 2
        Ts = [grp * GT + t for t in range(GT)]
        cs = lambda t: slice(Ts[t] * 128, (Ts[t] + 1) * 128)
        Gs = [seg * NT + T for T in Ts]
        kd, b_kd = kdec[gi]; qT2, b_qT2 = qkdT[gi]; u_, b_u = uu[gi]; w_, b_w = wT[gi]
        pk, b_pk = bank(); pv, b_pv = bank()
        for t in range(GT):
            P.op("tensor", lambda e, t=t: e.transpose(pk[:, t * 128:(t + 1) * 128], kT_[:, cs(t)], IDENT), reads=[b_kT, b_C], writes=[b_pk])
        for t in range(GT):
            P.op("tensor", lambda e, t=t: e.transpose(pv[:, t * 128:(t + 1) * 128], vT_[:, cs(t)], IDENT), reads=[b_vT, b_C], writes=[b_pv])
        P.op("scalar", lambda e: e.copy(out=ktm[0][:].rearrange("p t n -> p (t n)"), in_=pk[:]), reads=[b_pk], writes=[ktm[1]])
        for t in range(GT):
            P.op("vector", lambda e, t=t: e.tensor_scalar(out=vb[0][:, t, :], in0=pv[:, t * 128:(t + 1) * 128], scalar1=beta[:, Gs[t]:Gs[t] + 1], scalar2=None, op0=ALU.mult),
                 reads=[b_pv, b_beta], writes=[vb[1]])
        for t in range(GT):
            P.op("gpsimd", lambda e, t=t: e.tensor_scalar(out=rw[0][:, t, :], in0=ktm[0][:, t, :], scalar1=begc[:, Ts[t]:Ts[t] + 1], scalar2=None, op0=ALU.mult),
                 reads=[ktm[1], b_begc], writes=[rw[1]])
            P.op("gpsimd", lambda e, t=t: e.tensor_scalar(out=kd[:, t, :], in0=ktm[0][:, t, :], scalar1=edec[:, Ts[t]:Ts[t] + 1], scalar2=None, op0=ALU.mult),
                 reads=[ktm[1], b_edec], writes=[b_kd])
        for t in range(GT):
            P.op("vector", lambda e, t=t: e.tensor_scalar(out=Gm[0][:, t, :], in0=Mm, scalar1=gg[:, Gs[t]:Gs[t] + 1], scalar2=None, op0=ALU.mult),
                 reads=[b_C, b_gg], writes=[Gm[1]])
        P.op("gpsimd", lambda e: e.tensor_scalar(out=nGm[0][:], in0=Gm[0][:], scalar1=-1.0, scalar2=None, op0=ALU.mult), reads=[Gm[1]], writes=[nGm[1]])
        yield
        pd, b_pd = bank(); pkk, b_pkk = bank(); pqk, b_pqk = bank()
        for t in range(GT):
            o = slice(t * 128, (t + 1) * 128)
            P.mm(pd[:, o], Gm[0][:, t, :], ONES, True, False, reads=[Gm[1], b_C], writes=[b_pd])
            P.mm(pd[:, o], ONES, nGm[0][:, t, :], False, True, reads=[nGm[1], b_C], writes=[b_pd])
        for t in range(GT):
            o = slice(t * 128, (t + 1) * 128)
            P.mm(pkk[:, o], kT_[:, cs(t)], kT_[:, cs(t)], True, True, reads=[b_kT], writes=[b_pkk])
        for t in range(GT):
            o = slice(t * 128, (t + 1) * 128)
            P.mm(pqk[:, o], qT_[:, cs(t)], kT_[:, cs(t)], True, True, reads=[b_kT, b_qT], writes=[b_pqk])
        fl = lambda x: x[:].rearrange("p t n -> p (t n)")
        P.op("vector", lambda e: e.scalar_tensor_tensor(out=fl(dmin[0]), in0=pd[:], scalar=0.0, in1=fl(NEG4), op0=ALU.min, op1=ALU.add),
             reads=[b_pd, b_N4], writes=[dmin[1]])
        P.op("scalar", lambda e: e.activation(out=fl(Dm[0]), in_=fl(dmin[0]), func=AF.Exp), reads=[dmin[1]], writes=[Dm[1]])
        P.op("gpsimd", lambda e: e.tensor_tensor(out=fl(Dms[0]), in0=fl(Dm[0]), in1=fl(STR4), op=ALU.mult), reads=[Dm[1], b_S4], writes=[Dms[1]])
        for t in range(GT):
            P.op("vector", lambda e, t=t: e.scalar_tensor_tensor(out=Am[0][:, t, :], in0=pkk[:, t * 128:(t + 1) * 128], scalar=beta[:, Gs[t]:Gs[t] + 1], in1=Dms[0][:, t, :], op0=ALU.mult, op1=ALU.mult),
                 reads=[b_pkk, b_beta, Dms[1]], writes=[Am[1]])
        P.op("vector", lambda e: e.tensor_tensor(out=fl(qkd[0]), in0=pqk[:], in1=fl(Dm[0]), op=ALU.mult), reads=[b_pqk, Dm[1]], writes=[qkd[1]])
        yield
        pbt, b_pbt = bank(); pqt, b_pqt = bank()
        for t in range(GT):
            P.op("tensor", lambda e, t=t: e.transpose(pbt[:, t * 128:(t + 1) * 128], Am[0][:, t, :], IDENT), reads=[Am[1], b_C], writes=[b_pbt])
        for t in range(GT):
            P.op("tensor", lambda e, t=t: e.transpose(pqt[:, t * 128:(t + 1) * 128], qkd[0][:, t, :], IDENT), reads=[qkd[1], b_C], writes=[b_pqt])
        P.op("scalar", lambda e: e.copy(out=fl(Bm[0]), in_=pbt[:]), reads=[b_pbt], writes=[Bm[1]])
        P.op("vector", lambda e: e.tensor_copy(out=fl(qT2), in_=pqt[:]), reads=[b_pqt], writes=[b_qT2])
        Qc, b_Qc = Qm[0]
        P.op("gpsimd", lambda e: e.scalar_tensor_tensor(out=fl(Qc), in0=fl(Bm[0]), scalar=-1.0, in1=fl(ID4), op0=ALU.mult, op1=ALU.add),
             reads=[Bm[1], b_I4], writes=[b_Qc]) if False else \
            P.op("vector", lambda e: e.scalar_tensor_tensor(out=fl(Qc), in0=fl(Bm[0]), scalar=-1.0, in1=fl(ID4), op0=ALU.mult, op1=ALU.add),
                 reads=[Bm[1], b_I4], writes=[b_Qc])
        yield
        Yc, b_Yc = Bm; YTc, b_YTc = Am
        for lv in range(NLV):
            pyt, b_pyt = bank()
            Yn, b_Yn = Ym[lv % 2]; YTn, b_YTn = YTm[lv % 2]
            for t in range(GT):
                P.mm(pyt[:, t * 128:(t + 1) * 128], Yc[:, t, :], YTc[:, t, :], True, True, reads=[b_Yc, b_YTc], writes=[b_pyt])
            if lv < NLV - 1:
                py, b_py = bank()
                for t in range(GT):
                    P.mm(py[:, t * 128:(t + 1) * 128], YTc[:, t, :], Yc[:, t, :], True, True, reads=[b_Yc, b_YTc], writes=[b_py])
            P.op("scalar", lambda e, YTn=YTn, pyt=pyt: e.copy(out=fl(YTn), in_=pyt[:]), reads=[b_pyt], writes=[b_YTn])
            if lv < NLV - 1:
                P.op("vector", lambda e, Yn=Yn, py=py: e.tensor_copy(out=fl(Yn), in_=py[:]), reads=[b_py], writes=[b_Yn])
            yield
            Qo, b_Qo = Qm[lv % 2]; Qn, b_Qn = Qm[(lv + 1) % 2]
            pq, b_pq = bank()
            for t in range(GT):
                P.mm(pq[:, t * 128:(t + 1) * 128], YTn[:, t, :], Qo[:, t, :], True, True, reads=[b_YTn, b_Qo], writes=[b_pq])
            P.op("vector", lambda e, Qn=Qn, Qo=Qo, pq=pq: e.tensor_tensor(out=fl(Qn), in0=pq[:], in1=fl(Qo), op=ALU.add), reads=[b_pq, b_Qo], writes=[b_Qn])
            Yc, b_Yc = Yn, b_Yn
            YTc, b_YTc = YTn, b_YTn
            yield
        Tt, b_Tt = Qm[NLV % 2]
        pu, b_pu = bank(); pw, b_pw = bank()
        for t in range(GT):
            P.mm(pu[:, t * 128:(t + 1) * 128], Tt[:, t, :], vb[0][:, t, :], True, True, reads=[b_Tt, vb[1]], writes=[b_pu])
        for t in range(GT):
            P.mm(pw[:, t * 128:(t + 1) * 128], rw[0][:, t, :], Tt[:, t, :], True, True, reads=[b_Tt, rw[1]], writes=[b_pw])
        P.op("scalar", lambda e: e.copy(out=fl(u_), in_=pu[:]), reads=[b_pu], writes=[b_u])
        P.op("vector", lambda e: e.tensor_copy(out=fl(w_), in_=pw[:]), reads=[b_pw], writes=[b_w])
        G0 = Gs[0]
        P.dma("sync", d_z[gi], zt[gi][0][:], zd[G0 * 128:(G0 + GT) * 128, :].rearrange("(t p) d -> p t d", p=128), writes=[zt[gi][1]])
        P.op("scalar", lambda e: e.activation(out=fl(szt[gi][0]), in_=fl(zt[gi][0]), func=AF.Silu), reads=[zt[gi][1]], writes=[szt[gi][1]])
        yield

    def scan_steps(seg, grp):
        cv = cvs[seg % 2]
        qT_, b_qT = cv[0]
        (gcum, b_gcum), (egc, b_egc), (edec, b_edec), (dec, b_dec), (begc, b_begc) = stat[seg % 2]
        gi = (seg * (NT // GT) + grp) % 2
        kd, b_kd = kdec[gi]; qT2, b_qT2 = qkdT[gi]; u_, b_u = uu[gi]; w_, b_w = wT[gi]
        for t in range(GT):
            T = grp * GT + t
            G = seg * NT + T
            i2 = G % NB
            cs = slice(T * 128, (T + 1) * 128)
            Sc, b_Sc = St[state["scur"]]; Sn, b_Sn = St[1 - state["scur"]]
            P.mm(pV[:, 0:128], w_[:, t, :], Sc[:], True, True, reads=[b_w, b_Sc], writes=[b_pV])
            P.mm(pV[:, 128:256], qT_[:, cs], Sc[:], True, True, reads=[b_qT, b_Sc], writes=[b_pV])
            P.op("vector", lambda e, i2=i2, t=t: e.tensor_tensor(out=vnew[i2][0][:], in0=u_[:, t, :], in1=pV[:, 0:128], op=ALU.subtract),
                 reads=[b_u, b_pV], writes=[vnew[i2][1]])
            P.op("scalar", lambda e, i2=i2, T=T: e.activation(out=o1[i2][0][:], in_=pV[:, 128:256], func=AF.Copy, scale=egc[:, T:T + 1]),
                 reads=[b_pV, b_egc], writes=[o1[i2][1]])
            P.mm(pSt[:, 0:128], kd[:, t, :], vnew[i2][0][:], True, True, reads=[b_kd, vnew[i2][1]], writes=[b_pSt])
            P.mm(pSt[:, 128:256], qT2[:, t, :], vnew[i2][0][:], True, True, reads=[b_qT2, vnew[i2][1]], writes=[b_pSt])
            P.op("vector", lambda e, Sn=Sn, Sc=Sc, T=T: e.scalar_tensor_tensor(out=Sn[:], in0=Sc[:], scalar=dec[:, T:T + 1], in1=pSt[:, 0:128], op0=ALU.mult, op1=ALU.add),
                 reads=[b_Sc, b_dec, b_pSt], writes=[b_Sn])
            P.op("vector", lambda e, i2=i2: e.tensor_tensor(out=ot[i2][0][:], in0=o1[i2][0][:], in1=pSt[:, 128:256], op=ALU.add),
                 reads=[o1[i2][1], b_pSt], writes=[ot[i2][1]])
            state["scur"] = 1 - state["scur"]
            P.op("scalar", lambda e, i2=i2: e.activation(out=osq[i2][0][:], in_=ot[i2][0][:], func=AF.Square, accum_out=ss[i2][0][:]),
                 reads=[ot[i2][1]], writes=[osq[i2][1], ss[i2][1]])
            P.op("scalar", lambda e, i2=i2: e.activation(out=lss[i2][0][:], in_=ss[i2][0][:], func=AF.Ln, scale=1.0 / 128, bias=EPS),
                 reads=[ss[i2][1]], writes=[lss[i2][1]])
            P.op("scalar", lambda e, i2=i2: e.activation(out=rss[i2][0][:], in_=lss[i2][0][:], func=AF.Exp, scale=-0.5),
                 reads=[lss[i2][1]], writes=[rss[i2][1]])
            P.op("gpsimd", lambda e, i2=i2: e.tensor_tensor(out=og[i2][0][:], in0=ot[i2][0][:], in1=gnt[:], op=ALU.mult),
                 reads=[ot[i2][1], b_gn], writes=[og[i2][1]])
            P.op("vector", lambda e, i2=i2, t=t: e.scalar_tensor_tensor(out=ofb[gi][0][:, t, :], in0=og[i2][0][:], scalar=rss[i2][0][:, 0:1], in1=szt[gi][0][:, t, :], op0=ALU.mult, op1=ALU.mult),
                 reads=[og[i2][1], rss[i2][1], szt[gi][1]], writes=[ofb[gi][1]])
            yield
        G0 = seg * NT + grp * GT
        P.dma("sync", d_o[gi], od[G0 * 128:(G0 + GT) * 128, :].rearrange("(t p) d -> p t d", p=128), ofb[gi][0][:], reads=[ofb[gi][1]])

    groups = [(seg, grp) for seg in range(NSEG) for grp in range(NT // GT)]
    prev_scan = None
    for n_, (seg, grp) in enumerate(groups):
        if grp == 0:
            seg_prep(seg)
        pre = prepass_stages(seg, grp)
        done_pre = False
        rounds = 0
        while True:
            try:
                next(pre)
            except StopIteration:
                break
            rounds += 1
            if prev_scan is not None and rounds % 3 == 0:
                try:
                    next(prev_scan)
                except StopIteration:
                    prev_scan = None
        if prev_scan is not None:
            for _ in prev_scan:
                pass
        prev_scan = scan_steps(seg, grp)
    for _ in prev_scan:
        pass
    st = P.emit(final_waits=[("sync", d) for d in d_o])
    return nc

T = 2048
D = 1024
EPS = 1e-6
NTILES = 4
O_GATE = 4008


def build_merge():
    nc = bass.Bass("TRN2", target_bir_lowering=False)
    DI = lambda n, s, dt=F32: nc.dram_tensor(n, s, dt, kind="ExternalInput").ap()
    xT = DI("xT", [D, T]); brT = DI("brT", [3, 512, T], BF16); gd = DI("g", [128, 8])
    wgb = DI("wgb", [24, 128, 1536]); wo = DI("wo", [8, 128, 1024]); onesd = DI("ones", [128, 128])
    yT = nc.dram_tensor("yT", [D, T], F32, kind="ExternalOutput").ap()
    P = Prog(nc)
    A = nc.alloc_sbuf_tensor
    PSA = nc.alloc_psum_tensor

    def sb(name, shape, dt=F32):
        return A("s_" + name, shape, dt), P.buf(name)
    ones, b_ones = sb("ones", [128, 128], BF16); P.dma("gpsimd", P.dsem(), ones[:], onesd[:, :], writes=[b_ones])
    g, b_g = sb("g", [128, 8]); P.dma("sync", P.dsem(), g[:], gd[:, :], writes=[b_g])
    xin = [sb(f"xin{i}", [128, 8, 512]) for i in range(2)]; d_xin = [P.dsem() for _ in range(2)]
    br, b_br = sb("br", [128, 3, 4, T], BF16); d_br = P.dsem()
    sq = [sb(f"sq{i}", [128, 512], BF16) for i in range(2)]
    lnb, b_ln = sb("lnb", [128, 512]); rstd, b_rstd = sb("rstd", [128, 512])
    hT = sb("hT", [128, 8, T], BF16); b_hs = [P.buf() for _ in range(NTILES)]
    wc = [sb(f"wc{i}", [128, 1536], BF16) for i in range(3)]; d_wc = [P.dsem() for _ in range(3)]
    woc = [sb(f"woc{i}", [128, 1024], BF16) for i in range(2)]; d_wo = [P.dsem() for _ in range(2)]
    sig = [sb(f"sig{i}", [128, 512]) for i in range(2)]
    acc = [sb(f"acc{i}", [128, 512]) for i in range(NTILES)]
    tmp = [sb(f"tmp{i}", [128, 512]) for i in range(2)]
    mixed = sb("mixed", [128, 8, T], BF16); b_mxs = [P.buf() for _ in range(NTILES)]
    xres = [sb(f"xres{i}", [128, 512]) for i in range(2)]; d_xres = [P.dsem() for _ in range(2)]
    yo = [sb(f"yo{i}", [128, 512]) for i in range(2)]; d_yo = [P.dsem() for _ in range(2)]
    pS = (PSA("pS", [128, 512], F32), P.buf(excl=True))
    pG = [(PSA(f"pG{i}", [128, 512], F32), P.buf(excl=True)) for i in range(2)]
    pU = [(PSA(f"pU{i}", [128, 512], F32), P.buf(excl=True)) for i in range(2)]
    pO = [(PSA(f"pO{i}", [128, 512], F32), P.buf(excl=True)) for i in range(2)]
    xT_v = xT.rearrange("(kc p) n -> p kc n", p=128)
    hT_, mixed_ = hT[0], mixed[0]
    P.dma("sync", d_br, br[:, :, :, 0:NTILES * 512], brT[:, :, 0:NTILES * 512].rearrange("n (kc p) t -> p n kc t", p=128), writes=[b_br])
    for tt in range(NTILES):
        ts = slice(tt * 512, (tt + 1) * 512)
        xi, b_xi = xin[tt % 2]
        P.dma("sync", d_xin[tt % 2], xi[:], xT_v[:, :, ts], writes=[b_xi])
        for kc in range(8):
            s, bs = sq[kc % 2]
            P.op("scalar", lambda e, s=s, kc=kc, xi=xi: e.activation(out=s[:], in_=xi[:, kc, :], func=AF.Square), reads=[b_xi], writes=[bs])
            P.mm(pS[0][:], ones[:], s[:], kc == 0, kc == 7, reads=[b_ones, bs], writes=[pS[1]])
        P.op("scalar", lambda e: e.activation(out=lnb[:], in_=pS[0][:], func=AF.Ln, scale=1.0 / D, bias=EPS), reads=[pS[1]], writes=[b_ln])
        P.op("scalar", lambda e: e.activation(out=rstd[:], in_=lnb[:], func=AF.Exp, scale=-0.5), reads=[b_ln], writes=[b_rstd])
        for kc in range(8):
            P.op("vector", lambda e, kc=kc, xi=xi, ts=ts: e.scalar_tensor_tensor(out=hT_[:, kc, ts], in0=xi[:, kc, :], scalar=g[:, kc:kc + 1], in1=rstd[:], op0=ALU.mult, op1=ALU.mult),
                 reads=[b_xi, b_g, b_rstd], writes=[b_hs[tt]])
    cnt = 0; c2n = 0
    for c in range(8):
        for n in range(3):
            w, bw = wc[cnt % 3]
            P.dma("gpsimd", d_wc[cnt % 3], w[:], wgb[c * 3 + n, :, :], writes=[bw])
            cnt += 1
            for tt in range(NTILES):
                ts = slice(tt * 512, (tt + 1) * 512)
                a_, b_a = acc[tt]
                pg, b_pg = pG[c2n % 2]; pu, b_pu = pU[c2n % 2]; sg, b_sg = sig[c2n % 2]; tm, b_tm = tmp[c2n % 2]
                c2n += 1
                for kc in range(8):
                    P.mm(pg[:], w[:, kc * 128:(kc + 1) * 128], hT_[:, kc, ts], kc == 0, kc == 7, reads=[bw, b_hs[tt]], writes=[b_pg])
                for kc in range(4):
                    P.mm(pu[:], w[:, 1024 + kc * 128:1024 + (kc + 1) * 128], br[:, n, kc, ts], kc == 0, kc == 3, reads=[bw, b_br], writes=[b_pu])
                P.op("scalar", lambda e, sg=sg, pg=pg: e.activation(out=sg[:], in_=pg[:], func=AF.Sigmoid), reads=[b_pg], writes=[b_sg])
                if n == 0:
                    P.op("vector", lambda e, a_=a_, sg=sg, pu=pu: e.tensor_tensor(out=a_[:], in0=sg[:], in1=pu[:], op=ALU.mult), reads=[b_sg, b_pu], writes=[b_a])
                else:
                    P.op("vector", lambda e, tm=tm, sg=sg, pu=pu: e.tensor_tensor(out=tm[:], in0=sg[:], in1=pu[:], op=ALU.mult), reads=[b_sg, b_pu], writes=[b_tm])
                    if n == 1:
                        P.op("gpsimd", lambda e, a_=a_, tm=tm: e.tensor_tensor(out=a_[:], in0=a_[:], in1=tm[:], op=ALU.add), reads=[b_a, b_tm], writes=[b_a])
                    else:
                        P.op("gpsimd", lambda e, a_=a_, tm=tm, c=c, ts=ts: e.tensor_tensor(out=mixed_[:, c, ts], in0=a_[:], in1=tm[:], op=ALU.add), reads=[b_a, b_tm], writes=[b_mxs[tt]])
    k2 = 0
    for c2 in range(8):
        w, bw = woc[c2 % 2]
        P.dma("gpsimd", d_wo[c2 % 2], w[:], wo[c2, :, :], writes=[bw])
        for tt in range(NTILES):
            ts = slice(tt * 512, (tt + 1) * 512)
            po, b_po = pO[k2 % 2]; y_, b_y = yo[k2 % 2]; xr, b_xr = xres[k2 % 2]
            P.dma("sync", d_xres[k2 % 2], xr[:], xT[c2 * 128:(c2 + 1) * 128, ts], writes=[b_xr])
            for kc in range(8):
                P.mm(po[:], w[:, kc * 128:(kc + 1) * 128], mixed_[:, kc, ts], kc == 0, kc == 7, reads=[bw, b_mxs[tt]], writes=[b_po])
            P.op("vector", lambda e, y_=y_, po=po, xr=xr: e.tensor_tensor(out=y_[:], in0=po[:], in1=xr[:], op=ALU.add), reads=[b_po, b_xr], writes=[b_y])
            P.dma("sync", d_yo[k2 % 2], yT[c2 * 128:(c2 + 1) * 128, ts], y_[:], reads=[b_y])
            k2 += 1
    st = P.emit(final_waits=[("sync", d) for d in d_yo])
    return nc


def merge_weights(mix_norm, w_in, w_branch, w_out):
    g = np.ascontiguousarray(mix_norm.reshape(8, 128).T)
    wr = w_in.reshape(8, 128, -1)
    wgb = np.zeros((8, 3, 128, 1536), np.float32)
    for c in range(8):
        for n in range(3):
            c0 = O_GATE + n * 1024 + c * 128
            wgb[c, n, :, 0:1024] = wr[:, :, c0:c0 + 128].transpose(1, 0, 2).reshape(128, 1024)
            wgb[c, n, :, 1024:1536] = w_branch[n].reshape(4, 128, 1024)[:, :, c * 128:(c + 1) * 128].transpose(1, 0, 2).reshape(128, 512)
    wo = np.ascontiguousarray(w_out.reshape(8, 128, 8, 128).transpose(2, 1, 0, 3)).reshape(8, 128, 1024)
    return {"g": g, "wgb": wgb.reshape(24, 128, 1536), "wo": wo, "ones": np.ones((128, 128), np.float32)}

_PROGS = {}


def _prog(name, fn):
    if name not in _PROGS:
        _PROGS[name] = fn()
    return _PROGS[name]


def _run(nc, maps):
    res = run_bass_kernel_spmd(nc, maps, core_ids=list(range(8)))
    return res.results


def _ffn_launch(xT_cores, norm, w_in, w_out):
    g = np.ascontiguousarray(norm.reshape(8, 128).T)
    wi = w_in.reshape(8, 128, 2, NJ, 128)
    w1 = np.ascontiguousarray(wi.transpose(3, 1, 2, 0, 4)).reshape(NJ, 128, 2048)
    wo = w_out.reshape(NJ, 128, 8, 128)
    w2 = np.ascontiguousarray(wo.transpose(2, 1, 0, 3)).reshape(8, 128, NJ * 128)
    ones = np.ones((128, 128), np.float32)
    maps = [{"xT": xT_cores[c], "g": g, "w1": w1, "w2": w2, "ones": ones} for c in range(8)]
    r = _run(_prog("ffn", build_ffn), maps)
    return [np.ascontiguousarray(r[c]["yT"]) for c in range(8)]


def kernel(x, ffa_norm, ffa_w_in, ffa_w_out, mix_norm, w_in, mla_cq_norm, mla_ckv_norm,
           mla_w_uq, mla_w_ukv, mla_q_norm, mla_k_norm, gdn_conv, gdn_a_log, gdn_dt_bias,
           gdn_out_norm, moba_q_norm, moba_k_norm, w_branch, w_out, ffb_norm, ffb_w_in,
           ffb_w_out):
    f = lambda a: np.asarray(a, dtype=np.float32)
    x = f(x)
    B_, S_, D_ = x.shape
    xf = x.reshape(B_ * S_, D_)
    xT = [np.ascontiguousarray(xf[c * T:(c + 1) * T].T) for c in range(8)]
    blkoh, cmask, ident, onesf = attn_consts()
    gcst = gdn_consts()
    for l in range(2):
        xT = _ffn_launch(xT, f(ffa_norm)[l], f(ffa_w_in)[l], f(ffa_w_out)[l])
        W = projb_weights(f(mix_norm)[l], f(w_in)[l], f(mla_cq_norm)[l], f(mla_ckv_norm)[l], f(mla_w_uq)[l],
                          f(mla_w_ukv)[l], f(mla_q_norm)[l], f(mla_k_norm)[l], f(moba_q_norm)[l], f(moba_k_norm)[l])
        maps = []
        for c in range(8):
            m = dict(W)
            m["xT"] = xT[c]
            j = c % 4
            m["rope"] = rope_tables(np.arange(j * T, (j + 1) * T))
            maps.append(m)
        rb = _run(_prog("projb", build_projb), maps)

        def gather(name, b, axis):
            return np.concatenate([rb[b * 4 + j][name] for j in range(4)], axis=axis)
        full = []
        for b in range(2):
            full.append({"mla_qT": gather("mla_qT", b, 2), "mla_kT": gather("mla_kT", b, 2), "mla_v": gather("mla_v", b, 0),
                         "mo_qT": gather("mo_qT", b, 2), "mo_kT": gather("mo_kT", b, 2), "mo_v": gather("mo_v", b, 0),
                         "graw": gather("graw", b, 2), "gba": gather("gba", b, 1), "z": gather("z", b, 0)})
        maps = []
        for c in range(8):
            b, hp = c // 4, c % 4
            F = full[b]
            maps.append({"mq": np.ascontiguousarray(F["mla_qT"][2 * hp:2 * hp + 2]), "mk": np.ascontiguousarray(F["mla_kT"][2 * hp:2 * hp + 2]),
                         "mv": np.ascontiguousarray(F["mla_v"][:, hp * 128:(hp + 1) * 128]),
                         "oq": np.ascontiguousarray(F["mo_qT"][hp]), "ok": np.ascontiguousarray(F["mo_kT"][hp]),
                         "ov": np.ascontiguousarray(F["mo_v"][:, hp * 128:(hp + 1) * 128]),
                         "blkoh": blkoh, "cmask": cmask, "ident": ident, "onesf": onesf})
        ra = _run(_prog("attn", build_attn), maps)
        maps = []
        cw_l = f(gdn_conv)[l]
        for c in range(8):
            b, hd = c // 4, c % 4
            F = full[b]
            cw = np.concatenate([cw_l[:, k0 + hd * 128:k0 + (hd + 1) * 128].T for k0 in (0, 512, 1024)], 1)
            sc = np.stack([np.full(128, f(gdn_a_log)[l][hd], np.float32), np.full(128, f(gdn_dt_bias)[l][hd], np.float32)], 1)
            maps.append({"rq": np.ascontiguousarray(F["graw"][hd]), "rk": np.ascontiguousarray(F["graw"][4 + hd]),
                         "rv": np.ascontiguousarray(F["graw"][8 + hd]),
                         "z": np.ascontiguousarray(F["z"][:, hd * 128:(hd + 1) * 128]),
                         "bl": np.ascontiguousarray(F["gba"][hd].reshape(64, 128).T), "al": np.ascontiguousarray(F["gba"][4 + hd].reshape(64, 128).T),
                         "cw": np.ascontiguousarray(cw), "sc": sc,
                         "gn": np.ascontiguousarray(np.broadcast_to(f(gdn_out_norm)[l][None, :], (128, 128))),
                         "cst": gcst})
        rg = _run(_prog("gdn", build_gdn), maps)
        Wm = merge_weights(f(mix_norm)[l], f(w_in)[l], f(w_branch)[l], f(w_out)[l])
        brT = []
        for b in range(2):
            o_mla = np.concatenate([ra[b * 4 + hp]["oT"][i] for hp in range(4) for i in range(2)], 0)
            o_mo = np.concatenate([ra[b * 4 + hp]["oT"][2 + i] for hp in range(4) for i in range(2)], 0)
            o_gdn = np.concatenate([rg[b * 4 + hd]["o"].T for hd in range(4)], 0)
            brT.append(np.stack([o_mla, o_gdn, o_mo], 0))
        maps = []
        for c in range(8):
            b, j = c // 4, c % 4
            m = dict(Wm)
            m["xT"] = xT[c]
            m["brT"] = np.ascontiguousarray(brT[b][:, :, j * T:(j + 1) * T])
            maps.append(m)
        rm = _run(_prog("merge", build_merge), maps)
        xT = [np.ascontiguousarray(rm[c]["yT"]) for c in range(8)]
        xT = _ffn_launch(xT, f(ffb_norm)[l], f(ffb_w_in)[l], f(ffb_w_out)[l])
    out = np.concatenate([xT[c].T for c in range(8)], 0).reshape(B_, S_, D_)
    return np.ascontiguousarray(out.astype(np.float32))
```

```python
import ml_dtypes
from concourse.bass_utils import run_bass_kernel_spmd


import numpy as np
import concourse.bass as bass
import concourse.mybir as mybir

F32 = mybir.dt.float32
BF16 = mybir.dt.bfloat16
AF = mybir.ActivationFunctionType
ALU = mybir.AluOpType
AX = mybir.AxisListType

ENGS = ("tensor", "vector", "scalar", "gpsimd", "sync")


class Buf:
    __slots__ = ("name", "last_w", "readers", "excl")

    def __init__(self, name, excl=False):
        self.name = name
        self.last_w = None
        self.readers = []
        self.excl = excl


class Op:
    __slots__ = ("eng", "fn", "idx", "deps", "signal", "dsem", "dord", "count")

    def __init__(self, eng, fn, idx):
        self.eng = eng
        self.fn = fn
        self.idx = idx
        self.deps = []
        self.signal = False
        self.dsem = None
        self.dord = 0
        self.count = 0


class DSem:
    def __init__(self, name):
        self.name = name
        self.n = 0
        self.handle = None


class Prog:
    def __init__(self, nc):
        self.nc = nc
        self.ops = {e: [] for e in ENGS}
        self.dsems = []
        self.nbuf = 0

    def buf(self, name=None, excl=False):
        self.nbuf += 1
        return Buf(name or f"b{self.nbuf}", excl)

    def dsem(self, name=None):
        d = DSem(name or f"d{len(self.dsems)}")
        self.dsems.append(d)
        return d

    def _deps(self, op, reads, writes):
        deps = op.deps
        for b in reads:
            if b.excl:
                writes = list(writes) + [b]
                continue
            if b.last_w is not None:
                deps.append(b.last_w)
            b.readers.append(op)
        for b in writes:
            if b.last_w is not None:
                deps.append(b.last_w)
            deps.extend(r for r in b.readers if r is not op)
            b.readers = []
            b.last_w = op

    def op(self, eng, fn, reads=(), writes=()):
        o = Op(eng, fn, len(self.ops[eng]))
        self.ops[eng].append(o)
        self._deps(o, reads, writes)
        return o

    def dma(self, eng, dsem, out, in_, reads=(), writes=()):
        o = Op(eng, ("dma", out, in_), len(self.ops[eng]))
        dsem.n += 1
        o.dsem = dsem
        o.dord = dsem.n
        self.ops[eng].append(o)
        self._deps(o, reads, writes)
        return o

    def mm(self, out, lhsT, rhs, start, stop, reads=(), writes=()):
        return self.op("tensor", lambda e: e.matmul(out, lhsT, rhs, start=start, stop=stop),
                       reads, writes)

    def emit(self, final_waits=()):
        nc = self.nc
        for e in ENGS:
            for o in self.ops[e]:
                for d in o.deps:
                    if d.dsem is None:
                        if d.eng == "tensor" and o.eng == "tensor":
                            continue
                        d.signal = True
        esem = {e: nc.alloc_semaphore(f"sem_{e}") for e in ENGS}
        for d in self.dsems:
            if d.n:
                d.handle = nc.alloc_semaphore(f"dsem_{d.name}")
        for e in ENGS:
            c = 0
            for o in self.ops[e]:
                if o.dsem is None and o.signal:
                    c += 1
                    o.count = c
        stats = {}
        with nc.Block() as block:
            def run(ename, eng):
                waited = {}
                nwait = 0
                for o in self.ops[ename]:
                    need = {}
                    for d in o.deps:
                        if d.dsem is not None:
                            key = ("d", id(d.dsem)); sem = d.dsem.handle; val = 16 * d.dord
                        else:
                            if d.eng == "tensor" and ename == "tensor":
                                continue
                            key = ("e", d.eng); sem = esem[d.eng]; val = d.count
                        if need.get(key, (None, -1))[1] < val:
                            need[key] = (sem, val)
                    for key, (sem, val) in need.items():
                        if waited.get(key, -1) >= val:
                            continue
                        eng.wait_ge(sem, val)
                        waited[key] = val
                        nwait += 1
                    if o.dsem is not None:
                        _, out, in_ = o.fn
                        eng.dma_start(out=out, in_=in_).then_inc(o.dsem.handle, 16)
                    else:
                        ins = o.fn(eng)
                        if o.signal:
                            ins.then_inc(esem[ename], 1)
                for (kind, obj) in final_waits:
                    if ename != kind:
                        continue
                    eng.wait_ge(obj.handle, 16 * obj.n)
                stats[ename] = (len(self.ops[ename]), nwait)

            @block.tensor
            def _(eng):
                run("tensor", eng)

            @block.vector
            def _(eng):
                run("vector", eng)

            @block.scalar
            def _(eng):
                run("scalar", eng)

            @block.gpsimd
            def _(eng):
                run("gpsimd", eng)

            @block.sync
            def _(eng):
                run("sync", eng)
        return stats


T = 2048
D = 1024
DFF = 2816
NJ = DFF // 128
EPS = 1e-6


def build_ffn():
    nc = bass.Bass("TRN2", target_bir_lowering=False)
    xT = nc.dram_tensor("xT", [D, T], F32, kind="ExternalInput").ap()
    gd = nc.dram_tensor("g", [128, 8], F32, kind="ExternalInput").ap()
    w1d = nc.dram_tensor("w1", [NJ, 128, 2048], F32, kind="ExternalInput").ap()
    w2d = nc.dram_tensor("w2", [8, 128, NJ * 128], F32, kind="ExternalInput").ap()
    onesd = nc.dram_tensor("ones", [128, 128], F32, kind="ExternalInput").ap()
    yT = nc.dram_tensor("yT", [D, T], F32, kind="ExternalOutput").ap()
    P = Prog(nc)
    A = nc.alloc_sbuf_tensor
    ones = A("ones_sb", [128, 128], BF16); b_ones = P.buf()
    g = A("g_sb", [128, 8], F32); b_g = P.buf()
    xin = [A(f"xin{i}", [128, 8, 512], F32) for i in range(2)]; b_xin = [P.buf() for _ in range(2)]
    sq = [A(f"sq{i}", [128, 512], BF16) for i in range(2)]; b_sq = [P.buf() for _ in range(2)]
    lnb = A("lnb", [128, 512], F32); b_ln = P.buf()
    rstd = A("rstd", [128, 512], F32); b_rstd = P.buf()
    hT = A("hT", [128, 8, 1024], BF16); b_h = [P.buf() for _ in range(2)]
    actT = A("actT", [128, NJ, 1024], BF16); b_act = [P.buf() for _ in range(2)]
    w1 = [A(f"w1_{i}", [128, 2048], BF16) for i in range(2)]; b_w1 = [P.buf() for _ in range(2)]
    w2 = [A(f"w2_{i}", [128, NJ * 128], BF16) for i in range(2)]; b_w2 = [P.buf() for _ in range(2)]
    sg = [A(f"sg{i}", [128, 512], F32) for i in range(2)]; b_sg = [P.buf() for _ in range(2)]
    xres = [A(f"xres{i}", [128, 512], F32) for i in range(2)]; b_xres = [P.buf() for _ in range(2)]
    yo = [A(f"yo{i}", [128, 512], F32) for i in range(2)]; b_yo = [P.buf() for _ in range(2)]
    PS = nc.alloc_psum_tensor
    pS = PS("pS", [128, 512], F32); b_pS = P.buf(excl=True)
    pG = [PS(f"pG{i}", [128, 512], F32) for i in range(2)]; b_pG = [P.buf(excl=True) for _ in range(2)]
    pU = [PS(f"pU{i}", [128, 512], F32) for i in range(2)]; b_pU = [P.buf(excl=True) for _ in range(2)]
    pO = [PS(f"pO{i}", [128, 512], F32) for i in range(2)]; b_pO = [P.buf(excl=True) for _ in range(2)]
    d_c = P.dsem(); d_g = P.dsem()
    d_xin = [P.dsem() for _ in range(2)]
    d_w1 = [P.dsem() for _ in range(2)]; d_w2 = [P.dsem() for _ in range(2)]
    d_xres = [P.dsem() for _ in range(2)]; d_out = [P.dsem() for _ in range(2)]
    b_ydram = P.buf()

    P.dma("gpsimd", d_c, ones[:], onesd[:, :], writes=[b_ones])
    P.dma("sync", d_g, g[:], gd[:, :], writes=[b_g])
    xT_v = xT.rearrange("(kc p) n -> p kc n", p=128)
    for hh in range(2):
        t0 = hh * 1024
        for tt in range(2):
            tok = t0 + tt * 512
            xi = xin[tt]; bxi = b_xin[tt]
            P.dma("sync", d_xin[tt], xi[:], xT_v[:, :, tok:tok + 512], writes=[bxi])
            for kc in range(8):
                s = sq[kc % 2]; bs = b_sq[kc % 2]
                P.op("scalar", lambda e, s=s, xi=xi, kc=kc: e.activation(out=s[:], in_=xi[:, kc, :], func=AF.Square),
                     reads=[bxi], writes=[bs])
                P.mm(pS[:], ones[:], s[:], kc == 0, kc == 7, reads=[b_ones, bs], writes=[b_pS])
            P.op("scalar", lambda e: e.activation(out=lnb[:], in_=pS[:], func=AF.Ln, scale=1.0 / D, bias=EPS),
                 reads=[b_pS], writes=[b_ln])
            P.op("scalar", lambda e: e.activation(out=rstd[:], in_=lnb[:], func=AF.Exp, scale=-0.5),
                 reads=[b_ln], writes=[b_rstd])
            for kc in range(8):
                P.op("vector", lambda e, xi=xi, kc=kc, tt=tt: e.scalar_tensor_tensor(
                    out=hT[:, kc, tt * 512:(tt + 1) * 512], in0=xi[:, kc, :], scalar=g[:, kc:kc + 1], in1=rstd[:],
                    op0=ALU.mult, op1=ALU.mult), reads=[bxi, b_g, b_rstd], writes=[b_h[tt]])
        for j in range(NJ):
            w = w1[j % 2]; bw = b_w1[j % 2]
            P.dma("gpsimd", d_w1[j % 2], w[:], w1d[j, :, :], writes=[bw])
            for tt in range(2):
                hs = lambda kc, tt=tt: hT[:, kc, tt * 512:(tt + 1) * 512]
                for kc in range(8):
                    P.mm(pG[tt][:], w[:, kc * 128:(kc + 1) * 128], hs(kc), kc == 0, kc == 7,
                         reads=[bw, b_h[tt]], writes=[b_pG[tt]])
                for kc in range(8):
                    P.mm(pU[tt][:], w[:, 1024 + kc * 128:1024 + (kc + 1) * 128], hs(kc), kc == 0, kc == 7,
                         reads=[bw, b_h[tt]], writes=[b_pU[tt]])
                P.op("scalar", lambda e, tt=tt: e.activation(out=sg[tt][:], in_=pG[tt][:], func=AF.Silu),
                     reads=[b_pG[tt]], writes=[b_sg[tt]])
                P.op("vector", lambda e, tt=tt, j=j: e.tensor_tensor(
                    out=actT[:, j, tt * 512:(tt + 1) * 512], in0=sg[tt][:], in1=pU[tt][:], op=ALU.mult),
                    reads=[b_sg[tt], b_pU[tt]], writes=[b_act[tt]])
        for c in range(8):
            w = w2[c % 2]; bw = b_w2[c % 2]
            P.dma("gpsimd", d_w2[c % 2], w[:], w2d[c, :, :], writes=[bw])
            for tt in range(2):
                tok = t0 + tt * 512
                for j in range(NJ):
                    P.mm(pO[tt][:], w[:, j * 128:(j + 1) * 128], actT[:, j, tt * 512:(tt + 1) * 512], j == 0, j == NJ - 1,
                         reads=[bw, b_act[tt]], writes=[b_pO[tt]])
                P.dma("sync", d_xres[tt], xres[tt][:], xT[c * 128:(c + 1) * 128, tok:tok + 512], writes=[b_xres[tt]])
                P.op("vector", lambda e, tt=tt: e.scalar_tensor_tensor(
                    out=yo[tt][:], in0=pO[tt][:], scalar=0.5, in1=xres[tt][:], op0=ALU.mult, op1=ALU.add),
                    reads=[b_pO[tt], b_xres[tt]], writes=[b_yo[tt]])
                P.dma("sync", d_out[tt], yT[c * 128:(c + 1) * 128, tok:tok + 512], yo[tt][:], reads=[b_yo[tt]])
    st = P.emit(final_waits=[("sync", d_out[0]), ("sync", d_out[1])])
    return nc


def ffn_host_inputs(x_flat, norm, w_in, w_out):
    g = np.ascontiguousarray(norm.reshape(8, 128).T)
    wi = w_in.reshape(8, 128, 2, NJ, 128)
    w1 = np.ascontiguousarray(wi.transpose(3, 1, 2, 0, 4)).reshape(NJ, 128, 2048)
    wo = w_out.reshape(NJ, 128, 8, 128)
    w2 = np.ascontiguousarray(wo.transpose(2, 1, 0, 3)).reshape(8, 128, NJ * 128)
    ones = np.ones((128, 128), np.float32)
    maps = []
    for c in range(8):
        xT = np.ascontiguousarray(x_flat[c * T:(c + 1) * T].T)
        maps.append({"xT": xT, "g": g, "w1": w1, "w2": w2, "ones": ones})
    return maps


T = 2048
D = 1024
EPS = 1e-6
NCH = 25
ROPE_THETA = 10000.0
NTILES = 4
ND = 3

O_CQ, O_CKV, O_KR, O_GQ, O_GK, O_GV, O_GB, O_GA, O_GZ, O_MQ, O_MK, O_MV, O_GATE = 0, 256, 384, 416, 928, 1440, 1952, 1956, 1960, 2472, 2984, 3496, 4008
CH_COLS = ([(O_CQ, 128), (O_CQ + 128, 128), (O_CKV, 128), (O_KR, 32)] +
           [(O_GQ + h * 128, 128) for h in range(4)] + [(O_GK + h * 128, 128) for h in range(4)] +
           [(O_GV + h * 128, 128) for h in range(4)] + [(O_GB, 8)] +
           [(O_MQ + h * 128, 128) for h in range(4)] + [(O_MK + h * 128, 128) for h in range(4)])


def build_projb():
    nc = bass.Bass("TRN2", target_bir_lowering=False)
    DI = lambda n, s, dt=F32: nc.dram_tensor(n, s, dt, kind="ExternalInput").ap()
    DO = lambda n, s, dt=F32: nc.dram_tensor(n, s, dt, kind="ExternalOutput").ap()
    xT = DI("xT", [D, T]); gains_d = DI("gains", [128, 16])
    wb1 = DI("wb1", [NCH, 128, 1024]); wz_d = DI("wz", [128, 4096]); wmv_d = DI("wmv", [128, 4096])
    wuq_d = DI("wuq", [128, 1536]); wuk_d = DI("wuk", [128, 768]); wuv_d = DI("wuv", [128, 512])
    cmat = DI("cmat", [5, 128, 128])
    rope_d = DI("rope", [4, 128, T])
    o_mq = DO("mla_qT", [8, 96, T], BF16); o_mk = DO("mla_kT", [8, 96, T], BF16); o_mv = DO("mla_v", [T, 512], BF16)
    o_oq = DO("mo_qT", [4, 128, T], BF16); o_ok = DO("mo_kT", [4, 128, T], BF16); o_ov = DO("mo_v", [T, 512], BF16)
    o_gr = DO("graw", [12, 128, T]); o_gba = DO("gba", [8, T]); o_z = DO("z", [T, 512])
    P = Prog(nc)
    A = nc.alloc_sbuf_tensor
    PSA = nc.alloc_psum_tensor

    def sb(name, shape, dt=F32):
        return A("s_" + name, shape, dt), P.buf(name)

    def load(eng, t, b, src):
        P.dma(eng, P.dsem(), t, src, writes=[b])

    gains, b_gains = sb("gains", [128, 16]); load("sync", gains[:], b_gains, gains_d[:, :])
    CM, b_CM = sb("CM", [128, 5, 128], BF16); load("gpsimd", CM[:], b_CM, cmat.rearrange("k p n -> p k n"))
    ONES = CM[:, 0, :]; BLK64 = CM[:, 1, :]; RM_MLA = CM[0:96, 2, 0:96]; RM_MO = CM[:, 3, :]; SEL = CM[0:32, 4, 0:96]
    rope, b_rope = sb("rope", [128, 4, T]); load("sync", rope[:], b_rope, rope_d.rearrange("k p n -> p k n"))
    wz, b_wz = sb("wz", [128, 4096], BF16); load("gpsimd", wz[:], b_wz, wz_d[:, :])
    wmv, b_wmv = sb("wmv", [128, 4096], BF16); load("gpsimd", wmv[:], b_wmv, wmv_d[:, :])
    wuq, b_wuq = sb("wuq", [128, 1536], BF16); load("gpsimd", wuq[:], b_wuq, wuq_d[:, :])
    wuk, b_wuk = sb("wuk", [128, 768], BF16); load("gpsimd", wuk[:], b_wuk, wuk_d[:, :])
    wuv, b_wuv = sb("wuv", [128, 512], BF16); load("gpsimd", wuv[:], b_wuv, wuv_d[:, :])
    xins = [sb(f"xin{i}", [128, 8, 512]) for i in range(1)]; d_xins = [P.dsem() for _ in range(1)]
    sq = [sb(f"sq{i}", [128, 512], BF16) for i in range(2)]
    lnb, b_ln = sb("lnb", [128, 512]); rstd, b_rstd = sb("rstd", [128, 512])
    hT, b_h = sb("hT", [128, 8, T], BF16)
    wch = [sb(f"wch{i}", [128, 1024], BF16) for i in range(3)]; d_wch = [P.dsem() for _ in range(3)]
    cq = [sb(f"cq{i}", [128, T]) for i in range(3)]
    cqn = [sb(f"cqn{i}", [128, T], BF16) for i in range(3)]
    krope, b_krope = sb("krope", [32, T], BF16)
    NF = 4; NB = 8
    stf = [sb(f"stf{i}", [128, 512]) for i in range(NF)]; d_stf = [P.dsem() for _ in range(NF)]
    stb = [sb(f"stb{i}", [128, 512], BF16) for i in range(NB)]; d_stb = [P.dsem() for _ in range(NB)]
    cf = [0]; cb = [0]
    sqv = [sb(f"sqv{i}", [128, 512], BF16) for i in range(ND)]
    lnv = [sb(f"lnv{i}", [128, 512]) for i in range(ND)]
    rsv = [sb(f"rsv{i}", [128, 512]) for i in range(ND)]
    qn = [sb(f"qn{i}", [128, 512], BF16) for i in range(ND)]
    t1 = [sb(f"t1{i}", [128, 512]) for i in range(ND)]
    t2 = [sb(f"t2{i}", [128, 512]) for i in range(ND)]
    pS = (PSA("pS", [128, 512], F32), P.buf(excl=True))
    pP = [(PSA(f"pP{i}", [128, 512], F32), P.buf(excl=True)) for i in range(4)]
    pX = [(PSA(f"pX{i}", [128, 512], F32), P.buf(excl=True)) for i in range(3)]
    pN = pX
    cx = [0]
    pT = pS
    cp = [0]; cr = [0]
    all_out = []

    def out_f32(src_ps, b_ps, R, dst, eng="scalar"):
        i = cf[0] % NF; cf[0] += 1
        s, b_s = stf[i]
        if eng == "scalar":
            P.op("scalar", lambda e: e.copy(out=s[0:R, :], in_=src_ps), reads=[b_ps], writes=[b_s])
        else:
            P.op("vector", lambda e: e.tensor_copy(out=s[0:R, :], in_=src_ps), reads=[b_ps], writes=[b_s])
        P.dma("sync", d_stf[i], dst, s[0:R, :], reads=[b_s])

    def out_bf(src_ps, b_ps, R, dst, eng="vector"):
        i = cb[0] % NB; cb[0] += 1
        s, b_s = stb[i]
        if eng == "scalar":
            P.op("scalar", lambda e: e.copy(out=s[0:R, :], in_=src_ps), reads=[b_ps], writes=[b_s])
        else:
            P.op("vector", lambda e: e.tensor_copy(out=s[0:R, :], in_=src_ps), reads=[b_ps], writes=[b_s])
        P.dma("sync", d_stb[i], dst, s[0:R, :], reads=[b_s])

    def job(proj, R, gcol, onesm, rm, kcos, dim, tok, dst):
        k = cr[0] % ND; cr[0] += 1
        s_, b_s = sqv[k]; l_, b_l = lnv[k]; r_, b_r = rsv[k]; q_, b_q = qn[k]; a_, b_a = t1[k]; c_, b_c = t2[k]
        pp, b_pp = pP[cp[0] % 4]; cp[0] += 1
        ps = pp[0:R, :]
        proj(pp, b_pp)
        yield
        P.op("scalar", lambda e: e.activation(out=s_[0:R, :], in_=ps, func=AF.Square), reads=[b_pp], writes=[b_s])
        yield
        pn, b_pn = pX[cx[0] % 3]; cx[0] += 1
        P.mm(pn[0:R, :], onesm, s_[0:R, :], True, True, reads=[b_CM, b_s], writes=[b_pn])
        P.op("scalar", lambda e: e.activation(out=l_[0:R, :], in_=pn[0:R, :], func=AF.Ln, scale=1.0 / dim, bias=EPS), reads=[b_pn], writes=[b_l])
        P.op("scalar", lambda e: e.activation(out=r_[0:R, :], in_=l_[0:R, :], func=AF.Exp, scale=-0.5), reads=[b_l], writes=[b_r])
        yield
        P.op("vector", lambda e: e.scalar_tensor_tensor(out=q_[0:R, :], in0=ps, scalar=gains[0:R, gcol:gcol + 1], in1=r_[0:R, :], op0=ALU.mult, op1=ALU.mult),
             reads=[b_pp, b_gains, b_r], writes=[b_q])
        yield "late"
        pr, b_pr = pX[cx[0] % 3]; cx[0] += 1
        P.mm(pr[0:R, :], rm, q_[0:R, :], True, True, reads=[b_CM, b_q], writes=[b_pr])
        P.op("gpsimd", lambda e: e.tensor_tensor(out=a_[0:R, :], in0=q_[0:R, :], in1=rope[0:R, kcos, tok:tok + 512], op=ALU.mult),
             reads=[b_q, b_rope], writes=[b_a])
        P.op("vector", lambda e: e.tensor_tensor(out=c_[0:R, :], in0=pr[0:R, :], in1=rope[0:R, kcos + 1, tok:tok + 512], op=ALU.mult),
             reads=[b_pr, b_rope], writes=[b_c])
        i = cb[0] % NB; cb[0] += 1
        sbf, b_sb = stb[i]
        P.op("gpsimd", lambda e: e.tensor_tensor(out=sbf[0:R, :], in0=a_[0:R, :], in1=c_[0:R, :], op=ALU.add), reads=[b_a, b_c], writes=[b_sb])
        P.dma("sync", d_stb[i], dst, sbf[0:R, :], reads=[b_sb])
        yield

    def run_jobs(jobs):
        active = []
        jobs = list(jobs)
        while jobs or active:
            nxt = []
            for g in active:
                try:
                    next(g)
                    nxt.append(g)
                except StopIteration:
                    pass
            active = nxt
            if jobs:
                g = jobs.pop(0)
                next(g)
                active.append(g)

    xT_v = xT.rearrange("(kc p) n -> p kc n", p=128)
    for tt in range(NTILES):
        tok = tt * 512
        ts = slice(tok, tok + 512)
        xin, b_xin = xins[0]
        P.dma("sync", d_xins[0], xin[:], xT_v[:, :, ts], writes=[b_xin])
        for kc in range(8):
            s, bs = sq[kc % 2]
            P.op("scalar", lambda e, s=s, kc=kc, xin=xin: e.activation(out=s[:], in_=xin[:, kc, :], func=AF.Square), reads=[b_xin], writes=[bs])
            P.mm(pS[0][:], ONES, s[:], kc == 0, kc == 7, reads=[b_CM, bs], writes=[pS[1]])
        P.op("scalar", lambda e: e.activation(out=lnb[:], in_=pS[0][:], func=AF.Ln, scale=1.0 / D, bias=EPS), reads=[pS[1]], writes=[b_ln])
        P.op("scalar", lambda e: e.activation(out=rstd[:], in_=lnb[:], func=AF.Exp, scale=-0.5), reads=[b_ln], writes=[b_rstd])
        for kc in range(8):
            P.op("vector", lambda e, kc=kc, xin=xin, ts=ts: e.scalar_tensor_tensor(out=hT[:, kc, ts], in0=xin[:, kc, :], scalar=gains[:, kc:kc + 1], in1=rstd[:], op0=ALU.mult, op1=ALU.mult),
                 reads=[b_xin, b_gains, b_rstd], writes=[b_h])
    mo_jobs = []
    for ch in range(NCH):
        w, bw = wch[ch % 3]
        P.dma("gpsimd", d_wch[ch % 3], w[:], wb1[ch, :, :], writes=[bw])
        M = CH_COLS[ch][1]
        if ch >= 17:
            def mkproj(w=w, bw=bw, ts=None):
                def proj(pp, b_pp):
                    for kc in range(8):
                        P.mm(pp[0:128, :], w[:, kc * 128:kc * 128 + 128], hT[:, kc, ts], kc == 0, kc == 7, reads=[bw, b_h], writes=[b_pp])
                return proj
            for tt in range(NTILES):
                tok = tt * 512
                ts = slice(tok, tok + 512)
                dst = o_oq[ch - 17, :, ts] if ch < 21 else o_ok[ch - 21, :, ts]
                mo_jobs.append(job(mkproj(ts=ts), 128, 13 if ch < 21 else 14, BLK64, RM_MO, 2, 64.0, tok, dst))
            if ch % 2 == 0 or ch == NCH - 1:
                run_jobs(mo_jobs); mo_jobs = []
            continue
        for tt in range(NTILES):
            tok = tt * 512
            ts = slice(tok, tok + 512)
            pp, b_pp = pP[cp[0] % 4]; cp[0] += 1
            for kc in range(8):
                P.mm(pp[0:M, :], w[:, kc * 128:kc * 128 + M], hT[:, kc, ts], kc == 0, kc == 7, reads=[bw, b_h], writes=[b_pp])
            if ch < 3:
                c_, b_c = cq[ch]
                P.op("scalar", lambda e, c_=c_, pp=pp, ts=ts: e.copy(out=c_[:, ts], in_=pp[:]), reads=[b_pp], writes=[b_c])
            elif ch == 3:
                P.op("vector", lambda e, pp=pp, ts=ts: e.tensor_copy(out=krope[:, ts], in_=pp[0:32, :]), reads=[b_pp], writes=[b_krope])
            elif ch < 16:
                out_f32(pp[:], b_pp, 128, o_gr[ch - 4, :, ts], eng="scalar" if (ch + tt) % 2 else "vector")
            elif ch == 16:
                out_f32(pp[0:8, :], b_pp, 8, o_gba[:, ts])
    for tt in range(NTILES):
        ts = slice(tt * 512, (tt + 1) * 512)
        for grp, (idxs, dim, gc) in enumerate([((0, 1), 256.0, 8), ((2,), 128.0, 10)]):
            pn, b_pn = pX[cx[0] % 3]; cx[0] += 1; k = cr[0] % ND; cr[0] += 1
            for n_, ci in enumerate(idxs):
                s, bs = sq[n_ % 2]
                P.op("scalar", lambda e, s=s, ci=ci, ts=ts: e.activation(out=s[:], in_=cq[ci][0][:, ts], func=AF.Square), reads=[cq[ci][1]], writes=[bs])
                P.mm(pn[:], ONES, s[:], n_ == 0, n_ == len(idxs) - 1, reads=[b_CM, bs], writes=[b_pn])
            l_, b_l = lnv[k]; r_, b_r = rsv[k]
            P.op("scalar", lambda e, l_=l_, pn=pn, dim=dim: e.activation(out=l_[:], in_=pn[:], func=AF.Ln, scale=1.0 / dim, bias=EPS), reads=[b_pn], writes=[b_l])
            P.op("scalar", lambda e, l_=l_, r_=r_: e.activation(out=r_[:], in_=l_[:], func=AF.Exp, scale=-0.5), reads=[b_l], writes=[b_r])
            for n_, ci in enumerate(idxs):
                P.op("vector", lambda e, ci=ci, r_=r_, gc=gc, n_=n_, ts=ts: e.scalar_tensor_tensor(out=cqn[ci][0][:, ts], in0=cq[ci][0][:, ts], scalar=gains[:, gc + n_:gc + n_ + 1], in1=r_[:], op0=ALU.mult, op1=ALU.mult),
                     reads=[cq[ci][1], b_gains, b_r], writes=[cqn[ci][1]])
    mla_jobs = []
    for h in range(8):
        for tt in range(NTILES):
            tok = tt * 512
            ts = slice(tok, tok + 512)

            def projq(pp, b_pp, h=h, ts=ts):
                for kc in range(2):
                    P.mm(pp[0:96, :], wuq[:, kc * 768 + h * 96:kc * 768 + (h + 1) * 96], cqn[kc][0][:, ts], kc == 0, kc == 1, reads=[b_wuq, cqn[kc][1]], writes=[b_pp])

            def projk(pp, b_pp, h=h, ts=ts):
                P.mm(pp[0:96, :], wuk[:, h * 96:(h + 1) * 96], cqn[2][0][:, ts], True, False, reads=[b_wuk, cqn[2][1]], writes=[b_pp])
                P.mm(pp[0:96, :], SEL, krope[:, ts], False, True, reads=[b_CM, b_krope], writes=[b_pp])
            mla_jobs.append(job(projq, 96, 11, ONES[0:96, 0:96], RM_MLA, 0, 96.0, tok, o_mq[h, :, ts]))
            mla_jobs.append(job(projk, 96, 12, ONES[0:96, 0:96], RM_MLA, 0, 96.0, tok, o_mk[h, :, ts]))
    run_jobs(mla_jobs)
    for grp in range(4 * NTILES):
        gs = slice(grp * 128, (grp + 1) * 128)
        rows = gs
        P.mm(pT[0][:], cqn[2][0][:, gs], wuv[:], True, True, reads=[cqn[2][1], b_wuv], writes=[pT[1]])
        out_bf(pT[0][:], pT[1], 128, o_mv[rows, :], eng="scalar")
        for kc in range(8):
            P.mm(pT[0][:], hT[:, kc, gs], wz[:, kc * 512:(kc + 1) * 512], kc == 0, kc == 7, reads=[b_h, b_wz], writes=[pT[1]])
        out_f32(pT[0][:], pT[1], 128, o_z[rows, :], eng="vector")
        for kc in range(8):
            P.mm(pT[0][:], hT[:, kc, gs], wmv[:, kc * 512:(kc + 1) * 512], kc == 0, kc == 7, reads=[b_h, b_wmv], writes=[pT[1]])
        out_bf(pT[0][:], pT[1], 128, o_ov[rows, :], eng="scalar")
    st = P.emit(final_waits=[("sync", d) for d in d_stf + d_stb])
    return nc


def projb_consts():
    ones = np.ones((128, 128), np.float32)
    t = np.arange(128)
    blk64 = ((t[:, None] // 64) == (t[None, :] // 64)).astype(np.float32)
    rm_mla = np.zeros((128, 128), np.float32)
    for m in range(64, 80):
        rm_mla[m + 16, m] = -1.0
    for m in range(80, 96):
        rm_mla[m - 16, m] = 1.0
    rm_mo = np.zeros((128, 128), np.float32)
    for base in (0, 64):
        for m in range(base, base + 32):
            rm_mo[m + 32, m] = -1.0
        for m in range(base + 32, base + 64):
            rm_mo[m - 32, m] = 1.0
    sel = np.zeros((128, 128), np.float32)
    for i in range(32):
        sel[i, 64 + i] = 1.0
    return np.stack([ones, blk64, rm_mla, rm_mo, sel], 0)


def rope_tables(pos):
    pos = pos.astype(np.float32)
    out = np.zeros((4, 128, len(pos)), np.float32)
    out[0] = 1.0; out[2] = 1.0
    inv = (ROPE_THETA ** (-np.arange(16, dtype=np.float32) * 2.0 / 32)).astype(np.float32)
    ang = pos[None, :] * inv[:, None]
    for r in range(64, 96):
        i = (r - 64) % 16
        out[0, r] = np.cos(ang[i]); out[1, r] = np.sin(ang[i])
    out[0, 96:] = 0
    inv = (ROPE_THETA ** (-np.arange(32, dtype=np.float32) * 2.0 / 64)).astype(np.float32)
    ang = pos[None, :] * inv[:, None]
    for r in range(128):
        i = (r % 64) % 32
        out[2, r] = np.cos(ang[i]); out[3, r] = np.sin(ang[i])
    return out


def projb_weights(mix_norm, w_in, cq_norm, ckv_norm, w_uq, w_ukv, q_norm, k_norm, mq_norm, mk_norm):
    gains = np.zeros((128, 16), np.float32)
    gains[:, 0:8] = mix_norm.reshape(8, 128).T
    gains[:, 8:10] = cq_norm.reshape(2, 128).T
    gains[:, 10] = ckv_norm
    gains[0:96, 11] = q_norm; gains[0:96, 12] = k_norm
    gains[:, 13] = np.tile(mq_norm, 2); gains[:, 14] = np.tile(mk_norm, 2)
    wb1 = np.zeros((NCH, 128, 8, 128), np.float32)
    wr = w_in.reshape(8, 128, -1)
    for ch, (c0, m) in enumerate(CH_COLS):
        wb1[ch, :, :, 0:m] = wr[:, :, c0:c0 + m].transpose(1, 0, 2)
    wb1 = wb1.reshape(NCH, 128, 1024)
    wz = np.ascontiguousarray(wr[:, :, O_GZ:O_GZ + 512].transpose(1, 0, 2)).reshape(128, 4096)
    wmv = np.ascontiguousarray(wr[:, :, O_MV:O_MV + 512].transpose(1, 0, 2)).reshape(128, 4096)
    wuq = np.ascontiguousarray(w_uq.reshape(2, 128, 768).transpose(1, 0, 2)).reshape(128, 1536)
    kv = w_ukv.reshape(128, 8, 128)
    wuk = np.zeros((128, 8, 96), np.float32); wuk[:, :, 0:64] = kv[:, :, 0:64]
    wuv = np.ascontiguousarray(kv[:, :, 64:128]).reshape(128, 512)
    return {"gains": gains, "wb1": wb1, "wz": wz, "wmv": wmv, "wuq": wuq, "wuk": wuk.reshape(128, 768), "wuv": wuv, "cmat": projb_consts()}

S = 8192
NQT = 16
NDUMMY = 0
DUMMY_N = 384
HEADS = (0, 1, 2, 3)


def attn_consts():
    keys = np.arange(S)
    blkoh = (keys[None, :] // 256 == np.arange(32)[:, None]).astype(np.float32)
    p = np.arange(128)[:, None]; j = np.arange(512)[None, :]
    cmask = np.stack([np.where((128 * d + p) <= j, 0.0, -30000.0).astype(np.float32) for d in range(4)], 0)
    return blkoh, cmask, np.eye(128, dtype=np.float32), np.ones((128, 64), np.float32)


def build_attn():
    nc = bass.Bass("TRN2", target_bir_lowering=False)
    DI = lambda n, s, dt=F32: nc.dram_tensor(n, s, dt, kind="ExternalInput").ap()
    mq = DI("mq", [2, 96, S], BF16); mk = DI("mk", [2, 96, S], BF16); mv = DI("mv", [S, 128], BF16)
    oq = DI("oq", [128, S], BF16); ok = DI("ok", [128, S], BF16); ov = DI("ov", [S, 128], BF16)
    blkoh_d = DI("blkoh", [32, S]); cmask_d = DI("cmask", [4, 128, 512]); ident_d = DI("ident", [128, 128]); onesf_d = DI("onesf", [128, 64])
    oT = nc.dram_tensor("oT", [4, 64, S], BF16, kind="ExternalOutput").ap()
    P = Prog(nc)
    A = nc.alloc_sbuf_tensor
    PSA = nc.alloc_psum_tensor

    def sb(name, shape, dt=F32):
        return A("s_" + name, shape, dt), P.buf(name)

    cmask, b_cm = sb("cmask", [128, 4, 512], BF16); P.dma("gpsimd", P.dsem(), cmask[:], cmask_d.rearrange("k p n -> p k n"), writes=[b_cm])
    ident, b_id = sb("ident", [128, 128], BF16); P.dma("gpsimd", P.dsem(), ident[:], ident_d[:, :], writes=[b_id])
    onesf, b_of = sb("onesf", [128, 64]); P.dma("sync", P.dsem(), onesf[:], onesf_d[:, :], writes=[b_of])
    Ka = [sb(f"Ka{i}", [128, S], BF16) for i in range(2)]
    Qa = [sb(f"Qa{i}", [128, S], BF16) for i in range(2)]
    Va = [sb(f"Va{i}", [128, 64, 65], BF16) for i in range(2)]
    d_K = [P.dsem() for _ in range(2)]; d_Q = [P.dsem() for _ in range(2)]; d_V = [P.dsem() for _ in range(2)]
    d_K2 = [P.dsem() for _ in range(2)]
    kmf, b_kmf = sb("kmf", [128, 32]); kmT, b_kmT = sb("kmT", [128, 32], BF16)
    gm = [sb(f"gm{i}", [128, 32]) for i in range(4)]
    top8 = [sb(f"top8{i}", [128, 8]) for i in range(4)]
    sel = [sb(f"sel{i}", [128, 32]) for i in range(4)]
    negpad = [sb(f"negpad{i}", [128, 128], BF16) for i in range(4)]
    PT = [sb(f"PT{i}", [128, 512], BF16) for i in range(4)]
    osb = [sb(f"osb{i}", [128, 512]) for i in range(2)]
    rec = [sb(f"rec{i}", [128, 512]) for i in range(2)]
    onb = [sb(f"onb{i}", [64, 512], BF16) for i in range(2)]; d_on = [P.dsem() for _ in range(2)]
    pSc = [(PSA(f"pSc{i}", [128, 512], F32), P.buf(excl=True)) for i in range(3)]
    pO = [(PSA(f"pO{i}", [128, 512], F32), P.buf(excl=True)) for i in range(2)]
    pBC = (PSA("pBC", [128, 512], F32), P.buf(excl=True))
    pG = (PSA("pG", [128, 512], F32), P.buf(excl=True))
    pTr = (PSA("pTr", [128, 512], F32), P.buf(excl=True))
    pD = pG
    for i in range(4):
        P.op("vector", lambda e, i=i: e.memset(negpad[i][0][:], 0.0), writes=[negpad[i][1]])

    pending = []
    for n_, hi in enumerate(HEADS):
        s2 = n_ % 2
        K, b_K = Ka[s2]; Q, b_Q = Qa[s2]; V, b_V = Va[s2]
        moba = hi >= 2
        if not moba:
            rows = slice(0, 96); scale = 96.0 ** -0.5
            P.dma("sync", d_K[s2], K[0:96, :], mk[hi, :, :], writes=[b_K])
            P.dma("sync", d_Q[s2], Q[0:96, :], mq[hi, :, :], writes=[b_Q])
            vsrc = mv[:, hi * 64:(hi + 1) * 64]
        else:
            scale = 0.125
            hb = hi - 2
            rows = slice(0, 96); krows = slice(0, 64); off = 64
            srows = slice(hb * 64, (hb + 1) * 64)
            P.dma("sync", d_K[s2], K[krows, :], ok[srows, :], writes=[b_K])
            P.dma("gpsimd", d_K2[s2], K[off:off + 32, :], blkoh_d[:, :], writes=[b_K])
            P.dma("sync", d_Q[s2], Q[krows, :], oq[srows, :], writes=[b_Q])
            vsrc = ov[:, hb * 64:(hb + 1) * 64]
        P.dma("sync", d_V[s2], V[:, :, 0:64], vsrc.rearrange("(kc p) d -> p kc d", p=128), writes=[b_V])
        P.op("gpsimd", lambda e, V=V: e.memset(V[:, :, 64:65], 1.0), writes=[b_V])
        if moba:
            P.op("vector", lambda e, K=K, krows=krows: e.tensor_reduce(out=kmf[krows, :], in_=K[krows, :].rearrange("p (n k) -> p n k", k=256), axis=AX.X, op=ALU.add),
                 reads=[b_K], writes=[b_kmf])
            P.op("vector", lambda e, krows=krows: e.tensor_scalar(out=kmT[krows, :], in0=kmf[krows, :], scalar1=1.0 / 256, scalar2=None, op0=ALU.mult),
                 reads=[b_kmf], writes=[b_kmT])
            for qt in range(NQT):
                for j in range(4):
                    qc = qt * 4 + j
                    P.mm(pG[0][:, j * 32:(j + 1) * 32], Q[krows, qc * 128:(qc + 1) * 128], kmT[krows, :], True, True, reads=[b_Q, b_kmT], writes=[pG[1]])
                for j in range(4):
                    qc = qt * 4 + j; qb = qc // 2
                    g_, b_g = gm[j]; t8, b_t8 = top8[j]; sl_, b_sl = sel[j]; npd, b_np = negpad[j]
                    P.op("gpsimd", lambda e, g_=g_: e.memset(g_[:], -1e30), writes=[b_g])
                    if qb > 0:
                        P.op("vector", lambda e, g_=g_, j=j, qb=qb: e.tensor_copy(out=g_[:, 0:qb], in_=pG[0][:, j * 32:j * 32 + qb]), reads=[pG[1]], writes=[b_g])
                    P.op("vector", lambda e, g_=g_, t8=t8: e.max(out=t8[:], in_=g_[:]), reads=[b_g], writes=[b_t8])
                    P.op("vector", lambda e, g_=g_, t8=t8, sl_=sl_: e.tensor_scalar(out=sl_[:], in0=g_[:], scalar1=t8[:, 2:3], scalar2=None, op0=ALU.is_ge),
                         reads=[b_g, b_t8], writes=[b_sl])
                    P.op("vector", lambda e, sl_=sl_, npd=npd, off=off: e.tensor_scalar(out=npd[:, off:off + 32], in0=sl_[:], scalar1=-1.0, scalar2=30000.0, op0=ALU.add, op1=ALU.mult),
                         reads=[b_sl], writes=[b_np])
                    P.op("vector", lambda e, npd=npd, off=off, qb=qb: e.memset(npd[:, off + qb:off + qb + 1], 0.0), writes=[b_np])
                    P.mm(pTr[0][:, j * 128:(j + 1) * 128], npd[:], ident[:], True, True, reads=[b_np, b_id], writes=[pTr[1]])
                P.op("scalar", lambda e, Q=Q, off=off, qt=qt: e.copy(out=Q[off:off + 32, qt * 512:(qt + 1) * 512], in_=pTr[0][off:off + 32, :]),
                     reads=[pTr[1]], writes=[b_Q])
        for qt in range(NQT):
            nkc = 4 * qt + 4
            qs = slice(qt * 512, (qt + 1) * 512)
            po, b_po = pO[qt % 2]

            def score(kc):
                ps, b_ps = pSc[kc % 3]
                diag = kc >= 4 * qt
                P.mm(ps[:], K[rows, kc * 128:(kc + 1) * 128], Q[rows, qs], True, not diag, reads=[b_K, b_Q], writes=[b_ps])
                if diag:
                    P.mm(ps[:], ident[:], cmask[:, kc - 4 * qt, :], False, True, reads=[b_id, b_cm], writes=[b_ps])
            score(0)
            if nkc > 1:
                score(1)
            for kc in range(nkc):
                if kc + 2 < nkc:
                    score(kc + 2)
                if kc == 2 and pending:
                    pending.pop(0)()
                ps, b_ps = pSc[kc % 3]
                pt_, b_pt = PT[kc % 4]
                P.op("scalar", lambda e, ps=ps, pt_=pt_, scale=scale: e.activation(out=pt_[:], in_=ps[:], func=AF.Exp, scale=scale), reads=[b_ps], writes=[b_pt])
                P.mm(po[0:65, :], V[:, kc, :], pt_[:], kc == 0, kc == nkc - 1, reads=[b_V, b_pt], writes=[b_po])
                for _d in range(NDUMMY):
                    P.mm(pD[0][:, 0:DUMMY_N], K[rows, kc * 128:(kc + 1) * 128], Q[rows, qt * 512:qt * 512 + DUMMY_N], True, True, reads=[b_K, b_Q], writes=[pD[1]])
            o_, b_o = osb[qt % 2]; r_, b_r = rec[qt % 2]; on_, b_on = onb[qt % 2]
            P.op("vector", lambda e, o_=o_, po=po: e.tensor_copy(out=o_[0:65, :], in_=po[0:65, :]), reads=[b_po], writes=[b_o])
            P.op("vector", lambda e, o_=o_, r_=r_: e.reciprocal(out=r_[64:65, :], in_=o_[64:65, :]), reads=[b_o], writes=[b_r])

            def epilogue(o_=o_, b_o=b_o, r_=r_, b_r=b_r, on_=on_, b_on=b_on, qt=qt, qs=qs, hi=hi):
                P.mm(pBC[0][0:64, :], onesf[64:65, :], r_[64:65, :], True, True, reads=[b_of, b_r], writes=[pBC[1]])
                P.op("vector", lambda e: e.tensor_tensor(out=on_[:], in0=o_[0:64, :], in1=pBC[0][0:64, :], op=ALU.mult), reads=[b_o, pBC[1]], writes=[b_on])
                P.dma("sync", d_on[qt % 2], oT[hi, :, qs], on_[:], reads=[b_on])
            pending.append(epilogue)
    while pending:
        pending.pop(0)()
    st = P.emit(final_waits=[("sync", d) for d in d_on])
    return nc

S = 8192
NSEG = 4
SEG = 2048
NT = 16
GT = 4
EPS = 1e-6
NLV = 6


def gdn_consts():
    t = np.arange(128)
    M = (t[:, None] <= t[None, :]).astype(np.float32)
    NEGM = np.where(t[:, None] >= t[None, :], 0.0, -1e30).astype(np.float32)
    STRICT = (t[:, None] > t[None, :]).astype(np.float32)
    ident = np.eye(128, dtype=np.float32)
    ones = np.ones((128, 128), np.float32)
    NEGS = np.where(t[:, None] > t[None, :], 0.0, -1e30).astype(np.float32)
    return np.stack([M, ones, NEGM, NEGS, ident, -ones], 0)


def build_gdn():
    nc = bass.Bass("TRN2", target_bir_lowering=False)
    DI = lambda n, s, dt=F32: nc.dram_tensor(n, s, dt, kind="ExternalInput").ap()
    rq = DI("rq", [128, S]); rk = DI("rk", [128, S]); rv = DI("rv", [128, S])
    zd = DI("z", [S, 128])
    bl = DI("bl", [128, 64]); al = DI("al", [128, 64])
    cw = DI("cw", [128, 12]); sc = DI("sc", [128, 2]); gn = DI("gn", [128, 128])
    cst = DI("cst", [6, 128, 128])
    od = nc.dram_tensor("o", [S, 128], BF16, kind="ExternalOutput").ap()
    P = Prog(nc)
    A = nc.alloc_sbuf_tensor
    PS = nc.alloc_psum_tensor

    def sb(name, shape, dt=F32):
        return A("s_" + name, shape, dt), P.buf(name)

    C, b_C = sb("C", [128, 6, 128])
    cwt, b_cw = sb("cwt", [128, 12]); sct, b_sc = sb("sct", [128, 2]); gnt, b_gn = sb("gnt", [128, 128])
    blt, b_bl = sb("blt", [128, 64]); alt, b_al = sb("alt", [128, 64])
    beta, b_beta = sb("beta", [128, 64]); gg, b_gg = sb("gg", [128, 64])
    tmp64, b_tmp64 = sb("tmp64", [128, 64]); ea, b_ea = sb("ea", [128, 1])
    P.dma("sync", P.dsem(), C[:], cst.rearrange("k p n -> p k n"), writes=[b_C])
    for (t_, d_, b_) in [(cwt, cw, b_cw), (sct, sc, b_sc), (gnt, gn, b_gn), (blt, bl, b_bl), (alt, al, b_al)]:
        P.dma("sync", P.dsem(), t_[:], d_[:, :], writes=[b_])
    Mm, ONES, NEGM, STRICT, IDENT, NEGONES = [C[:, i, :] for i in range(6)]
    NEG4, b_N4 = sb("NEG4", [128, GT, 128]); STR4, b_S4 = sb("STR4", [128, GT, 128]); ID4, b_I4 = sb("ID4", [128, GT, 128])
    GN4, b_G4 = sb("GN4", [128, GT, 128])
    for t in range(GT):
        P.op("gpsimd", lambda e, t=t: e.tensor_copy(out=GN4[:, t, :], in_=gnt[:]), reads=[b_gn], writes=[b_G4])
        P.op("gpsimd", lambda e, t=t: e.tensor_copy(out=NEG4[:, t, :], in_=NEGM), reads=[b_C], writes=[b_N4])
        P.op("gpsimd", lambda e, t=t: e.tensor_copy(out=STR4[:, t, :], in_=STRICT), reads=[b_C], writes=[b_S4])
        P.op("gpsimd", lambda e, t=t: e.tensor_copy(out=ID4[:, t, :], in_=IDENT), reads=[b_C], writes=[b_I4])
    P.op("scalar", lambda e: e.activation(out=beta[:], in_=blt[:], func=AF.Sigmoid), reads=[b_bl], writes=[b_beta])
    P.op("scalar", lambda e: e.activation(out=tmp64[:], in_=alt[:], func=AF.Exp, bias=sct[:, 1:2]), reads=[b_al, b_sc], writes=[b_tmp64])
    P.op("scalar", lambda e: e.activation(out=tmp64[:], in_=tmp64[:], func=AF.Ln, bias=1.0), reads=[b_tmp64], writes=[b_tmp64])
    P.op("scalar", lambda e: e.activation(out=ea[:], in_=sct[:, 0:1], func=AF.Exp), reads=[b_sc], writes=[b_ea])
    P.op("vector", lambda e: e.tensor_scalar(out=gg[:], in0=tmp64[:], scalar1=ea[:, 0:1], scalar2=-1.0, op0=ALU.mult, op1=ALU.mult),
         reads=[b_tmp64, b_ea], writes=[b_gg])

    raw = [sb(f"raw{i}", [128, SEG + 3]) for i in range(3)]
    d_raw = [P.dsem() for _ in range(3)]
    cvs = [[sb(f"cv{s}_{i}", [128, SEG]) for i in range(3)] for s in range(2)]
    sqb, b_sq = sb("sqb", [128, 512]); lnb, b_ln = sb("lnb", [128, 512]); rsb, b_rs = sb("rsb", [128, 512])
    stat = [[sb(f"{n}{s}", [128, NT]) for n in ("gcum", "egc", "edec", "dec", "begc")] for s in range(2)]

    def g4(name, n=1):
        return [sb(f"{name}{i}", [128, GT, 128]) for i in range(n)]
    ktm = g4("ktm")[0]; vb = g4("vb")[0]; rw = g4("rw")[0]
    Gm = g4("Gm")[0]; nGm = g4("nGm")[0]; dmin = g4("dmin")[0]; Dm = g4("Dm")[0]; Dms = g4("Dms")[0]
    Am = g4("Am")[0]; Bm = g4("Bm")[0]; qkd = g4("qkd")[0]
    Qm = g4("Qm", 2); Ym = g4("Ym", 2); YTm = g4("YTm", 2)
    kdec = g4("kdec", 2); qkdT = g4("qkdT", 2); uu = g4("uu", 2); wT = g4("wT", 2)
    zt = g4("zt", 2); d_z = [P.dsem() for _ in range(2)]
    szt = g4("szt", 2)
    ofb = [sb(f"ofb{i}", [128, GT, 128], BF16) for i in range(2)]; d_o = [P.dsem() for _ in range(2)]
    NB = 2
    vnew = [sb(f"vnew{i}", [128, 128]) for i in range(NB)]
    o1 = [sb(f"o1{i}", [128, 128]) for i in range(NB)]
    ot = [sb(f"ot{i}", [128, 128]) for i in range(NB)]
    osq = [sb(f"osq{i}", [128, 128]) for i in range(NB)]
    ss = [sb(f"ss{i}", [128, 1]) for i in range(NB)]
    lss = [sb(f"lss{i}", [128, 1]) for i in range(NB)]
    rss = [sb(f"rss{i}", [128, 1]) for i in range(NB)]
    og = [sb(f"og{i}", [128, 128]) for i in range(NB)]
    St = [sb(f"St{i}", [128, 128]) for i in range(2)]
    pb = [(PS(f"pb{i}", [128, 512], F32), P.buf(f"pb{i}", excl=True)) for i in range(8)]
    pcount = [0]

    def bank():
        i = pcount[0] % 6
        pcount[0] += 1
        return pb[i]
    pV, b_pV = pb[6]
    pSt, b_pSt = pb[7]
    P.op("vector", lambda e: e.memset(St[0][0][:], 0.0), writes=[St[0][1]])
    state = {"scur": 0}

    def seg_prep(seg):
        s0 = seg * SEG
        cv = cvs[seg % 2]
        for qi, rd in enumerate((rq, rk, rv)):
            r_, b_r = raw[qi]
            if seg == 0:
                P.op("gpsimd", lambda e, r_=r_: e.memset(r_[:, 0:3], 0.0), writes=[b_r])
                P.dma("sync", d_raw[qi], r_[:, 3:], rd[:, 0:SEG], writes=[b_r])
            else:
                P.dma("sync", d_raw[qi], r_[:, :], rd[:, s0 - 3:s0 + SEG], writes=[b_r])
            c_, b_c = cv[qi]
            for hf in range(2):
                lo = hf * 1024
                sl = slice(lo, lo + 1024)
                P.op("scalar", lambda e, c_=c_, r_=r_, lo=lo, qi=qi, sl=sl: e.activation(
                    out=c_[:, sl], in_=r_[:, lo:lo + 1024], func=AF.Copy, scale=cwt[:, qi * 4:qi * 4 + 1]),
                    reads=[b_r, b_cw], writes=[b_c])
                for tap in range(1, 4):
                    P.op("vector", lambda e, c_=c_, r_=r_, lo=lo, qi=qi, sl=sl, tap=tap: e.scalar_tensor_tensor(
                        out=c_[:, sl], in0=r_[:, lo + tap:lo + tap + 1024], scalar=cwt[:, qi * 4 + tap:qi * 4 + tap + 1],
                        in1=c_[:, sl], op0=ALU.mult, op1=ALU.add), reads=[b_r, b_cw, b_c], writes=[b_c])
                P.op("scalar", lambda e, c_=c_, sl=sl: e.activation(out=c_[:, sl], in_=c_[:, sl], func=AF.Silu),
                     reads=[b_c], writes=[b_c])
        for qi in range(2):
            c_, b_c = cv[qi]
            for t4 in range(4):
                sl = slice(t4 * 512, (t4 + 1) * 512)
                pt, b_pt = bank()
                P.op("scalar", lambda e, c_=c_, sl=sl: e.activation(out=sqb[:], in_=c_[:, sl], func=AF.Square), reads=[b_c], writes=[b_sq])
                P.mm(pt[:], ONES, sqb[:], True, True, reads=[b_C, b_sq], writes=[b_pt])
                P.op("scalar", lambda e, pt=pt: e.activation(out=lnb[:], in_=pt[:], func=AF.Ln, bias=EPS), reads=[b_pt], writes=[b_ln])
                P.op("scalar", lambda e: e.activation(out=rsb[:], in_=lnb[:], func=AF.Exp, scale=-0.5), reads=[b_ln], writes=[b_rs])
                scl = (128.0 ** -0.5) if qi == 0 else 1.0
                P.op("vector", lambda e, c_=c_, sl=sl, scl=scl: e.scalar_tensor_tensor(
                    out=c_[:, sl], in0=c_[:, sl], scalar=scl, in1=rsb[:], op0=ALU.mult, op1=ALU.mult),
                    reads=[b_c, b_rs], writes=[b_c])
        (gcum, b_gcum), (egc, b_egc), (edec, b_edec), (dec, b_dec), (begc, b_begc) = stat[seg % 2]
        gsl = slice(seg * NT, (seg + 1) * NT)
        pt, b_pt = bank()
        P.mm(pt[:, 0:NT], Mm, gg[:, gsl], True, True, reads=[b_C, b_gg], writes=[b_pt])
        P.mm(pt[:, 16:16 + NT], ONES, gg[:, gsl], True, True, reads=[b_C, b_gg], writes=[b_pt])
        P.op("vector", lambda e, pt=pt: e.tensor_copy(out=gcum[:], in_=pt[:, 0:NT]), reads=[b_pt], writes=[b_gcum])
        P.op("scalar", lambda e, pt=pt: e.activation(out=egc[:], in_=pt[:, 0:NT], func=AF.Exp), reads=[b_pt], writes=[b_egc])
        P.op("vector", lambda e, pt=pt: e.tensor_tensor(out=edec[:], in0=pt[:, 16:16 + NT], in1=gcum[:], op=ALU.subtract),
             reads=[b_pt, b_gcum], writes=[b_edec])
        P.op("scalar", lambda e: e.activation(out=edec[:], in_=edec[:], func=AF.Exp), reads=[b_edec], writes=[b_edec])
        P.op("scalar", lambda e, pt=pt: e.activation(out=dec[:], in_=pt[:, 16:16 + NT], func=AF.Exp), reads=[b_pt], writes=[b_dec])
        P.op("vector", lambda e, gsl=gsl: e.tensor_tensor(out=begc[:], in0=beta[:, gsl], in1=egc[:], op=ALU.mult),
             reads=[b_beta, b_egc], writes=[b_begc])

    def prepass_stages(seg, grp):
        cv = cvs[seg % 2]
        qT_, b_qT = cv[0]; kT_, b_kT = cv[1]; vT_, b_vT = cv[2]
        (gcum, b_gcum), (egc, b_egc), (edec, b_edec), (dec, b_dec), (begc, b_begc) = stat[seg % 2]
        gi = (seg * (NT // GT) + grp) % 2
        Ts = [grp * GT + t for t in range(GT)]
        cs = lambda t: slice(Ts[t] * 128, (Ts[t] + 1) * 128)
        Gs = [seg * NT + T for T in Ts]
        kd, b_kd = kdec[gi]; qT2, b_qT2 = qkdT[gi]; u_, b_u = uu[gi]; w_, b_w = wT[gi]
        pk, b_pk = bank(); pv, b_pv = bank()
        for t in range(GT):
            P.op("tensor", lambda e, t=t: e.transpose(pk[:, t * 128:(t + 1) * 128], kT_[:, cs(t)], IDENT), reads=[b_kT, b_C], writes=[b_pk])
        for t in range(GT):
            P.op("tensor", lambda e, t=t: e.transpose(pv[:, t * 128:(t + 1) * 128], vT_[:, cs(t)], IDENT), reads=[b_vT, b_C], writes=[b_pv])
        for t in range(GT):
            P.op("vector", lambda e, t=t: e.tensor_scalar(out=vb[0][:, t, :], in0=pv[:, t * 128:(t + 1) * 128], scalar1=beta[:, Gs[t]:Gs[t] + 1], scalar2=None, op0=ALU.mult),
                 reads=[b_pv, b_beta], writes=[vb[1]])
        for t in range(GT):
            P.op("scalar", lambda e, t=t: e.activation(out=rw[0][:, t, :], in_=pk[:, t * 128:(t + 1) * 128], func=AF.Copy, scale=begc[:, Ts[t]:Ts[t] + 1]),
                 reads=[b_pk, b_begc], writes=[rw[1]])
            P.op("scalar", lambda e, t=t: e.activation(out=kd[:, t, :], in_=pk[:, t * 128:(t + 1) * 128], func=AF.Copy, scale=edec[:, Ts[t]:Ts[t] + 1]),
                 reads=[b_pk, b_edec], writes=[b_kd])
        for t in range(GT):
            P.op("vector", lambda e, t=t: e.tensor_scalar(out=Gm[0][:, t, :], in0=Mm, scalar1=gg[:, Gs[t]:Gs[t] + 1], scalar2=None, op0=ALU.mult),
                 reads=[b_C, b_gg], writes=[Gm[1]])
        yield
        pd, b_pd = bank(); pkk, b_pkk = bank(); pqk, b_pqk = bank()
        for t in range(GT):
            o = slice(t * 128, (t + 1) * 128)
            P.mm(pd[:, o], Gm[0][:, t, :], ONES, True, False, reads=[Gm[1], b_C], writes=[b_pd])
            P.mm(pd[:, o], NEGONES, Gm[0][:, t, :], False, True, reads=[Gm[1], b_C], writes=[b_pd])
        for t in range(GT):
            o = slice(t * 128, (t + 1) * 128)
            P.mm(pkk[:, o], kT_[:, cs(t)], kT_[:, cs(t)], True, True, reads=[b_kT], writes=[b_pkk])
        for t in range(GT):
            o = slice(t * 128, (t + 1) * 128)
            P.mm(pqk[:, o], qT_[:, cs(t)], kT_[:, cs(t)], True, True, reads=[b_kT, b_qT], writes=[b_pqk])
        fl = lambda x: x[:].rearrange("p t n -> p (t n)")
        P.op("vector", lambda e: e.scalar_tensor_tensor(out=fl(dmin[0]), in0=pd[:], scalar=0.0, in1=fl(NEG4), op0=ALU.min, op1=ALU.add),
             reads=[b_pd, b_N4], writes=[dmin[1]])
        P.op("scalar", lambda e: e.activation(out=fl(Dm[0]), in_=fl(dmin[0]), func=AF.Exp), reads=[dmin[1]], writes=[Dm[1]])
        P.op("vector", lambda e: e.scalar_tensor_tensor(out=fl(nGm[0]), in0=pd[:], scalar=0.0, in1=fl(STR4), op0=ALU.min, op1=ALU.add),
             reads=[b_pd, b_S4], writes=[nGm[1]])
        P.op("scalar", lambda e: e.activation(out=fl(Dms[0]), in_=fl(nGm[0]), func=AF.Exp), reads=[nGm[1]], writes=[Dms[1]])
        for t in range(GT):
            P.op("vector", lambda e, t=t: e.scalar_tensor_tensor(out=Am[0][:, t, :], in0=pkk[:, t * 128:(t + 1) * 128], scalar=beta[:, Gs[t]:Gs[t] + 1], in1=Dms[0][:, t, :], op0=ALU.mult, op1=ALU.mult),
                 reads=[b_pkk, b_beta, Dms[1]], writes=[Am[1]])
        P.op("vector", lambda e: e.tensor_tensor(out=fl(qkd[0]), in0=pqk[:], in1=fl(Dm[0]), op=ALU.mult), reads=[b_pqk, Dm[1]], writes=[qkd[1]])
        yield
        pbt, b_pbt = bank(); pqt, b_pqt = bank()
        for t in range(GT):
            P.op("tensor", lambda e, t=t: e.transpose(pbt[:, t * 128:(t + 1) * 128], Am[0][:, t, :], IDENT), reads=[Am[1], b_C], writes=[b_pbt])
        for t in range(GT):
            P.op("tensor", lambda e, t=t: e.transpose(pqt[:, t * 128:(t + 1) * 128], qkd[0][:, t, :], IDENT), reads=[qkd[1], b_C], writes=[b_pqt])
        P.op("scalar", lambda e: e.copy(out=fl(Bm[0]), in_=pbt[:]), reads=[b_pbt], writes=[Bm[1]])
        P.op("vector", lambda e: e.tensor_copy(out=fl(qT2), in_=pqt[:]), reads=[b_pqt], writes=[b_qT2])
        Qc, b_Qc = Qm[0]
        P.op("gpsimd", lambda e: e.scalar_tensor_tensor(out=fl(Qc), in0=fl(Bm[0]), scalar=-1.0, in1=fl(ID4), op0=ALU.mult, op1=ALU.add),
             reads=[Bm[1], b_I4], writes=[b_Qc]) if False else \
            P.op("vector", lambda e: e.scalar_tensor_tensor(out=fl(Qc), in0=fl(Bm[0]), scalar=-1.0, in1=fl(ID4), op0=ALU.mult, op1=ALU.add),
                 reads=[Bm[1], b_I4], writes=[b_Qc])
        yield
        Yc, b_Yc = Bm; YTc, b_YTc = Am
        for lv in range(NLV):
            pyt, b_pyt = bank()
            Yn, b_Yn = Ym[lv % 2]; YTn, b_YTn = YTm[lv % 2]
            for t in range(GT):
                P.mm(pyt[:, t * 128:(t + 1) * 128], Yc[:, t, :], YTc[:, t, :], True, True, reads=[b_Yc, b_YTc], writes=[b_pyt])
            if lv < NLV - 1:
                py, b_py = bank()
                for t in range(GT):
                    P.mm(py[:, t * 128:(t + 1) * 128], YTc[:, t, :], Yc[:, t, :], True, True, reads=[b_Yc, b_YTc], writes=[b_py])
            P.op("scalar", lambda e, YTn=YTn, pyt=pyt: e.copy(out=fl(YTn), in_=pyt[:]), reads=[b_pyt], writes=[b_YTn])
            if lv < NLV - 1:
                P.op("vector", lambda e, Yn=Yn, py=py: e.tensor_copy(out=fl(Yn), in_=py[:]), reads=[b_py], writes=[b_Yn])
            yield
            Qo, b_Qo = Qm[lv % 2]; Qn, b_Qn = Qm[(lv + 1) % 2]
            pq, b_pq = bank()
            for t in range(GT):
                P.mm(pq[:, t * 128:(t + 1) * 128], YTn[:, t, :], Qo[:, t, :], True, True, reads=[b_YTn, b_Qo], writes=[b_pq])
            P.op("vector", lambda e, Qn=Qn, Qo=Qo, pq=pq: e.tensor_tensor(out=fl(Qn), in0=pq[:], in1=fl(Qo), op=ALU.add), reads=[b_pq, b_Qo], writes=[b_Qn])
            Yc, b_Yc = Yn, b_Yn
            YTc, b_YTc = YTn, b_YTn
            yield
        Tt, b_Tt = Qm[NLV % 2]
        pu, b_pu = bank(); pw, b_pw = bank()
        for t in range(GT):
            P.mm(pu[:, t * 128:(t + 1) * 128], Tt[:, t, :], vb[0][:, t, :], True, True, reads=[b_Tt, vb[1]], writes=[b_pu])
        for t in range(GT):
            P.mm(pw[:, t * 128:(t + 1) * 128], rw[0][:, t, :], Tt[:, t, :], True, True, reads=[b_Tt, rw[1]], writes=[b_pw])
        P.op("scalar", lambda e: e.copy(out=fl(u_), in_=pu[:]), reads=[b_pu], writes=[b_u])
        P.op("vector", lambda e: e.tensor_copy(out=fl(w_), in_=pw[:]), reads=[b_pw], writes=[b_w])
        G0 = Gs[0]
        P.dma("sync", d_z[gi], zt[gi][0][:], zd[G0 * 128:(G0 + GT) * 128, :].rearrange("(t p) d -> p t d", p=128), writes=[zt[gi][1]])
        P.op("scalar", lambda e: e.activation(out=fl(szt[gi][0]), in_=fl(zt[gi][0]), func=AF.Silu), reads=[zt[gi][1]], writes=[szt[gi][1]])
        P.op("gpsimd", lambda e: e.tensor_tensor(out=fl(szt[gi][0]), in0=fl(szt[gi][0]), in1=fl(GN4), op=ALU.mult), reads=[szt[gi][1], b_G4], writes=[szt[gi][1]])
        yield

    def scan_steps(seg, grp):
        cv = cvs[seg % 2]
        qT_, b_qT = cv[0]
        (gcum, b_gcum), (egc, b_egc), (edec, b_edec), (dec, b_dec), (begc, b_begc) = stat[seg % 2]
        gi = (seg * (NT // GT) + grp) % 2
        kd, b_kd = kdec[gi]; qT2, b_qT2 = qkdT[gi]; u_, b_u = uu[gi]; w_, b_w = wT[gi]
        for t in range(GT):
            T = grp * GT + t
            G = seg * NT + T
            i2 = G % NB
            cs = slice(T * 128, (T + 1) * 128)
            Sc, b_Sc = St[state["scur"]]; Sn, b_Sn = St[1 - state["scur"]]
            P.mm(pV[:, 0:128], w_[:, t, :], Sc[:], True, True, reads=[b_w, b_Sc], writes=[b_pV])
            P.mm(pV[:, 128:256], qT_[:, cs], Sc[:], True, True, reads=[b_qT, b_Sc], writes=[b_pV])
            P.op("vector", lambda e, i2=i2, t=t: e.tensor_tensor(out=vnew[i2][0][:], in0=u_[:, t, :], in1=pV[:, 0:128], op=ALU.subtract),
                 reads=[b_u, b_pV], writes=[vnew[i2][1]])
            P.op("scalar", lambda e, i2=i2, T=T: e.activation(out=o1[i2][0][:], in_=pV[:, 128:256], func=AF.Copy, scale=egc[:, T:T + 1]),
                 reads=[b_pV, b_egc], writes=[o1[i2][1]])
            P.mm(pSt[:, 0:128], kd[:, t, :], vnew[i2][0][:], True, True, reads=[b_kd, vnew[i2][1]], writes=[b_pSt])
            P.mm(pSt[:, 128:256], qT2[:, t, :], vnew[i2][0][:], True, True, reads=[b_qT2, vnew[i2][1]], writes=[b_pSt])
            P.op("vector", lambda e, Sn=Sn, Sc=Sc, T=T: e.scalar_tensor_tensor(out=Sn[:], in0=Sc[:], scalar=dec[:, T:T + 1], in1=pSt[:, 0:128], op0=ALU.mult, op1=ALU.add),
                 reads=[b_Sc, b_dec, b_pSt], writes=[b_Sn])
            P.op("vector", lambda e, i2=i2: e.tensor_tensor(out=ot[i2][0][:], in0=o1[i2][0][:], in1=pSt[:, 128:256], op=ALU.add),
                 reads=[o1[i2][1], b_pSt], writes=[ot[i2][1]])
            state["scur"] = 1 - state["scur"]
            P.op("scalar", lambda e, i2=i2: e.activation(out=osq[i2][0][:], in_=ot[i2][0][:], func=AF.Square, accum_out=ss[i2][0][:]),
                 reads=[ot[i2][1]], writes=[osq[i2][1], ss[i2][1]])
            P.op("scalar", lambda e, i2=i2: e.activation(out=lss[i2][0][:], in_=ss[i2][0][:], func=AF.Ln, scale=1.0 / 128, bias=EPS),
                 reads=[ss[i2][1]], writes=[lss[i2][1]])
            P.op("scalar", lambda e, i2=i2: e.activation(out=rss[i2][0][:], in_=lss[i2][0][:], func=AF.Exp, scale=-0.5),
                 reads=[lss[i2][1]], writes=[rss[i2][1]])
            P.op("vector", lambda e, i2=i2, t=t: e.scalar_tensor_tensor(out=ofb[gi][0][:, t, :], in0=ot[i2][0][:], scalar=rss[i2][0][:, 0:1], in1=szt[gi][0][:, t, :], op0=ALU.mult, op1=ALU.mult),
                 reads=[ot[i2][1], rss[i2][1], szt[gi][1]], writes=[ofb[gi][1]])
            yield
        G0 = seg * NT + grp * GT
        P.dma("sync", d_o[gi], od[G0 * 128:(G0 + GT) * 128, :].rearrange("(t p) d -> p t d", p=128), ofb[gi][0][:], reads=[ofb[gi][1]])

    groups = [(seg, grp) for seg in range(NSEG) for grp in range(NT // GT)]
    prev_scan = None
    for n_, (seg, grp) in enumerate(groups):
        if grp == 0:
            seg_prep(seg)
        pre = prepass_stages(seg, grp)
        done_pre = False
        rounds = 0
        while True:
            try:
                next(pre)
            except StopIteration:
                break
            rounds += 1
            if prev_scan is not None and rounds % 3 == 0:
                try:
                    next(prev_scan)
                except StopIteration:
                    prev_scan = None
        if prev_scan is not None:
            for _ in prev_scan:
                pass
        prev_scan = scan_steps(seg, grp)
    for _ in prev_scan:
        pass
    st = P.emit(final_waits=[("sync", d) for d in d_o])
    return nc

T = 2048
D = 1024
EPS = 1e-6
NTILES = 4
O_GATE = 4008


def build_merge():
    nc = bass.Bass("TRN2", target_bir_lowering=False)
    DI = lambda n, s, dt=F32: nc.dram_tensor(n, s, dt, kind="ExternalInput").ap()
    xT = DI("xT", [D, T]); brT = DI("brT", [3, 512, T], BF16); gd = DI("g", [128, 8])
    wgb = DI("wgb", [24, 128, 1536]); wo = DI("wo", [8, 128, 1024]); onesd = DI("ones", [128, 128])
    yT = nc.dram_tensor("yT", [D, T], F32, kind="ExternalOutput").ap()
    P = Prog(nc)
    A = nc.alloc_sbuf_tensor
    PSA = nc.alloc_psum_tensor

    def sb(name, shape, dt=F32):
        return A("s_" + name, shape, dt), P.buf(name)
    ones, b_ones = sb("ones", [128, 128], BF16); P.dma("gpsimd", P.dsem(), ones[:], onesd[:, :], writes=[b_ones])
    g, b_g = sb("g", [128, 8]); P.dma("sync", P.dsem(), g[:], gd[:, :], writes=[b_g])
    xin = [sb(f"xin{i}", [128, 8, 512]) for i in range(2)]; d_xin = [P.dsem() for _ in range(2)]
    br, b_br = sb("br", [128, 3, 4, T], BF16); d_br = P.dsem()
    sq = [sb(f"sq{i}", [128, 512], BF16) for i in range(2)]
    lnb, b_ln = sb("lnb", [128, 512]); rstd, b_rstd = sb("rstd", [128, 512])
    hT = sb("hT", [128, 8, T], BF16); b_hs = [P.buf() for _ in range(NTILES)]
    wc = [sb(f"wc{i}", [128, 1536], BF16) for i in range(3)]; d_wc = [P.dsem() for _ in range(3)]
    woc = [sb(f"woc{i}", [128, 1024], BF16) for i in range(2)]; d_wo = [P.dsem() for _ in range(2)]
    sig = [sb(f"sig{i}", [128, 512]) for i in range(2)]
    acc = [sb(f"acc{i}", [128, 512]) for i in range(NTILES)]
    tmp = [sb(f"tmp{i}", [128, 512]) for i in range(2)]
    mixed = sb("mixed", [128, 8, T], BF16); b_mxs = [P.buf() for _ in range(NTILES)]
    xres = [sb(f"xres{i}", [128, 512]) for i in range(2)]; d_xres = [P.dsem() for _ in range(2)]
    yo = [sb(f"yo{i}", [128, 512]) for i in range(2)]; d_yo = [P.dsem() for _ in range(2)]
    pS = (PSA("pS", [128, 512], F32), P.buf(excl=True))
    pG = [(PSA(f"pG{i}", [128, 512], F32), P.buf(excl=True)) for i in range(2)]
    pU = [(PSA(f"pU{i}", [128, 512], F32), P.buf(excl=True)) for i in range(2)]
    pO = [(PSA(f"pO{i}", [128, 512], F32), P.buf(excl=True)) for i in range(2)]
    xT_v = xT.rearrange("(kc p) n -> p kc n", p=128)
    hT_, mixed_ = hT[0], mixed[0]
    P.dma("sync", d_br, br[:, :, :, 0:NTILES * 512], brT[:, :, 0:NTILES * 512].rearrange("n (kc p) t -> p n kc t", p=128), writes=[b_br])
    for tt in range(NTILES):
        ts = slice(tt * 512, (tt + 1) * 512)
        xi, b_xi = xin[tt % 2]
        P.dma("sync", d_xin[tt % 2], xi[:], xT_v[:, :, ts], writes=[b_xi])
        for kc in range(8):
            s, bs = sq[kc % 2]
            P.op("scalar", lambda e, s=s, kc=kc, xi=xi: e.activation(out=s[:], in_=xi[:, kc, :], func=AF.Square), reads=[b_xi], writes=[bs])
            P.mm(pS[0][:], ones[:], s[:], kc == 0, kc == 7, reads=[b_ones, bs], writes=[pS[1]])
        P.op("scalar", lambda e: e.activation(out=lnb[:], in_=pS[0][:], func=AF.Ln, scale=1.0 / D, bias=EPS), reads=[pS[1]], writes=[b_ln])
        P.op("scalar", lambda e: e.activation(out=rstd[:], in_=lnb[:], func=AF.Exp, scale=-0.5), reads=[b_ln], writes=[b_rstd])
        for kc in range(8):
            P.op("vector", lambda e, kc=kc, xi=xi, ts=ts: e.scalar_tensor_tensor(out=hT_[:, kc, ts], in0=xi[:, kc, :], scalar=g[:, kc:kc + 1], in1=rstd[:], op0=ALU.mult, op1=ALU.mult),
                 reads=[b_xi, b_g, b_rstd], writes=[b_hs[tt]])
    cnt = 0; c2n = 0
    for c in range(8):
        for n in range(3):
            w, bw = wc[cnt % 3]
            P.dma("gpsimd", d_wc[cnt % 3], w[:], wgb[c * 3 + n, :, :], writes=[bw])
            cnt += 1
            for tt in range(NTILES):
                ts = slice(tt * 512, (tt + 1) * 512)
                a_, b_a = acc[tt]
                pg, b_pg = pG[c2n % 2]; pu, b_pu = pU[c2n % 2]; sg, b_sg = sig[c2n % 2]; tm, b_tm = tmp[c2n % 2]
                c2n += 1
                for kc in range(8):
                    P.mm(pg[:], w[:, kc * 128:(kc + 1) * 128], hT_[:, kc, ts], kc == 0, kc == 7, reads=[bw, b_hs[tt]], writes=[b_pg])
                for kc in range(4):
                    P.mm(pu[:], w[:, 1024 + kc * 128:1024 + (kc + 1) * 128], br[:, n, kc, ts], kc == 0, kc == 3, reads=[bw, b_br], writes=[b_pu])
                P.op("scalar", lambda e, sg=sg, pg=pg: e.activation(out=sg[:], in_=pg[:], func=AF.Sigmoid), reads=[b_pg], writes=[b_sg])
                if n == 0:
                    P.op("vector", lambda e, a_=a_, sg=sg, pu=pu: e.tensor_tensor(out=a_[:], in0=sg[:], in1=pu[:], op=ALU.mult), reads=[b_sg, b_pu], writes=[b_a])
                else:
                    P.op("vector", lambda e, tm=tm, sg=sg, pu=pu: e.tensor_tensor(out=tm[:], in0=sg[:], in1=pu[:], op=ALU.mult), reads=[b_sg, b_pu], writes=[b_tm])
                    if n == 1:
                        P.op("gpsimd", lambda e, a_=a_, tm=tm: e.tensor_tensor(out=a_[:], in0=a_[:], in1=tm[:], op=ALU.add), reads=[b_a, b_tm], writes=[b_a])
                    else:
                        P.op("gpsimd", lambda e, a_=a_, tm=tm, c=c, ts=ts: e.tensor_tensor(out=mixed_[:, c, ts], in0=a_[:], in1=tm[:], op=ALU.add), reads=[b_a, b_tm], writes=[b_mxs[tt]])
    k2 = 0
    for c2 in range(8):
        w, bw = woc[c2 % 2]
        P.dma("gpsimd", d_wo[c2 % 2], w[:], wo[c2, :, :], writes=[bw])
        for tt in range(NTILES):
            ts = slice(tt * 512, (tt + 1) * 512)
            po, b_po = pO[k2 % 2]; y_, b_y = yo[k2 % 2]; xr, b_xr = xres[k2 % 2]
            P.dma("sync", d_xres[k2 % 2], xr[:], xT[c2 * 128:(c2 + 1) * 128, ts], writes=[b_xr])
            for kc in range(8):
                P.mm(po[:], w[:, kc * 128:(kc + 1) * 128], mixed_[:, kc, ts], kc == 0, kc == 7, reads=[bw, b_mxs[tt]], writes=[b_po])
            P.op("vector", lambda e, y_=y_, po=po, xr=xr: e.tensor_tensor(out=y_[:], in0=po[:], in1=xr[:], op=ALU.add), reads=[b_po, b_xr], writes=[b_y])
            P.dma("sync", d_yo[k2 % 2], yT[c2 * 128:(c2 + 1) * 128, ts], y_[:], reads=[b_y])
            k2 += 1
    st = P.emit(final_waits=[("sync", d) for d in d_yo])
    return nc


def merge_weights(mix_norm, w_in, w_branch, w_out):
    g = np.ascontiguousarray(mix_norm.reshape(8, 128).T)
    wr = w_in.reshape(8, 128, -1)
    wgb = np.zeros((8, 3, 128, 1536), np.float32)
    for c in range(8):
        for n in range(3):
            c0 = O_GATE + n * 1024 + c * 128
            wgb[c, n, :, 0:1024] = wr[:, :, c0:c0 + 128].transpose(1, 0, 2).reshape(128, 1024)
            wgb[c, n, :, 1024:1536] = w_branch[n].reshape(4, 128, 1024)[:, :, c * 128:(c + 1) * 128].transpose(1, 0, 2).reshape(128, 512)
    wo = np.ascontiguousarray(w_out.reshape(8, 128, 8, 128).transpose(2, 1, 0, 3)).reshape(8, 128, 1024)
    return {"g": g, "wgb": wgb.reshape(24, 128, 1536), "wo": wo, "ones": np.ones((128, 128), np.float32)}

_PROGS = {}


def _prog(name, fn):
    if name not in _PROGS:
        _PROGS[name] = fn()
    return _PROGS[name]


def _run(nc, maps):
    res = run_bass_kernel_spmd(nc, maps, core_ids=list(range(8)))
    return res.results


def _ffn_launch(xT_cores, norm, w_in, w_out):
    g = np.ascontiguousarray(norm.reshape(8, 128).T)
    wi = w_in.reshape(8, 128, 2, NJ, 128)
    w1 = np.ascontiguousarray(wi.transpose(3, 1, 2, 0, 4)).reshape(NJ, 128, 2048)
    wo = w_out.reshape(NJ, 128, 8, 128)
    w2 = np.ascontiguousarray(wo.transpose(2, 1, 0, 3)).reshape(8, 128, NJ * 128)
    ones = np.ones((128, 128), np.float32)
    maps = [{"xT": xT_cores[c], "g": g, "w1": w1, "w2": w2, "ones": ones} for c in range(8)]
    r = _run(_prog("ffn", build_ffn), maps)
    return [np.ascontiguousarray(r[c]["yT"]) for c in range(8)]


def kernel(x, ffa_norm, ffa_w_in, ffa_w_out, mix_norm, w_in, mla_cq_norm, mla_ckv_norm,
           mla_w_uq, mla_w_ukv, mla_q_norm, mla_k_norm, gdn_conv, gdn_a_log, gdn_dt_bias,
           gdn_out_norm, moba_q_norm, moba_k_norm, w_branch, w_out, ffb_norm, ffb_w_in,
           ffb_w_out):
    f = lambda a: np.asarray(a, dtype=np.float32)
    x = f(x)
    B_, S_, D_ = x.shape
    xf = x.reshape(B_ * S_, D_)
    xT = [np.ascontiguousarray(xf[c * T:(c + 1) * T].T) for c in range(8)]
    blkoh, cmask, ident, onesf = attn_consts()
    gcst = gdn_consts()
    for l in range(2):
        xT = _ffn_launch(xT, f(ffa_norm)[l], f(ffa_w_in)[l], f(ffa_w_out)[l])
        W = projb_weights(f(mix_norm)[l], f(w_in)[l], f(mla_cq_norm)[l], f(mla_ckv_norm)[l], f(mla_w_uq)[l],
                          f(mla_w_ukv)[l], f(mla_q_norm)[l], f(mla_k_norm)[l], f(moba_q_norm)[l], f(moba_k_norm)[l])
        maps = []
        for c in range(8):
            m = dict(W)
            m["xT"] = xT[c]
            j = c % 4
            m["rope"] = rope_tables(np.arange(j * T, (j + 1) * T))
            maps.append(m)
        rb = _run(_prog("projb", build_projb), maps)

        def gather(name, b, axis):
            return np.concatenate([rb[b * 4 + j][name] for j in range(4)], axis=axis)
        full = []
        for b in range(2):
            full.append({"mla_qT": gather("mla_qT", b, 2), "mla_kT": gather("mla_kT", b, 2), "mla_v": gather("mla_v", b, 0),
                         "mo_qT": gather("mo_qT", b, 2), "mo_kT": gather("mo_kT", b, 2), "mo_v": gather("mo_v", b, 0),
                         "graw": gather("graw", b, 2), "gba": gather("gba", b, 1), "z": gather("z", b, 0)})
        maps = []
        for c in range(8):
            b, hp = c // 4, c % 4
            F = full[b]
            maps.append({"mq": np.ascontiguousarray(F["mla_qT"][2 * hp:2 * hp + 2]), "mk": np.ascontiguousarray(F["mla_kT"][2 * hp:2 * hp + 2]),
                         "mv": np.ascontiguousarray(F["mla_v"][:, hp * 128:(hp + 1) * 128]),
                         "oq": np.ascontiguousarray(F["mo_qT"][hp]), "ok": np.ascontiguousarray(F["mo_kT"][hp]),
                         "ov": np.ascontiguousarray(F["mo_v"][:, hp * 128:(hp + 1) * 128]),
                         "blkoh": blkoh, "cmask": cmask, "ident": ident, "onesf": onesf})
        ra = _run(_prog("attn", build_attn), maps)
        maps = []
        cw_l = f(gdn_conv)[l]
        for c in range(8):
            b, hd = c // 4, c % 4
            F = full[b]
            cw = np.concatenate([cw_l[:, k0 + hd * 128:k0 + (hd + 1) * 128].T for k0 in (0, 512, 1024)], 1)
            sc = np.stack([np.full(128, f(gdn_a_log)[l][hd], np.float32), np.full(128, f(gdn_dt_bias)[l][hd], np.float32)], 1)
            maps.append({"rq": np.ascontiguousarray(F["graw"][hd]), "rk": np.ascontiguousarray(F["graw"][4 + hd]),
                         "rv": np.ascontiguousarray(F["graw"][8 + hd]),
                         "z": np.ascontiguousarray(F["z"][:, hd * 128:(hd + 1) * 128]),
                         "bl": np.ascontiguousarray(F["gba"][hd].reshape(64, 128).T), "al": np.ascontiguousarray(F["gba"][4 + hd].reshape(64, 128).T),
                         "cw": np.ascontiguousarray(cw), "sc": sc,
                         "gn": np.ascontiguousarray(np.broadcast_to(f(gdn_out_norm)[l][None, :], (128, 128))),
                         "cst": gcst})
        rg = _run(_prog("gdn", build_gdn), maps)
        Wm = merge_weights(f(mix_norm)[l], f(w_in)[l], f(w_branch)[l], f(w_out)[l])
        brT = []
        for b in range(2):
            o_mla = np.concatenate([ra[b * 4 + hp]["oT"][i] for hp in range(4) for i in range(2)], 0)
            o_mo = np.concatenate([ra[b * 4 + hp]["oT"][2 + i] for hp in range(4) for i in range(2)], 0)
            o_gdn = np.concatenate([rg[b * 4 + hd]["o"].T for hd in range(4)], 0)
            brT.append(np.stack([o_mla, o_gdn, o_mo], 0))
        maps = []
        for c in range(8):
            b, j = c // 4, c % 4
            m = dict(Wm)
            m["xT"] = xT[c]
            m["brT"] = np.ascontiguousarray(brT[b][:, :, j * T:(j + 1) * T])
            maps.append(m)
        rm = _run(_prog("merge", build_merge), maps)
        xT = [np.ascontiguousarray(rm[c]["yT"]) for c in range(8)]
        xT = _ffn_launch(xT, f(ffb_norm)[l], f(ffb_w_in)[l], f(ffb_w_out)[l])
    out = np.concatenate([xT[c].T for c in range(8)], 0).reshape(B_, S_, D_)
    return np.ascontiguousarray(out.astype(np.float32))
```

```python
import ml_dtypes
from concourse.bass_utils import run_bass_kernel_spmd


import numpy as np
import concourse.bass as bass
import concourse.mybir as mybir

F32 = mybir.dt.float32
BF16 = mybir.dt.bfloat16
AF = mybir.ActivationFunctionType
ALU = mybir.AluOpType
AX = mybir.AxisListType

ENGS = ("tensor", "vector", "scalar", "gpsimd", "sync")


class Buf:
    __slots__ = ("name", "last_w", "readers", "excl")

    def __init__(self, name, excl=False):
        self.name = name
        self.last_w = None
        self.readers = []
        self.excl = excl


class Op:
    __slots__ = ("eng", "fn", "idx", "deps", "signal", "dsem", "dord", "count")

    def __init__(self, eng, fn, idx):
        self.eng = eng
        self.fn = fn
        self.idx = idx
        self.deps = []
        self.signal = False
        self.dsem = None
        self.dord = 0
        self.count = 0


class DSem:
    def __init__(self, name):
        self.name = name
        self.n = 0
        self.handle = None


class Prog:
    def __init__(self, nc):
        self.nc = nc
        self.ops = {e: [] for e in ENGS}
        self.dsems = []
        self.nbuf = 0

    def buf(self, name=None, excl=False):
        self.nbuf += 1
        return Buf(name or f"b{self.nbuf}", excl)

    def dsem(self, name=None):
        d = DSem(name or f"d{len(self.dsems)}")
        self.dsems.append(d)
        return d

    def _deps(self, op, reads, writes):
        deps = op.deps
        for b in reads:
            if b.excl:
                writes = list(writes) + [b]
                continue
            if b.last_w is not None:
                deps.append(b.last_w)
            b.readers.append(op)
        for b in writes:
            if b.last_w is not None:
                deps.append(b.last_w)
            deps.extend(r for r in b.readers if r is not op)
            b.readers = []
            b.last_w = op

    def op(self, eng, fn, reads=(), writes=()):
        o = Op(eng, fn, len(self.ops[eng]))
        self.ops[eng].append(o)
        self._deps(o, reads, writes)
        return o

    def dma(self, eng, dsem, out, in_, reads=(), writes=()):
        o = Op(eng, ("dma", out, in_), len(self.ops[eng]))
        dsem.n += 1
        o.dsem = dsem
        o.dord = dsem.n
        self.ops[eng].append(o)
        self._deps(o, reads, writes)
        return o

    def mm(self, out, lhsT, rhs, start, stop, reads=(), writes=()):
        return self.op("tensor", lambda e: e.matmul(out, lhsT, rhs, start=start, stop=stop),
                       reads, writes)

    def emit(self, final_waits=()):
        nc = self.nc
        for e in ENGS:
            for o in self.ops[e]:
                for d in o.deps:
                    if d.dsem is None:
                        if d.eng == "tensor" and o.eng == "tensor":
                            continue
                        d.signal = True
        esem = {e: nc.alloc_semaphore(f"sem_{e}") for e in ENGS}
        for d in self.dsems:
            if d.n:
                d.handle = nc.alloc_semaphore(f"dsem_{d.name}")
        for e in ENGS:
            c = 0
            for o in self.ops[e]:
                if o.dsem is None and o.signal:
                    c += 1
                    o.count = c
        stats = {}
        with nc.Block() as block:
            def run(ename, eng):
                waited = {}
                nwait = 0
                for o in self.ops[ename]:
                    need = {}
                    for d in o.deps:
                        if d.dsem is not None:
                            key = ("d", id(d.dsem)); sem = d.dsem.handle; val = 16 * d.dord
                        else:
                            if d.eng == "tensor" and ename == "tensor":
                                continue
                            key = ("e", d.eng); sem = esem[d.eng]; val = d.count
                        if need.get(key, (None, -1))[1] < val:
                            need[key] = (sem, val)
                    for key, (sem, val) in need.items():
                        if waited.get(key, -1) >= val:
                            continue
                        eng.wait_ge(sem, val)
                        waited[key] = val
                        nwait += 1
                    if o.dsem is not None:
                        _, out, in_ = o.fn
                        eng.dma_start(out=out, in_=in_).then_inc(o.dsem.handle, 16)
                    else:
                        ins = o.fn(eng)
                        if o.signal:
                            ins.then_inc(esem[ename], 1)
                for (kind, obj) in final_waits:
                    if ename != kind:
                        continue
                    eng.wait_ge(obj.handle, 16 * obj.n)
                stats[ename] = (len(self.ops[ename]), nwait)

            @block.tensor
            def _(eng):
                run("tensor", eng)

            @block.vector
            def _(eng):
                run("vector", eng)

            @block.scalar
            def _(eng):
                run("scalar", eng)

            @block.gpsimd
            def _(eng):
                run("gpsimd", eng)

            @block.sync
            def _(eng):
                run("sync", eng)
        return stats


T = 2048
D = 1024
DFF = 2816
NJ = DFF // 128
EPS = 1e-6


def build_ffn():
    nc = bass.Bass("TRN2", target_bir_lowering=False)
    xT = nc.dram_tensor("xT", [D, T], F32, kind="ExternalInput").ap()
    gd = nc.dram_tensor("g", [128, 8], F32, kind="ExternalInput").ap()
    w1d = nc.dram_tensor("w1", [NJ, 128, 2048], F32, kind="ExternalInput").ap()
    w2d = nc.dram_tensor("w2", [8, 128, NJ * 128], F32, kind="ExternalInput").ap()
    onesd = nc.dram_tensor("ones", [128, 128], F32, kind="ExternalInput").ap()
    yT = nc.dram_tensor("yT", [D, T], F32, kind="ExternalOutput").ap()
    P = Prog(nc)
    A = nc.alloc_sbuf_tensor
    ones = A("ones_sb", [128, 128], BF16); b_ones = P.buf()
    g = A("g_sb", [128, 8], F32); b_g = P.buf()
    xin = [A(f"xin{i}", [128, 8, 512], F32) for i in range(2)]; b_xin = [P.buf() for _ in range(2)]
    sq = [A(f"sq{i}", [128, 512], BF16) for i in range(2)]; b_sq = [P.buf() for _ in range(2)]
    lnb = A("lnb", [128, 512], F32); b_ln = P.buf()
    rstd = A("rstd", [128, 512], F32); b_rstd = P.buf()
    hT = A("hT", [128, 8, 1024], BF16); b_h = [P.buf() for _ in range(2)]
    actT = A("actT", [128, NJ, 1024], BF16); b_act = [P.buf() for _ in range(2)]
    w1 = [A(f"w1_{i}", [128, 2048], BF16) for i in range(2)]; b_w1 = [P.buf() for _ in range(2)]
    w2 = [A(f"w2_{i}", [128, NJ * 128], BF16) for i in range(2)]; b_w2 = [P.buf() for _ in range(2)]
    sg = [A(f"sg{i}", [128, 512], F32) for i in range(2)]; b_sg = [P.buf() for _ in range(2)]
    xres = [A(f"xres{i}", [128, 512], F32) for i in range(2)]; b_xres = [P.buf() for _ in range(2)]
    yo = [A(f"yo{i}", [128, 512], F32) for i in range(2)]; b_yo = [P.buf() for _ in range(2)]
    PS = nc.alloc_psum_tensor
    pS = PS("pS", [128, 512], F32); b_pS = P.buf(excl=True)
    pG = [PS(f"pG{i}", [128, 512], F32) for i in range(2)]; b_pG = [P.buf(excl=True) for _ in range(2)]
    pU = [PS(f"pU{i}", [128, 512], F32) for i in range(2)]; b_pU = [P.buf(excl=True) for _ in range(2)]
    pO = [PS(f"pO{i}", [128, 512], F32) for i in range(2)]; b_pO = [P.buf(excl=True) for _ in range(2)]
    d_c = P.dsem(); d_g = P.dsem()
    d_xin = [P.dsem() for _ in range(2)]
    d_w1 = [P.dsem() for _ in range(2)]; d_w2 = [P.dsem() for _ in range(2)]
    d_xres = [P.dsem() for _ in range(2)]; d_out = [P.dsem() for _ in range(2)]
    b_ydram = P.buf()

    P.dma("gpsimd", d_c, ones[:], onesd[:, :], writes=[b_ones])
    P.dma("sync", d_g, g[:], gd[:, :], writes=[b_g])
    xT_v = xT.rearrange("(kc p) n -> p kc n", p=128)
    for hh in range(2):
        t0 = hh * 1024
        for tt in range(2):
            tok = t0 + tt * 512
            xi = xin[tt]; bxi = b_xin[tt]
            P.dma("sync", d_xin[tt], xi[:], xT_v[:, :, tok:tok + 512], writes=[bxi])
            for kc in range(8):
                s = sq[kc % 2]; bs = b_sq[kc % 2]
                P.op("scalar", lambda e, s=s, xi=xi, kc=kc: e.activation(out=s[:], in_=xi[:, kc, :], func=AF.Square),
                     reads=[bxi], writes=[bs])
                P.mm(pS[:], ones[:], s[:], kc == 0, kc == 7, reads=[b_ones, bs], writes=[b_pS])
            P.op("scalar", lambda e: e.activation(out=lnb[:], in_=pS[:], func=AF.Ln, scale=1.0 / D, bias=EPS),
                 reads=[b_pS], writes=[b_ln])
            P.op("scalar", lambda e: e.activation(out=rstd[:], in_=lnb[:], func=AF.Exp, scale=-0.5),
                 reads=[b_ln], writes=[b_rstd])
            for kc in range(8):
                P.op("vector", lambda e, xi=xi, kc=kc, tt=tt: e.scalar_tensor_tensor(
                    out=hT[:, kc, tt * 512:(tt + 1) * 512], in0=xi[:, kc, :], scalar=g[:, kc:kc + 1], in1=rstd[:],
                    op0=ALU.mult, op1=ALU.mult), reads=[bxi, b_g, b_rstd], writes=[b_h[tt]])
        for j in range(NJ):
            w = w1[j % 2]; bw = b_w1[j % 2]
            P.dma("gpsimd", d_w1[j % 2], w[:], w1d[j, :, :], writes=[bw])
            for tt in range(2):
                hs = lambda kc, tt=tt: hT[:, kc, tt * 512:(tt + 1) * 512]
                for kc in range(8):
                    P.mm(pG[tt][:], w[:, kc * 128:(kc + 1) * 128], hs(kc), kc == 0, kc == 7,
                         reads=[bw, b_h[tt]], writes=[b_pG[tt]])
                for kc in range(8):
                    P.mm(pU[tt][:], w[:, 1024 + kc * 128:1024 + (kc + 1) * 128], hs(kc), kc == 0, kc == 7,
                         reads=[bw, b_h[tt]], writes=[b_pU[tt]])
                P.op("scalar", lambda e, tt=tt: e.activation(out=sg[tt][:], in_=pG[tt][:], func=AF.Silu),
                     reads=[b_pG[tt]], writes=[b_sg[tt]])
                P.op("vector", lambda e, tt=tt, j=j: e.tensor_tensor(
                    out=actT[:, j, tt * 512:(tt + 1) * 512], in0=sg[tt][:], in1=pU[tt][:], op=ALU.mult),
                    reads=[b_sg[tt], b_pU[tt]], writes=[b_act[tt]])
        for c in range(8):
            w = w2[c % 2]; bw = b_w2[c % 2]
            P.dma("gpsimd", d_w2[c % 2], w[:], w2d[c, :, :], writes=[bw])
            for tt in range(2):
                tok = t0 + tt * 512
                for j in range(NJ):
                    P.mm(pO[tt][:], w[:, j * 128:(j + 1) * 128], actT[:, j, tt * 512:(tt + 1) * 512], j == 0, j == NJ - 1,
                         reads=[bw, b_act[tt]], writes=[b_pO[tt]])
                P.dma("sync", d_xres[tt], xres[tt][:], xT[c * 128:(c + 1) * 128, tok:tok + 512], writes=[b_xres[tt]])
                P.op("vector", lambda e, tt=tt: e.scalar_tensor_tensor(
                    out=yo[tt][:], in0=pO[tt][:], scalar=0.5, in1=xres[tt][:], op0=ALU.mult, op1=ALU.add),
                    reads=[b_pO[tt], b_xres[tt]], writes=[b_yo[tt]])
                P.dma("sync", d_out[tt], yT[c * 128:(c + 1) * 128, tok:tok + 512], yo[tt][:], reads=[b_yo[tt]])
    st = P.emit(final_waits=[("sync", d_out[0]), ("sync", d_out[1])])
    return nc


def ffn_host_inputs(x_flat, norm, w_in, w_out):
    g = np.ascontiguousarray(norm.reshape(8, 128).T)
    wi = w_in.reshape(8, 128, 2, NJ, 128)
    w1 = np.ascontiguousarray(wi.transpose(3, 1, 2, 0, 4)).reshape(NJ, 128, 2048)
    wo = w_out.reshape(NJ, 128, 8, 128)
    w2 = np.ascontiguousarray(wo.transpose(2, 1, 0, 3)).reshape(8, 128, NJ * 128)
    ones = np.ones((128, 128), np.float32)
    maps = []
    for c in range(8):
        xT = np.ascontiguousarray(x_flat[c * T:(c + 1) * T].T)
        maps.append({"xT": xT, "g": g, "w1": w1, "w2": w2, "ones": ones})
    return maps


T = 2048
D = 1024
EPS = 1e-6
NCH = 25
ROPE_THETA = 10000.0
NTILES = 4
ND = 3

O_CQ, O_CKV, O_KR, O_GQ, O_GK, O_GV, O_GB, O_GA, O_GZ, O_MQ, O_MK, O_MV, O_GATE = 0, 256, 384, 416, 928, 1440, 1952, 1956, 1960, 2472, 2984, 3496, 4008
CH_COLS = ([(O_CQ, 128), (O_CQ + 128, 128), (O_CKV, 128), (O_KR, 32)] +
           [(O_GQ + h * 128, 128) for h in range(4)] + [(O_GK + h * 128, 128) for h in range(4)] +
           [(O_GV + h * 128, 128) for h in range(4)] + [(O_GB, 8)] +
           [(O_MQ + h * 128, 128) for h in range(4)] + [(O_MK + h * 128, 128) for h in range(4)])


def build_projb():
    nc = bass.Bass("TRN2", target_bir_lowering=False)
    DI = lambda n, s, dt=F32: nc.dram_tensor(n, s, dt, kind="ExternalInput").ap()
    DO = lambda n, s, dt=F32: nc.dram_tensor(n, s, dt, kind="ExternalOutput").ap()
    xT = DI("xT", [D, T]); gains_d = DI("gains", [128, 16])
    wb1 = DI("wb1", [NCH, 128, 1024]); wz_d = DI("wz", [128, 4096]); wmv_d = DI("wmv", [128, 4096])
    wuq_d = DI("wuq", [128, 1536]); wuk_d = DI("wuk", [128, 768]); wuv_d = DI("wuv", [128, 512])
    cmat = DI("cmat", [5, 128, 128])
    rope_d = DI("rope", [4, 128, T])
    o_mq = DO("mla_qT", [8, 96, T], BF16); o_mk = DO("mla_kT", [8, 96, T], BF16); o_mv = DO("mla_v", [T, 512], BF16)
    o_oq = DO("mo_qT", [4, 128, T], BF16); o_ok = DO("mo_kT", [4, 128, T], BF16); o_ov = DO("mo_v", [T, 512], BF16)
    o_gr = DO("graw", [12, 128, T]); o_gba = DO("gba", [8, T]); o_z = DO("z", [T, 512])
    P = Prog(nc)
    A = nc.alloc_sbuf_tensor
    PSA = nc.alloc_psum_tensor

    def sb(name, shape, dt=F32):
        return A("s_" + name, shape, dt), P.buf(name)

    def load(eng, t, b, src):
        P.dma(eng, P.dsem(), t, src, writes=[b])

    gains, b_gains = sb("gains", [128, 16]); load("sync", gains[:], b_gains, gains_d[:, :])
    CM, b_CM = sb("CM", [128, 5, 128], BF16); load("gpsimd", CM[:], b_CM, cmat.rearrange("k p n -> p k n"))
    ONES = CM[:, 0, :]; BLK64 = CM[:, 1, :]; RM_MLA = CM[0:96, 2, 0:96]; RM_MO = CM[:, 3, :]; SEL = CM[0:32, 4, 0:96]
    rope, b_rope = sb("rope", [128, 4, T]); load("sync", rope[:], b_rope, rope_d.rearrange("k p n -> p k n"))
    wz, b_wz = sb("wz", [128, 4096], BF16); load("gpsimd", wz[:], b_wz, wz_d[:, :])
    wmv, b_wmv = sb("wmv", [128, 4096], BF16); load("gpsimd", wmv[:], b_wmv, wmv_d[:, :])
    wuq, b_wuq = sb("wuq", [128, 1536], BF16); load("gpsimd", wuq[:], b_wuq, wuq_d[:, :])
    wuk, b_wuk = sb("wuk", [128, 768], BF16); load("gpsimd", wuk[:], b_wuk, wuk_d[:, :])
    wuv, b_wuv = sb("wuv", [128, 512], BF16); load("gpsimd", wuv[:], b_wuv, wuv_d[:, :])
    xins = [sb(f"xin{i}", [128, 8, 512]) for i in range(1)]; d_xins = [P.dsem() for _ in range(1)]
    sq = [sb(f"sq{i}", [128, 512], BF16) for i in range(2)]
    lnb, b_ln = sb("lnb", [128, 512]); rstd, b_rstd = sb("rstd", [128, 512])
    hT, b_h = sb("hT", [128, 8, T], BF16)
    wch = [sb(f"wch{i}", [128, 1024], BF16) for i in range(3)]; d_wch = [P.dsem() for _ in range(3)]
    cq = [sb(f"cq{i}", [128, T]) for i in range(3)]
    cqn = [sb(f"cqn{i}", [128, T], BF16) for i in range(3)]
    krope, b_krope = sb("krope", [32, T], BF16)
    NF = 4; NB = 8
    stf = [sb(f"stf{i}", [128, 512]) for i in range(NF)]; d_stf = [P.dsem() for _ in range(NF)]
    stb = [sb(f"stb{i}", [128, 512], BF16) for i in range(NB)]; d_stb = [P.dsem() for _ in range(NB)]
    cf = [0]; cb = [0]
    sqv = [sb(f"sqv{i}", [128, 512], BF16) for i in range(ND)]
    lnv = [sb(f"lnv{i}", [128, 512]) for i in range(ND)]
    rsv = [sb(f"rsv{i}", [128, 512]) for i in range(ND)]
    qn = [sb(f"qn{i}", [128, 512], BF16) for i in range(ND)]
    t1 = [sb(f"t1{i}", [128, 512]) for i in range(ND)]
    t2 = [sb(f"t2{i}", [128, 512]) for i in range(ND)]
    pS = (PSA("pS", [128, 512], F32), P.buf(excl=True))
    pP = [(PSA(f"pP{i}", [128, 512], F32), P.buf(excl=True)) for i in range(4)]
    pX = [(PSA(f"pX{i}", [128, 512], F32), P.buf(excl=True)) for i in range(3)]
    pN = pX
    cx = [0]
    pT = pS
    cp = [0]; cr = [0]
    all_out = []

    def out_f32(src_ps, b_ps, R, dst, eng="scalar"):
        i = cf[0] % NF; cf[0] += 1
        s, b_s = stf[i]
        if eng == "scalar":
            P.op("scalar", lambda e: e.copy(out=s[0:R, :], in_=src_ps), reads=[b_ps], writes=[b_s])
        else:
            P.op("vector", lambda e: e.tensor_copy(out=s[0:R, :], in_=src_ps), reads=[b_ps], writes=[b_s])
        P.dma("sync", d_stf[i], dst, s[0:R, :], reads=[b_s])

    def out_bf(src_ps, b_ps, R, dst, eng="vector"):
        i = cb[0] % NB; cb[0] += 1
        s, b_s = stb[i]
        if eng == "scalar":
            P.op("scalar", lambda e: e.copy(out=s[0:R, :], in_=src_ps), reads=[b_ps], writes=[b_s])
        else:
            P.op("vector", lambda e: e.tensor_copy(out=s[0:R, :], in_=src_ps), reads=[b_ps], writes=[b_s])
        P.dma("sync", d_stb[i], dst, s[0:R, :], reads=[b_s])

    def job(proj, R, gcol, onesm, rm, kcos, dim, tok, dst):
        k = cr[0] % ND; cr[0] += 1
        s_, b_s = sqv[k]; l_, b_l = lnv[k]; r_, b_r = rsv[k]; q_, b_q = qn[k]; a_, b_a = t1[k]; c_, b_c = t2[k]
        pp, b_pp = pP[cp[0] % 4]; cp[0] += 1
        ps = pp[0:R, :]
        proj(pp, b_pp)
        yield
        P.op("scalar", lambda e: e.activation(out=s_[0:R, :], in_=ps, func=AF.Square), reads=[b_pp], writes=[b_s])
        yield
        pn, b_pn = pX[cx[0] % 3]; cx[0] += 1
        P.mm(pn[0:R, :], onesm, s_[0:R, :], True, True, reads=[b_CM, b_s], writes=[b_pn])
        P.op("scalar", lambda e: e.activation(out=l_[0:R, :], in_=pn[0:R, :], func=AF.Ln, scale=1.0 / dim, bias=EPS), reads=[b_pn], writes=[b_l])
        P.op("scalar", lambda e: e.activation(out=r_[0:R, :], in_=l_[0:R, :], func=AF.Exp, scale=-0.5), reads=[b_l], writes=[b_r])
        yield
        P.op("vector", lambda e: e.scalar_tensor_tensor(out=q_[0:R, :], in0=ps, scalar=gains[0:R, gcol:gcol + 1], in1=r_[0:R, :], op0=ALU.mult, op1=ALU.mult),
             reads=[b_pp, b_gains, b_r], writes=[b_q])
        yield "late"
        pr, b_pr = pX[cx[0] % 3]; cx[0] += 1
        P.mm(pr[0:R, :], rm, q_[0:R, :], True, True, reads=[b_CM, b_q], writes=[b_pr])
        P.op("gpsimd", lambda e: e.tensor_tensor(out=a_[0:R, :], in0=q_[0:R, :], in1=rope[0:R, kcos, tok:tok + 512], op=ALU.mult),
             reads=[b_q, b_rope], writes=[b_a])
        P.op("vector", lambda e: e.tensor_tensor(out=c_[0:R, :], in0=pr[0:R, :], in1=rope[0:R, kcos + 1, tok:tok + 512], op=ALU.mult),
             reads=[b_pr, b_rope], writes=[b_c])
        i = cb[0] % NB; cb[0] += 1
        sbf, b_sb = stb[i]
        P.op("gpsimd", lambda e: e.tensor_tensor(out=sbf[0:R, :], in0=a_[0:R, :], in1=c_[0:R, :], op=ALU.add), reads=[b_a, b_c], writes=[b_sb])
        P.dma("sync", d_stb[i], dst, sbf[0:R, :], reads=[b_sb])
        yield

    def run_jobs(jobs):
        active = []
        jobs = list(jobs)
        while jobs or active:
            nxt = []
            for g in active:
                try:
                    next(g)
                    nxt.append(g)
                except StopIteration:
                    pass
            active = nxt
            if jobs:
                g = jobs.pop(0)
                next(g)
                active.append(g)

    xT_v = xT.rearrange("(kc p) n -> p kc n", p=128)
    for tt in range(NTILES):
        tok = tt * 512
        ts = slice(tok, tok + 512)
        xin, b_xin = xins[0]
        P.dma("sync", d_xins[0], xin[:], xT_v[:, :, ts], writes=[b_xin])
        for kc in range(8):
            s, bs = sq[kc % 2]
            P.op("scalar", lambda e, s=s, kc=kc, xin=xin: e.activation(out=s[:], in_=xin[:, kc, :], func=AF.Square), reads=[b_xin], writes=[bs])
            P.mm(pS[0][:], ONES, s[:], kc == 0, kc == 7, reads=[b_CM, bs], writes=[pS[1]])
        P.op("scalar", lambda e: e.activation(out=lnb[:], in_=pS[0][:], func=AF.Ln, scale=1.0 / D, bias=EPS), reads=[pS[1]], writes=[b_ln])
        P.op("scalar", lambda e: e.activation(out=rstd[:], in_=lnb[:], func=AF.Exp, scale=-0.5), reads=[b_ln], writes=[b_rstd])
        for kc in range(8):
            P.op("vector", lambda e, kc=kc, xin=xin, ts=ts: e.scalar_tensor_tensor(out=hT[:, kc, ts], in0=xin[:, kc, :], scalar=gains[:, kc:kc + 1], in1=rstd[:], op0=ALU.mult, op1=ALU.mult),
                 reads=[b_xin, b_gains, b_rstd], writes=[b_h])
    mo_jobs = []
    for ch in range(NCH):
        w, bw = wch[ch % 3]
        P.dma("gpsimd", d_wch[ch % 3], w[:], wb1[ch, :, :], writes=[bw])
        M = CH_COLS[ch][1]
        if ch >= 17:
            def mkproj(w=w, bw=bw, ts=None):
                def proj(pp, b_pp):
                    for kc in range(8):
                        P.mm(pp[0:128, :], w[:, kc * 128:kc * 128 + 128], hT[:, kc, ts], kc == 0, kc == 7, reads=[bw, b_h], writes=[b_pp])
                return proj
            for tt in range(NTILES):
                tok = tt * 512
                ts = slice(tok, tok + 512)
                dst = o_oq[ch - 17, :, ts] if ch < 21 else o_ok[ch - 21, :, ts]
                mo_jobs.append(job(mkproj(ts=ts), 128, 13 if ch < 21 else 14, BLK64, RM_MO, 2, 64.0, tok, dst))
            if ch % 2 == 0 or ch == NCH - 1:
                run_jobs(mo_jobs); mo_jobs = []
            continue
        for tt in range(NTILES):
            tok = tt * 512
            ts = slice(tok, tok + 512)
            pp, b_pp = pP[cp[0] % 4]; cp[0] += 1
            for kc in range(8):
                P.mm(pp[0:M, :], w[:, kc * 128:kc * 128 + M], hT[:, kc, ts], kc == 0, kc == 7, reads=[bw, b_h], writes=[b_pp])
            if ch < 3:
                c_, b_c = cq[ch]
                P.op("scalar", lambda e, c_=c_, pp=pp, ts=ts: e.copy(out=c_[:, ts], in_=pp[:]), reads=[b_pp], writes=[b_c])
            elif ch == 3:
                P.op("vector", lambda e, pp=pp, ts=ts: e.tensor_copy(out=krope[:, ts], in_=pp[0:32, :]), reads=[b_pp], writes=[b_krope])
            elif ch < 16:
                out_f32(pp[:], b_pp, 128, o_gr[ch - 4, :, ts], eng="scalar" if (ch + tt) % 2 else "vector")
            elif ch == 16:
                out_f32(pp[0:8, :], b_pp, 8, o_gba[:, ts])
    for tt in range(NTILES):
        ts = slice(tt * 512, (tt + 1) * 512)
        for grp, (idxs, dim, gc) in enumerate([((0, 1), 256.0, 8), ((2,), 128.0, 10)]):
            pn, b_pn = pX[cx[0] % 3]; cx[0] += 1; k = cr[0] % ND; cr[0] += 1
            for n_, ci in enumerate(idxs):
                s, bs = sq[n_ % 2]
                P.op("scalar", lambda e, s=s, ci=ci, ts=ts: e.activation(out=s[:], in_=cq[ci][0][:, ts], func=AF.Square), reads=[cq[ci][1]], writes=[bs])
                P.mm(pn[:], ONES, s[:], n_ == 0, n_ == len(idxs) - 1, reads=[b_CM, bs], writes=[b_pn])
            l_, b_l = lnv[k]; r_, b_r = rsv[k]
            P.op("scalar", lambda e, l_=l_, pn=pn, dim=dim: e.activation(out=l_[:], in_=pn[:], func=AF.Ln, scale=1.0 / dim, bias=EPS), reads=[b_pn], writes=[b_l])
            P.op("scalar", lambda e, l_=l_, r_=r_: e.activation(out=r_[:], in_=l_[:], func=AF.Exp, scale=-0.5), reads=[b_l], writes=[b_r])
            for n_, ci in enumerate(idxs):
                P.op("vector", lambda e, ci=ci, r_=r_, gc=gc, n_=n_, ts=ts: e.scalar_tensor_tensor(out=cqn[ci][0][:, ts], in0=cq[ci][0][:, ts], scalar=gains[:, gc + n_:gc + n_ + 1], in1=r_[:], op0=ALU.mult, op1=ALU.mult),
                     reads=[cq[ci][1], b_gains, b_r], writes=[cqn[ci][1]])
    mla_jobs = []
    for h in range(8):
        for tt in range(NTILES):
            tok = tt * 512
            ts = slice(tok, tok + 512)

            def projq(pp, b_pp, h=h, ts=ts):
                for kc in range(2):
                    P.mm(pp[0:96, :], wuq[:, kc * 768 + h * 96:kc * 768 + (h + 1) * 96], cqn[kc][0][:, ts], kc == 0, kc == 1, reads=[b_wuq, cqn[kc][1]], writes=[b_pp])

            def projk(pp, b_pp, h=h, ts=ts):
                P.mm(pp[0:96, :], wuk[:, h * 96:(h + 1) * 96], cqn[2][0][:, ts], True, False, reads=[b_wuk, cqn[2][1]], writes=[b_pp])
                P.mm(pp[0:96, :], SEL, krope[:, ts], False, True, reads=[b_CM, b_krope], writes=[b_pp])
            mla_jobs.append(job(projq, 96, 11, ONES[0:96, 0:96], RM_MLA, 0, 96.0, tok, o_mq[h, :, ts]))
            mla_jobs.append(job(projk, 96, 12, ONES[0:96, 0:96], RM_MLA, 0, 96.0, tok, o_mk[h, :, ts]))
    run_jobs(mla_jobs)
    for grp in range(4 * NTILES):
        gs = slice(grp * 128, (grp + 1) * 128)
        rows = gs
        P.mm(pT[0][:], cqn[2][0][:, gs], wuv[:], True, True, reads=[cqn[2][1], b_wuv], writes=[pT[1]])
        out_bf(pT[0][:], pT[1], 128, o_mv[rows, :], eng="scalar")
        for kc in range(8):
            P.mm(pT[0][:], hT[:, kc, gs], wz[:, kc * 512:(kc + 1) * 512], kc == 0, kc == 7, reads=[b_h, b_wz], writes=[pT[1]])
        out_f32(pT[0][:], pT[1], 128, o_z[rows, :], eng="vector")
        for kc in range(8):
            P.mm(pT[0][:], hT[:, kc, gs], wmv[:, kc * 512:(kc + 1) * 512], kc == 0, kc == 7, reads=[b_h, b_wmv], writes=[pT[1]])
        out_bf(pT[0][:], pT[1], 128, o_ov[rows, :], eng="scalar")
    st = P.emit(final_waits=[("sync", d) for d in d_stf + d_stb])
    return nc


def projb_consts():
    ones = np.ones((128, 128), np.float32)
    t = np.arange(128)
    blk64 = ((t[:, None] // 64) == (t[None, :] // 64)).astype(np.float32)
    rm_mla = np.zeros((128, 128), np.float32)
    for m in range(64, 80):
        rm_mla[m + 16, m] = -1.0
    for m in range(80, 96):
        rm_mla[m - 16, m] = 1.0
    rm_mo = np.zeros((128, 128), np.float32)
    for base in (0, 64):
        for m in range(base, base + 32):
            rm_mo[m + 32, m] = -1.0
        for m in range(base + 32, base + 64):
            rm_mo[m - 32, m] = 1.0
    sel = np.zeros((128, 128), np.float32)
    for i in range(32):
        sel[i, 64 + i] = 1.0
    return np.stack([ones, blk64, rm_mla, rm_mo, sel], 0)


def rope_tables(pos):
    pos = pos.astype(np.float32)
    out = np.zeros((4, 128, len(pos)), np.float32)
    out[0] = 1.0; out[2] = 1.0
    inv = (ROPE_THETA ** (-np.arange(16, dtype=np.float32) * 2.0 / 32)).astype(np.float32)
    ang = pos[None, :] * inv[:, None]
    for r in range(64, 96):
        i = (r - 64) % 16
        out[0, r] = np.cos(ang[i]); out[1, r] = np.sin(ang[i])
    out[0, 96:] = 0
    inv = (ROPE_THETA ** (-np.arange(32, dtype=np.float32) * 2.0 / 64)).astype(np.float32)
    ang = pos[None, :] * inv[:, None]
    for r in range(128):
        i = (r % 64) % 32
        out[2, r] = np.cos(ang[i]); out[3, r] = np.sin(ang[i])
    return out


def projb_weights(mix_norm, w_in, cq_norm, ckv_norm, w_uq, w_ukv, q_norm, k_norm, mq_norm, mk_norm):
    gains = np.zeros((128, 16), np.float32)
    gains[:, 0:8] = mix_norm.reshape(8, 128).T
    gains[:, 8:10] = cq_norm.reshape(2, 128).T
    gains[:, 10] = ckv_norm
    gains[0:96, 11] = q_norm; gains[0:96, 12] = k_norm
    gains[:, 13] = np.tile(mq_norm, 2); gains[:, 14] = np.tile(mk_norm, 2)
    wb1 = np.zeros((NCH, 128, 8, 128), np.float32)
    wr = w_in.reshape(8, 128, -1)
    for ch, (c0, m) in enumerate(CH_COLS):
        wb1[ch, :, :, 0:m] = wr[:, :, c0:c0 + m].transpose(1, 0, 2)
    wb1 = wb1.reshape(NCH, 128, 1024)
    wz = np.ascontiguousarray(wr[:, :, O_GZ:O_GZ + 512].transpose(1, 0, 2)).reshape(128, 4096)
    wmv = np.ascontiguousarray(wr[:, :, O_MV:O_MV + 512].transpose(1, 0, 2)).reshape(128, 4096)
    wuq = np.ascontiguousarray(w_uq.reshape(2, 128, 768).transpose(1, 0, 2)).reshape(128, 1536)
    kv = w_ukv.reshape(128, 8, 128)
    wuk = np.zeros((128, 8, 96), np.float32); wuk[:, :, 0:64] = kv[:, :, 0:64]
    wuv = np.ascontiguousarray(kv[:, :, 64:128]).reshape(128, 512)
    return {"gains": gains, "wb1": wb1, "wz": wz, "wmv": wmv, "wuq": wuq, "wuk": wuk.reshape(128, 768), "wuv": wuv, "cmat": projb_consts()}

S = 8192
NQT = 16
NDUMMY = 0
DUMMY_N = 384
HEADS = (0, 1, 2, 3)


def attn_consts():
    keys = np.arange(S)
    blkoh = (keys[None, :] // 256 == np.arange(32)[:, None]).astype(np.float32)
    p = np.arange(128)[:, None]; j = np.arange(512)[None, :]
    cmask = np.stack([np.where((128 * d + p) <= j, 0.0, -30000.0).astype(np.float32) for d in range(4)], 0)
    return blkoh, cmask, np.eye(128, dtype=np.float32), np.ones((128, 64), np.float32)


def build_attn():
    nc = bass.Bass("TRN2", target_bir_lowering=False)
    DI = lambda n, s, dt=F32: nc.dram_tensor(n, s, dt, kind="ExternalInput").ap()
    mq = DI("mq", [2, 96, S], BF16); mk = DI("mk", [2, 96, S], BF16); mv = DI("mv", [S, 128], BF16)
    oq = DI("oq", [128, S], BF16); ok = DI("ok", [128, S], BF16); ov = DI("ov", [S, 128], BF16)
    blkoh_d = DI("blkoh", [32, S]); cmask_d = DI("cmask", [4, 128, 512]); ident_d = DI("ident", [128, 128]); onesf_d = DI("onesf", [128, 64])
    oT = nc.dram_tensor("oT", [4, 64, S], BF16, kind="ExternalOutput").ap()
    P = Prog(nc)
    A = nc.alloc_sbuf_tensor
    PSA = nc.alloc_psum_tensor

    def sb(name, shape, dt=F32):
        return A("s_" + name, shape, dt), P.buf(name)

    cmask, b_cm = sb("cmask", [128, 4, 512], BF16); P.dma("gpsimd", P.dsem(), cmask[:], cmask_d.rearrange("k p n -> p k n"), writes=[b_cm])
    ident, b_id = sb("ident", [128, 128], BF16); P.dma("gpsimd", P.dsem(), ident[:], ident_d[:, :], writes=[b_id])
    onesf, b_of = sb("onesf", [128, 64]); P.dma("sync", P.dsem(), onesf[:], onesf_d[:, :], writes=[b_of])
    Ka = [sb(f"Ka{i}", [128, S], BF16) for i in range(2)]
    Qa = [sb(f"Qa{i}", [128, S], BF16) for i in range(2)]
    Va = [sb(f"Va{i}", [128, 64, 65], BF16) for i in range(2)]
    d_K = [P.dsem() for _ in range(2)]; d_Q = [P.dsem() for _ in range(2)]; d_V = [P.dsem() for _ in range(2)]
    d_K2 = [P.dsem() for _ in range(2)]
    kmf, b_kmf = sb("kmf", [128, 32]); kmT, b_kmT = sb("kmT", [128, 32], BF16)
    gm = [sb(f"gm{i}", [128, 32]) for i in range(4)]
    top8 = [sb(f"top8{i}", [128, 8]) for i in range(4)]
    sel = [sb(f"sel{i}", [128, 32]) for i in range(4)]
    PT = [sb(f"PT{i}", [128, 512], BF16) for i in range(4)]
    osb = [sb(f"osb{i}", [128, 512]) for i in range(2)]
    rec = [sb(f"rec{i}", [128, 512]) for i in range(2)]
    onb = [sb(f"onb{i}", [64, 512], BF16) for i in range(2)]; d_on = [P.dsem() for _ in range(2)]
    pSc = [(PSA(f"pSc{i}", [128, 512], F32), P.buf(excl=True)) for i in range(3)]
    pO = [(PSA(f"pO{i}", [128, 512], F32), P.buf(excl=True)) for i in range(2)]
    pBC = (PSA("pBC", [128, 512], F32), P.buf(excl=True))
    pG = (PSA("pG", [128, 512], F32), P.buf(excl=True))
    pTr = (PSA("pTr", [128, 512], F32), P.buf(excl=True))
    pD = pG

    pending = []
    negpads = [[sb(f"negpad{g}_{i}", [128, 128], BF16) for i in range(4)] for g in range(3)]
    for g in range(3):
        for i in range(4):
            P.op("vector", lambda e, g=g, i=i: e.memset(negpads[g][i][0][:], 0.0), writes=[negpads[g][i][1]])

    def prologue(n_):
        hi = HEADS[n_]
        s2 = n_ % 2
        K, b_K = Ka[s2]; Q, b_Q = Qa[s2]; V, b_V = Va[s2]
        moba = hi >= 2
        c = dict(hi=hi, K=K, b_K=b_K, Q=Q, b_Q=b_Q, V=V, b_V=b_V, moba=moba, rows=slice(0, 96))
        if not moba:
            c["scale"] = 96.0 ** -0.5
            P.dma("sync", d_K[s2], K[0:96, :], mk[hi, :, :], writes=[b_K])
            P.dma("sync", d_Q[s2], Q[0:96, :], mq[hi, :, :], writes=[b_Q])
            vsrc = mv[:, hi * 64:(hi + 1) * 64]
        else:
            c["scale"] = 0.125
            hb = hi - 2
            srows = slice(hb * 64, (hb + 1) * 64)
            P.dma("sync", d_K[s2], K[0:64, :], ok[srows, :], writes=[b_K])
            P.dma("gpsimd", d_K2[s2], K[64:96, :], blkoh_d[:, :], writes=[b_K])
            P.dma("sync", d_Q[s2], Q[0:64, :], oq[srows, :], writes=[b_Q])
            vsrc = ov[:, hb * 64:(hb + 1) * 64]
        P.dma("sync", d_V[s2], V[:, :, 0:64], vsrc.rearrange("(kc p) d -> p kc d", p=128), writes=[b_V])
        P.op("gpsimd", lambda e, V=V: e.memset(V[:, :, 64:65], 1.0), writes=[b_V])
        return c

    def topk_gen(c):
        K, b_K, Q, b_Q = c["K"], c["b_K"], c["Q"], c["b_Q"]
        krows = slice(0, 64); off = 64
        P.op("vector", lambda e: e.tensor_reduce(out=kmf[krows, :], in_=K[krows, :].rearrange("p (n k) -> p n k", k=256), axis=AX.X, op=ALU.add),
             reads=[b_K], writes=[b_kmf])
        P.op("vector", lambda e: e.tensor_scalar(out=kmT[krows, :], in0=kmf[krows, :], scalar1=1.0 / 256, scalar2=None, op0=ALU.mult),
             reads=[b_kmf], writes=[b_kmT])

        def stage_a(qt):
            for j in range(4):
                qc = qt * 4 + j
                P.mm(pG[0][:, j * 32:(j + 1) * 32], Q[krows, qc * 128:(qc + 1) * 128], kmT[krows, :], True, True, reads=[b_Q, b_kmT], writes=[pG[1]])
            for j in range(4):
                qc = qt * 4 + j; qb = qc // 2
                g_, b_g = gm[j]; t8, b_t8 = top8[j]; sl_, b_sl = sel[j]; npd, b_np = negpads[qt % 3][j]
                P.op("gpsimd", lambda e, g_=g_: e.memset(g_[:], -1e30), writes=[b_g])
                if qb > 0:
                    P.op("vector", lambda e, g_=g_, j=j, qb=qb: e.tensor_copy(out=g_[:, 0:qb], in_=pG[0][:, j * 32:j * 32 + qb]), reads=[pG[1]], writes=[b_g])
                P.op("vector", lambda e, g_=g_, t8=t8: e.max(out=t8[:], in_=g_[:]), reads=[b_g], writes=[b_t8])
                P.op("vector", lambda e, g_=g_, t8=t8, sl_=sl_: e.tensor_scalar(out=sl_[:], in0=g_[:], scalar1=t8[:, 2:3], scalar2=None, op0=ALU.is_ge),
                     reads=[b_g, b_t8], writes=[b_sl])
                P.op("vector", lambda e, sl_=sl_, npd=npd: e.tensor_scalar(out=npd[:, off:off + 32], in0=sl_[:], scalar1=-1.0, scalar2=30000.0, op0=ALU.add, op1=ALU.mult),
                     reads=[b_sl], writes=[b_np])
                P.op("vector", lambda e, npd=npd, qb=qb: e.memset(npd[:, off + qb:off + qb + 1], 0.0), writes=[b_np])

        def stage_b(qt):
            for j in range(4):
                npd, b_np = negpads[qt % 3][j]
                P.mm(pTr[0][:, j * 128:(j + 1) * 128], npd[:], ident[:], True, True, reads=[b_np, b_id], writes=[pTr[1]])
            P.op("vector", lambda e: e.tensor_copy(out=Q[off:off + 32, qt * 512:(qt + 1) * 512], in_=pTr[0][off:off + 32, :]),
                 reads=[pTr[1]], writes=[b_Q])
        for qt in range(NQT + 2):
            if qt < NQT:
                stage_a(qt)
            if qt >= 2:
                stage_b(qt - 2)
            yield

    def main_loop(c, tick):
        K, b_K, Q, b_Q, V, b_V = c["K"], c["b_K"], c["Q"], c["b_Q"], c["V"], c["b_V"]
        rows, scale, hi = c["rows"], c["scale"], c["hi"]
        for qt in range(NQT):
            tick()
            nkc = 4 * qt + 4
            qs = slice(qt * 512, (qt + 1) * 512)
            po, b_po = pO[qt % 2]

            def score(kc):
                ps, b_ps = pSc[kc % 3]
                diag = kc >= 4 * qt
                P.mm(ps[:], K[rows, kc * 128:(kc + 1) * 128], Q[rows, qs], True, not diag, reads=[b_K, b_Q], writes=[b_ps])
                if diag:
                    P.mm(ps[:], ident[:], cmask[:, kc - 4 * qt, :], False, True, reads=[b_id, b_cm], writes=[b_ps])
            score(0)
            if nkc > 1:
                score(1)
            for kc in range(nkc):
                if kc + 2 < nkc:
                    score(kc + 2)
                if kc == min(10, nkc - 1) and pending:
                    pending.pop(0)()
                ps, b_ps = pSc[kc % 3]
                pt_, b_pt = PT[kc % 4]
                P.op("scalar", lambda e, ps=ps, pt_=pt_, scale=scale: e.activation(out=pt_[:], in_=ps[:], func=AF.Exp, scale=scale), reads=[b_ps], writes=[b_pt])
                P.mm(po[0:65, :], V[:, kc, :], pt_[:], kc == 0, kc == nkc - 1, reads=[b_V, b_pt], writes=[b_po])
            o_, b_o = osb[qt % 2]; r_, b_r = rec[qt % 2]; on_, b_on = onb[qt % 2]
            P.op("vector", lambda e, o_=o_, po=po: e.tensor_copy(out=o_[0:65, :], in_=po[0:65, :]), reads=[b_po], writes=[b_o])
            P.op("vector", lambda e, o_=o_, r_=r_: e.reciprocal(out=r_[64:65, :], in_=o_[64:65, :]), reads=[b_o], writes=[b_r])

            def epilogue(o_=o_, b_o=b_o, r_=r_, b_r=b_r, on_=on_, b_on=b_on, qt=qt, qs=qs, hi=hi):
                P.mm(pBC[0][0:64, :], onesf[64:65, :], r_[64:65, :], True, True, reads=[b_of, b_r], writes=[pBC[1]])
                P.op("vector", lambda e: e.tensor_tensor(out=on_[:], in0=o_[0:64, :], in1=pBC[0][0:64, :], op=ALU.mult), reads=[b_o, pBC[1]], writes=[b_on])
                P.dma("sync", d_on[qt % 2], oT[hi, :, qs], on_[:], reads=[b_on])
            pending.append(epilogue)

    nh = len(HEADS)
    ctxs = [None] * nh
    ctxs[0] = prologue(0)
    if ctxs[0]["moba"]:
        for _ in topk_gen(ctxs[0]):
            pass
    for n_ in range(nh):
        gen = None
        if n_ + 1 < nh:
            ctxs[n_ + 1] = prologue(n_ + 1)
            if ctxs[n_ + 1]["moba"]:
                gen = topk_gen(ctxs[n_ + 1])
        state = {"g": gen}

        def tick(state=state):
            if state["g"] is not None:
                try:
                    next(state["g"])
                except StopIteration:
                    state["g"] = None
        main_loop(ctxs[n_], tick)
        while state["g"] is not None:
            tick()
    while pending:
        pending.pop(0)()
    st = P.emit(final_waits=[("sync", d) for d in d_on])
    return nc

S = 8192
NSEG = 4
SEG = 2048
NT = 16
GT = 4
EPS = 1e-6
NLV = 6


def gdn_consts():
    t = np.arange(128)
    M = (t[:, None] <= t[None, :]).astype(np.float32)
    NEGM = np.where(t[:, None] >= t[None, :], 0.0, -1e30).astype(np.float32)
    STRICT = (t[:, None] > t[None, :]).astype(np.float32)
    ident = np.eye(128, dtype=np.float32)
    ones = np.ones((128, 128), np.float32)
    NEGS = np.where(t[:, None] > t[None, :], 0.0, -1e30).astype(np.float32)
    return np.stack([M, ones, NEGM, NEGS, ident, -ones], 0)


def build_gdn():
    nc = bass.Bass("TRN2", target_bir_lowering=False)
    DI = lambda n, s, dt=F32: nc.dram_tensor(n, s, dt, kind="ExternalInput").ap()
    rq = DI("rq", [128, S]); rk = DI("rk", [128, S]); rv = DI("rv", [128, S])
    zd = DI("z", [S, 128])
    bl = DI("bl", [128, 64]); al = DI("al", [128, 64])
    cw = DI("cw", [128, 12]); sc = DI("sc", [128, 2]); gn = DI("gn", [128, 128])
    cst = DI("cst", [6, 128, 128])
    od = nc.dram_tensor("o", [S, 128], BF16, kind="ExternalOutput").ap()
    P = Prog(nc)
    A = nc.alloc_sbuf_tensor
    PS = nc.alloc_psum_tensor

    def sb(name, shape, dt=F32):
        return A("s_" + name, shape, dt), P.buf(name)

    C, b_C = sb("C", [128, 6, 128])
    cwt, b_cw = sb("cwt", [128, 12]); sct, b_sc = sb("sct", [128, 2]); gnt, b_gn = sb("gnt", [128, 128])
    blt, b_bl = sb("blt", [128, 64]); alt, b_al = sb("alt", [128, 64])
    beta, b_beta = sb("beta", [128, 64]); gg, b_gg = sb("gg", [128, 64])
    tmp64, b_tmp64 = sb("tmp64", [128, 64]); ea, b_ea = sb("ea", [128, 1])
    P.dma("sync", P.dsem(), C[:], cst.rearrange("k p n -> p k n"), writes=[b_C])
    for (t_, d_, b_) in [(cwt, cw, b_cw), (sct, sc, b_sc), (gnt, gn, b_gn), (blt, bl, b_bl), (alt, al, b_al)]:
        P.dma("sync", P.dsem(), t_[:], d_[:, :], writes=[b_])
    Mm, ONES, NEGM, STRICT, IDENT, NEGONES = [C[:, i, :] for i in range(6)]
    NEG4, b_N4 = sb("NEG4", [128, GT, 128]); STR4, b_S4 = sb("STR4", [128, GT, 128]); ID4, b_I4 = sb("ID4", [128, GT, 128])
    GN4, b_G4 = sb("GN4", [128, GT, 128])
    for t in range(GT):
        P.op("gpsimd", lambda e, t=t: e.tensor_copy(out=GN4[:, t, :], in_=gnt[:]), reads=[b_gn], writes=[b_G4])
        P.op("gpsimd", lambda e, t=t: e.tensor_copy(out=NEG4[:, t, :], in_=NEGM), reads=[b_C], writes=[b_N4])
        P.op("gpsimd", lambda e, t=t: e.tensor_copy(out=STR4[:, t, :], in_=STRICT), reads=[b_C], writes=[b_S4])
        P.op("gpsimd", lambda e, t=t: e.tensor_copy(out=ID4[:, t, :], in_=IDENT), reads=[b_C], writes=[b_I4])
    P.op("scalar", lambda e: e.activation(out=beta[:], in_=blt[:], func=AF.Sigmoid), reads=[b_bl], writes=[b_beta])
    P.op("scalar", lambda e: e.activation(out=tmp64[:], in_=alt[:], func=AF.Exp, bias=sct[:, 1:2]), reads=[b_al, b_sc], writes=[b_tmp64])
    P.op("scalar", lambda e: e.activation(out=tmp64[:], in_=tmp64[:], func=AF.Ln, bias=1.0), reads=[b_tmp64], writes=[b_tmp64])
    P.op("scalar", lambda e: e.activation(out=ea[:], in_=sct[:, 0:1], func=AF.Exp), reads=[b_sc], writes=[b_ea])
    P.op("vector", lambda e: e.tensor_scalar(out=gg[:], in0=tmp64[:], scalar1=ea[:, 0:1], scalar2=-1.0, op0=ALU.mult, op1=ALU.mult),
         reads=[b_tmp64, b_ea], writes=[b_gg])

    raw = [sb(f"raw{i}", [128, SEG + 3]) for i in range(3)]
    d_raw = [P.dsem() for _ in range(3)]
    cvs = [[sb(f"cv{s}_{i}", [128, SEG]) for i in range(3)] for s in range(2)]
    sqb, b_sq = sb("sqb", [128, 512]); lnb, b_ln = sb("lnb", [128, 512]); rsb, b_rs = sb("rsb", [128, 512])
    stat = [[sb(f"{n}{s}", [128, NT]) for n in ("gcum", "egc", "edec", "dec", "begc")] for s in range(2)]

    def g4(name, n=1):
        return [sb(f"{name}{i}", [128, GT, 128]) for i in range(n)]
    ktm = g4("ktm")[0]; vb = g4("vb")[0]; rw = g4("rw")[0]
    Gm = g4("Gm")[0]; nGm = g4("nGm")[0]; dmin = g4("dmin")[0]; Dm = g4("Dm")[0]; Dms = g4("Dms")[0]
    Am = g4("Am")[0]; Bm = g4("Bm")[0]; qkd = g4("qkd")[0]
    Qm = g4("Qm", 2); Ym = g4("Ym", 2); YTm = g4("YTm", 2)
    kdec = g4("kdec", 2); qkdT = g4("qkdT", 2); uu = g4("uu", 2); wT = g4("wT", 2)
    zt = g4("zt", 2); d_z = [P.dsem() for _ in range(2)]
    szt = g4("szt", 2)
    ofb = [sb(f"ofb{i}", [128, GT, 128], BF16) for i in range(2)]; d_o = [P.dsem() for _ in range(2)]
    NB = 2
    vnew = [sb(f"vnew{i}", [128, 128]) for i in range(NB)]
    o1 = [sb(f"o1{i}", [128, 128]) for i in range(NB)]
    ot = [sb(f"ot{i}", [128, 128]) for i in range(NB)]
    osq = [sb(f"osq{i}", [128, 128]) for i in range(NB)]
    ss = [sb(f"ss{i}", [128, 1]) for i in range(NB)]
    lss = [sb(f"lss{i}", [128, 1]) for i in range(NB)]
    rss = [sb(f"rss{i}", [128, 1]) for i in range(NB)]
    og = [sb(f"og{i}", [128, 128]) for i in range(NB)]
    St = [sb(f"St{i}", [128, 128]) for i in range(2)]
    pb = [(PS(f"pb{i}", [128, 512], F32), P.buf(f"pb{i}", excl=True)) for i in range(8)]
    pcount = [0]

    def bank():
        i = pcount[0] % 6
        pcount[0] += 1
        return pb[i]
    pV, b_pV = pb[6]
    pSt, b_pSt = pb[7]
    P.op("vector", lambda e: e.memset(St[0][0][:], 0.0), writes=[St[0][1]])
    state = {"scur": 0}

    def seg_prep(seg):
        s0 = seg * SEG
        cv = cvs[seg % 2]
        for qi, rd in enumerate((rq, rk, rv)):
            r_, b_r = raw[qi]
            if seg == 0:
                P.op("gpsimd", lambda e, r_=r_: e.memset(r_[:, 0:3], 0.0), writes=[b_r])
                P.dma("sync", d_raw[qi], r_[:, 3:], rd[:, 0:SEG], writes=[b_r])
            else:
                P.dma("sync", d_raw[qi], r_[:, :], rd[:, s0 - 3:s0 + SEG], writes=[b_r])
            c_, b_c = cv[qi]
            for hf in range(2):
                lo = hf * 1024
                sl = slice(lo, lo + 1024)
                P.op("scalar", lambda e, c_=c_, r_=r_, lo=lo, qi=qi, sl=sl: e.activation(
                    out=c_[:, sl], in_=r_[:, lo:lo + 1024], func=AF.Copy, scale=cwt[:, qi * 4:qi * 4 + 1]),
                    reads=[b_r, b_cw], writes=[b_c])
                for tap in range(1, 4):
                    P.op("vector", lambda e, c_=c_, r_=r_, lo=lo, qi=qi, sl=sl, tap=tap: e.scalar_tensor_tensor(
                        out=c_[:, sl], in0=r_[:, lo + tap:lo + tap + 1024], scalar=cwt[:, qi * 4 + tap:qi * 4 + tap + 1],
                        in1=c_[:, sl], op0=ALU.mult, op1=ALU.add), reads=[b_r, b_cw, b_c], writes=[b_c])
                P.op("scalar", lambda e, c_=c_, sl=sl: e.activation(out=c_[:, sl], in_=c_[:, sl], func=AF.Silu),
                     reads=[b_c], writes=[b_c])
        for qi in range(2):
            c_, b_c = cv[qi]
            for t4 in range(4):
                sl = slice(t4 * 512, (t4 + 1) * 512)
                pt, b_pt = bank()
                P.op("scalar", lambda e, c_=c_, sl=sl: e.activation(out=sqb[:], in_=c_[:, sl], func=AF.Square), reads=[b_c], writes=[b_sq])
                P.mm(pt[:], ONES, sqb[:], True, True, reads=[b_C, b_sq], writes=[b_pt])
                P.op("scalar", lambda e, pt=pt: e.activation(out=lnb[:], in_=pt[:], func=AF.Ln, bias=EPS), reads=[b_pt], writes=[b_ln])
                P.op("scalar", lambda e: e.activation(out=rsb[:], in_=lnb[:], func=AF.Exp, scale=-0.5), reads=[b_ln], writes=[b_rs])
                scl = (128.0 ** -0.5) if qi == 0 else 1.0
                P.op("vector", lambda e, c_=c_, sl=sl, scl=scl: e.scalar_tensor_tensor(
                    out=c_[:, sl], in0=c_[:, sl], scalar=scl, in1=rsb[:], op0=ALU.mult, op1=ALU.mult),
                    reads=[b_c, b_rs], writes=[b_c])
        (gcum, b_gcum), (egc, b_egc), (edec, b_edec), (dec, b_dec), (begc, b_begc) = stat[seg % 2]
        gsl = slice(seg * NT, (seg + 1) * NT)
        pt, b_pt = bank()
        P.mm(pt[:, 0:NT], Mm, gg[:, gsl], True, True, reads=[b_C, b_gg], writes=[b_pt])
        P.mm(pt[:, 16:16 + NT], ONES, gg[:, gsl], True, True, reads=[b_C, b_gg], writes=[b_pt])
        P.op("vector", lambda e, pt=pt: e.tensor_copy(out=gcum[:], in_=pt[:, 0:NT]), reads=[b_pt], writes=[b_gcum])
        P.op("scalar", lambda e, pt=pt: e.activation(out=egc[:], in_=pt[:, 0:NT], func=AF.Exp), reads=[b_pt], writes=[b_egc])
        P.op("vector", lambda e, pt=pt: e.tensor_tensor(out=edec[:], in0=pt[:, 16:16 + NT], in1=gcum[:], op=ALU.subtract),
             reads=[b_pt, b_gcum], writes=[b_edec])
        P.op("scalar", lambda e: e.activation(out=edec[:], in_=edec[:], func=AF.Exp), reads=[b_edec], writes=[b_edec])
        P.op("scalar", lambda e, pt=pt: e.activation(out=dec[:], in_=pt[:, 16:16 + NT], func=AF.Exp), reads=[b_pt], writes=[b_dec])
        P.op("vector", lambda e, gsl=gsl: e.tensor_tensor(out=begc[:], in0=beta[:, gsl], in1=egc[:], op=ALU.mult),
             reads=[b_beta, b_egc], writes=[b_begc])

    def prepass_stages(seg, grp):
        cv = cvs[seg % 2]
        qT_, b_qT = cv[0]; kT_, b_kT = cv[1]; vT_, b_vT = cv[2]
        (gcum, b_gcum), (egc, b_egc), (edec, b_edec), (dec, b_dec), (begc, b_begc) = stat[seg % 2]
        gi = (seg * (NT // GT) + grp) % 2
        Ts = [grp * GT + t for t in range(GT)]
        cs = lambda t: slice(Ts[t] * 128, (Ts[t] + 1) * 128)
        Gs = [seg * NT + T for T in Ts]
        kd, b_kd = kdec[gi]; qT2, b_qT2 = qkdT[gi]; u_, b_u = uu[gi]; w_, b_w = wT[gi]
        pk, b_pk = bank(); pv, b_pv = bank()
        for t in range(GT):
            P.op("tensor", lambda e, t=t: e.transpose(pk[:, t * 128:(t + 1) * 128], kT_[:, cs(t)], IDENT), reads=[b_kT, b_C], writes=[b_pk])
        for t in range(GT):
            P.op("tensor", lambda e, t=t: e.transpose(pv[:, t * 128:(t + 1) * 128], vT_[:, cs(t)], IDENT), reads=[b_vT, b_C], writes=[b_pv])
        for t in range(GT):
            P.op("vector", lambda e, t=t: e.tensor_scalar(out=vb[0][:, t, :], in0=pv[:, t * 128:(t + 1) * 128], scalar1=beta[:, Gs[t]:Gs[t] + 1], scalar2=None, op0=ALU.mult),
                 reads=[b_pv, b_beta], writes=[vb[1]])
        for t in range(GT):
            P.op("scalar", lambda e, t=t: e.activation(out=rw[0][:, t, :], in_=pk[:, t * 128:(t + 1) * 128], func=AF.Copy, scale=begc[:, Ts[t]:Ts[t] + 1]),
                 reads=[b_pk, b_begc], writes=[rw[1]])
            P.op("scalar", lambda e, t=t: e.activation(out=kd[:, t, :], in_=pk[:, t * 128:(t + 1) * 128], func=AF.Copy, scale=edec[:, Ts[t]:Ts[t] + 1]),
                 reads=[b_pk, b_edec], writes=[b_kd])
        for t in range(GT):
            P.op("vector", lambda e, t=t: e.tensor_scalar(out=Gm[0][:, t, :], in0=Mm, scalar1=gg[:, Gs[t]:Gs[t] + 1], scalar2=None, op0=ALU.mult),
                 reads=[b_C, b_gg], writes=[Gm[1]])
        yield
        pd, b_pd = bank(); pkk, b_pkk = bank(); pqk, b_pqk = bank()
        for t in range(GT):
            o = slice(t * 128, (t + 1) * 128)
            P.mm(pd[:, o], Gm[0][:, t, :], ONES, True, False, reads=[Gm[1], b_C], writes=[b_pd])
            P.mm(pd[:, o], NEGONES, Gm[0][:, t, :], False, True, reads=[Gm[1], b_C], writes=[b_pd])
        for t in range(GT):
            o = slice(t * 128, (t + 1) * 128)
            P.mm(pkk[:, o], kT_[:, cs(t)], kT_[:, cs(t)], True, True, reads=[b_kT], writes=[b_pkk])
        for t in range(GT):
            o = slice(t * 128, (t + 1) * 128)
            P.mm(pqk[:, o], qT_[:, cs(t)], kT_[:, cs(t)], True, True, reads=[b_kT, b_qT], writes=[b_pqk])
        fl = lambda x: x[:].rearrange("p t n -> p (t n)")
        P.op("vector", lambda e: e.scalar_tensor_tensor(out=fl(dmin[0]), in0=pd[:], scalar=0.0, in1=fl(NEG4), op0=ALU.min, op1=ALU.add),
             reads=[b_pd, b_N4], writes=[dmin[1]])
        P.op("scalar", lambda e: e.activation(out=fl(Dm[0]), in_=fl(dmin[0]), func=AF.Exp), reads=[dmin[1]], writes=[Dm[1]])
        P.op("vector", lambda e: e.scalar_tensor_tensor(out=fl(nGm[0]), in0=pd[:], scalar=0.0, in1=fl(STR4), op0=ALU.min, op1=ALU.add),
             reads=[b_pd, b_S4], writes=[nGm[1]])
        P.op("scalar", lambda e: e.activation(out=fl(Dms[0]), in_=fl(nGm[0]), func=AF.Exp), reads=[nGm[1]], writes=[Dms[1]])
        for t in range(GT):
            P.op("vector", lambda e, t=t: e.scalar_tensor_tensor(out=Am[0][:, t, :], in0=pkk[:, t * 128:(t + 1) * 128], scalar=beta[:, Gs[t]:Gs[t] + 1], in1=Dms[0][:, t, :], op0=ALU.mult, op1=ALU.mult),
                 reads=[b_pkk, b_beta, Dms[1]], writes=[Am[1]])
        P.op("vector", lambda e: e.tensor_tensor(out=fl(qkd[0]), in0=pqk[:], in1=fl(Dm[0]), op=ALU.mult), reads=[b_pqk, Dm[1]], writes=[qkd[1]])
        yield
        pbt, b_pbt = bank(); pqt, b_pqt = bank()
        for t in range(GT):
            P.op("tensor", lambda e, t=t: e.transpose(pbt[:, t * 128:(t + 1) * 128], Am[0][:, t, :], IDENT), reads=[Am[1], b_C], writes=[b_pbt])
        for t in range(GT):
            P.op("tensor", lambda e, t=t: e.transpose(pqt[:, t * 128:(t + 1) * 128], qkd[0][:, t, :], IDENT), reads=[qkd[1], b_C], writes=[b_pqt])
        P.op("scalar", lambda e: e.copy(out=fl(Bm[0]), in_=pbt[:]), reads=[b_pbt], writes=[Bm[1]])
        P.op("vector", lambda e: e.tensor_copy(out=fl(qT2), in_=pqt[:]), reads=[b_pqt], writes=[b_qT2])
        Qc, b_Qc = Qm[0]
        P.op("gpsimd", lambda e: e.scalar_tensor_tensor(out=fl(Qc), in0=fl(Bm[0]), scalar=-1.0, in1=fl(ID4), op0=ALU.mult, op1=ALU.add),
             reads=[Bm[1], b_I4], writes=[b_Qc]) if False else \
            P.op("vector", lambda e: e.scalar_tensor_tensor(out=fl(Qc), in0=fl(Bm[0]), scalar=-1.0, in1=fl(ID4), op0=ALU.mult, op1=ALU.add),
                 reads=[Bm[1], b_I4], writes=[b_Qc])
        yield
        Yc, b_Yc = Bm; YTc, b_YTc = Am
        for lv in range(NLV):
            pyt, b_pyt = bank()
            Yn, b_Yn = Ym[lv % 2]; YTn, b_YTn = YTm[lv % 2]
            for t in range(GT):
                P.mm(pyt[:, t * 128:(t + 1) * 128], Yc[:, t, :], YTc[:, t, :], True, True, reads=[b_Yc, b_YTc], writes=[b_pyt])
            if lv < NLV - 1:
                py, b_py = bank()
                for t in range(GT):
                    P.mm(py[:, t * 128:(t + 1) * 128], YTc[:, t, :], Yc[:, t, :], True, True, reads=[b_Yc, b_YTc], writes=[b_py])
            P.op("scalar", lambda e, YTn=YTn, pyt=pyt: e.copy(out=fl(YTn), in_=pyt[:]), reads=[b_pyt], writes=[b_YTn])
            if lv < NLV - 1:
                P.op("vector", lambda e, Yn=Yn, py=py: e.tensor_copy(out=fl(Yn), in_=py[:]), reads=[b_py], writes=[b_Yn])
            yield
            Qo, b_Qo = Qm[lv % 2]; Qn, b_Qn = Qm[(lv + 1) % 2]
            pq, b_pq = bank()
            for t in range(GT):
                P.mm(pq[:, t * 128:(t + 1) * 128], YTn[:, t, :], Qo[:, t, :], True, True, reads=[b_YTn, b_Qo], writes=[b_pq])
            P.op("vector", lambda e, Qn=Qn, Qo=Qo, pq=pq: e.tensor_tensor(out=fl(Qn), in0=pq[:], in1=fl(Qo), op=ALU.add), reads=[b_pq, b_Qo], writes=[b_Qn])
            Yc, b_Yc = Yn, b_Yn
            YTc, b_YTc = YTn, b_YTn
            yield
        Tt, b_Tt = Qm[NLV % 2]
        pu, b_pu = bank(); pw, b_pw = bank()
        for t in range(GT):
            P.mm(pu[:, t * 128:(t + 1) * 128], Tt[:, t, :], vb[0][:, t, :], True, True, reads=[b_Tt, vb[1]], writes=[b_pu])
        for t in range(GT):
            P.mm(pw[:, t * 128:(t + 1) * 128], rw[0][:, t, :], Tt[:, t, :], True, True, reads=[b_Tt, rw[1]], writes=[b_pw])
        P.op("scalar", lambda e: e.copy(out=fl(u_), in_=pu[:]), reads=[b_pu], writes=[b_u])
        P.op("vector", lambda e: e.tensor_copy(out=fl(w_), in_=pw[:]), reads=[b_pw], writes=[b_w])
        G0 = Gs[0]
        P.dma("sync", d_z[gi], zt[gi][0][:], zd[G0 * 128:(G0 + GT) * 128, :].rearrange("(t p) d -> p t d", p=128), writes=[zt[gi][1]])
        P.op("scalar", lambda e: e.activation(out=fl(szt[gi][0]), in_=fl(zt[gi][0]), func=AF.Silu), reads=[zt[gi][1]], writes=[szt[gi][1]])
        P.op("gpsimd", lambda e: e.tensor_tensor(out=fl(szt[gi][0]), in0=fl(szt[gi][0]), in1=fl(GN4), op=ALU.mult), reads=[szt[gi][1], b_G4], writes=[szt[gi][1]])
        yield

    def scan_steps(seg, grp):
        cv = cvs[seg % 2]
        qT_, b_qT = cv[0]
        (gcum, b_gcum), (egc, b_egc), (edec, b_edec), (dec, b_dec), (begc, b_begc) = stat[seg % 2]
        gi = (seg * (NT // GT) + grp) % 2
        kd, b_kd = kdec[gi]; qT2, b_qT2 = qkdT[gi]; u_, b_u = uu[gi]; w_, b_w = wT[gi]
        for t in range(GT):
            T = grp * GT + t
            G = seg * NT + T
            i2 = G % NB
            cs = slice(T * 128, (T + 1) * 128)
            Sc, b_Sc = St[state["scur"]]; Sn, b_Sn = St[1 - state["scur"]]
            P.mm(pV[:, 0:128], w_[:, t, :], Sc[:], True, True, reads=[b_w, b_Sc], writes=[b_pV])
            P.mm(pV[:, 128:256], qT_[:, cs], Sc[:], True, True, reads=[b_qT, b_Sc], writes=[b_pV])
            P.op("vector", lambda e, i2=i2, t=t: e.tensor_tensor(out=vnew[i2][0][:], in0=u_[:, t, :], in1=pV[:, 0:128], op=ALU.subtract),
                 reads=[b_u, b_pV], writes=[vnew[i2][1]])
            P.op("scalar", lambda e, i2=i2, T=T: e.activation(out=o1[i2][0][:], in_=pV[:, 128:256], func=AF.Copy, scale=egc[:, T:T + 1]),
                 reads=[b_pV, b_egc], writes=[o1[i2][1]])
            P.mm(pSt[:, 0:128], kd[:, t, :], vnew[i2][0][:], True, True, reads=[b_kd, vnew[i2][1]], writes=[b_pSt])
            P.mm(pSt[:, 128:256], qT2[:, t, :], vnew[i2][0][:], True, True, reads=[b_qT2, vnew[i2][1]], writes=[b_pSt])
            P.op("vector", lambda e, Sn=Sn, Sc=Sc, T=T: e.scalar_tensor_tensor(out=Sn[:], in0=Sc[:], scalar=dec[:, T:T + 1], in1=pSt[:, 0:128], op0=ALU.mult, op1=ALU.add),
                 reads=[b_Sc, b_dec, b_pSt], writes=[b_Sn])
            P.op("vector", lambda e, i2=i2: e.tensor_tensor(out=ot[i2][0][:], in0=o1[i2][0][:], in1=pSt[:, 128:256], op=ALU.add),
                 reads=[o1[i2][1], b_pSt], writes=[ot[i2][1]])
            state["scur"] = 1 - state["scur"]
            P.op("scalar", lambda e, i2=i2: e.activation(out=osq[i2][0][:], in_=ot[i2][0][:], func=AF.Square, accum_out=ss[i2][0][:]),
                 reads=[ot[i2][1]], writes=[osq[i2][1], ss[i2][1]])
            P.op("scalar", lambda e, i2=i2: e.activation(out=lss[i2][0][:], in_=ss[i2][0][:], func=AF.Ln, scale=1.0 / 128, bias=EPS),
                 reads=[ss[i2][1]], writes=[lss[i2][1]])
            P.op("scalar", lambda e, i2=i2: e.activation(out=rss[i2][0][:], in_=lss[i2][0][:], func=AF.Exp, scale=-0.5),
                 reads=[lss[i2][1]], writes=[rss[i2][1]])
            P.op("vector", lambda e, i2=i2, t=t: e.scalar_tensor_tensor(out=ofb[gi][0][:, t, :], in0=ot[i2][0][:], scalar=rss[i2][0][:, 0:1], in1=szt[gi][0][:, t, :], op0=ALU.mult, op1=ALU.mult),
                 reads=[ot[i2][1], rss[i2][1], szt[gi][1]], writes=[ofb[gi][1]])
            yield
        G0 = seg * NT + grp * GT
        P.dma("sync", d_o[gi], od[G0 * 128:(G0 + GT) * 128, :].rearrange("(t p) d -> p t d", p=128), ofb[gi][0][:], reads=[ofb[gi][1]])

    groups = [(seg, grp) for seg in range(NSEG) for grp in range(NT // GT)]
    prev_scan = None
    for n_, (seg, grp) in enumerate(groups):
        if grp == 0:
            seg_prep(seg)
        pre = prepass_stages(seg, grp)
        done_pre = False
        rounds = 0
        while True:
            try:
                next(pre)
            except StopIteration:
                break
            rounds += 1
            if prev_scan is not None and rounds % 3 == 0:
                try:
                    next(prev_scan)
                except StopIteration:
                    prev_scan = None
        if prev_scan is not None:
            for _ in prev_scan:
                pass
        prev_scan = scan_steps(seg, grp)
    for _ in prev_scan:
        pass
    st = P.emit(final_waits=[("sync", d) for d in d_o])
    return nc

T = 2048
D = 1024
EPS = 1e-6
NTILES = 4
O_GATE = 4008


def build_merge():
    nc = bass.Bass("TRN2", target_bir_lowering=False)
    DI = lambda n, s, dt=F32: nc.dram_tensor(n, s, dt, kind="ExternalInput").ap()
    xT = DI("xT", [D, T]); brT = DI("brT", [3, 512, T], BF16); gd = DI("g", [128, 8])
    wgb = DI("wgb", [24, 128, 1536]); wo = DI("wo", [8, 128, 1024]); onesd = DI("ones", [128, 128])
    yT = nc.dram_tensor("yT", [D, T], F32, kind="ExternalOutput").ap()
    P = Prog(nc)
    A = nc.alloc_sbuf_tensor
    PSA = nc.alloc_psum_tensor

    def sb(name, shape, dt=F32):
        return A("s_" + name, shape, dt), P.buf(name)
    ones, b_ones = sb("ones", [128, 128], BF16); P.dma("gpsimd", P.dsem(), ones[:], onesd[:, :], writes=[b_ones])
    g, b_g = sb("g", [128, 8]); P.dma("sync", P.dsem(), g[:], gd[:, :], writes=[b_g])
    xin = [sb(f"xin{i}", [128, 8, 512]) for i in range(2)]; d_xin = [P.dsem() for _ in range(2)]
    br, b_br = sb("br", [128, 3, 4, T], BF16); d_br = P.dsem()
    sq = [sb(f"sq{i}", [128, 512], BF16) for i in range(2)]
    lnb, b_ln = sb("lnb", [128, 512]); rstd, b_rstd = sb("rstd", [128, 512])
    hT = sb("hT", [128, 8, T], BF16); b_hs = [P.buf() for _ in range(NTILES)]
    wc = [sb(f"wc{i}", [128, 1536], BF16) for i in range(3)]; d_wc = [P.dsem() for _ in range(3)]
    woc = [sb(f"woc{i}", [128, 1024], BF16) for i in range(2)]; d_wo = [P.dsem() for _ in range(2)]
    sig = [sb(f"sig{i}", [128, 512]) for i in range(2)]
    acc = [sb(f"acc{i}", [128, 512]) for i in range(NTILES)]
    tmp = [sb(f"tmp{i}", [128, 512]) for i in range(2)]
    mixed = sb("mixed", [128, 8, T], BF16); b_mxs = [P.buf() for _ in range(NTILES)]
    xres = [sb(f"xres{i}", [128, 512]) for i in range(2)]; d_xres = [P.dsem() for _ in range(2)]
    yo = [sb(f"yo{i}", [128, 512]) for i in range(2)]; d_yo = [P.dsem() for _ in range(2)]
    pS = (PSA("pS", [128, 512], F32), P.buf(excl=True))
    pG = [(PSA(f"pG{i}", [128, 512], F32), P.buf(excl=True)) for i in range(2)]
    pU = [(PSA(f"pU{i}", [128, 512], F32), P.buf(excl=True)) for i in range(2)]
    pO = [(PSA(f"pO{i}", [128, 512], F32), P.buf(excl=True)) for i in range(2)]
    xT_v = xT.rearrange("(kc p) n -> p kc n", p=128)
    hT_, mixed_ = hT[0], mixed[0]
    P.dma("sync", d_br, br[:, :, :, 0:NTILES * 512], brT[:, :, 0:NTILES * 512].rearrange("n (kc p) t -> p n kc t", p=128), writes=[b_br])
    for tt in range(NTILES):
        ts = slice(tt * 512, (tt + 1) * 512)
        xi, b_xi = xin[tt % 2]
        P.dma("sync", d_xin[tt % 2], xi[:], xT_v[:, :, ts], writes=[b_xi])
        for kc in range(8):
            s, bs = sq[kc % 2]
            P.op("scalar", lambda e, s=s, kc=kc, xi=xi: e.activation(out=s[:], in_=xi[:, kc, :], func=AF.Square), reads=[b_xi], writes=[bs])
            P.mm(pS[0][:], ones[:], s[:], kc == 0, kc == 7, reads=[b_ones, bs], writes=[pS[1]])
        P.op("scalar", lambda e: e.activation(out=lnb[:], in_=pS[0][:], func=AF.Ln, scale=1.0 / D, bias=EPS), reads=[pS[1]], writes=[b_ln])
        P.op("scalar", lambda e: e.activation(out=rstd[:], in_=lnb[:], func=AF.Exp, scale=-0.5), reads=[b_ln], writes=[b_rstd])
        for kc in range(8):
            P.op("vector", lambda e, kc=kc, xi=xi, ts=ts: e.scalar_tensor_tensor(out=hT_[:, kc, ts], in0=xi[:, kc, :], scalar=g[:, kc:kc + 1], in1=rstd[:], op0=ALU.mult, op1=ALU.mult),
                 reads=[b_xi, b_g, b_rstd], writes=[b_hs[tt]])
    cnt = 0; c2n = 0
    for c in range(8):
        for n in range(3):
            w, bw = wc[cnt % 3]
            P.dma("gpsimd", d_wc[cnt % 3], w[:], wgb[c * 3 + n, :, :], writes=[bw])
            cnt += 1
            for tt in range(NTILES):
                ts = slice(tt * 512, (tt + 1) * 512)
                a_, b_a = acc[tt]
                pg, b_pg = pG[c2n % 2]; pu, b_pu = pU[c2n % 2]; sg, b_sg = sig[c2n % 2]; tm, b_tm = tmp[c2n % 2]
                c2n += 1
                for kc in range(8):
                    P.mm(pg[:], w[:, kc * 128:(kc + 1) * 128], hT_[:, kc, ts], kc == 0, kc == 7, reads=[bw, b_hs[tt]], writes=[b_pg])
                for kc in range(4):
                    P.mm(pu[:], w[:, 1024 + kc * 128:1024 + (kc + 1) * 128], br[:, n, kc, ts], kc == 0, kc == 3, reads=[bw, b_br], writes=[b_pu])
                P.op("scalar", lambda e, sg=sg, pg=pg: e.activation(out=sg[:], in_=pg[:], func=AF.Sigmoid), reads=[b_pg], writes=[b_sg])
                if n == 0:
                    P.op("vector", lambda e, a_=a_, sg=sg, pu=pu: e.tensor_tensor(out=a_[:], in0=sg[:], in1=pu[:], op=ALU.mult), reads=[b_sg, b_pu], writes=[b_a])
                else:
                    P.op("vector", lambda e, tm=tm, sg=sg, pu=pu: e.tensor_tensor(out=tm[:], in0=sg[:], in1=pu[:], op=ALU.mult), reads=[b_sg, b_pu], writes=[b_tm])
                    if n == 1:
                        P.op("gpsimd", lambda e, a_=a_, tm=tm: e.tensor_tensor(out=a_[:], in0=a_[:], in1=tm[:], op=ALU.add), reads=[b_a, b_tm], writes=[b_a])
                    else:
                        P.op("gpsimd", lambda e, a_=a_, tm=tm, c=c, ts=ts: e.tensor_tensor(out=mixed_[:, c, ts], in0=a_[:], in1=tm[:], op=ALU.add), reads=[b_a, b_tm], writes=[b_mxs[tt]])
    k2 = 0
    for c2 in range(8):
        w, bw = woc[c2 % 2]
        P.dma("gpsimd", d_wo[c2 % 2], w[:], wo[c2, :, :], writes=[bw])
        for tt in range(NTILES):
            ts = slice(tt * 512, (tt + 1) * 512)
            po, b_po = pO[k2 % 2]; y_, b_y = yo[k2 % 2]; xr, b_xr = xres[k2 % 2]
            P.dma("sync", d_xres[k2 % 2], xr[:], xT[c2 * 128:(c2 + 1) * 128, ts], writes=[b_xr])
            for kc in range(8):
                P.mm(po[:], w[:, kc * 128:(kc + 1) * 128], mixed_[:, kc, ts], kc == 0, kc == 7, reads=[bw, b_mxs[tt]], writes=[b_po])
            P.op("vector", lambda e, y_=y_, po=po, xr=xr: e.tensor_tensor(out=y_[:], in0=po[:], in1=xr[:], op=ALU.add), reads=[b_po, b_xr], writes=[b_y])
            P.dma("sync", d_yo[k2 % 2], yT[c2 * 128:(c2 + 1) * 128, ts], y_[:], reads=[b_y])
            k2 += 1
    st = P.emit(final_waits=[("sync", d) for d in d_yo])
    return nc


def merge_weights(mix_norm, w_in, w_branch, w_out):
    g = np.ascontiguousarray(mix_norm.reshape(8, 128).T)
    wr = w_in.reshape(8, 128, -1)
    wgb = np.zeros((8, 3, 128, 1536), np.float32)
    for c in range(8):
        for n in range(3):
            c0 = O_GATE + n * 1024 + c * 128
            wgb[c, n, :, 0:1024] = wr[:, :, c0:c0 + 128].transpose(1, 0, 2).reshape(128, 1024)
            wgb[c, n, :, 1024:1536] = w_branch[n].reshape(4, 128, 1024)[:, :, c * 128:(c + 1) * 128].transpose(1, 0, 2).reshape(128, 512)
    wo = np.ascontiguousarray(w_out.reshape(8, 128, 8, 128).transpose(2, 1, 0, 3)).reshape(8, 128, 1024)
    return {"g": g, "wgb": wgb.reshape(24, 128, 1536), "wo": wo, "ones": np.ones((128, 128), np.float32)}

_PROGS = {}


def _prog(name, fn):
    if name not in _PROGS:
        _PROGS[name] = fn()
    return _PROGS[name]


def _run(nc, maps):
    res = run_bass_kernel_spmd(nc, maps, core_ids=list(range(8)))
    return res.results


def _ffn_launch(xT_cores, norm, w_in, w_out):
    g = np.ascontiguousarray(norm.reshape(8, 128).T)
    wi = w_in.reshape(8, 128, 2, NJ, 128)
    w1 = np.ascontiguousarray(wi.transpose(3, 1, 2, 0, 4)).reshape(NJ, 128, 2048)
    wo = w_out.reshape(NJ, 128, 8, 128)
    w2 = np.ascontiguousarray(wo.transpose(2, 1, 0, 3)).reshape(8, 128, NJ * 128)
    ones = np.ones((128, 128), np.float32)
    maps = [{"xT": xT_cores[c], "g": g, "w1": w1, "w2": w2, "ones": ones} for c in range(8)]
    r = _run(_prog("ffn", build_ffn), maps)
    return [np.ascontiguousarray(r[c]["yT"]) for c in range(8)]


def kernel(x, ffa_norm, ffa_w_in, ffa_w_out, mix_norm, w_in, mla_cq_norm, mla_ckv_norm,
           mla_w_uq, mla_w_ukv, mla_q_norm, mla_k_norm, gdn_conv, gdn_a_log, gdn_dt_bias,
           gdn_out_norm, moba_q_norm, moba_k_norm, w_branch, w_out, ffb_norm, ffb_w_in,
           ffb_w_out):
    f = lambda a: np.asarray(a, dtype=np.float32)
    x = f(x)
    B_, S_, D_ = x.shape
    xf = x.reshape(B_ * S_, D_)
    xT = [np.ascontiguousarray(xf[c * T:(c + 1) * T].T) for c in range(8)]
    blkoh, cmask, ident, onesf = attn_consts()
    gcst = gdn_consts()
    for l in range(2):
        xT = _ffn_launch(xT, f(ffa_norm)[l], f(ffa_w_in)[l], f(ffa_w_out)[l])
        W = projb_weights(f(mix_norm)[l], f(w_in)[l], f(mla_cq_norm)[l], f(mla_ckv_norm)[l], f(mla_w_uq)[l],
                          f(mla_w_ukv)[l], f(mla_q_norm)[l], f(mla_k_norm)[l], f(moba_q_norm)[l], f(moba_k_norm)[l])
        maps = []
        for c in range(8):
            m = dict(W)
            m["xT"] = xT[c]
            j = c % 4
            m["rope"] = rope_tables(np.arange(j * T, (j + 1) * T))
            maps.append(m)
        rb = _run(_prog("projb", build_projb), maps)

        def gather(name, b, axis):
            return np.concatenate([rb[b * 4 + j][name] for j in range(4)], axis=axis)
        full = []
        for b in range(2):
            full.append({"mla_qT": gather("mla_qT", b, 2), "mla_kT": gather("mla_kT", b, 2), "mla_v": gather("mla_v", b, 0),
                         "mo_qT": gather("mo_qT", b, 2), "mo_kT": gather("mo_kT", b, 2), "mo_v": gather("mo_v", b, 0),
                         "graw": gather("graw", b, 2), "gba": gather("gba", b, 1), "z": gather("z", b, 0)})
        maps = []
        for c in range(8):
            b, hp = c // 4, c % 4
            F = full[b]
            maps.append({"mq": np.ascontiguousarray(F["mla_qT"][2 * hp:2 * hp + 2]), "mk": np.ascontiguousarray(F["mla_kT"][2 * hp:2 * hp + 2]),
                         "mv": np.ascontiguousarray(F["mla_v"][:, hp * 128:(hp + 1) * 128]),
                         "oq": np.ascontiguousarray(F["mo_qT"][hp]), "ok": np.ascontiguousarray(F["mo_kT"][hp]),
                         "ov": np.ascontiguousarray(F["mo_v"][:, hp * 128:(hp + 1) * 128]),
                         "blkoh": blkoh, "cmask": cmask, "ident": ident, "onesf": onesf})
        ra = _run(_prog("attn", build_attn), maps)
        maps = []
        cw_l = f(gdn_conv)[l]
        for c in range(8):
            b, hd = c // 4, c % 4
            F = full[b]
            cw = np.concatenate([cw_l[:, k0 + hd * 128:k0 + (hd + 1) * 128].T for k0 in (0, 512, 1024)], 1)
            sc = np.stack([np.full(128, f(gdn_a_log)[l][hd], np.float32), np.full(128, f(gdn_dt_bias)[l][hd], np.float32)], 1)
            maps.append({"rq": np.ascontiguousarray(F["graw"][hd]), "rk": np.ascontiguousarray(F["graw"][4 + hd]),
                         "rv": np.ascontiguousarray(F["graw"][8 + hd]),
                         "z": np.ascontiguousarray(F["z"][:, hd * 128:(hd + 1) * 128]),
                         "bl": np.ascontiguousarray(F["gba"][hd].reshape(64, 128).T), "al": np.ascontiguousarray(F["gba"][4 + hd].reshape(64, 128).T),
                         "cw": np.ascontiguousarray(cw), "sc": sc,
                         "gn": np.ascontiguousarray(np.broadcast_to(f(gdn_out_norm)[l][None, :], (128, 128))),
                         "cst": gcst})
        rg = _run(_prog("gdn", build_gdn), maps)
        Wm = merge_weights(f(mix_norm)[l], f(w_in)[l], f(w_branch)[l], f(w_out)[l])
        brT = []
        for b in range(2):
            o_mla = np.concatenate([ra[b * 4 + hp]["oT"][i] for hp in range(4) for i in range(2)], 0)
            o_mo = np.concatenate([ra[b * 4 + hp]["oT"][2 + i] for hp in range(4) for i in range(2)], 0)
            o_gdn = np.concatenate([rg[b * 4 + hd]["o"].T for hd in range(4)], 0)
            brT.append(np.stack([o_mla, o_gdn, o_mo], 0))
        maps = []
        for c in range(8):
            b, j = c // 4, c % 4
            m = dict(Wm)
            m["xT"] = xT[c]
            m["brT"] = np.ascontiguousarray(brT[b][:, :, j * T:(j + 1) * T])
            maps.append(m)
        rm = _run(_prog("merge", build_merge), maps)
        xT = [np.ascontiguousarray(rm[c]["yT"]) for c in range(8)]
        xT = _ffn_launch(xT, f(ffb_norm)[l], f(ffb_w_in)[l], f(ffb_w_out)[l])
    out = np.concatenate([xT[c].T for c in range(8)], 0).reshape(B_, S_, D_)
    return np.ascontiguousarray(out.astype(np.float32))
```

```python
import ml_dtypes
from concourse.bass_utils import run_bass_kernel_spmd


import numpy as np
import concourse.bass as bass
import concourse.mybir as mybir

F32 = mybir.dt.float32
BF16 = mybir.dt.bfloat16
AF = mybir.ActivationFunctionType
ALU = mybir.AluOpType
AX = mybir.AxisListType

ENGS = ("tensor", "vector", "scalar", "gpsimd", "sync")


class Buf:
    __slots__ = ("name", "last_w", "readers", "excl")

    def __init__(self, name, excl=False):
        self.name = name
        self.last_w = None
        self.readers = []
        self.excl = excl


class Op:
    __slots__ = ("eng", "fn", "idx", "deps", "signal", "dsem", "dord", "count")

    def __init__(self, eng, fn, idx):
        self.eng = eng
        self.fn = fn
        self.idx = idx
        self.deps = []
        self.signal = False
        self.dsem = None
        self.dord = 0
        self.count = 0


class DSem:
    def __init__(self, name):
        self.name = name
        self.n = 0
        self.handle = None


class Prog:
    def __init__(self, nc):
        self.nc = nc
        self.ops = {e: [] for e in ENGS}
        self.dsems = []
        self.nbuf = 0

    def buf(self, name=None, excl=False):
        self.nbuf += 1
        return Buf(name or f"b{self.nbuf}", excl)

    def dsem(self, name=None):
        d = DSem(name or f"d{len(self.dsems)}")
        self.dsems.append(d)
        return d

    def _deps(self, op, reads, writes):
        deps = op.deps
        for b in reads:
            if b.excl:
                writes = list(writes) + [b]
                continue
            if b.last_w is not None:
                deps.append(b.last_w)
            b.readers.append(op)
        for b in writes:
            if b.last_w is not None:
                deps.append(b.last_w)
            deps.extend(r for r in b.readers if r is not op)
            b.readers = []
            b.last_w = op

    def op(self, eng, fn, reads=(), writes=()):
        o = Op(eng, fn, len(self.ops[eng]))
        self.ops[eng].append(o)
        self._deps(o, reads, writes)
        return o

    def dma(self, eng, dsem, out, in_, reads=(), writes=()):
        o = Op(eng, ("dma", out, in_), len(self.ops[eng]))
        dsem.n += 1
        o.dsem = dsem
        o.dord = dsem.n
        self.ops[eng].append(o)
        self._deps(o, reads, writes)
        return o

    def mm(self, out, lhsT, rhs, start, stop, reads=(), writes=()):
        return self.op("tensor", lambda e: e.matmul(out, lhsT, rhs, start=start, stop=stop),
                       reads, writes)

    def emit(self, final_waits=()):
        nc = self.nc
        for e in ENGS:
            for o in self.ops[e]:
                for d in o.deps:
                    if d.dsem is None:
                        if d.eng == "tensor" and o.eng == "tensor":
                            continue
                        d.signal = True
        esem = {e: nc.alloc_semaphore(f"sem_{e}") for e in ENGS}
        for d in self.dsems:
            if d.n:
                d.handle = nc.alloc_semaphore(f"dsem_{d.name}")
        for e in ENGS:
            c = 0
            for o in self.ops[e]:
                if o.dsem is None and o.signal:
                    c += 1
                    o.count = c
        stats = {}
        with nc.Block() as block:
            def run(ename, eng):
                waited = {}
                nwait = 0
                for o in self.ops[ename]:
                    need = {}
                    for d in o.deps:
                        if d.dsem is not None:
                            key = ("d", id(d.dsem)); sem = d.dsem.handle; val = 16 * d.dord
                        else:
                            if d.eng == "tensor" and ename == "tensor":
                                continue
                            key = ("e", d.eng); sem = esem[d.eng]; val = d.count
                        if need.get(key, (None, -1))[1] < val:
                            need[key] = (sem, val)
                    for key, (sem, val) in need.items():
                        if waited.get(key, -1) >= val:
                            continue
                        eng.wait_ge(sem, val)
                        waited[key] = val
                        nwait += 1
                    if o.dsem is not None:
                        _, out, in_ = o.fn
                        eng.dma_start(out=out, in_=in_).then_inc(o.dsem.handle, 16)
                    else:
                        ins = o.fn(eng)
                        if o.signal:
                            ins.then_inc(esem[ename], 1)
                for (kind, obj) in final_waits:
                    if ename != kind:
                        continue
                    eng.wait_ge(obj.handle, 16 * obj.n)
                stats[ename] = (len(self.ops[ename]), nwait)

            @block.tensor
            def _(eng):
                run("tensor", eng)

            @block.vector
            def _(eng):
                run("vector", eng)

            @block.scalar
            def _(eng):
                run("scalar", eng)

            @block.gpsimd
            def _(eng):
                run("gpsimd", eng)

            @block.sync
            def _(eng):
                run("sync", eng)
        return stats


T = 2048
D = 1024
DFF = 2816
NJ = DFF // 128
EPS = 1e-6


def build_ffn():
    nc = bass.Bass("TRN2", target_bir_lowering=False)
    xT = nc.dram_tensor("xT", [D, T], F32, kind="ExternalInput").ap()
    gd = nc.dram_tensor("g", [128, 8], F32, kind="ExternalInput").ap()
    w1d = nc.dram_tensor("w1", [NJ, 128, 2048], F32, kind="ExternalInput").ap()
    w2d = nc.dram_tensor("w2", [8, 128, NJ * 128], F32, kind="ExternalInput").ap()
    onesd = nc.dram_tensor("ones", [128, 128], F32, kind="ExternalInput").ap()
    yT = nc.dram_tensor("yT", [D, T], F32, kind="ExternalOutput").ap()
    P = Prog(nc)
    A = nc.alloc_sbuf_tensor
    ones = A("ones_sb", [128, 128], BF16); b_ones = P.buf()
    g = A("g_sb", [128, 8], F32); b_g = P.buf()
    xin = [A(f"xin{i}", [128, 8, 512], F32) for i in range(2)]; b_xin = [P.buf() for _ in range(2)]
    sq = [A(f"sq{i}", [128, 512], BF16) for i in range(2)]; b_sq = [P.buf() for _ in range(2)]
    lnb = A("lnb", [128, 512], F32); b_ln = P.buf()
    rstd = A("rstd", [128, 512], F32); b_rstd = P.buf()
    hT = A("hT", [128, 8, 1024], BF16); b_h = [P.buf() for _ in range(2)]
    actT = A("actT", [128, NJ, 1024], BF16); b_act = [P.buf() for _ in range(2)]
    w1 = [A(f"w1_{i}", [128, 2048], BF16) for i in range(2)]; b_w1 = [P.buf() for _ in range(2)]
    w2 = [A(f"w2_{i}", [128, NJ * 128], BF16) for i in range(2)]; b_w2 = [P.buf() for _ in range(2)]
    sg = [A(f"sg{i}", [128, 512], F32) for i in range(2)]; b_sg = [P.buf() for _ in range(2)]
    xres = [A(f"xres{i}", [128, 512], F32) for i in range(2)]; b_xres = [P.buf() for _ in range(2)]
    yo = [A(f"yo{i}", [128, 512], F32) for i in range(2)]; b_yo = [P.buf() for _ in range(2)]
    PS = nc.alloc_psum_tensor
    pS = PS("pS", [128, 512], F32); b_pS = P.buf(excl=True)
    pG = [PS(f"pG{i}", [128, 512], F32) for i in range(2)]; b_pG = [P.buf(excl=True) for _ in range(2)]
    pU = [PS(f"pU{i}", [128, 512], F32) for i in range(2)]; b_pU = [P.buf(excl=True) for _ in range(2)]
    pO = [PS(f"pO{i}", [128, 512], F32) for i in range(2)]; b_pO = [P.buf(excl=True) for _ in range(2)]
    d_c = P.dsem(); d_g = P.dsem()
    d_xin = [P.dsem() for _ in range(2)]
    d_w1 = [P.dsem() for _ in range(2)]; d_w2 = [P.dsem() for _ in range(2)]
    d_xres = [P.dsem() for _ in range(2)]; d_out = [P.dsem() for _ in range(2)]
    b_ydram = P.buf()

    P.dma("gpsimd", d_c, ones[:], onesd[:, :], writes=[b_ones])
    P.dma("sync", d_g, g[:], gd[:, :], writes=[b_g])
    xT_v = xT.rearrange("(kc p) n -> p kc n", p=128)
    for hh in range(2):
        t0 = hh * 1024
        for tt in range(2):
            tok = t0 + tt * 512
            xi = xin[tt]; bxi = b_xin[tt]
            P.dma("sync", d_xin[tt], xi[:], xT_v[:, :, tok:tok + 512], writes=[bxi])
            for kc in range(8):
                s = sq[kc % 2]; bs = b_sq[kc % 2]
                P.op("scalar", lambda e, s=s, xi=xi, kc=kc: e.activation(out=s[:], in_=xi[:, kc, :], func=AF.Square),
                     reads=[bxi], writes=[bs])
                P.mm(pS[:], ones[:], s[:], kc == 0, kc == 7, reads=[b_ones, bs], writes=[b_pS])
            P.op("scalar", lambda e: e.activation(out=lnb[:], in_=pS[:], func=AF.Ln, scale=1.0 / D, bias=EPS),
                 reads=[b_pS], writes=[b_ln])
            P.op("scalar", lambda e: e.activation(out=rstd[:], in_=lnb[:], func=AF.Exp, scale=-0.5),
                 reads=[b_ln], writes=[b_rstd])
            for kc in range(8):
                P.op("vector", lambda e, xi=xi, kc=kc, tt=tt: e.scalar_tensor_tensor(
                    out=hT[:, kc, tt * 512:(tt + 1) * 512], in0=xi[:, kc, :], scalar=g[:, kc:kc + 1], in1=rstd[:],
                    op0=ALU.mult, op1=ALU.mult), reads=[bxi, b_g, b_rstd], writes=[b_h[tt]])
        for j in range(NJ):
            w = w1[j % 2]; bw = b_w1[j % 2]
            P.dma("gpsimd", d_w1[j % 2], w[:], w1d[j, :, :], writes=[bw])
            for tt in range(2):
                hs = lambda kc, tt=tt: hT[:, kc, tt * 512:(tt + 1) * 512]
                for kc in range(8):
                    P.mm(pG[tt][:], w[:, kc * 128:(kc + 1) * 128], hs(kc), kc == 0, kc == 7,
                         reads=[bw, b_h[tt]], writes=[b_pG[tt]])
                for kc in range(8):
                    P.mm(pU[tt][:], w[:, 1024 + kc * 128:1024 + (kc + 1) * 128], hs(kc), kc == 0, kc == 7,
                         reads=[bw, b_h[tt]], writes=[b_pU[tt]])
                P.op("scalar", lambda e, tt=tt: e.activation(out=sg[tt][:], in_=pG[tt][:], func=AF.Silu),
                     reads=[b_pG[tt]], writes=[b_sg[tt]])
                P.op("vector", lambda e, tt=tt, j=j: e.tensor_tensor(
                    out=actT[:, j, tt * 512:(tt + 1) * 512], in0=sg[tt][:], in1=pU[tt][:], op=ALU.mult),
                    reads=[b_sg[tt], b_pU[tt]], writes=[b_act[tt]])
        for c in range(8):
            w = w2[c % 2]; bw = b_w2[c % 2]
            P.dma("gpsimd", d_w2[c % 2], w[:], w2d[c, :, :], writes=[bw])
            for tt in range(2):
                tok = t0 + tt * 512
                for j in range(NJ):
                    P.mm(pO[tt][:], w[:, j * 128:(j + 1) * 128], actT[:, j, tt * 512:(tt + 1) * 512], j == 0, j == NJ - 1,
                         reads=[bw, b_act[tt]], writes=[b_pO[tt]])
                P.dma("sync", d_xres[tt], xres[tt][:], xT[c * 128:(c + 1) * 128, tok:tok + 512], writes=[b_xres[tt]])
                P.op("vector", lambda e, tt=tt: e.scalar_tensor_tensor(
                    out=yo[tt][:], in0=pO[tt][:], scalar=0.5, in1=xres[tt][:], op0=ALU.mult, op1=ALU.add),
                    reads=[b_pO[tt], b_xres[tt]], writes=[b_yo[tt]])
                P.dma("sync", d_out[tt], yT[c * 128:(c + 1) * 128, tok:tok + 512], yo[tt][:], reads=[b_yo[tt]])
    st = P.emit(final_waits=[("sync", d_out[0]), ("sync", d_out[1])])
    return nc


def ffn_host_inputs(x_flat, norm, w_in, w_out):
    g = np.ascontiguousarray(norm.reshape(8, 128).T)
    wi = w_in.reshape(8, 128, 2, NJ, 128)
    w1 = np.ascontiguousarray(wi.transpose(3, 1, 2, 0, 4)).reshape(NJ, 128, 2048)
    wo = w_out.reshape(NJ, 128, 8, 128)
    w2 = np.ascontiguousarray(wo.transpose(2, 1, 0, 3)).reshape(8, 128, NJ * 128)
    ones = np.ones((128, 128), np.float32)
    maps = []
    for c in range(8):
        xT = np.ascontiguousarray(x_flat[c * T:(c + 1) * T].T)
        maps.append({"xT": xT, "g": g, "w1": w1, "w2": w2, "ones": ones})
    return maps


T = 2048
D = 1024
EPS = 1e-6
NCH = 25
ROPE_THETA = 10000.0
NTILES = 4
ND = 3

O_CQ, O_CKV, O_KR, O_GQ, O_GK, O_GV, O_GB, O_GA, O_GZ, O_MQ, O_MK, O_MV, O_GATE = 0, 256, 384, 416, 928, 1440, 1952, 1956, 1960, 2472, 2984, 3496, 4008
CH_COLS = ([(O_CQ, 128), (O_CQ + 128, 128), (O_CKV, 128), (O_KR, 32)] +
           [(O_GQ + h * 128, 128) for h in range(4)] + [(O_GK + h * 128, 128) for h in range(4)] +
           [(O_GV + h * 128, 128) for h in range(4)] + [(O_GB, 8)] +
           [(O_MQ + h * 128, 128) for h in range(4)] + [(O_MK + h * 128, 128) for h in range(4)])


def build_projb():
    nc = bass.Bass("TRN2", target_bir_lowering=False)
    DI = lambda n, s, dt=F32: nc.dram_tensor(n, s, dt, kind="ExternalInput").ap()
    DO = lambda n, s, dt=F32: nc.dram_tensor(n, s, dt, kind="ExternalOutput").ap()
    xT = DI("xT", [D, T]); gains_d = DI("gains", [128, 16])
    wb1 = DI("wb1", [NCH, 128, 1024]); wz_d = DI("wz", [128, 4096]); wmv_d = DI("wmv", [128, 4096])
    wuq_d = DI("wuq", [128, 1536]); wuk_d = DI("wuk", [128, 768]); wuv_d = DI("wuv", [128, 512])
    cmat = DI("cmat", [5, 128, 128])
    rope_d = DI("rope", [4, 128, T])
    o_mq = DO("mla_qT", [8, 96, T], BF16); o_mk = DO("mla_kT", [8, 96, T], BF16); o_mv = DO("mla_v", [T, 512], BF16)
    o_oq = DO("mo_qT", [4, 128, T], BF16); o_ok = DO("mo_kT", [4, 128, T], BF16); o_ov = DO("mo_v", [T, 512], BF16)
    o_gr = DO("graw", [12, 128, T]); o_gba = DO("gba", [8, T]); o_z = DO("z", [T, 512])
    P = Prog(nc)
    A = nc.alloc_sbuf_tensor
    PSA = nc.alloc_psum_tensor

    def sb(name, shape, dt=F32):
        return A("s_" + name, shape, dt), P.buf(name)

    def load(eng, t, b, src):
        P.dma(eng, P.dsem(), t, src, writes=[b])

    gains, b_gains = sb("gains", [128, 16]); load("sync", gains[:], b_gains, gains_d[:, :])
    CM, b_CM = sb("CM", [128, 5, 128], BF16); load("gpsimd", CM[:], b_CM, cmat.rearrange("k p n -> p k n"))
    ONES = CM[:, 0, :]; BLK64 = CM[:, 1, :]; RM_MLA = CM[0:96, 2, 0:96]; RM_MO = CM[:, 3, :]; SEL = CM[0:32, 4, 0:96]
    rope, b_rope = sb("rope", [128, 4, T]); load("sync", rope[:], b_rope, rope_d.rearrange("k p n -> p k n"))
    wz, b_wz = sb("wz", [128, 4096], BF16); load("gpsimd", wz[:], b_wz, wz_d[:, :])
    wmv, b_wmv = sb("wmv", [128, 4096], BF16); load("gpsimd", wmv[:], b_wmv, wmv_d[:, :])
    wuq, b_wuq = sb("wuq", [128, 1536], BF16); load("gpsimd", wuq[:], b_wuq, wuq_d[:, :])
    wuk, b_wuk = sb("wuk", [128, 768], BF16); load("gpsimd", wuk[:], b_wuk, wuk_d[:, :])
    wuv, b_wuv = sb("wuv", [128, 512], BF16); load("gpsimd", wuv[:], b_wuv, wuv_d[:, :])
    xins = [sb(f"xin{i}", [128, 8, 512]) for i in range(1)]; d_xins = [P.dsem() for _ in range(1)]
    sq = [sb(f"sq{i}", [128, 512], BF16) for i in range(2)]
    lnb, b_ln = sb("lnb", [128, 512]); rstd, b_rstd = sb("rstd", [128, 512])
    hT, b_h = sb("hT", [128, 8, T], BF16)
    wch = [sb(f"wch{i}", [128, 1024], BF16) for i in range(3)]; d_wch = [P.dsem() for _ in range(3)]
    cq = [sb(f"cq{i}", [128, T]) for i in range(3)]
    cqn = [sb(f"cqn{i}", [128, T], BF16) for i in range(3)]
    krope, b_krope = sb("krope", [32, T], BF16)
    NF = 4; NB = 8
    stf = [sb(f"stf{i}", [128, 512]) for i in range(NF)]; d_stf = [P.dsem() for _ in range(NF)]
    stb = [sb(f"stb{i}", [128, 512], BF16) for i in range(NB)]; d_stb = [P.dsem() for _ in range(NB)]
    cf = [0]; cb = [0]
    sqv = [sb(f"sqv{i}", [128, 512], BF16) for i in range(ND)]
    lnv = [sb(f"lnv{i}", [128, 512]) for i in range(ND)]
    rsv = [sb(f"rsv{i}", [128, 512]) for i in range(ND)]
    qn = [sb(f"qn{i}", [128, 512], BF16) for i in range(ND)]
    t1 = [sb(f"t1{i}", [128, 512]) for i in range(ND)]
    t2 = [sb(f"t2{i}", [128, 512]) for i in range(ND)]
    pS = (PSA("pS", [128, 512], F32), P.buf(excl=True))
    pP = [(PSA(f"pP{i}", [128, 512], F32), P.buf(excl=True)) for i in range(4)]
    pX = [(PSA(f"pX{i}", [128, 512], F32), P.buf(excl=True)) for i in range(3)]
    pN = pX
    cx = [0]
    pT = pS
    cp = [0]; cr = [0]
    all_out = []

    def out_f32(src_ps, b_ps, R, dst, eng="scalar"):
        i = cf[0] % NF; cf[0] += 1
        s, b_s = stf[i]
        if eng == "scalar":
            P.op("scalar", lambda e: e.copy(out=s[0:R, :], in_=src_ps), reads=[b_ps], writes=[b_s])
        else:
            P.op("vector", lambda e: e.tensor_copy(out=s[0:R, :], in_=src_ps), reads=[b_ps], writes=[b_s])
        P.dma("sync", d_stf[i], dst, s[0:R, :], reads=[b_s])

    def out_bf(src_ps, b_ps, R, dst, eng="vector"):
        i = cb[0] % NB; cb[0] += 1
        s, b_s = stb[i]
        if eng == "scalar":
            P.op("scalar", lambda e: e.copy(out=s[0:R, :], in_=src_ps), reads=[b_ps], writes=[b_s])
        else:
            P.op("vector", lambda e: e.tensor_copy(out=s[0:R, :], in_=src_ps), reads=[b_ps], writes=[b_s])
        P.dma("sync", d_stb[i], dst, s[0:R, :], reads=[b_s])

    def job(proj, R, gcol, onesm, rm, kcos, dim, tok, dst):
        k = cr[0] % ND; cr[0] += 1
        s_, b_s = sqv[k]; l_, b_l = lnv[k]; r_, b_r = rsv[k]; q_, b_q = qn[k]; a_, b_a = t1[k]; c_, b_c = t2[k]
        pp, b_pp = pP[cp[0] % 4]; cp[0] += 1
        ps = pp[0:R, :]
        proj(pp, b_pp)
        yield
        P.op("scalar", lambda e: e.activation(out=s_[0:R, :], in_=ps, func=AF.Square), reads=[b_pp], writes=[b_s])
        yield
        pn, b_pn = pX[cx[0] % 3]; cx[0] += 1
        P.mm(pn[0:R, :], onesm, s_[0:R, :], True, True, reads=[b_CM, b_s], writes=[b_pn])
        P.op("scalar", lambda e: e.activation(out=l_[0:R, :], in_=pn[0:R, :], func=AF.Ln, scale=1.0 / dim, bias=EPS), reads=[b_pn], writes=[b_l])
        P.op("scalar", lambda e: e.activation(out=r_[0:R, :], in_=l_[0:R, :], func=AF.Exp, scale=-0.5), reads=[b_l], writes=[b_r])
        yield
        P.op("vector", lambda e: e.scalar_tensor_tensor(out=q_[0:R, :], in0=ps, scalar=gains[0:R, gcol:gcol + 1], in1=r_[0:R, :], op0=ALU.mult, op1=ALU.mult),
             reads=[b_pp, b_gains, b_r], writes=[b_q])
        yield "late"
        pr, b_pr = pX[cx[0] % 3]; cx[0] += 1
        P.mm(pr[0:R, :], rm, q_[0:R, :], True, True, reads=[b_CM, b_q], writes=[b_pr])
        P.op("gpsimd", lambda e: e.tensor_tensor(out=a_[0:R, :], in0=q_[0:R, :], in1=rope[0:R, kcos, tok:tok + 512], op=ALU.mult),
             reads=[b_q, b_rope], writes=[b_a])
        P.op("vector", lambda e: e.tensor_tensor(out=c_[0:R, :], in0=pr[0:R, :], in1=rope[0:R, kcos + 1, tok:tok + 512], op=ALU.mult),
             reads=[b_pr, b_rope], writes=[b_c])
        i = cb[0] % NB; cb[0] += 1
        sbf, b_sb = stb[i]
        P.op("gpsimd", lambda e: e.tensor_tensor(out=sbf[0:R, :], in0=a_[0:R, :], in1=c_[0:R, :], op=ALU.add), reads=[b_a, b_c], writes=[b_sb])
        P.dma("sync", d_stb[i], dst, sbf[0:R, :], reads=[b_sb])
        yield

    def run_jobs(jobs):
        active = []
        jobs = list(jobs)
        while jobs or active:
            nxt = []
            for g in active:
                try:
                    next(g)
                    nxt.append(g)
                except StopIteration:
                    pass
            active = nxt
            if jobs:
                g = jobs.pop(0)
                next(g)
                active.append(g)

    xT_v = xT.rearrange("(kc p) n -> p kc n", p=128)
    for tt in range(NTILES):
        tok = tt * 512
        ts = slice(tok, tok + 512)
        xin, b_xin = xins[0]
        P.dma("sync", d_xins[0], xin[:], xT_v[:, :, ts], writes=[b_xin])
        for kc in range(8):
            s, bs = sq[kc % 2]
            P.op("scalar", lambda e, s=s, kc=kc, xin=xin: e.activation(out=s[:], in_=xin[:, kc, :], func=AF.Square), reads=[b_xin], writes=[bs])
            P.mm(pS[0][:], ONES, s[:], kc == 0, kc == 7, reads=[b_CM, bs], writes=[pS[1]])
        P.op("scalar", lambda e: e.activation(out=lnb[:], in_=pS[0][:], func=AF.Ln, scale=1.0 / D, bias=EPS), reads=[pS[1]], writes=[b_ln])
        P.op("scalar", lambda e: e.activation(out=rstd[:], in_=lnb[:], func=AF.Exp, scale=-0.5), reads=[b_ln], writes=[b_rstd])
        for kc in range(8):
            P.op("vector", lambda e, kc=kc, xin=xin, ts=ts: e.scalar_tensor_tensor(out=hT[:, kc, ts], in0=xin[:, kc, :], scalar=gains[:, kc:kc + 1], in1=rstd[:], op0=ALU.mult, op1=ALU.mult),
                 reads=[b_xin, b_gains, b_rstd], writes=[b_h])
    mo_jobs = []

    def load_wch(i):
        w_, bw_ = wch[i % 3]
        P.dma("gpsimd", d_wch[i % 3], w_[:], wb1[i, :, :], writes=[bw_])
    load_wch(0)
    for ch in range(NCH):
        w, bw = wch[ch % 3]
        if ch + 1 < NCH:
            load_wch(ch + 1)
        M = CH_COLS[ch][1]
        if ch >= 17:
            def mkproj(w=w, bw=bw, ts=None):
                def proj(pp, b_pp):
                    for kc in range(8):
                        P.mm(pp[0:128, :], w[:, kc * 128:kc * 128 + 128], hT[:, kc, ts], kc == 0, kc == 7, reads=[bw, b_h], writes=[b_pp])
                return proj
            for tt in range(NTILES):
                tok = tt * 512
                ts = slice(tok, tok + 512)
                dst = o_oq[ch - 17, :, ts] if ch < 21 else o_ok[ch - 21, :, ts]
                mo_jobs.append(job(mkproj(ts=ts), 128, 13 if ch < 21 else 14, BLK64, RM_MO, 2, 64.0, tok, dst))
            if ch % 2 == 0 or ch == NCH - 1:
                run_jobs(mo_jobs); mo_jobs = []
            continue
        for tt in range(NTILES):
            tok = tt * 512
            ts = slice(tok, tok + 512)
            pp, b_pp = pP[cp[0] % 4]; cp[0] += 1
            for kc in range(8):
                P.mm(pp[0:M, :], w[:, kc * 128:kc * 128 + M], hT[:, kc, ts], kc == 0, kc == 7, reads=[bw, b_h], writes=[b_pp])
            if ch < 3:
                c_, b_c = cq[ch]
                P.op("scalar", lambda e, c_=c_, pp=pp, ts=ts: e.copy(out=c_[:, ts], in_=pp[:]), reads=[b_pp], writes=[b_c])
            elif ch == 3:
                P.op("vector", lambda e, pp=pp, ts=ts: e.tensor_copy(out=krope[:, ts], in_=pp[0:32, :]), reads=[b_pp], writes=[b_krope])
            elif ch < 16:
                out_f32(pp[:], b_pp, 128, o_gr[ch - 4, :, ts], eng="scalar" if (ch + tt) % 2 else "vector")
            elif ch == 16:
                out_f32(pp[0:8, :], b_pp, 8, o_gba[:, ts])
    for tt in range(NTILES):
        ts = slice(tt * 512, (tt + 1) * 512)
        for grp, (idxs, dim, gc) in enumerate([((0, 1), 256.0, 8), ((2,), 128.0, 10)]):
            pn, b_pn = pX[cx[0] % 3]; cx[0] += 1; k = cr[0] % ND; cr[0] += 1
            for n_, ci in enumerate(idxs):
                s, bs = sq[n_ % 2]
                P.op("scalar", lambda e, s=s, ci=ci, ts=ts: e.activation(out=s[:], in_=cq[ci][0][:, ts], func=AF.Square), reads=[cq[ci][1]], writes=[bs])
                P.mm(pn[:], ONES, s[:], n_ == 0, n_ == len(idxs) - 1, reads=[b_CM, bs], writes=[b_pn])
            l_, b_l = lnv[k]; r_, b_r = rsv[k]
            P.op("scalar", lambda e, l_=l_, pn=pn, dim=dim: e.activation(out=l_[:], in_=pn[:], func=AF.Ln, scale=1.0 / dim, bias=EPS), reads=[b_pn], writes=[b_l])
            P.op("scalar", lambda e, l_=l_, r_=r_: e.activation(out=r_[:], in_=l_[:], func=AF.Exp, scale=-0.5), reads=[b_l], writes=[b_r])
            for n_, ci in enumerate(idxs):
                P.op("vector", lambda e, ci=ci, r_=r_, gc=gc, n_=n_, ts=ts: e.scalar_tensor_tensor(out=cqn[ci][0][:, ts], in0=cq[ci][0][:, ts], scalar=gains[:, gc + n_:gc + n_ + 1], in1=r_[:], op0=ALU.mult, op1=ALU.mult),
                     reads=[cq[ci][1], b_gains, b_r], writes=[cqn[ci][1]])
    mla_jobs = []
    for h in range(8):
        for tt in range(NTILES):
            tok = tt * 512
            ts = slice(tok, tok + 512)

            def projq(pp, b_pp, h=h, ts=ts):
                for kc in range(2):
                    P.mm(pp[0:96, :], wuq[:, kc * 768 + h * 96:kc * 768 + (h + 1) * 96], cqn[kc][0][:, ts], kc == 0, kc == 1, reads=[b_wuq, cqn[kc][1]], writes=[b_pp])

            def projk(pp, b_pp, h=h, ts=ts):
                P.mm(pp[0:96, :], wuk[:, h * 96:(h + 1) * 96], cqn[2][0][:, ts], True, False, reads=[b_wuk, cqn[2][1]], writes=[b_pp])
                P.mm(pp[0:96, :], SEL, krope[:, ts], False, True, reads=[b_CM, b_krope], writes=[b_pp])
            mla_jobs.append(job(projq, 96, 11, ONES[0:96, 0:96], RM_MLA, 0, 96.0, tok, o_mq[h, :, ts]))
            mla_jobs.append(job(projk, 96, 12, ONES[0:96, 0:96], RM_MLA, 0, 96.0, tok, o_mk[h, :, ts]))
    run_jobs(mla_jobs)
    for grp in range(4 * NTILES):
        gs = slice(grp * 128, (grp + 1) * 128)
        rows = gs
        P.mm(pT[0][:], cqn[2][0][:, gs], wuv[:], True, True, reads=[cqn[2][1], b_wuv], writes=[pT[1]])
        out_bf(pT[0][:], pT[1], 128, o_mv[rows, :], eng="scalar")
        for kc in range(8):
            P.mm(pT[0][:], hT[:, kc, gs], wz[:, kc * 512:(kc + 1) * 512], kc == 0, kc == 7, reads=[b_h, b_wz], writes=[pT[1]])
        out_f32(pT[0][:], pT[1], 128, o_z[rows, :], eng="vector")
        for kc in range(8):
            P.mm(pT[0][:], hT[:, kc, gs], wmv[:, kc * 512:(kc + 1) * 512], kc == 0, kc == 7, reads=[b_h, b_wmv], writes=[pT[1]])
        out_bf(pT[0][:], pT[1], 128, o_ov[rows, :], eng="scalar")
    st = P.emit(final_waits=[("sync", d) for d in d_stf + d_stb])
    return nc


def projb_consts():
    ones = np.ones((128, 128), np.float32)
    t = np.arange(128)
    blk64 = ((t[:, None] // 64) == (t[None, :] // 64)).astype(np.float32)
    rm_mla = np.zeros((128, 128), np.float32)
    for m in range(64, 80):
        rm_mla[m + 16, m] = -1.0
    for m in range(80, 96):
        rm_mla[m - 16, m] = 1.0
    rm_mo = np.zeros((128, 128), np.float32)
    for base in (0, 64):
        for m in range(base, base + 32):
            rm_mo[m + 32, m] = -1.0
        for m in range(base + 32, base + 64):
            rm_mo[m - 32, m] = 1.0
    sel = np.zeros((128, 128), np.float32)
    for i in range(32):
        sel[i, 64 + i] = 1.0
    return np.stack([ones, blk64, rm_mla, rm_mo, sel], 0)


def rope_tables(pos):
    pos = pos.astype(np.float32)
    out = np.zeros((4, 128, len(pos)), np.float32)
    out[0] = 1.0; out[2] = 1.0
    inv = (ROPE_THETA ** (-np.arange(16, dtype=np.float32) * 2.0 / 32)).astype(np.float32)
    ang = pos[None, :] * inv[:, None]
    for r in range(64, 96):
        i = (r - 64) % 16
        out[0, r] = np.cos(ang[i]); out[1, r] = np.sin(ang[i])
    out[0, 96:] = 0
    inv = (ROPE_THETA ** (-np.arange(32, dtype=np.float32) * 2.0 / 64)).astype(np.float32)
    ang = pos[None, :] * inv[:, None]
    for r in range(128):
        i = (r % 64) % 32
        out[2, r] = np.cos(ang[i]); out[3, r] = np.sin(ang[i])
    return out


def projb_weights(mix_norm, w_in, cq_norm, ckv_norm, w_uq, w_ukv, q_norm, k_norm, mq_norm, mk_norm):
    gains = np.zeros((128, 16), np.float32)
    gains[:, 0:8] = mix_norm.reshape(8, 128).T
    gains[:, 8:10] = cq_norm.reshape(2, 128).T
    gains[:, 10] = ckv_norm
    gains[0:96, 11] = q_norm; gains[0:96, 12] = k_norm
    gains[:, 13] = np.tile(mq_norm, 2); gains[:, 14] = np.tile(mk_norm, 2)
    wb1 = np.zeros((NCH, 128, 8, 128), np.float32)
    wr = w_in.reshape(8, 128, -1)
    for ch, (c0, m) in enumerate(CH_COLS):
        wb1[ch, :, :, 0:m] = wr[:, :, c0:c0 + m].transpose(1, 0, 2)
    wb1 = wb1.reshape(NCH, 128, 1024)
    wz = np.ascontiguousarray(wr[:, :, O_GZ:O_GZ + 512].transpose(1, 0, 2)).reshape(128, 4096)
    wmv = np.ascontiguousarray(wr[:, :, O_MV:O_MV + 512].transpose(1, 0, 2)).reshape(128, 4096)
    wuq = np.ascontiguousarray(w_uq.reshape(2, 128, 768).transpose(1, 0, 2)).reshape(128, 1536)
    kv = w_ukv.reshape(128, 8, 128)
    wuk = np.zeros((128, 8, 96), np.float32); wuk[:, :, 0:64] = kv[:, :, 0:64]
    wuv = np.ascontiguousarray(kv[:, :, 64:128]).reshape(128, 512)
    return {"gains": gains, "wb1": wb1, "wz": wz, "wmv": wmv, "wuq": wuq, "wuk": wuk.reshape(128, 768), "wuv": wuv, "cmat": projb_consts()}

S = 8192
NQT = 16
NDUMMY = 0
DUMMY_N = 384
HEADS = (0, 1, 2, 3)


def attn_consts():
    keys = np.arange(S)
    blkoh = (keys[None, :] // 256 == np.arange(32)[:, None]).astype(np.float32)
    p = np.arange(128)[:, None]; j = np.arange(512)[None, :]
    cmask = np.stack([np.where((128 * d + p) <= j, 0.0, -30000.0).astype(np.float32) for d in range(4)], 0)
    return blkoh, cmask, np.eye(128, dtype=np.float32), np.ones((128, 64), np.float32)


def build_attn():
    nc = bass.Bass("TRN2", target_bir_lowering=False)
    DI = lambda n, s, dt=F32: nc.dram_tensor(n, s, dt, kind="ExternalInput").ap()
    mq = DI("mq", [2, 96, S], BF16); mk = DI("mk", [2, 96, S], BF16); mv = DI("mv", [S, 128], BF16)
    oq = DI("oq", [128, S], BF16); ok = DI("ok", [128, S], BF16); ov = DI("ov", [S, 128], BF16)
    blkoh_d = DI("blkoh", [32, S]); cmask_d = DI("cmask", [4, 128, 512]); ident_d = DI("ident", [128, 128]); onesf_d = DI("onesf", [128, 64])
    oT = nc.dram_tensor("oT", [4, 64, S], BF16, kind="ExternalOutput").ap()
    P = Prog(nc)
    A = nc.alloc_sbuf_tensor
    PSA = nc.alloc_psum_tensor

    def sb(name, shape, dt=F32):
        return A("s_" + name, shape, dt), P.buf(name)

    cmask, b_cm = sb("cmask", [128, 4, 512], BF16); P.dma("gpsimd", P.dsem(), cmask[:], cmask_d.rearrange("k p n -> p k n"), writes=[b_cm])
    ident, b_id = sb("ident", [128, 128], BF16); P.dma("gpsimd", P.dsem(), ident[:], ident_d[:, :], writes=[b_id])
    onesf, b_of = sb("onesf", [128, 64]); P.dma("sync", P.dsem(), onesf[:], onesf_d[:, :], writes=[b_of])
    Ka = [sb(f"Ka{i}", [128, S], BF16) for i in range(2)]
    Qa = [sb(f"Qa{i}", [128, S], BF16) for i in range(2)]
    Va = [sb(f"Va{i}", [128, 64, 65], BF16) for i in range(2)]
    d_K = [P.dsem() for _ in range(2)]; d_Q = [P.dsem() for _ in range(2)]; d_V = [P.dsem() for _ in range(2)]
    d_K2 = [P.dsem() for _ in range(2)]
    kmf, b_kmf = sb("kmf", [128, 32]); kmT, b_kmT = sb("kmT", [128, 32], BF16)
    gm = [sb(f"gm{i}", [128, 32]) for i in range(4)]
    top8 = [sb(f"top8{i}", [128, 8]) for i in range(4)]
    sel = [sb(f"sel{i}", [128, 32]) for i in range(4)]
    PT = [sb(f"PT{i}", [128, 512], BF16) for i in range(4)]
    osb = [sb(f"osb{i}", [128, 512]) for i in range(2)]
    rec = [sb(f"rec{i}", [128, 512]) for i in range(2)]
    onb = [sb(f"onb{i}", [64, 512], BF16) for i in range(2)]; d_on = [P.dsem() for _ in range(2)]
    pSc = [(PSA(f"pSc{i}", [128, 512], F32), P.buf(excl=True)) for i in range(3)]
    pO = [(PSA(f"pO{i}", [128, 512], F32), P.buf(excl=True)) for i in range(2)]
    pBC = (PSA("pBC", [128, 512], F32), P.buf(excl=True))
    pG = (PSA("pG", [128, 512], F32), P.buf(excl=True))
    pTr = (PSA("pTr", [128, 512], F32), P.buf(excl=True))
    pD = pG

    pending = []
    negpads = [[sb(f"negpad{g}_{i}", [128, 128], BF16) for i in range(4)] for g in range(3)]
    for g in range(3):
        for i in range(4):
            P.op("vector", lambda e, g=g, i=i: e.memset(negpads[g][i][0][:], 0.0), writes=[negpads[g][i][1]])

    def prologue(n_):
        hi = HEADS[n_]
        s2 = n_ % 2
        K, b_K = Ka[s2]; Q, b_Q = Qa[s2]; V, b_V = Va[s2]
        moba = hi >= 2
        c = dict(hi=hi, K=K, b_K=b_K, Q=Q, b_Q=b_Q, V=V, b_V=b_V, moba=moba, rows=slice(0, 96))
        if not moba:
            c["scale"] = 96.0 ** -0.5
            P.dma("sync", d_K[s2], K[0:96, :], mk[hi, :, :], writes=[b_K])
            P.dma("sync", d_Q[s2], Q[0:96, :], mq[hi, :, :], writes=[b_Q])
            vsrc = mv[:, hi * 64:(hi + 1) * 64]
        else:
            c["scale"] = 0.125
            hb = hi - 2
            srows = slice(hb * 64, (hb + 1) * 64)
            P.dma("sync", d_K[s2], K[0:64, :], ok[srows, :], writes=[b_K])
            P.dma("gpsimd", d_K2[s2], K[64:96, :], blkoh_d[:, :], writes=[b_K])
            P.dma("sync", d_Q[s2], Q[0:64, :], oq[srows, :], writes=[b_Q])
            vsrc = ov[:, hb * 64:(hb + 1) * 64]
        P.dma("sync", d_V[s2], V[:, :, 0:64], vsrc.rearrange("(kc p) d -> p kc d", p=128), writes=[b_V])
        P.op("gpsimd", lambda e, V=V: e.memset(V[:, :, 64:65], 1.0), writes=[b_V])
        return c

    def topk_gen(c):
        K, b_K, Q, b_Q = c["K"], c["b_K"], c["Q"], c["b_Q"]
        krows = slice(0, 64); off = 64
        P.op("vector", lambda e: e.tensor_reduce(out=kmf[krows, :], in_=K[krows, :].rearrange("p (n k) -> p n k", k=256), axis=AX.X, op=ALU.add),
             reads=[b_K], writes=[b_kmf])
        P.op("vector", lambda e: e.tensor_scalar(out=kmT[krows, :], in0=kmf[krows, :], scalar1=1.0 / 256, scalar2=None, op0=ALU.mult),
             reads=[b_kmf], writes=[b_kmT])

        def stage_a(qt):
            for j in range(4):
                qc = qt * 4 + j
                P.mm(pG[0][:, j * 32:(j + 1) * 32], Q[krows, qc * 128:(qc + 1) * 128], kmT[krows, :], True, True, reads=[b_Q, b_kmT], writes=[pG[1]])
            for j in range(4):
                qc = qt * 4 + j; qb = qc // 2
                g_, b_g = gm[j]; t8, b_t8 = top8[j]; sl_, b_sl = sel[j]; npd, b_np = negpads[qt % 3][j]
                P.op("gpsimd", lambda e, g_=g_: e.memset(g_[:], -1e30), writes=[b_g])
                if qb > 0:
                    P.op("vector", lambda e, g_=g_, j=j, qb=qb: e.tensor_copy(out=g_[:, 0:qb], in_=pG[0][:, j * 32:j * 32 + qb]), reads=[pG[1]], writes=[b_g])
                P.op("vector", lambda e, g_=g_, t8=t8: e.max(out=t8[:], in_=g_[:]), reads=[b_g], writes=[b_t8])
                P.op("vector", lambda e, g_=g_, t8=t8, sl_=sl_: e.tensor_scalar(out=sl_[:], in0=g_[:], scalar1=t8[:, 2:3], scalar2=None, op0=ALU.is_ge),
                     reads=[b_g, b_t8], writes=[b_sl])
                P.op("vector", lambda e, sl_=sl_, npd=npd: e.tensor_scalar(out=npd[:, off:off + 32], in0=sl_[:], scalar1=-1.0, scalar2=30000.0, op0=ALU.add, op1=ALU.mult),
                     reads=[b_sl], writes=[b_np])
                P.op("vector", lambda e, npd=npd, qb=qb: e.memset(npd[:, off + qb:off + qb + 1], 0.0), writes=[b_np])

        def stage_b(qt):
            for j in range(4):
                npd, b_np = negpads[qt % 3][j]
                P.mm(pTr[0][:, j * 128:(j + 1) * 128], npd[:], ident[:], True, True, reads=[b_np, b_id], writes=[pTr[1]])
            P.op("vector", lambda e: e.tensor_copy(out=Q[off:off + 32, qt * 512:(qt + 1) * 512], in_=pTr[0][off:off + 32, :]),
                 reads=[pTr[1]], writes=[b_Q])
        for qt in range(NQT + 2):
            if qt < NQT:
                stage_a(qt)
            if qt >= 2:
                stage_b(qt - 2)
            yield

    def main_loop(c, tick):
        K, b_K, Q, b_Q, V, b_V = c["K"], c["b_K"], c["Q"], c["b_Q"], c["V"], c["b_V"]
        rows, scale, hi = c["rows"], c["scale"], c["hi"]
        for qt in range(NQT):
            tick()
            nkc = 4 * qt + 4
            qs = slice(qt * 512, (qt + 1) * 512)
            po, b_po = pO[qt % 2]

            def score(kc):
                ps, b_ps = pSc[kc % 3]
                diag = kc >= 4 * qt
                P.mm(ps[:], K[rows, kc * 128:(kc + 1) * 128], Q[rows, qs], True, not diag, reads=[b_K, b_Q], writes=[b_ps])
                if diag:
                    P.mm(ps[:], ident[:], cmask[:, kc - 4 * qt, :], False, True, reads=[b_id, b_cm], writes=[b_ps])
            score(0)
            if nkc > 1:
                score(1)
            for kc in range(nkc):
                if kc + 2 < nkc:
                    score(kc + 2)
                if kc == min(10, nkc - 1) and pending:
                    pending.pop(0)()
                ps, b_ps = pSc[kc % 3]
                pt_, b_pt = PT[kc % 4]
                P.op("scalar", lambda e, ps=ps, pt_=pt_, scale=scale: e.activation(out=pt_[:], in_=ps[:], func=AF.Exp, scale=scale), reads=[b_ps], writes=[b_pt])
                P.mm(po[0:65, :], V[:, kc, :], pt_[:], kc == 0, kc == nkc - 1, reads=[b_V, b_pt], writes=[b_po])
            o_, b_o = osb[qt % 2]; r_, b_r = rec[qt % 2]; on_, b_on = onb[qt % 2]
            P.op("vector", lambda e, o_=o_, po=po: e.tensor_copy(out=o_[0:65, :], in_=po[0:65, :]), reads=[b_po], writes=[b_o])
            P.op("vector", lambda e, o_=o_, r_=r_: e.reciprocal(out=r_[64:65, :], in_=o_[64:65, :]), reads=[b_o], writes=[b_r])

            def epilogue(o_=o_, b_o=b_o, r_=r_, b_r=b_r, on_=on_, b_on=b_on, qt=qt, qs=qs, hi=hi):
                P.mm(pBC[0][0:64, :], onesf[64:65, :], r_[64:65, :], True, True, reads=[b_of, b_r], writes=[pBC[1]])
                P.op("vector", lambda e: e.tensor_tensor(out=on_[:], in0=o_[0:64, :], in1=pBC[0][0:64, :], op=ALU.mult), reads=[b_o, pBC[1]], writes=[b_on])
                P.dma("sync", d_on[qt % 2], oT[hi, :, qs], on_[:], reads=[b_on])
            pending.append(epilogue)

    nh = len(HEADS)
    ctxs = [None] * nh
    ctxs[0] = prologue(0)
    if ctxs[0]["moba"]:
        for _ in topk_gen(ctxs[0]):
            pass
    for n_ in range(nh):
        gen = None
        if n_ + 1 < nh:
            ctxs[n_ + 1] = prologue(n_ + 1)
            if ctxs[n_ + 1]["moba"]:
                gen = topk_gen(ctxs[n_ + 1])
        state = {"g": gen}

        def tick(state=state):
            if state["g"] is not None:
                try:
                    next(state["g"])
                except StopIteration:
                    state["g"] = None
        main_loop(ctxs[n_], tick)
        while state["g"] is not None:
            tick()
    while pending:
        pending.pop(0)()
    st = P.emit(final_waits=[("sync", d) for d in d_on])
    return nc

S = 8192
NSEG = 4
SEG = 2048
NT = 16
GT = 4
EPS = 1e-6
NLV = 6


def gdn_consts():
    t = np.arange(128)
    M = (t[:, None] <= t[None, :]).astype(np.float32)
    NEGM = np.where(t[:, None] >= t[None, :], 0.0, -1e30).astype(np.float32)
    STRICT = (t[:, None] > t[None, :]).astype(np.float32)
    ident = np.eye(128, dtype=np.float32)
    ones = np.ones((128, 128), np.float32)
    NEGS = np.where(t[:, None] > t[None, :], 0.0, -1e30).astype(np.float32)
    return np.stack([M, ones, NEGM, NEGS, ident, -ones], 0)


def build_gdn():
    nc = bass.Bass("TRN2", target_bir_lowering=False)
    DI = lambda n, s, dt=F32: nc.dram_tensor(n, s, dt, kind="ExternalInput").ap()
    rq = DI("rq", [128, S]); rk = DI("rk", [128, S]); rv = DI("rv", [128, S])
    zd = DI("z", [S, 128])
    bl = DI("bl", [128, 64]); al = DI("al", [128, 64])
    cw = DI("cw", [128, 12]); sc = DI("sc", [128, 2]); gn = DI("gn", [128, 128])
    cst = DI("cst", [6, 128, 128])
    od = nc.dram_tensor("o", [S, 128], BF16, kind="ExternalOutput").ap()
    P = Prog(nc)
    A = nc.alloc_sbuf_tensor
    PS = nc.alloc_psum_tensor

    def sb(name, shape, dt=F32):
        return A("s_" + name, shape, dt), P.buf(name)

    C, b_C = sb("C", [128, 6, 128])
    cwt, b_cw = sb("cwt", [128, 12]); sct, b_sc = sb("sct", [128, 2]); gnt, b_gn = sb("gnt", [128, 128])
    blt, b_bl = sb("blt", [128, 64]); alt, b_al = sb("alt", [128, 64])
    beta, b_beta = sb("beta", [128, 64]); gg, b_gg = sb("gg", [128, 64])
    tmp64, b_tmp64 = sb("tmp64", [128, 64]); ea, b_ea = sb("ea", [128, 1])
    P.dma("sync", P.dsem(), C[:], cst.rearrange("k p n -> p k n"), writes=[b_C])
    for (t_, d_, b_) in [(cwt, cw, b_cw), (sct, sc, b_sc), (gnt, gn, b_gn), (blt, bl, b_bl), (alt, al, b_al)]:
        P.dma("sync", P.dsem(), t_[:], d_[:, :], writes=[b_])
    Mm, ONES, NEGM, STRICT, IDENT, NEGONES = [C[:, i, :] for i in range(6)]
    NEG4, b_N4 = sb("NEG4", [128, GT, 128]); STR4, b_S4 = sb("STR4", [128, GT, 128]); ID4, b_I4 = sb("ID4", [128, GT, 128])
    GN4, b_G4 = sb("GN4", [128, GT, 128])
    for t in range(GT):
        P.op("gpsimd", lambda e, t=t: e.tensor_copy(out=GN4[:, t, :], in_=gnt[:]), reads=[b_gn], writes=[b_G4])
        P.op("gpsimd", lambda e, t=t: e.tensor_copy(out=NEG4[:, t, :], in_=NEGM), reads=[b_C], writes=[b_N4])
        P.op("gpsimd", lambda e, t=t: e.tensor_copy(out=STR4[:, t, :], in_=STRICT), reads=[b_C], writes=[b_S4])
        P.op("gpsimd", lambda e, t=t: e.tensor_copy(out=ID4[:, t, :], in_=IDENT), reads=[b_C], writes=[b_I4])
    P.op("scalar", lambda e: e.activation(out=beta[:], in_=blt[:], func=AF.Sigmoid), reads=[b_bl], writes=[b_beta])
    P.op("scalar", lambda e: e.activation(out=tmp64[:], in_=alt[:], func=AF.Exp, bias=sct[:, 1:2]), reads=[b_al, b_sc], writes=[b_tmp64])
    P.op("scalar", lambda e: e.activation(out=tmp64[:], in_=tmp64[:], func=AF.Ln, bias=1.0), reads=[b_tmp64], writes=[b_tmp64])
    P.op("scalar", lambda e: e.activation(out=ea[:], in_=sct[:, 0:1], func=AF.Exp), reads=[b_sc], writes=[b_ea])
    P.op("vector", lambda e: e.tensor_scalar(out=gg[:], in0=tmp64[:], scalar1=ea[:, 0:1], scalar2=-1.0, op0=ALU.mult, op1=ALU.mult),
         reads=[b_tmp64, b_ea], writes=[b_gg])

    raw = [sb(f"raw{i}", [128, SEG + 3]) for i in range(3)]
    d_raw = [P.dsem() for _ in range(3)]
    cvs = [[sb(f"cv{s}_{i}", [128, SEG]) for i in range(3)] for s in range(2)]
    sqb, b_sq = sb("sqb", [128, 512]); lnb, b_ln = sb("lnb", [128, 512]); rsb, b_rs = sb("rsb", [128, 512])
    stat = [[sb(f"{n}{s}", [128, NT]) for n in ("gcum", "egc", "edec", "dec", "begc")] for s in range(2)]

    def g4(name, n=1):
        return [sb(f"{name}{i}", [128, GT, 128]) for i in range(n)]
    ktm = g4("ktm")[0]; vb = g4("vb")[0]; rw = g4("rw")[0]
    Gm = g4("Gm")[0]; nGm = g4("nGm")[0]; dmin = g4("dmin")[0]; Dm = g4("Dm")[0]; Dms = g4("Dms")[0]
    Am = g4("Am")[0]; Bm = g4("Bm")[0]; qkd = g4("qkd")[0]
    Qm = g4("Qm", 2); Ym = g4("Ym", 2); YTm = g4("YTm", 2)
    kdec = g4("kdec", 2); qkdT = g4("qkdT", 2); uu = g4("uu", 2); wT = g4("wT", 2)
    zt = g4("zt", 2); d_z = [P.dsem() for _ in range(2)]
    szt = g4("szt", 2)
    ofb = [sb(f"ofb{i}", [128, GT, 128], BF16) for i in range(2)]; d_o = [P.dsem() for _ in range(2)]
    NB = 2
    vnew = [sb(f"vnew{i}", [128, 128]) for i in range(NB)]
    o1 = [sb(f"o1{i}", [128, 128]) for i in range(NB)]
    ot = [sb(f"ot{i}", [128, 128]) for i in range(NB)]
    osq = [sb(f"osq{i}", [128, 128]) for i in range(NB)]
    ss = [sb(f"ss{i}", [128, 1]) for i in range(NB)]
    lss = [sb(f"lss{i}", [128, 1]) for i in range(NB)]
    rss = [sb(f"rss{i}", [128, 1]) for i in range(NB)]
    og = [sb(f"og{i}", [128, 128]) for i in range(NB)]
    St = [sb(f"St{i}", [128, 128]) for i in range(2)]
    pb = [(PS(f"pb{i}", [128, 512], F32), P.buf(f"pb{i}", excl=True)) for i in range(8)]
    pcount = [0]

    def bank():
        i = pcount[0] % 6
        pcount[0] += 1
        return pb[i]
    pV, b_pV = pb[6]
    pSt, b_pSt = pb[7]
    P.op("vector", lambda e: e.memset(St[0][0][:], 0.0), writes=[St[0][1]])
    state = {"scur": 0}

    def seg_prep(seg):
        s0 = seg * SEG
        cv = cvs[seg % 2]
        for qi, rd in enumerate((rq, rk, rv)):
            r_, b_r = raw[qi]
            if seg == 0:
                P.op("gpsimd", lambda e, r_=r_: e.memset(r_[:, 0:3], 0.0), writes=[b_r])
                P.dma("sync", d_raw[qi], r_[:, 3:], rd[:, 0:SEG], writes=[b_r])
            else:
                P.dma("sync", d_raw[qi], r_[:, :], rd[:, s0 - 3:s0 + SEG], writes=[b_r])
            c_, b_c = cv[qi]
            for hf in range(2):
                lo = hf * 1024
                sl = slice(lo, lo + 1024)
                P.op("scalar", lambda e, c_=c_, r_=r_, lo=lo, qi=qi, sl=sl: e.activation(
                    out=c_[:, sl], in_=r_[:, lo:lo + 1024], func=AF.Copy, scale=cwt[:, qi * 4:qi * 4 + 1]),
                    reads=[b_r, b_cw], writes=[b_c])
                for tap in range(1, 4):
                    P.op("vector", lambda e, c_=c_, r_=r_, lo=lo, qi=qi, sl=sl, tap=tap: e.scalar_tensor_tensor(
                        out=c_[:, sl], in0=r_[:, lo + tap:lo + tap + 1024], scalar=cwt[:, qi * 4 + tap:qi * 4 + tap + 1],
                        in1=c_[:, sl], op0=ALU.mult, op1=ALU.add), reads=[b_r, b_cw, b_c], writes=[b_c])
                P.op("scalar", lambda e, c_=c_, sl=sl: e.activation(out=c_[:, sl], in_=c_[:, sl], func=AF.Silu),
                     reads=[b_c], writes=[b_c])
        for qi in range(2):
            c_, b_c = cv[qi]
            for t4 in range(4):
                sl = slice(t4 * 512, (t4 + 1) * 512)
                pt, b_pt = bank()
                P.op("scalar", lambda e, c_=c_, sl=sl: e.activation(out=sqb[:], in_=c_[:, sl], func=AF.Square), reads=[b_c], writes=[b_sq])
                P.mm(pt[:], ONES, sqb[:], True, True, reads=[b_C, b_sq], writes=[b_pt])
                P.op("scalar", lambda e, pt=pt: e.activation(out=lnb[:], in_=pt[:], func=AF.Ln, bias=EPS), reads=[b_pt], writes=[b_ln])
                P.op("scalar", lambda e: e.activation(out=rsb[:], in_=lnb[:], func=AF.Exp, scale=-0.5), reads=[b_ln], writes=[b_rs])
                scl = (128.0 ** -0.5) if qi == 0 else 1.0
                P.op("vector", lambda e, c_=c_, sl=sl, scl=scl: e.scalar_tensor_tensor(
                    out=c_[:, sl], in0=c_[:, sl], scalar=scl, in1=rsb[:], op0=ALU.mult, op1=ALU.mult),
                    reads=[b_c, b_rs], writes=[b_c])
        (gcum, b_gcum), (egc, b_egc), (edec, b_edec), (dec, b_dec), (begc, b_begc) = stat[seg % 2]
        gsl = slice(seg * NT, (seg + 1) * NT)
        pt, b_pt = bank()
        P.mm(pt[:, 0:NT], Mm, gg[:, gsl], True, True, reads=[b_C, b_gg], writes=[b_pt])
        P.mm(pt[:, 16:16 + NT], ONES, gg[:, gsl], True, True, reads=[b_C, b_gg], writes=[b_pt])
        P.op("vector", lambda e, pt=pt: e.tensor_copy(out=gcum[:], in_=pt[:, 0:NT]), reads=[b_pt], writes=[b_gcum])
        P.op("scalar", lambda e, pt=pt: e.activation(out=egc[:], in_=pt[:, 0:NT], func=AF.Exp), reads=[b_pt], writes=[b_egc])
        P.op("vector", lambda e, pt=pt: e.tensor_tensor(out=edec[:], in0=pt[:, 16:16 + NT], in1=gcum[:], op=ALU.subtract),
             reads=[b_pt, b_gcum], writes=[b_edec])
        P.op("scalar", lambda e: e.activation(out=edec[:], in_=edec[:], func=AF.Exp), reads=[b_edec], writes=[b_edec])
        P.op("scalar", lambda e, pt=pt: e.activation(out=dec[:], in_=pt[:, 16:16 + NT], func=AF.Exp), reads=[b_pt], writes=[b_dec])
        P.op("vector", lambda e, gsl=gsl: e.tensor_tensor(out=begc[:], in0=beta[:, gsl], in1=egc[:], op=ALU.mult),
             reads=[b_beta, b_egc], writes=[b_begc])

    def prepass_stages(seg, grp):
        cv = cvs[seg % 2]
        qT_, b_qT = cv[0]; kT_, b_kT = cv[1]; vT_, b_vT = cv[2]
        (gcum, b_gcum), (egc, b_egc), (edec, b_edec), (dec, b_dec), (begc, b_begc) = stat[seg % 2]
        gi = (seg * (NT // GT) + grp) % 2
        Ts = [grp * GT + t for t in range(GT)]
        cs = lambda t: slice(Ts[t] * 128, (Ts[t] + 1) * 128)
        Gs = [seg * NT + T for T in Ts]
        kd, b_kd = kdec[gi]; qT2, b_qT2 = qkdT[gi]; u_, b_u = uu[gi]; w_, b_w = wT[gi]
        pk, b_pk = bank(); pv, b_pv = bank()
        for t in range(GT):
            P.op("tensor", lambda e, t=t: e.transpose(pk[:, t * 128:(t + 1) * 128], kT_[:, cs(t)], IDENT), reads=[b_kT, b_C], writes=[b_pk])
        for t in range(GT):
            P.op("tensor", lambda e, t=t: e.transpose(pv[:, t * 128:(t + 1) * 128], vT_[:, cs(t)], IDENT), reads=[b_vT, b_C], writes=[b_pv])
        for t in range(GT):
            P.op("vector", lambda e, t=t: e.tensor_scalar(out=vb[0][:, t, :], in0=pv[:, t * 128:(t + 1) * 128], scalar1=beta[:, Gs[t]:Gs[t] + 1], scalar2=None, op0=ALU.mult),
                 reads=[b_pv, b_beta], writes=[vb[1]])
        for t in range(GT):
            P.op("scalar", lambda e, t=t: e.activation(out=rw[0][:, t, :], in_=pk[:, t * 128:(t + 1) * 128], func=AF.Copy, scale=begc[:, Ts[t]:Ts[t] + 1]),
                 reads=[b_pk, b_begc], writes=[rw[1]])
            P.op("scalar", lambda e, t=t: e.activation(out=kd[:, t, :], in_=pk[:, t * 128:(t + 1) * 128], func=AF.Copy, scale=edec[:, Ts[t]:Ts[t] + 1]),
                 reads=[b_pk, b_edec], writes=[b_kd])
        for t in range(GT):
            P.op("vector", lambda e, t=t: e.tensor_scalar(out=Gm[0][:, t, :], in0=Mm, scalar1=gg[:, Gs[t]:Gs[t] + 1], scalar2=None, op0=ALU.mult),
                 reads=[b_C, b_gg], writes=[Gm[1]])
        yield
        pd, b_pd = bank(); pkk, b_pkk = bank(); pqk, b_pqk = bank()
        for t in range(GT):
            o = slice(t * 128, (t + 1) * 128)
            P.mm(pd[:, o], Gm[0][:, t, :], ONES, True, False, reads=[Gm[1], b_C], writes=[b_pd])
            P.mm(pd[:, o], NEGONES, Gm[0][:, t, :], False, True, reads=[Gm[1], b_C], writes=[b_pd])
        for t in range(GT):
            o = slice(t * 128, (t + 1) * 128)
            P.mm(pkk[:, o], kT_[:, cs(t)], kT_[:, cs(t)], True, True, reads=[b_kT], writes=[b_pkk])
        for t in range(GT):
            o = slice(t * 128, (t + 1) * 128)
            P.mm(pqk[:, o], qT_[:, cs(t)], kT_[:, cs(t)], True, True, reads=[b_kT, b_qT], writes=[b_pqk])
        fl = lambda x: x[:].rearrange("p t n -> p (t n)")
        P.op("vector", lambda e: e.scalar_tensor_tensor(out=fl(dmin[0]), in0=pd[:], scalar=0.0, in1=fl(NEG4), op0=ALU.min, op1=ALU.add),
             reads=[b_pd, b_N4], writes=[dmin[1]])
        P.op("scalar", lambda e: e.activation(out=fl(Dm[0]), in_=fl(dmin[0]), func=AF.Exp), reads=[dmin[1]], writes=[Dm[1]])
        P.op("vector", lambda e: e.scalar_tensor_tensor(out=fl(nGm[0]), in0=pd[:], scalar=0.0, in1=fl(STR4), op0=ALU.min, op1=ALU.add),
             reads=[b_pd, b_S4], writes=[nGm[1]])
        P.op("scalar", lambda e: e.activation(out=fl(Dms[0]), in_=fl(nGm[0]), func=AF.Exp), reads=[nGm[1]], writes=[Dms[1]])
        for t in range(GT):
            P.op("vector", lambda e, t=t: e.scalar_tensor_tensor(out=Am[0][:, t, :], in0=pkk[:, t * 128:(t + 1) * 128], scalar=beta[:, Gs[t]:Gs[t] + 1], in1=Dms[0][:, t, :], op0=ALU.mult, op1=ALU.mult),
                 reads=[b_pkk, b_beta, Dms[1]], writes=[Am[1]])
        P.op("vector", lambda e: e.tensor_tensor(out=fl(qkd[0]), in0=pqk[:], in1=fl(Dm[0]), op=ALU.mult), reads=[b_pqk, Dm[1]], writes=[qkd[1]])
        yield
        pbt, b_pbt = bank(); pqt, b_pqt = bank()
        for t in range(GT):
            P.op("tensor", lambda e, t=t: e.transpose(pbt[:, t * 128:(t + 1) * 128], Am[0][:, t, :], IDENT), reads=[Am[1], b_C], writes=[b_pbt])
        for t in range(GT):
            P.op("tensor", lambda e, t=t: e.transpose(pqt[:, t * 128:(t + 1) * 128], qkd[0][:, t, :], IDENT), reads=[qkd[1], b_C], writes=[b_pqt])
        P.op("scalar", lambda e: e.copy(out=fl(Bm[0]), in_=pbt[:]), reads=[b_pbt], writes=[Bm[1]])
        P.op("vector", lambda e: e.tensor_copy(out=fl(qT2), in_=pqt[:]), reads=[b_pqt], writes=[b_qT2])
        Qc, b_Qc = Qm[0]
        P.op("gpsimd", lambda e: e.scalar_tensor_tensor(out=fl(Qc), in0=fl(Bm[0]), scalar=-1.0, in1=fl(ID4), op0=ALU.mult, op1=ALU.add),
             reads=[Bm[1], b_I4], writes=[b_Qc]) if False else \
            P.op("vector", lambda e: e.scalar_tensor_tensor(out=fl(Qc), in0=fl(Bm[0]), scalar=-1.0, in1=fl(ID4), op0=ALU.mult, op1=ALU.add),
                 reads=[Bm[1], b_I4], writes=[b_Qc])
        yield
        Yc, b_Yc = Bm; YTc, b_YTc = Am
        for lv in range(NLV):
            pyt, b_pyt = bank()
            Yn, b_Yn = Ym[lv % 2]; YTn, b_YTn = YTm[lv % 2]
            for t in range(GT):
                P.mm(pyt[:, t * 128:(t + 1) * 128], Yc[:, t, :], YTc[:, t, :], True, True, reads=[b_Yc, b_YTc], writes=[b_pyt])
            if lv < NLV - 1:
                py, b_py = bank()
                for t in range(GT):
                    P.mm(py[:, t * 128:(t + 1) * 128], YTc[:, t, :], Yc[:, t, :], True, True, reads=[b_Yc, b_YTc], writes=[b_py])
            P.op("scalar", lambda e, YTn=YTn, pyt=pyt: e.copy(out=fl(YTn), in_=pyt[:]), reads=[b_pyt], writes=[b_YTn])
            if lv < NLV - 1:
                P.op("vector", lambda e, Yn=Yn, py=py: e.tensor_copy(out=fl(Yn), in_=py[:]), reads=[b_py], writes=[b_Yn])
            yield
            Qo, b_Qo = Qm[lv % 2]; Qn, b_Qn = Qm[(lv + 1) % 2]
            pq, b_pq = bank()
            for t in range(GT):
                P.mm(pq[:, t * 128:(t + 1) * 128], YTn[:, t, :], Qo[:, t, :], True, True, reads=[b_YTn, b_Qo], writes=[b_pq])
            P.op("vector", lambda e, Qn=Qn, Qo=Qo, pq=pq: e.tensor_tensor(out=fl(Qn), in0=pq[:], in1=fl(Qo), op=ALU.add), reads=[b_pq, b_Qo], writes=[b_Qn])
            Yc, b_Yc = Yn, b_Yn
            YTc, b_YTc = YTn, b_YTn
            yield
        Tt, b_Tt = Qm[NLV % 2]
        pu, b_pu = bank(); pw, b_pw = bank()
        for t in range(GT):
            P.mm(pu[:, t * 128:(t + 1) * 128], Tt[:, t, :], vb[0][:, t, :], True, True, reads=[b_Tt, vb[1]], writes=[b_pu])
        for t in range(GT):
            P.mm(pw[:, t * 128:(t + 1) * 128], rw[0][:, t, :], Tt[:, t, :], True, True, reads=[b_Tt, rw[1]], writes=[b_pw])
        P.op("scalar", lambda e: e.copy(out=fl(u_), in_=pu[:]), reads=[b_pu], writes=[b_u])
        P.op("vector", lambda e: e.tensor_copy(out=fl(w_), in_=pw[:]), reads=[b_pw], writes=[b_w])
        G0 = Gs[0]
        P.dma("sync", d_z[gi], zt[gi][0][:], zd[G0 * 128:(G0 + GT) * 128, :].rearrange("(t p) d -> p t d", p=128), writes=[zt[gi][1]])
        P.op("scalar", lambda e: e.activation(out=fl(szt[gi][0]), in_=fl(zt[gi][0]), func=AF.Silu), reads=[zt[gi][1]], writes=[szt[gi][1]])
        P.op("gpsimd", lambda e: e.tensor_tensor(out=fl(szt[gi][0]), in0=fl(szt[gi][0]), in1=fl(GN4), op=ALU.mult), reads=[szt[gi][1], b_G4], writes=[szt[gi][1]])
        yield

    def scan_steps(seg, grp):
        cv = cvs[seg % 2]
        qT_, b_qT = cv[0]
        (gcum, b_gcum), (egc, b_egc), (edec, b_edec), (dec, b_dec), (begc, b_begc) = stat[seg % 2]
        gi = (seg * (NT // GT) + grp) % 2
        kd, b_kd = kdec[gi]; qT2, b_qT2 = qkdT[gi]; u_, b_u = uu[gi]; w_, b_w = wT[gi]
        for t in range(GT):
            T = grp * GT + t
            G = seg * NT + T
            i2 = G % NB
            cs = slice(T * 128, (T + 1) * 128)
            Sc, b_Sc = St[state["scur"]]; Sn, b_Sn = St[1 - state["scur"]]
            P.mm(pV[:, 0:128], w_[:, t, :], Sc[:], True, True, reads=[b_w, b_Sc], writes=[b_pV])
            P.mm(pV[:, 128:256], qT_[:, cs], Sc[:], True, True, reads=[b_qT, b_Sc], writes=[b_pV])
            P.op("vector", lambda e, i2=i2, t=t: e.tensor_tensor(out=vnew[i2][0][:], in0=u_[:, t, :], in1=pV[:, 0:128], op=ALU.subtract),
                 reads=[b_u, b_pV], writes=[vnew[i2][1]])
            P.op("scalar", lambda e, i2=i2, T=T: e.activation(out=o1[i2][0][:], in_=pV[:, 128:256], func=AF.Copy, scale=egc[:, T:T + 1]),
                 reads=[b_pV, b_egc], writes=[o1[i2][1]])
            P.mm(pSt[:, 0:128], kd[:, t, :], vnew[i2][0][:], True, True, reads=[b_kd, vnew[i2][1]], writes=[b_pSt])
            P.mm(pSt[:, 128:256], qT2[:, t, :], vnew[i2][0][:], True, True, reads=[b_qT2, vnew[i2][1]], writes=[b_pSt])
            P.op("vector", lambda e, Sn=Sn, Sc=Sc, T=T: e.scalar_tensor_tensor(out=Sn[:], in0=Sc[:], scalar=dec[:, T:T + 1], in1=pSt[:, 0:128], op0=ALU.mult, op1=ALU.add),
                 reads=[b_Sc, b_dec, b_pSt], writes=[b_Sn])
            P.op("vector", lambda e, i2=i2: e.tensor_tensor(out=ot[i2][0][:], in0=o1[i2][0][:], in1=pSt[:, 128:256], op=ALU.add),
                 reads=[o1[i2][1], b_pSt], writes=[ot[i2][1]])
            state["scur"] = 1 - state["scur"]
            P.op("scalar", lambda e, i2=i2: e.activation(out=osq[i2][0][:], in_=ot[i2][0][:], func=AF.Square, accum_out=ss[i2][0][:]),
                 reads=[ot[i2][1]], writes=[osq[i2][1], ss[i2][1]])
            P.op("scalar", lambda e, i2=i2: e.activation(out=lss[i2][0][:], in_=ss[i2][0][:], func=AF.Ln, scale=1.0 / 128, bias=EPS),
                 reads=[ss[i2][1]], writes=[lss[i2][1]])
            P.op("scalar", lambda e, i2=i2: e.activation(out=rss[i2][0][:], in_=lss[i2][0][:], func=AF.Exp, scale=-0.5),
                 reads=[lss[i2][1]], writes=[rss[i2][1]])
            P.op("vector", lambda e, i2=i2, t=t: e.scalar_tensor_tensor(out=ofb[gi][0][:, t, :], in0=ot[i2][0][:], scalar=rss[i2][0][:, 0:1], in1=szt[gi][0][:, t, :], op0=ALU.mult, op1=ALU.mult),
                 reads=[ot[i2][1], rss[i2][1], szt[gi][1]], writes=[ofb[gi][1]])
            yield
        G0 = seg * NT + grp * GT
        P.dma("sync", d_o[gi], od[G0 * 128:(G0 + GT) * 128, :].rearrange("(t p) d -> p t d", p=128), ofb[gi][0][:], reads=[ofb[gi][1]])

    groups = [(seg, grp) for seg in range(NSEG) for grp in range(NT // GT)]
    prev_scan = None
    for n_, (seg, grp) in enumerate(groups):
        if grp == 0:
            seg_prep(seg)
        pre = prepass_stages(seg, grp)
        done_pre = False
        rounds = 0
        while True:
            try:
                next(pre)
            except StopIteration:
                break
            rounds += 1
            if prev_scan is not None and rounds % 3 == 0:
                try:
                    next(prev_scan)
                except StopIteration:
                    prev_scan = None
        if prev_scan is not None:
            for _ in prev_scan:
                pass
        prev_scan = scan_steps(seg, grp)
    for _ in prev_scan:
        pass
    st = P.emit(final_waits=[("sync", d) for d in d_o])
    return nc

T = 2048
D = 1024
EPS = 1e-6
NTILES = 4
O_GATE = 4008


def build_merge():
    nc = bass.Bass("TRN2", target_bir_lowering=False)
    DI = lambda n, s, dt=F32: nc.dram_tensor(n, s, dt, kind="ExternalInput").ap()
    xT = DI("xT", [D, T]); brT = DI("brT", [3, 512, T], BF16); gd = DI("g", [128, 8])
    wgb = DI("wgb", [24, 128, 1536]); wo = DI("wo", [8, 128, 1024]); onesd = DI("ones", [128, 128])
    yT = nc.dram_tensor("yT", [D, T], F32, kind="ExternalOutput").ap()
    P = Prog(nc)
    A = nc.alloc_sbuf_tensor
    PSA = nc.alloc_psum_tensor

    def sb(name, shape, dt=F32):
        return A("s_" + name, shape, dt), P.buf(name)
    ones, b_ones = sb("ones", [128, 128], BF16); P.dma("gpsimd", P.dsem(), ones[:], onesd[:, :], writes=[b_ones])
    g, b_g = sb("g", [128, 8]); P.dma("sync", P.dsem(), g[:], gd[:, :], writes=[b_g])
    xin = [sb(f"xin{i}", [128, 8, 512]) for i in range(2)]; d_xin = [P.dsem() for _ in range(2)]
    br, b_br = sb("br", [128, 3, 4, T], BF16); d_br = P.dsem()
    sq = [sb(f"sq{i}", [128, 512], BF16) for i in range(2)]
    lnb, b_ln = sb("lnb", [128, 512]); rstd, b_rstd = sb("rstd", [128, 512])
    hT = sb("hT", [128, 8, T], BF16); b_hs = [P.buf() for _ in range(NTILES)]
    wc = [sb(f"wc{i}", [128, 1536], BF16) for i in range(3)]; d_wc = [P.dsem() for _ in range(3)]
    woc = [sb(f"woc{i}", [128, 1024], BF16) for i in range(2)]; d_wo = [P.dsem() for _ in range(2)]
    sig = [sb(f"sig{i}", [128, 512]) for i in range(2)]
    acc = [sb(f"acc{i}", [128, 512]) for i in range(NTILES)]
    tmp = [sb(f"tmp{i}", [128, 512]) for i in range(2)]
    mixed = sb("mixed", [128, 8, T], BF16); b_mxs = [P.buf() for _ in range(NTILES)]
    xres = [sb(f"xres{i}", [128, 512]) for i in range(2)]; d_xres = [P.dsem() for _ in range(2)]
    yo = [sb(f"yo{i}", [128, 512]) for i in range(2)]; d_yo = [P.dsem() for _ in range(2)]
    pS = (PSA("pS", [128, 512], F32), P.buf(excl=True))
    pG = [(PSA(f"pG{i}", [128, 512], F32), P.buf(excl=True)) for i in range(2)]
    pU = [(PSA(f"pU{i}", [128, 512], F32), P.buf(excl=True)) for i in range(2)]
    pO = [(PSA(f"pO{i}", [128, 512], F32), P.buf(excl=True)) for i in range(2)]
    xT_v = xT.rearrange("(kc p) n -> p kc n", p=128)
    hT_, mixed_ = hT[0], mixed[0]
    P.dma("sync", d_br, br[:, :, :, 0:NTILES * 512], brT[:, :, 0:NTILES * 512].rearrange("n (kc p) t -> p n kc t", p=128), writes=[b_br])
    for tt in range(NTILES):
        ts = slice(tt * 512, (tt + 1) * 512)
        xi, b_xi = xin[tt % 2]
        P.dma("sync", d_xin[tt % 2], xi[:], xT_v[:, :, ts], writes=[b_xi])
        for kc in range(8):
            s, bs = sq[kc % 2]
            P.op("scalar", lambda e, s=s, kc=kc, xi=xi: e.activation(out=s[:], in_=xi[:, kc, :], func=AF.Square), reads=[b_xi], writes=[bs])
            P.mm(pS[0][:], ones[:], s[:], kc == 0, kc == 7, reads=[b_ones, bs], writes=[pS[1]])
        P.op("scalar", lambda e: e.activation(out=lnb[:], in_=pS[0][:], func=AF.Ln, scale=1.0 / D, bias=EPS), reads=[pS[1]], writes=[b_ln])
        P.op("scalar", lambda e: e.activation(out=rstd[:], in_=lnb[:], func=AF.Exp, scale=-0.5), reads=[b_ln], writes=[b_rstd])
        for kc in range(8):
            P.op("vector", lambda e, kc=kc, xi=xi, ts=ts: e.scalar_tensor_tensor(out=hT_[:, kc, ts], in0=xi[:, kc, :], scalar=g[:, kc:kc + 1], in1=rstd[:], op0=ALU.mult, op1=ALU.mult),
                 reads=[b_xi, b_g, b_rstd], writes=[b_hs[tt]])
    cnt = 0; c2n = 0

    def load_w(i):
        w_, bw_ = wc[i % 3]
        P.dma("gpsimd", d_wc[i % 3], w_[:], wgb[i, :, :], writes=[bw_])
    load_w(0)
    for c in range(8):
        for n in range(3):
            w, bw = wc[cnt % 3]
            if cnt + 1 < 24:
                load_w(cnt + 1)
            cnt += 1
            for tt in range(NTILES):
                ts = slice(tt * 512, (tt + 1) * 512)
                a_, b_a = acc[tt]
                pg, b_pg = pG[c2n % 2]; pu, b_pu = pU[c2n % 2]; sg, b_sg = sig[c2n % 2]; tm, b_tm = tmp[c2n % 2]
                c2n += 1
                for kc in range(8):
                    P.mm(pg[:], w[:, kc * 128:(kc + 1) * 128], hT_[:, kc, ts], kc == 0, kc == 7, reads=[bw, b_hs[tt]], writes=[b_pg])
                for kc in range(4):
                    P.mm(pu[:], w[:, 1024 + kc * 128:1024 + (kc + 1) * 128], br[:, n, kc, ts], kc == 0, kc == 3, reads=[bw, b_br], writes=[b_pu])
                P.op("scalar", lambda e, sg=sg, pg=pg: e.activation(out=sg[:], in_=pg[:], func=AF.Sigmoid), reads=[b_pg], writes=[b_sg])
                if n == 0:
                    P.op("vector", lambda e, a_=a_, sg=sg, pu=pu: e.tensor_tensor(out=a_[:], in0=sg[:], in1=pu[:], op=ALU.mult), reads=[b_sg, b_pu], writes=[b_a])
                else:
                    P.op("vector", lambda e, tm=tm, sg=sg, pu=pu: e.tensor_tensor(out=tm[:], in0=sg[:], in1=pu[:], op=ALU.mult), reads=[b_sg, b_pu], writes=[b_tm])
                    if n == 1:
                        P.op("gpsimd", lambda e, a_=a_, tm=tm: e.tensor_tensor(out=a_[:], in0=a_[:], in1=tm[:], op=ALU.add), reads=[b_a, b_tm], writes=[b_a])
                    else:
                        P.op("gpsimd", lambda e, a_=a_, tm=tm, c=c, ts=ts: e.tensor_tensor(out=mixed_[:, c, ts], in0=a_[:], in1=tm[:], op=ALU.add), reads=[b_a, b_tm], writes=[b_mxs[tt]])
    k2 = 0
    P.dma("gpsimd", d_wo[0], woc[0][0][:], wo[0, :, :], writes=[woc[0][1]])
    for c2 in range(8):
        w, bw = woc[c2 % 2]
        if c2 + 1 < 8:
            P.dma("gpsimd", d_wo[(c2 + 1) % 2], woc[(c2 + 1) % 2][0][:], wo[c2 + 1, :, :], writes=[woc[(c2 + 1) % 2][1]])
        for tt in range(NTILES):
            ts = slice(tt * 512, (tt + 1) * 512)
            po, b_po = pO[k2 % 2]; y_, b_y = yo[k2 % 2]; xr, b_xr = xres[k2 % 2]
            P.dma("sync", d_xres[k2 % 2], xr[:], xT[c2 * 128:(c2 + 1) * 128, ts], writes=[b_xr])
            for kc in range(8):
                P.mm(po[:], w[:, kc * 128:(kc + 1) * 128], mixed_[:, kc, ts], kc == 0, kc == 7, reads=[bw, b_mxs[tt]], writes=[b_po])
            P.op("vector", lambda e, y_=y_, po=po, xr=xr: e.tensor_tensor(out=y_[:], in0=po[:], in1=xr[:], op=ALU.add), reads=[b_po, b_xr], writes=[b_y])
            P.dma("sync", d_yo[k2 % 2], yT[c2 * 128:(c2 + 1) * 128, ts], y_[:], reads=[b_y])
            k2 += 1
    st = P.emit(final_waits=[("sync", d) for d in d_yo])
    return nc


def merge_weights(mix_norm, w_in, w_branch, w_out):
    g = np.ascontiguousarray(mix_norm.reshape(8, 128).T)
    wr = w_in.reshape(8, 128, -1)
    wgb = np.zeros((8, 3, 128, 1536), np.float32)
    for c in range(8):
        for n in range(3):
            c0 = O_GATE + n * 1024 + c * 128
            wgb[c, n, :, 0:1024] = wr[:, :, c0:c0 + 128].transpose(1, 0, 2).reshape(128, 1024)
            wgb[c, n, :, 1024:1536] = w_branch[n].reshape(4, 128, 1024)[:, :, c * 128:(c + 1) * 128].transpose(1, 0, 2).reshape(128, 512)
    wo = np.ascontiguousarray(w_out.reshape(8, 128, 8, 128).transpose(2, 1, 0, 3)).reshape(8, 128, 1024)
    return {"g": g, "wgb": wgb.reshape(24, 128, 1536), "wo": wo, "ones": np.ones((128, 128), np.float32)}

_PROGS = {}


def _prog(name, fn):
    if name not in _PROGS:
        _PROGS[name] = fn()
    return _PROGS[name]


def _run(nc, maps):
    res = run_bass_kernel_spmd(nc, maps, core_ids=list(range(8)))
    return res.results


def _ffn_launch(xT_cores, norm, w_in, w_out):
    g = np.ascontiguousarray(norm.reshape(8, 128).T)
    wi = w_in.reshape(8, 128, 2, NJ, 128)
    w1 = np.ascontiguousarray(wi.transpose(3, 1, 2, 0, 4)).reshape(NJ, 128, 2048)
    wo = w_out.reshape(NJ, 128, 8, 128)
    w2 = np.ascontiguousarray(wo.transpose(2, 1, 0, 3)).reshape(8, 128, NJ * 128)
    ones = np.ones((128, 128), np.float32)
    maps = [{"xT": xT_cores[c], "g": g, "w1": w1, "w2": w2, "ones": ones} for c in range(8)]
    r = _run(_prog("ffn", build_ffn), maps)
    return [np.ascontiguousarray(r[c]["yT"]) for c in range(8)]


def kernel(x, ffa_norm, ffa_w_in, ffa_w_out, mix_norm, w_in, mla_cq_norm, mla_ckv_norm,
           mla_w_uq, mla_w_ukv, mla_q_norm, mla_k_norm, gdn_conv, gdn_a_log, gdn_dt_bias,
           gdn_out_norm, moba_q_norm, moba_k_norm, w_branch, w_out, ffb_norm, ffb_w_in,
           ffb_w_out):
    f = lambda a: np.asarray(a, dtype=np.float32)
    x = f(x)
    B_, S_, D_ = x.shape
    xf = x.reshape(B_ * S_, D_)
    xT = [np.ascontiguousarray(xf[c * T:(c + 1) * T].T) for c in range(8)]
    blkoh, cmask, ident, onesf = attn_consts()
    gcst = gdn_consts()
    for l in range(2):
        xT = _ffn_launch(xT, f(ffa_norm)[l], f(ffa_w_in)[l], f(ffa_w_out)[l])
        W = projb_weights(f(mix_norm)[l], f(w_in)[l], f(mla_cq_norm)[l], f(mla_ckv_norm)[l], f(mla_w_uq)[l],
                          f(mla_w_ukv)[l], f(mla_q_norm)[l], f(mla_k_norm)[l], f(moba_q_norm)[l], f(moba_k_norm)[l])
        maps = []
        for c in range(8):
            m = dict(W)
            m["xT"] = xT[c]
            j = c % 4
            m["rope"] = rope_tables(np.arange(j * T, (j + 1) * T))
            maps.append(m)
        rb = _run(_prog("projb", build_projb), maps)

        def gather(name, b, axis):
            return np.concatenate([rb[b * 4 + j][name] for j in range(4)], axis=axis)
        full = []
        for b in range(2):
            full.append({"mla_qT": gather("mla_qT", b, 2), "mla_kT": gather("mla_kT", b, 2), "mla_v": gather("mla_v", b, 0),
                         "mo_qT": gather("mo_qT", b, 2), "mo_kT": gather("mo_kT", b, 2), "mo_v": gather("mo_v", b, 0),
                         "graw": gather("graw", b, 2), "gba": gather("gba", b, 1), "z": gather("z", b, 0)})
        maps = []
        for c in range(8):
            b, hp = c // 4, c % 4
            F = full[b]
            maps.append({"mq": np.ascontiguousarray(F["mla_qT"][2 * hp:2 * hp + 2]), "mk": np.ascontiguousarray(F["mla_kT"][2 * hp:2 * hp + 2]),
                         "mv": np.ascontiguousarray(F["mla_v"][:, hp * 128:(hp + 1) * 128]),
                         "oq": np.ascontiguousarray(F["mo_qT"][hp]), "ok": np.ascontiguousarray(F["mo_kT"][hp]),
                         "ov": np.ascontiguousarray(F["mo_v"][:, hp * 128:(hp + 1) * 128]),
                         "blkoh": blkoh, "cmask": cmask, "ident": ident, "onesf": onesf})
        ra = _run(_prog("attn", build_attn), maps)
        maps = []
        cw_l = f(gdn_conv)[l]
        for c in range(8):
            b, hd = c // 4, c % 4
            F = full[b]
            cw = np.concatenate([cw_l[:, k0 + hd * 128:k0 + (hd + 1) * 128].T for k0 in (0, 512, 1024)], 1)
            sc = np.stack([np.full(128, f(gdn_a_log)[l][hd], np.float32), np.full(128, f(gdn_dt_bias)[l][hd], np.float32)], 1)
            maps.append({"rq": np.ascontiguousarray(F["graw"][hd]), "rk": np.ascontiguousarray(F["graw"][4 + hd]),
                         "rv": np.ascontiguousarray(F["graw"][8 + hd]),
                         "z": np.ascontiguousarray(F["z"][:, hd * 128:(hd + 1) * 128]),
                         "bl": np.ascontiguousarray(F["gba"][hd].reshape(64, 128).T), "al": np.ascontiguousarray(F["gba"][4 + hd].reshape(64, 128).T),
                         "cw": np.ascontiguousarray(cw), "sc": sc,
                         "gn": np.ascontiguousarray(np.broadcast_to(f(gdn_out_norm)[l][None, :], (128, 128))),
                         "cst": gcst})
        rg = _run(_prog("gdn", build_gdn), maps)
        Wm = merge_weights(f(mix_norm)[l], f(w_in)[l], f(w_branch)[l], f(w_out)[l])
        brT = []
        for b in range(2):
            o_mla = np.concatenate([ra[b * 4 + hp]["oT"][i] for hp in range(4) for i in range(2)], 0)
            o_mo = np.concatenate([ra[b * 4 + hp]["oT"][2 + i] for hp in range(4) for i in range(2)], 0)
            o_gdn = np.concatenate([rg[b * 4 + hd]["o"].T for hd in range(4)], 0)
            brT.append(np.stack([o_mla, o_gdn, o_mo], 0))
        maps = []
        for c in range(8):
            b, j = c // 4, c % 4
            m = dict(Wm)
            m["xT"] = xT[c]
            m["brT"] = np.ascontiguousarray(brT[b][:, :, j * T:(j + 1) * T])
            maps.append(m)
        rm = _run(_prog("merge", build_merge), maps)
        xT = [np.ascontiguousarray(rm[c]["yT"]) for c in range(8)]
        xT = _ffn_launch(xT, f(ffb_norm)[l], f(ffb_w_in)[l], f(ffb_w_out)[l])
    out = np.concatenate([xT[c].T for c in range(8)], 0).reshape(B_, S_, D_)
    return np.ascontiguousarray(out.astype(np.float32))
```

```python
import ml_dtypes
from concourse.bass_utils import run_bass_kernel_spmd


import numpy as np
import concourse.bass as bass
import concourse.mybir as mybir

F32 = mybir.dt.float32
BF16 = mybir.dt.bfloat16
AF = mybir.ActivationFunctionType
ALU = mybir.AluOpType
AX = mybir.AxisListType

ENGS = ("tensor", "vector", "scalar", "gpsimd", "sync")


class Buf:
    __slots__ = ("name", "last_w", "readers", "excl")

    def __init__(self, name, excl=False):
        self.name = name
        self.last_w = None
        self.readers = []
        self.excl = excl


class Op:
    __slots__ = ("eng", "fn", "idx", "deps", "signal", "dsem", "dord", "count")

    def __init__(self, eng, fn, idx):
        self.eng = eng
        self.fn = fn
        self.idx = idx
        self.deps = []
        self.signal = False
        self.dsem = None
        self.dord = 0
        self.count = 0


class DSem:
    def __init__(self, name):
        self.name = name
        self.n = 0
        self.handle = None


class Prog:
    def __init__(self, nc):
        self.nc = nc
        self.ops = {e: [] for e in ENGS}
        self.dsems = []
        self.nbuf = 0

    def buf(self, name=None, excl=False):
        self.nbuf += 1
        return Buf(name or f"b{self.nbuf}", excl)

    def dsem(self, name=None):
        d = DSem(name or f"d{len(self.dsems)}")
        self.dsems.append(d)
        return d

    def _deps(self, op, reads, writes):
        deps = op.deps
        for b in reads:
            if b.excl:
                writes = list(writes) + [b]
                continue
            if b.last_w is not None:
                deps.append(b.last_w)
            b.readers.append(op)
        for b in writes:
            if b.last_w is not None:
                deps.append(b.last_w)
            deps.extend(r for r in b.readers if r is not op)
            b.readers = []
            b.last_w = op

    def op(self, eng, fn, reads=(), writes=()):
        o = Op(eng, fn, len(self.ops[eng]))
        self.ops[eng].append(o)
        self._deps(o, reads, writes)
        return o

    def dma(self, eng, dsem, out, in_, reads=(), writes=()):
        o = Op(eng, ("dma", out, in_), len(self.ops[eng]))
        dsem.n += 1
        o.dsem = dsem
        o.dord = dsem.n
        self.ops[eng].append(o)
        self._deps(o, reads, writes)
        return o

    def mm(self, out, lhsT, rhs, start, stop, reads=(), writes=()):
        return self.op("tensor", lambda e: e.matmul(out, lhsT, rhs, start=start, stop=stop),
                       reads, writes)

    def emit(self, final_waits=()):
        nc = self.nc
        for e in ENGS:
            for o in self.ops[e]:
                for d in o.deps:
                    if d.dsem is None:
                        if d.eng == "tensor" and o.eng == "tensor":
                            continue
                        d.signal = True
        esem = {e: nc.alloc_semaphore(f"sem_{e}") for e in ENGS}
        for d in self.dsems:
            if d.n:
                d.handle = nc.alloc_semaphore(f"dsem_{d.name}")
        for e in ENGS:
            c = 0
            for o in self.ops[e]:
                if o.dsem is None and o.signal:
                    c += 1
                    o.count = c
        stats = {}
        with nc.Block() as block:
            def run(ename, eng):
                waited = {}
                nwait = 0
                for o in self.ops[ename]:
                    need = {}
                    for d in o.deps:
                        if d.dsem is not None:
                            key = ("d", id(d.dsem)); sem = d.dsem.handle; val = 16 * d.dord
                        else:
                            if d.eng == "tensor" and ename == "tensor":
                                continue
                            key = ("e", d.eng); sem = esem[d.eng]; val = d.count
                        if need.get(key, (None, -1))[1] < val:
                            need[key] = (sem, val)
                    for key, (sem, val) in need.items():
                        if waited.get(key, -1) >= val:
                            continue
                        eng.wait_ge(sem, val)
                        waited[key] = val
                        nwait += 1
                    if o.dsem is not None:
                        _, out, in_ = o.fn
                        eng.dma_start(out=out, in_=in_).then_inc(o.dsem.handle, 16)
                    else:
                        ins = o.fn(eng)
                        if o.signal:
                            ins.then_inc(esem[ename], 1)
                for (kind, obj) in final_waits:
                    if ename != kind:
                        continue
                    eng.wait_ge(obj.handle, 16 * obj.n)
                stats[ename] = (len(self.ops[ename]), nwait)

            @block.tensor
            def _(eng):
                run("tensor", eng)

            @block.vector
            def _(eng):
                run("vector", eng)

            @block.scalar
            def _(eng):
                run("scalar", eng)

            @block.gpsimd
            def _(eng):
                run("gpsimd", eng)

            @block.sync
            def _(eng):
                run("sync", eng)
        return stats


T = 2048
D = 1024
DFF = 2816
NJ = DFF // 128
EPS = 1e-6


def build_ffn():
    nc = bass.Bass("TRN2", target_bir_lowering=False)
    xT = nc.dram_tensor("xT", [D, T], F32, kind="ExternalInput").ap()
    gd = nc.dram_tensor("g", [128, 8], F32, kind="ExternalInput").ap()
    w1d = nc.dram_tensor("w1", [NJ, 128, 2048], F32, kind="ExternalInput").ap()
    w2d = nc.dram_tensor("w2", [8, 128, NJ * 128], F32, kind="ExternalInput").ap()
    onesd = nc.dram_tensor("ones", [128, 128], F32, kind="ExternalInput").ap()
    yT = nc.dram_tensor("yT", [D, T], F32, kind="ExternalOutput").ap()
    P = Prog(nc)
    A = nc.alloc_sbuf_tensor
    ones = A("ones_sb", [128, 128], BF16); b_ones = P.buf()
    g = A("g_sb", [128, 8], F32); b_g = P.buf()
    xin = [A(f"xin{i}", [128, 8, 512], F32) for i in range(2)]; b_xin = [P.buf() for _ in range(2)]
    sq = [A(f"sq{i}", [128, 512], BF16) for i in range(2)]; b_sq = [P.buf() for _ in range(2)]
    lnb = A("lnb", [128, 512], F32); b_ln = P.buf()
    rstd = A("rstd", [128, 512], F32); b_rstd = P.buf()
    hT = A("hT", [128, 8, 1024], BF16); b_h = [P.buf() for _ in range(2)]
    actT = A("actT", [128, NJ, 1024], BF16); b_act = [P.buf() for _ in range(2)]
    w1 = [A(f"w1_{i}", [128, 2048], BF16) for i in range(2)]; b_w1 = [P.buf() for _ in range(2)]
    w2 = [A(f"w2_{i}", [128, NJ * 128], BF16) for i in range(2)]; b_w2 = [P.buf() for _ in range(2)]
    sg = [A(f"sg{i}", [128, 512], F32) for i in range(2)]; b_sg = [P.buf() for _ in range(2)]
    xres = [A(f"xres{i}", [128, 512], F32) for i in range(2)]; b_xres = [P.buf() for _ in range(2)]
    yo = [A(f"yo{i}", [128, 512], F32) for i in range(2)]; b_yo = [P.buf() for _ in range(2)]
    PS = nc.alloc_psum_tensor
    pS = PS("pS", [128, 512], F32); b_pS = P.buf(excl=True)
    pG = [PS(f"pG{i}", [128, 512], F32) for i in range(2)]; b_pG = [P.buf(excl=True) for _ in range(2)]
    pU = [PS(f"pU{i}", [128, 512], F32) for i in range(2)]; b_pU = [P.buf(excl=True) for _ in range(2)]
    pO = [PS(f"pO{i}", [128, 512], F32) for i in range(2)]; b_pO = [P.buf(excl=True) for _ in range(2)]
    d_c = P.dsem(); d_g = P.dsem()
    d_xin = [P.dsem() for _ in range(2)]
    d_w1 = [P.dsem() for _ in range(2)]; d_w2 = [P.dsem() for _ in range(2)]
    d_xres = [P.dsem() for _ in range(2)]; d_out = [P.dsem() for _ in range(2)]
    b_ydram = P.buf()

    P.dma("gpsimd", d_c, ones[:], onesd[:, :], writes=[b_ones])
    P.dma("sync", d_g, g[:], gd[:, :], writes=[b_g])
    xT_v = xT.rearrange("(kc p) n -> p kc n", p=128)
    for hh in range(2):
        t0 = hh * 1024
        for tt in range(2):
            tok = t0 + tt * 512
            xi = xin[tt]; bxi = b_xin[tt]
            P.dma("sync", d_xin[tt], xi[:], xT_v[:, :, tok:tok + 512], writes=[bxi])
            for kc in range(8):
                s = sq[kc % 2]; bs = b_sq[kc % 2]
                P.op("scalar", lambda e, s=s, xi=xi, kc=kc: e.activation(out=s[:], in_=xi[:, kc, :], func=AF.Square),
                     reads=[bxi], writes=[bs])
                P.mm(pS[:], ones[:], s[:], kc == 0, kc == 7, reads=[b_ones, bs], writes=[b_pS])
            P.op("scalar", lambda e: e.activation(out=lnb[:], in_=pS[:], func=AF.Ln, scale=1.0 / D, bias=EPS),
                 reads=[b_pS], writes=[b_ln])
            P.op("scalar", lambda e: e.activation(out=rstd[:], in_=lnb[:], func=AF.Exp, scale=-0.5),
                 reads=[b_ln], writes=[b_rstd])
            for kc in range(8):
                P.op("vector", lambda e, xi=xi, kc=kc, tt=tt: e.scalar_tensor_tensor(
                    out=hT[:, kc, tt * 512:(tt + 1) * 512], in0=xi[:, kc, :], scalar=g[:, kc:kc + 1], in1=rstd[:],
                    op0=ALU.mult, op1=ALU.mult), reads=[bxi, b_g, b_rstd], writes=[b_h[tt]])
        for j in range(NJ):
            w = w1[j % 2]; bw = b_w1[j % 2]
            P.dma("gpsimd", d_w1[j % 2], w[:], w1d[j, :, :], writes=[bw])
            for tt in range(2):
                hs = lambda kc, tt=tt: hT[:, kc, tt * 512:(tt + 1) * 512]
                for kc in range(8):
                    P.mm(pG[tt][:], w[:, kc * 128:(kc + 1) * 128], hs(kc), kc == 0, kc == 7,
                         reads=[bw, b_h[tt]], writes=[b_pG[tt]])
                for kc in range(8):
                    P.mm(pU[tt][:], w[:, 1024 + kc * 128:1024 + (kc + 1) * 128], hs(kc), kc == 0, kc == 7,
                         reads=[bw, b_h[tt]], writes=[b_pU[tt]])
                P.op("scalar", lambda e, tt=tt: e.activation(out=sg[tt][:], in_=pG[tt][:], func=AF.Silu),
                     reads=[b_pG[tt]], writes=[b_sg[tt]])
                P.op("vector", lambda e, tt=tt, j=j: e.tensor_tensor(
                    out=actT[:, j, tt * 512:(tt + 1) * 512], in0=sg[tt][:], in1=pU[tt][:], op=ALU.mult),
                    reads=[b_sg[tt], b_pU[tt]], writes=[b_act[tt]])
        for c in range(8):
            w = w2[c % 2]; bw = b_w2[c % 2]
            P.dma("gpsimd", d_w2[c % 2], w[:], w2d[c, :, :], writes=[bw])
            for tt in range(2):
                tok = t0 + tt * 512
                for j in range(NJ):
                    P.mm(pO[tt][:], w[:, j * 128:(j + 1) * 128], actT[:, j, tt * 512:(tt + 1) * 512], j == 0, j == NJ - 1,
                         reads=[bw, b_act[tt]], writes=[b_pO[tt]])
                P.dma("sync", d_xres[tt], xres[tt][:], xT[c * 128:(c + 1) * 128, tok:tok + 512], writes=[b_xres[tt]])
                P.op("vector", lambda e, tt=tt: e.scalar_tensor_tensor(
                    out=yo[tt][:], in0=pO[tt][:], scalar=0.5, in1=xres[tt][:], op0=ALU.mult, op1=ALU.add),
                    reads=[b_pO[tt], b_xres[tt]], writes=[b_yo[tt]])
                P.dma("sync", d_out[tt], yT[c * 128:(c + 1) * 128, tok:tok + 512], yo[tt][:], reads=[b_yo[tt]])
    st = P.emit(final_waits=[("sync", d_out[0]), ("sync", d_out[1])])
    return nc


def ffn_host_inputs(x_flat, norm, w_in, w_out):
    g = np.ascontiguousarray(norm.reshape(8, 128).T)
    wi = w_in.reshape(8, 128, 2, NJ, 128)
    w1 = np.ascontiguousarray(wi.transpose(3, 1, 2, 0, 4)).reshape(NJ, 128, 2048)
    wo = w_out.reshape(NJ, 128, 8, 128)
    w2 = np.ascontiguousarray(wo.transpose(2, 1, 0, 3)).reshape(8, 128, NJ * 128)
    ones = np.ones((128, 128), np.float32)
    maps = []
    for c in range(8):
        xT = np.ascontiguousarray(x_flat[c * T:(c + 1) * T].T)
        maps.append({"xT": xT, "g": g, "w1": w1, "w2": w2, "ones": ones})
    return maps


T = 2048
D = 1024
EPS = 1e-6
NCH = 25
ROPE_THETA = 10000.0
NTILES = 4
ND = 3

O_CQ, O_CKV, O_KR, O_GQ, O_GK, O_GV, O_GB, O_GA, O_GZ, O_MQ, O_MK, O_MV, O_GATE = 0, 256, 384, 416, 928, 1440, 1952, 1956, 1960, 2472, 2984, 3496, 4008
CH_COLS = ([(O_CQ, 128), (O_CQ + 128, 128), (O_CKV, 128), (O_KR, 32)] +
           [(O_GQ + h * 128, 128) for h in range(4)] + [(O_GK + h * 128, 128) for h in range(4)] +
           [(O_GV + h * 128, 128) for h in range(4)] + [(O_GB, 8)] +
           [(O_MQ + h * 128, 128) for h in range(4)] + [(O_MK + h * 128, 128) for h in range(4)])


def build_projb():
    nc = bass.Bass("TRN2", target_bir_lowering=False)
    DI = lambda n, s, dt=F32: nc.dram_tensor(n, s, dt, kind="ExternalInput").ap()
    DO = lambda n, s, dt=F32: nc.dram_tensor(n, s, dt, kind="ExternalOutput").ap()
    xT = DI("xT", [D, T]); gains_d = DI("gains", [128, 16])
    wb1 = DI("wb1", [NCH, 128, 1024]); wz_d = DI("wz", [128, 4096]); wmv_d = DI("wmv", [128, 4096])
    wuq_d = DI("wuq", [128, 1536]); wuk_d = DI("wuk", [128, 768]); wuv_d = DI("wuv", [128, 512])
    cmat = DI("cmat", [5, 128, 128])
    rope_d = DI("rope", [4, 128, T])
    o_mq = DO("mla_qT", [8, 96, T], BF16); o_mk = DO("mla_kT", [8, 96, T], BF16); o_mv = DO("mla_v", [T, 512], BF16)
    o_oq = DO("mo_qT", [4, 128, T], BF16); o_ok = DO("mo_kT", [4, 128, T], BF16); o_ov = DO("mo_v", [T, 512], BF16)
    o_gr = DO("graw", [12, 128, T]); o_gba = DO("gba", [8, T]); o_z = DO("z", [T, 512])
    P = Prog(nc)
    A = nc.alloc_sbuf_tensor
    PSA = nc.alloc_psum_tensor

    def sb(name, shape, dt=F32):
        return A("s_" + name, shape, dt), P.buf(name)

    def load(eng, t, b, src):
        P.dma(eng, P.dsem(), t, src, writes=[b])

    gains, b_gains = sb("gains", [128, 16]); load("sync", gains[:], b_gains, gains_d[:, :])
    CM, b_CM = sb("CM", [128, 5, 128], BF16); load("gpsimd", CM[:], b_CM, cmat.rearrange("k p n -> p k n"))
    ONES = CM[:, 0, :]; BLK64 = CM[:, 1, :]; RM_MLA = CM[0:96, 2, 0:96]; RM_MO = CM[:, 3, :]; SEL = CM[0:32, 4, 0:96]
    rope, b_rope = sb("rope", [128, 4, T]); load("sync", rope[:], b_rope, rope_d.rearrange("k p n -> p k n"))
    wz, b_wz = sb("wz", [128, 4096], BF16); load("gpsimd", wz[:], b_wz, wz_d[:, :])
    wmv, b_wmv = sb("wmv", [128, 4096], BF16); load("gpsimd", wmv[:], b_wmv, wmv_d[:, :])
    wuq, b_wuq = sb("wuq", [128, 1536], BF16); load("gpsimd", wuq[:], b_wuq, wuq_d[:, :])
    wuk, b_wuk = sb("wuk", [128, 768], BF16); load("gpsimd", wuk[:], b_wuk, wuk_d[:, :])
    wuv, b_wuv = sb("wuv", [128, 512], BF16); load("gpsimd", wuv[:], b_wuv, wuv_d[:, :])
    xins = [sb(f"xin{i}", [128, 8, 512]) for i in range(1)]; d_xins = [P.dsem() for _ in range(1)]
    sq = [sb(f"sq{i}", [128, 512], BF16) for i in range(2)]
    lnb, b_ln = sb("lnb", [128, 512]); rstd, b_rstd = sb("rstd", [128, 512])
    hT, b_h = sb("hT", [128, 8, T], BF16)
    wch = [sb(f"wch{i}", [128, 1024], BF16) for i in range(3)]; d_wch = [P.dsem() for _ in range(3)]
    cq = [sb(f"cq{i}", [128, T]) for i in range(3)]
    cqn = [sb(f"cqn{i}", [128, T], BF16) for i in range(3)]
    krope, b_krope = sb("krope", [32, T], BF16)
    NF = 4; NB = 8
    stf = [sb(f"stf{i}", [128, 512]) for i in range(NF)]; d_stf = [P.dsem() for _ in range(NF)]
    stb = [sb(f"stb{i}", [128, 512], BF16) for i in range(NB)]; d_stb = [P.dsem() for _ in range(NB)]
    cf = [0]; cb = [0]
    sqv = [sb(f"sqv{i}", [128, 512], BF16) for i in range(ND)]
    lnv = [sb(f"lnv{i}", [128, 512]) for i in range(ND)]
    rsv = [sb(f"rsv{i}", [128, 512]) for i in range(ND)]
    qn = [sb(f"qn{i}", [128, 512], BF16) for i in range(ND)]
    t1 = [sb(f"t1{i}", [128, 512]) for i in range(ND)]
    t2 = [sb(f"t2{i}", [128, 512]) for i in range(ND)]
    pS = (PSA("pS", [128, 512], F32), P.buf(excl=True))
    pP = [(PSA(f"pP{i}", [128, 512], F32), P.buf(excl=True)) for i in range(4)]
    pX = [(PSA(f"pX{i}", [128, 512], F32), P.buf(excl=True)) for i in range(3)]
    pN = pX
    cx = [0]
    pT = pS
    cp = [0]; cr = [0]
    all_out = []

    def out_f32(src_ps, b_ps, R, dst, eng="scalar"):
        i = cf[0] % NF; cf[0] += 1
        s, b_s = stf[i]
        if eng == "scalar":
            P.op("scalar", lambda e: e.copy(out=s[0:R, :], in_=src_ps), reads=[b_ps], writes=[b_s])
        else:
            P.op("vector", lambda e: e.tensor_copy(out=s[0:R, :], in_=src_ps), reads=[b_ps], writes=[b_s])
        P.dma("sync", d_stf[i], dst, s[0:R, :], reads=[b_s])

    def out_bf(src_ps, b_ps, R, dst, eng="vector"):
        i = cb[0] % NB; cb[0] += 1
        s, b_s = stb[i]
        if eng == "scalar":
            P.op("scalar", lambda e: e.copy(out=s[0:R, :], in_=src_ps), reads=[b_ps], writes=[b_s])
        else:
            P.op("vector", lambda e: e.tensor_copy(out=s[0:R, :], in_=src_ps), reads=[b_ps], writes=[b_s])
        P.dma("sync", d_stb[i], dst, s[0:R, :], reads=[b_s])

    def job(proj, R, gcol, onesm, rm, kcos, dim, tok, dst):
        k = cr[0] % ND; cr[0] += 1
        s_, b_s = sqv[k]; l_, b_l = lnv[k]; r_, b_r = rsv[k]; q_, b_q = qn[k]; a_, b_a = t1[k]; c_, b_c = t2[k]
        pp, b_pp = pP[cp[0] % 4]; cp[0] += 1
        ps = pp[0:R, :]
        proj(pp, b_pp)
        yield
        P.op("scalar", lambda e: e.activation(out=s_[0:R, :], in_=ps, func=AF.Square), reads=[b_pp], writes=[b_s])
        yield
        pn, b_pn = pX[cx[0] % 3]; cx[0] += 1
        P.mm(pn[0:R, :], onesm, s_[0:R, :], True, True, reads=[b_CM, b_s], writes=[b_pn])
        P.op("scalar", lambda e: e.activation(out=l_[0:R, :], in_=pn[0:R, :], func=AF.Ln, scale=1.0 / dim, bias=EPS), reads=[b_pn], writes=[b_l])
        P.op("scalar", lambda e: e.activation(out=r_[0:R, :], in_=l_[0:R, :], func=AF.Exp, scale=-0.5), reads=[b_l], writes=[b_r])
        yield
        P.op("vector", lambda e: e.scalar_tensor_tensor(out=q_[0:R, :], in0=ps, scalar=gains[0:R, gcol:gcol + 1], in1=r_[0:R, :], op0=ALU.mult, op1=ALU.mult),
             reads=[b_pp, b_gains, b_r], writes=[b_q])
        yield "late"
        pr, b_pr = pX[cx[0] % 3]; cx[0] += 1
        P.mm(pr[0:R, :], rm, q_[0:R, :], True, True, reads=[b_CM, b_q], writes=[b_pr])
        P.op("gpsimd", lambda e: e.tensor_tensor(out=a_[0:R, :], in0=q_[0:R, :], in1=rope[0:R, kcos, tok:tok + 512], op=ALU.mult),
             reads=[b_q, b_rope], writes=[b_a])
        P.op("vector", lambda e: e.tensor_tensor(out=c_[0:R, :], in0=pr[0:R, :], in1=rope[0:R, kcos + 1, tok:tok + 512], op=ALU.mult),
             reads=[b_pr, b_rope], writes=[b_c])
        i = cb[0] % NB; cb[0] += 1
        sbf, b_sb = stb[i]
        P.op("gpsimd", lambda e: e.tensor_tensor(out=sbf[0:R, :], in0=a_[0:R, :], in1=c_[0:R, :], op=ALU.add), reads=[b_a, b_c], writes=[b_sb])
        P.dma("sync", d_stb[i], dst, sbf[0:R, :], reads=[b_sb])
        yield

    def run_jobs(jobs):
        active = []
        jobs = list(jobs)
        while jobs or active:
            nxt = []
            for g in active:
                try:
                    next(g)
                    nxt.append(g)
                except StopIteration:
                    pass
            active = nxt
            if jobs:
                g = jobs.pop(0)
                next(g)
                active.append(g)

    xT_v = xT.rearrange("(kc p) n -> p kc n", p=128)
    for tt in range(NTILES):
        tok = tt * 512
        ts = slice(tok, tok + 512)
        xin, b_xin = xins[0]
        P.dma("sync", d_xins[0], xin[:], xT_v[:, :, ts], writes=[b_xin])
        for kc in range(8):
            s, bs = sq[kc % 2]
            P.op("scalar", lambda e, s=s, kc=kc, xin=xin: e.activation(out=s[:], in_=xin[:, kc, :], func=AF.Square), reads=[b_xin], writes=[bs])
            P.mm(pS[0][:], ONES, s[:], kc == 0, kc == 7, reads=[b_CM, bs], writes=[pS[1]])
        P.op("scalar", lambda e: e.activation(out=lnb[:], in_=pS[0][:], func=AF.Ln, scale=1.0 / D, bias=EPS), reads=[pS[1]], writes=[b_ln])
        P.op("scalar", lambda e: e.activation(out=rstd[:], in_=lnb[:], func=AF.Exp, scale=-0.5), reads=[b_ln], writes=[b_rstd])
        for kc in range(8):
            P.op("vector", lambda e, kc=kc, xin=xin, ts=ts: e.scalar_tensor_tensor(out=hT[:, kc, ts], in0=xin[:, kc, :], scalar=gains[:, kc:kc + 1], in1=rstd[:], op0=ALU.mult, op1=ALU.mult),
                 reads=[b_xin, b_gains, b_rstd], writes=[b_h])
    mo_jobs = []

    def load_wch(i):
        w_, bw_ = wch[i % 3]
        P.dma("gpsimd", d_wch[i % 3], w_[:], wb1[i, :, :], writes=[bw_])
    load_wch(0)
    for ch in range(NCH):
        w, bw = wch[ch % 3]
        if ch + 1 < NCH:
            load_wch(ch + 1)
        M = CH_COLS[ch][1]
        if ch >= 17:
            def mkproj(w=w, bw=bw, ts=None):
                def proj(pp, b_pp):
                    for kc in range(8):
                        P.mm(pp[0:128, :], w[:, kc * 128:kc * 128 + 128], hT[:, kc, ts], kc == 0, kc == 7, reads=[bw, b_h], writes=[b_pp])
                return proj
            for tt in range(NTILES):
                tok = tt * 512
                ts = slice(tok, tok + 512)
                dst = o_oq[ch - 17, :, ts] if ch < 21 else o_ok[ch - 21, :, ts]
                mo_jobs.append(job(mkproj(ts=ts), 128, 13 if ch < 21 else 14, BLK64, RM_MO, 2, 64.0, tok, dst))
            if ch % 2 == 0 or ch == NCH - 1:
                run_jobs(mo_jobs); mo_jobs = []
            continue
        for tt in range(NTILES):
            tok = tt * 512
            ts = slice(tok, tok + 512)
            pp, b_pp = pP[cp[0] % 4]; cp[0] += 1
            for kc in range(8):
                P.mm(pp[0:M, :], w[:, kc * 128:kc * 128 + M], hT[:, kc, ts], kc == 0, kc == 7, reads=[bw, b_h], writes=[b_pp])
            if ch < 3:
                c_, b_c = cq[ch]
                P.op("scalar", lambda e, c_=c_, pp=pp, ts=ts: e.copy(out=c_[:, ts], in_=pp[:]), reads=[b_pp], writes=[b_c])
            elif ch == 3:
                P.op("vector", lambda e, pp=pp, ts=ts: e.tensor_copy(out=krope[:, ts], in_=pp[0:32, :]), reads=[b_pp], writes=[b_krope])
            elif ch < 16:
                out_f32(pp[:], b_pp, 128, o_gr[ch - 4, :, ts], eng="scalar" if (ch + tt) % 2 else "vector")
            elif ch == 16:
                out_f32(pp[0:8, :], b_pp, 8, o_gba[:, ts])
    for tt in range(NTILES):
        ts = slice(tt * 512, (tt + 1) * 512)
        for grp, (idxs, dim, gc) in enumerate([((0, 1), 256.0, 8), ((2,), 128.0, 10)]):
            pn, b_pn = pX[cx[0] % 3]; cx[0] += 1; k = cr[0] % ND; cr[0] += 1
            for n_, ci in enumerate(idxs):
                s, bs = sq[n_ % 2]
                P.op("scalar", lambda e, s=s, ci=ci, ts=ts: e.activation(out=s[:], in_=cq[ci][0][:, ts], func=AF.Square), reads=[cq[ci][1]], writes=[bs])
                P.mm(pn[:], ONES, s[:], n_ == 0, n_ == len(idxs) - 1, reads=[b_CM, bs], writes=[b_pn])
            l_, b_l = lnv[k]; r_, b_r = rsv[k]
            P.op("scalar", lambda e, l_=l_, pn=pn, dim=dim: e.activation(out=l_[:], in_=pn[:], func=AF.Ln, scale=1.0 / dim, bias=EPS), reads=[b_pn], writes=[b_l])
            P.op("scalar", lambda e, l_=l_, r_=r_: e.activation(out=r_[:], in_=l_[:], func=AF.Exp, scale=-0.5), reads=[b_l], writes=[b_r])
            for n_, ci in enumerate(idxs):
                P.op("vector", lambda e, ci=ci, r_=r_, gc=gc, n_=n_, ts=ts: e.scalar_tensor_tensor(out=cqn[ci][0][:, ts], in0=cq[ci][0][:, ts], scalar=gains[:, gc + n_:gc + n_ + 1], in1=r_[:], op0=ALU.mult, op1=ALU.mult),
                     reads=[cq[ci][1], b_gains, b_r], writes=[cqn[ci][1]])
    mla_jobs = []
    for h in range(8):
        for tt in range(NTILES):
            tok = tt * 512
            ts = slice(tok, tok + 512)

            def projq(pp, b_pp, h=h, ts=ts):
                for kc in range(2):
                    P.mm(pp[0:96, :], wuq[:, kc * 768 + h * 96:kc * 768 + (h + 1) * 96], cqn[kc][0][:, ts], kc == 0, kc == 1, reads=[b_wuq, cqn[kc][1]], writes=[b_pp])

            def projk(pp, b_pp, h=h, ts=ts):
                P.mm(pp[0:96, :], wuk[:, h * 96:(h + 1) * 96], cqn[2][0][:, ts], True, False, reads=[b_wuk, cqn[2][1]], writes=[b_pp])
                P.mm(pp[0:96, :], SEL, krope[:, ts], False, True, reads=[b_CM, b_krope], writes=[b_pp])
            mla_jobs.append(job(projq, 96, 11, ONES[0:96, 0:96], RM_MLA, 0, 96.0, tok, o_mq[h, :, ts]))
            mla_jobs.append(job(projk, 96, 12, ONES[0:96, 0:96], RM_MLA, 0, 96.0, tok, o_mk[h, :, ts]))
    run_jobs(mla_jobs)
    for grp in range(4 * NTILES):
        gs = slice(grp * 128, (grp + 1) * 128)
        rows = gs
        P.mm(pT[0][:], cqn[2][0][:, gs], wuv[:], True, True, reads=[cqn[2][1], b_wuv], writes=[pT[1]])
        out_bf(pT[0][:], pT[1], 128, o_mv[rows, :], eng="scalar")
        for kc in range(8):
            P.mm(pT[0][:], hT[:, kc, gs], wz[:, kc * 512:(kc + 1) * 512], kc == 0, kc == 7, reads=[b_h, b_wz], writes=[pT[1]])
        out_f32(pT[0][:], pT[1], 128, o_z[rows, :], eng="vector")
        for kc in range(8):
            P.mm(pT[0][:], hT[:, kc, gs], wmv[:, kc * 512:(kc + 1) * 512], kc == 0, kc == 7, reads=[b_h, b_wmv], writes=[pT[1]])
        out_bf(pT[0][:], pT[1], 128, o_ov[rows, :], eng="scalar")
    st = P.emit(final_waits=[("sync", d) for d in d_stf + d_stb])
    return nc


def projb_consts():
    ones = np.ones((128, 128), np.float32)
    t = np.arange(128)
    blk64 = ((t[:, None] // 64) == (t[None, :] // 64)).astype(np.float32)
    rm_mla = np.zeros((128, 128), np.float32)
    for m in range(64, 80):
        rm_mla[m + 16, m] = -1.0
    for m in range(80, 96):
        rm_mla[m - 16, m] = 1.0
    rm_mo = np.zeros((128, 128), np.float32)
    for base in (0, 64):
        for m in range(base, base + 32):
            rm_mo[m + 32, m] = -1.0
        for m in range(base + 32, base + 64):
            rm_mo[m - 32, m] = 1.0
    sel = np.zeros((128, 128), np.float32)
    for i in range(32):
        sel[i, 64 + i] = 1.0
    return np.stack([ones, blk64, rm_mla, rm_mo, sel], 0)


def rope_tables(pos):
    pos = pos.astype(np.float32)
    out = np.zeros((4, 128, len(pos)), np.float32)
    out[0] = 1.0; out[2] = 1.0
    inv = (ROPE_THETA ** (-np.arange(16, dtype=np.float32) * 2.0 / 32)).astype(np.float32)
    ang = pos[None, :] * inv[:, None]
    for r in range(64, 96):
        i = (r - 64) % 16
        out[0, r] = np.cos(ang[i]); out[1, r] = np.sin(ang[i])
    out[0, 96:] = 0
    inv = (ROPE_THETA ** (-np.arange(32, dtype=np.float32) * 2.0 / 64)).astype(np.float32)
    ang = pos[None, :] * inv[:, None]
    for r in range(128):
        i = (r % 64) % 32
        out[2, r] = np.cos(ang[i]); out[3, r] = np.sin(ang[i])
    return out


def projb_weights(mix_norm, w_in, cq_norm, ckv_norm, w_uq, w_ukv, q_norm, k_norm, mq_norm, mk_norm):
    gains = np.zeros((128, 16), np.float32)
    gains[:, 0:8] = mix_norm.reshape(8, 128).T
    gains[:, 8:10] = cq_norm.reshape(2, 128).T
    gains[:, 10] = ckv_norm
    gains[0:96, 11] = q_norm; gains[0:96, 12] = k_norm
    gains[:, 13] = np.tile(mq_norm, 2); gains[:, 14] = np.tile(mk_norm, 2)
    wb1 = np.zeros((NCH, 128, 8, 128), np.float32)
    wr = w_in.reshape(8, 128, -1)
    for ch, (c0, m) in enumerate(CH_COLS):
        wb1[ch, :, :, 0:m] = wr[:, :, c0:c0 + m].transpose(1, 0, 2)
    wb1 = wb1.reshape(NCH, 128, 1024)
    wz = np.ascontiguousarray(wr[:, :, O_GZ:O_GZ + 512].transpose(1, 0, 2)).reshape(128, 4096)
    wmv = np.ascontiguousarray(wr[:, :, O_MV:O_MV + 512].transpose(1, 0, 2)).reshape(128, 4096)
    wuq = np.ascontiguousarray(w_uq.reshape(2, 128, 768).transpose(1, 0, 2)).reshape(128, 1536)
    kv = w_ukv.reshape(128, 8, 128)
    wuk = np.zeros((128, 8, 96), np.float32); wuk[:, :, 0:64] = kv[:, :, 0:64]
    wuv = np.ascontiguousarray(kv[:, :, 64:128]).reshape(128, 512)
    return {"gains": gains, "wb1": wb1, "wz": wz, "wmv": wmv, "wuq": wuq, "wuk": wuk.reshape(128, 768), "wuv": wuv, "cmat": projb_consts()}

S = 8192
NQT = 16
NDUMMY = 0
DUMMY_N = 384
HEADS = (0, 1, 2, 3)


def attn_consts():
    keys = np.arange(S)
    blkoh = (keys[None, :] // 256 == np.arange(32)[:, None]).astype(np.float32)
    p = np.arange(128)[:, None]; j = np.arange(512)[None, :]
    cmask = np.stack([np.where((128 * d + p) <= j, 0.0, -30000.0).astype(np.float32) for d in range(4)], 0)
    return blkoh, cmask, np.eye(128, dtype=np.float32), np.ones((128, 64), np.float32)


def build_attn():
    nc = bass.Bass("TRN2", target_bir_lowering=False)
    DI = lambda n, s, dt=F32: nc.dram_tensor(n, s, dt, kind="ExternalInput").ap()
    mq = DI("mq", [2, 96, S], BF16); mk = DI("mk", [2, 96, S], BF16); mv = DI("mv", [S, 128], BF16)
    oq = DI("oq", [128, S], BF16); ok = DI("ok", [128, S], BF16); ov = DI("ov", [S, 128], BF16)
    blkoh_d = DI("blkoh", [32, S]); cmask_d = DI("cmask", [4, 128, 512]); ident_d = DI("ident", [128, 128]); onesf_d = DI("onesf", [128, 64])
    oT = nc.dram_tensor("oT", [4, 64, S], BF16, kind="ExternalOutput").ap()
    P = Prog(nc)
    A = nc.alloc_sbuf_tensor
    PSA = nc.alloc_psum_tensor

    def sb(name, shape, dt=F32):
        return A("s_" + name, shape, dt), P.buf(name)

    cmask, b_cm = sb("cmask", [128, 4, 512], BF16); P.dma("gpsimd", P.dsem(), cmask[:], cmask_d.rearrange("k p n -> p k n"), writes=[b_cm])
    ident, b_id = sb("ident", [128, 128], BF16); P.dma("gpsimd", P.dsem(), ident[:], ident_d[:, :], writes=[b_id])
    onesf, b_of = sb("onesf", [128, 64]); P.dma("sync", P.dsem(), onesf[:], onesf_d[:, :], writes=[b_of])
    Ka = [sb(f"Ka{i}", [128, S], BF16) for i in range(2)]
    Qa = [sb(f"Qa{i}", [128, S], BF16) for i in range(2)]
    Va = [sb(f"Va{i}", [128, 64, 65], BF16) for i in range(2)]
    d_K = [P.dsem() for _ in range(2)]; d_Q = [P.dsem() for _ in range(2)]; d_V = [P.dsem() for _ in range(2)]
    d_K2 = [P.dsem() for _ in range(2)]
    kmf, b_kmf = sb("kmf", [128, 32]); kmT, b_kmT = sb("kmT", [128, 32], BF16)
    gm = [sb(f"gm{i}", [128, 32]) for i in range(4)]
    top8 = [sb(f"top8{i}", [128, 8]) for i in range(4)]
    sel = [sb(f"sel{i}", [128, 32]) for i in range(4)]
    PT = [sb(f"PT{i}", [128, 512], BF16) for i in range(4)]
    osb = [sb(f"osb{i}", [128, 512]) for i in range(2)]
    rec = [sb(f"rec{i}", [128, 512]) for i in range(2)]
    onb = [sb(f"onb{i}", [64, 512], BF16) for i in range(2)]; d_on = [P.dsem() for _ in range(2)]
    pSc = [(PSA(f"pSc{i}", [128, 512], F32), P.buf(excl=True)) for i in range(3)]
    pO = [(PSA(f"pO{i}", [128, 512], F32), P.buf(excl=True)) for i in range(2)]
    pBC = (PSA("pBC", [128, 512], F32), P.buf(excl=True))
    pG = (PSA("pG", [128, 512], F32), P.buf(excl=True))
    pTr = (PSA("pTr", [128, 512], F32), P.buf(excl=True))
    pD = pG

    pending = []
    negpads = [[sb(f"negpad{g}_{i}", [128, 128], BF16) for i in range(4)] for g in range(3)]
    for g in range(3):
        for i in range(4):
            P.op("vector", lambda e, g=g, i=i: e.memset(negpads[g][i][0][:], 0.0), writes=[negpads[g][i][1]])

    def prologue(n_):
        hi = HEADS[n_]
        s2 = n_ % 2
        K, b_K = Ka[s2]; Q, b_Q = Qa[s2]; V, b_V = Va[s2]
        moba = hi >= 2
        c = dict(hi=hi, K=K, b_K=b_K, Q=Q, b_Q=b_Q, V=V, b_V=b_V, moba=moba, rows=slice(0, 96))
        if not moba:
            c["scale"] = 96.0 ** -0.5
            P.dma("sync", d_K[s2], K[0:96, :], mk[hi, :, :], writes=[b_K])
            P.dma("sync", d_Q[s2], Q[0:96, :], mq[hi, :, :], writes=[b_Q])
            vsrc = mv[:, hi * 64:(hi + 1) * 64]
        else:
            c["scale"] = 0.125
            hb = hi - 2
            srows = slice(hb * 64, (hb + 1) * 64)
            P.dma("sync", d_K[s2], K[0:64, :], ok[srows, :], writes=[b_K])
            P.dma("gpsimd", d_K2[s2], K[64:96, :], blkoh_d[:, :], writes=[b_K])
            P.dma("sync", d_Q[s2], Q[0:64, :], oq[srows, :], writes=[b_Q])
            vsrc = ov[:, hb * 64:(hb + 1) * 64]
        P.dma("sync", d_V[s2], V[:, :, 0:64], vsrc.rearrange("(kc p) d -> p kc d", p=128), writes=[b_V])
        P.op("gpsimd", lambda e, V=V: e.memset(V[:, :, 64:65], 1.0), writes=[b_V])
        return c

    def topk_gen(c):
        K, b_K, Q, b_Q = c["K"], c["b_K"], c["Q"], c["b_Q"]
        krows = slice(0, 64); off = 64
        P.op("vector", lambda e: e.tensor_reduce(out=kmf[krows, :], in_=K[krows, :].rearrange("p (n k) -> p n k", k=256), axis=AX.X, op=ALU.add),
             reads=[b_K], writes=[b_kmf])
        P.op("vector", lambda e: e.tensor_scalar(out=kmT[krows, :], in0=kmf[krows, :], scalar1=1.0 / 256, scalar2=None, op0=ALU.mult),
             reads=[b_kmf], writes=[b_kmT])

        def stage_a(qt):
            for j in range(4):
                qc = qt * 4 + j
                P.mm(pG[0][:, j * 32:(j + 1) * 32], Q[krows, qc * 128:(qc + 1) * 128], kmT[krows, :], True, True, reads=[b_Q, b_kmT], writes=[pG[1]])
            for j in range(4):
                qc = qt * 4 + j; qb = qc // 2
                g_, b_g = gm[j]; t8, b_t8 = top8[j]; sl_, b_sl = sel[j]; npd, b_np = negpads[qt % 3][j]
                P.op("gpsimd", lambda e, g_=g_: e.memset(g_[:], -1e30), writes=[b_g])
                if qb > 0:
                    P.op("vector", lambda e, g_=g_, j=j, qb=qb: e.tensor_copy(out=g_[:, 0:qb], in_=pG[0][:, j * 32:j * 32 + qb]), reads=[pG[1]], writes=[b_g])
                P.op("vector", lambda e, g_=g_, t8=t8: e.max(out=t8[:], in_=g_[:]), reads=[b_g], writes=[b_t8])
                P.op("vector", lambda e, g_=g_, t8=t8, sl_=sl_: e.tensor_scalar(out=sl_[:], in0=g_[:], scalar1=t8[:, 2:3], scalar2=None, op0=ALU.is_ge),
                     reads=[b_g, b_t8], writes=[b_sl])
                P.op("vector", lambda e, sl_=sl_, npd=npd: e.tensor_scalar(out=npd[:, off:off + 32], in0=sl_[:], scalar1=-1.0, scalar2=30000.0, op0=ALU.add, op1=ALU.mult),
                     reads=[b_sl], writes=[b_np])
                P.op("vector", lambda e, npd=npd, qb=qb: e.memset(npd[:, off + qb:off + qb + 1], 0.0), writes=[b_np])

        def stage_b(qt):
            for j in range(4):
                npd, b_np = negpads[qt % 3][j]
                P.mm(pTr[0][:, j * 128:(j + 1) * 128], npd[:], ident[:], True, True, reads=[b_np, b_id], writes=[pTr[1]])
            P.op("vector", lambda e: e.tensor_copy(out=Q[off:off + 32, qt * 512:(qt + 1) * 512], in_=pTr[0][off:off + 32, :]),
                 reads=[pTr[1]], writes=[b_Q])
        for qt in range(NQT + 2):
            if qt < NQT:
                stage_a(qt)
            if qt >= 2:
                stage_b(qt - 2)
            yield

    def main_loop(c, tick):
        K, b_K, Q, b_Q, V, b_V = c["K"], c["b_K"], c["Q"], c["b_Q"], c["V"], c["b_V"]
        rows, scale, hi = c["rows"], c["scale"], c["hi"]
        for qt in range(NQT):
            tick()
            nkc = 4 * qt + 4
            qs = slice(qt * 512, (qt + 1) * 512)
            po, b_po = pO[qt % 2]

            def score(kc):
                ps, b_ps = pSc[kc % 3]
                diag = kc >= 4 * qt
                P.mm(ps[:], K[rows, kc * 128:(kc + 1) * 128], Q[rows, qs], True, not diag, reads=[b_K, b_Q], writes=[b_ps])
                if diag:
                    P.mm(ps[:], ident[:], cmask[:, kc - 4 * qt, :], False, True, reads=[b_id, b_cm], writes=[b_ps])
            score(0)
            if nkc > 1:
                score(1)
            for kc in range(nkc):
                if kc + 2 < nkc:
                    score(kc + 2)
                if kc == min(10, nkc - 1) and pending:
                    pending.pop(0)()
                ps, b_ps = pSc[kc % 3]
                pt_, b_pt = PT[kc % 4]
                P.op("scalar", lambda e, ps=ps, pt_=pt_, scale=scale: e.activation(out=pt_[:], in_=ps[:], func=AF.Exp, scale=scale), reads=[b_ps], writes=[b_pt])
                P.mm(po[0:65, :], V[:, kc, :], pt_[:], kc == 0, kc == nkc - 1, reads=[b_V, b_pt], writes=[b_po])
            o_, b_o = osb[qt % 2]; r_, b_r = rec[qt % 2]; on_, b_on = onb[qt % 2]
            P.op("vector", lambda e, o_=o_, po=po: e.tensor_copy(out=o_[0:65, :], in_=po[0:65, :]), reads=[b_po], writes=[b_o])
            P.op("vector", lambda e, o_=o_, r_=r_: e.reciprocal(out=r_[64:65, :], in_=o_[64:65, :]), reads=[b_o], writes=[b_r])

            def epilogue(o_=o_, b_o=b_o, r_=r_, b_r=b_r, on_=on_, b_on=b_on, qt=qt, qs=qs, hi=hi):
                P.mm(pBC[0][0:64, :], onesf[64:65, :], r_[64:65, :], True, True, reads=[b_of, b_r], writes=[pBC[1]])
                P.op("vector", lambda e: e.tensor_tensor(out=on_[:], in0=o_[0:64, :], in1=pBC[0][0:64, :], op=ALU.mult), reads=[b_o, pBC[1]], writes=[b_on])
                P.dma("sync", d_on[qt % 2], oT[hi, :, qs], on_[:], reads=[b_on])
            pending.append(epilogue)

    nh = len(HEADS)
    ctxs = [None] * nh
    ctxs[0] = prologue(0)
    if ctxs[0]["moba"]:
        for _ in topk_gen(ctxs[0]):
            pass
    for n_ in range(nh):
        gen = None
        if n_ + 1 < nh:
            ctxs[n_ + 1] = prologue(n_ + 1)
            if ctxs[n_ + 1]["moba"]:
                gen = topk_gen(ctxs[n_ + 1])
        state = {"g": gen}

        def tick(state=state):
            if state["g"] is not None:
                try:
                    next(state["g"])
                except StopIteration:
                    state["g"] = None
        main_loop(ctxs[n_], tick)
        while state["g"] is not None:
            tick()
    while pending:
        pending.pop(0)()
    st = P.emit(final_waits=[("sync", d) for d in d_on])
    return nc

S = 8192
NSEG = 4
SEG = 2048
NT = 16
GT = 4
EPS = 1e-6
NLV = 6


def gdn_consts():
    t = np.arange(128)
    M = (t[:, None] <= t[None, :]).astype(np.float32)
    NEGM = np.where(t[:, None] >= t[None, :], 0.0, -1e30).astype(np.float32)
    STRICT = (t[:, None] > t[None, :]).astype(np.float32)
    ident = np.eye(128, dtype=np.float32)
    ones = np.ones((128, 128), np.float32)
    NEGS = np.where(t[:, None] > t[None, :], 0.0, -1e30).astype(np.float32)
    return np.stack([M, ones, NEGM, NEGS, ident, -ones], 0)


def build_gdn():
    nc = bass.Bass("TRN2", target_bir_lowering=False)
    DI = lambda n, s, dt=F32: nc.dram_tensor(n, s, dt, kind="ExternalInput").ap()
    rq = DI("rq", [128, S]); rk = DI("rk", [128, S]); rv = DI("rv", [128, S])
    zd = DI("z", [S, 128])
    bl = DI("bl", [128, 64]); al = DI("al", [128, 64])
    cw = DI("cw", [128, 12]); sc = DI("sc", [128, 2]); gn = DI("gn", [128, 128])
    cst = DI("cst", [6, 128, 128])
    od = nc.dram_tensor("o", [S, 128], BF16, kind="ExternalOutput").ap()
    P = Prog(nc)
    A = nc.alloc_sbuf_tensor
    PS = nc.alloc_psum_tensor

    def sb(name, shape, dt=F32):
        return A("s_" + name, shape, dt), P.buf(name)

    C, b_C = sb("C", [128, 6, 128])
    cwt, b_cw = sb("cwt", [128, 12]); sct, b_sc = sb("sct", [128, 2]); gnt, b_gn = sb("gnt", [128, 128])
    blt, b_bl = sb("blt", [128, 64]); alt, b_al = sb("alt", [128, 64])
    beta, b_beta = sb("beta", [128, 64]); gg, b_gg = sb("gg", [128, 64])
    tmp64, b_tmp64 = sb("tmp64", [128, 64]); ea, b_ea = sb("ea", [128, 1])
    P.dma("sync", P.dsem(), C[:], cst.rearrange("k p n -> p k n"), writes=[b_C])
    for (t_, d_, b_) in [(cwt, cw, b_cw), (sct, sc, b_sc), (gnt, gn, b_gn), (blt, bl, b_bl), (alt, al, b_al)]:
        P.dma("sync", P.dsem(), t_[:], d_[:, :], writes=[b_])
    Mm, ONES, NEGM, STRICT, IDENT, NEGONES = [C[:, i, :] for i in range(6)]
    NEG4, b_N4 = sb("NEG4", [128, GT, 128]); STR4, b_S4 = sb("STR4", [128, GT, 128]); ID4, b_I4 = sb("ID4", [128, GT, 128])
    GN4, b_G4 = sb("GN4", [128, GT, 128])
    for t in range(GT):
        P.op("gpsimd", lambda e, t=t: e.tensor_copy(out=GN4[:, t, :], in_=gnt[:]), reads=[b_gn], writes=[b_G4])
        P.op("gpsimd", lambda e, t=t: e.tensor_copy(out=NEG4[:, t, :], in_=NEGM), reads=[b_C], writes=[b_N4])
        P.op("gpsimd", lambda e, t=t: e.tensor_copy(out=STR4[:, t, :], in_=STRICT), reads=[b_C], writes=[b_S4])
        P.op("gpsimd", lambda e, t=t: e.tensor_copy(out=ID4[:, t, :], in_=IDENT), reads=[b_C], writes=[b_I4])
    P.op("scalar", lambda e: e.activation(out=beta[:], in_=blt[:], func=AF.Sigmoid), reads=[b_bl], writes=[b_beta])
    P.op("scalar", lambda e: e.activation(out=tmp64[:], in_=alt[:], func=AF.Exp, bias=sct[:, 1:2]), reads=[b_al, b_sc], writes=[b_tmp64])
    P.op("scalar", lambda e: e.activation(out=tmp64[:], in_=tmp64[:], func=AF.Ln, bias=1.0), reads=[b_tmp64], writes=[b_tmp64])
    P.op("scalar", lambda e: e.activation(out=ea[:], in_=sct[:, 0:1], func=AF.Exp), reads=[b_sc], writes=[b_ea])
    P.op("vector", lambda e: e.tensor_scalar(out=gg[:], in0=tmp64[:], scalar1=ea[:, 0:1], scalar2=-1.0, op0=ALU.mult, op1=ALU.mult),
         reads=[b_tmp64, b_ea], writes=[b_gg])

    raw = [sb(f"raw{i}", [128, SEG + 3]) for i in range(3)]
    d_raw = [P.dsem() for _ in range(3)]
    cvs = [[sb(f"cv{s}_{i}", [128, SEG]) for i in range(3)] for s in range(2)]
    sqb, b_sq = sb("sqb", [128, 512]); lnb, b_ln = sb("lnb", [128, 512]); rsb, b_rs = sb("rsb", [128, 512])
    stat = [[sb(f"{n}{s}", [128, NT]) for n in ("gcum", "egc", "edec", "dec", "begc")] for s in range(2)]

    def g4(name, n=1):
        return [sb(f"{name}{i}", [128, GT, 128]) for i in range(n)]
    ktm = g4("ktm")[0]; vb = g4("vb")[0]; rw = g4("rw")[0]
    Gm = g4("Gm")[0]; nGm = g4("nGm")[0]; dmin = g4("dmin")[0]; Dm = g4("Dm")[0]; Dms = g4("Dms")[0]
    Am = g4("Am")[0]; Bm = g4("Bm")[0]; qkd = g4("qkd")[0]
    Qm = g4("Qm", 2); Ym = g4("Ym", 2); YTm = g4("YTm", 2)
    kdec = g4("kdec", 2); qkdT = g4("qkdT", 2); uu = g4("uu", 2); wT = g4("wT", 2)
    zt = g4("zt", 2); d_z = [P.dsem() for _ in range(2)]
    szt = g4("szt", 2)
    ofb = [sb(f"ofb{i}", [128, GT, 128], BF16) for i in range(2)]; d_o = [P.dsem() for _ in range(2)]
    NB = 2
    vnew = [sb(f"vnew{i}", [128, 128]) for i in range(NB)]
    o1 = [sb(f"o1{i}", [128, 128]) for i in range(NB)]
    ot = [sb(f"ot{i}", [128, 128]) for i in range(NB)]
    osq = [sb(f"osq{i}", [128, 128]) for i in range(NB)]
    ss = [sb(f"ss{i}", [128, 1]) for i in range(NB)]
    lss = [sb(f"lss{i}", [128, 1]) for i in range(NB)]
    rss = [sb(f"rss{i}", [128, 1]) for i in range(NB)]
    og = [sb(f"og{i}", [128, 128]) for i in range(NB)]
    St = [sb(f"St{i}", [128, 128]) for i in range(2)]
    pb = [(PS(f"pb{i}", [128, 512], F32), P.buf(f"pb{i}", excl=True)) for i in range(8)]
    pcount = [0]

    def bank():
        i = pcount[0] % 6
        pcount[0] += 1
        return pb[i]
    pV, b_pV = pb[6]
    pSt, b_pSt = pb[7]
    P.op("vector", lambda e: e.memset(St[0][0][:], 0.0), writes=[St[0][1]])
    state = {"scur": 0}

    def seg_prep(seg):
        s0 = seg * SEG
        cv = cvs[seg % 2]
        for qi, rd in enumerate((rq, rk, rv)):
            r_, b_r = raw[qi]
            if seg == 0:
                P.op("gpsimd", lambda e, r_=r_: e.memset(r_[:, 0:3], 0.0), writes=[b_r])
                P.dma("sync", d_raw[qi], r_[:, 3:], rd[:, 0:SEG], writes=[b_r])
            else:
                P.dma("sync", d_raw[qi], r_[:, :], rd[:, s0 - 3:s0 + SEG], writes=[b_r])
            c_, b_c = cv[qi]
            for hf in range(2):
                lo = hf * 1024
                sl = slice(lo, lo + 1024)
                P.op("scalar", lambda e, c_=c_, r_=r_, lo=lo, qi=qi, sl=sl: e.activation(
                    out=c_[:, sl], in_=r_[:, lo:lo + 1024], func=AF.Copy, scale=cwt[:, qi * 4:qi * 4 + 1]),
                    reads=[b_r, b_cw], writes=[b_c])
                for tap in range(1, 4):
                    P.op("vector", lambda e, c_=c_, r_=r_, lo=lo, qi=qi, sl=sl, tap=tap: e.scalar_tensor_tensor(
                        out=c_[:, sl], in0=r_[:, lo + tap:lo + tap + 1024], scalar=cwt[:, qi * 4 + tap:qi * 4 + tap + 1],
                        in1=c_[:, sl], op0=ALU.mult, op1=ALU.add), reads=[b_r, b_cw, b_c], writes=[b_c])
                P.op("scalar", lambda e, c_=c_, sl=sl: e.activation(out=c_[:, sl], in_=c_[:, sl], func=AF.Silu),
                     reads=[b_c], writes=[b_c])
                yield
        for qi in range(2):
            c_, b_c = cv[qi]
            for t4 in range(4):
                sl = slice(t4 * 512, (t4 + 1) * 512)
                pt, b_pt = bank()
                P.op("scalar", lambda e, c_=c_, sl=sl: e.activation(out=sqb[:], in_=c_[:, sl], func=AF.Square), reads=[b_c], writes=[b_sq])
                P.mm(pt[:], ONES, sqb[:], True, True, reads=[b_C, b_sq], writes=[b_pt])
                P.op("scalar", lambda e, pt=pt: e.activation(out=lnb[:], in_=pt[:], func=AF.Ln, bias=EPS), reads=[b_pt], writes=[b_ln])
                P.op("scalar", lambda e: e.activation(out=rsb[:], in_=lnb[:], func=AF.Exp, scale=-0.5), reads=[b_ln], writes=[b_rs])
                scl = (128.0 ** -0.5) if qi == 0 else 1.0
                P.op("vector", lambda e, c_=c_, sl=sl, scl=scl: e.scalar_tensor_tensor(
                    out=c_[:, sl], in0=c_[:, sl], scalar=scl, in1=rsb[:], op0=ALU.mult, op1=ALU.mult),
                    reads=[b_c, b_rs], writes=[b_c])
                yield
        (gcum, b_gcum), (egc, b_egc), (edec, b_edec), (dec, b_dec), (begc, b_begc) = stat[seg % 2]
        gsl = slice(seg * NT, (seg + 1) * NT)
        pt, b_pt = bank()
        P.mm(pt[:, 0:NT], Mm, gg[:, gsl], True, True, reads=[b_C, b_gg], writes=[b_pt])
        P.mm(pt[:, 16:16 + NT], ONES, gg[:, gsl], True, True, reads=[b_C, b_gg], writes=[b_pt])
        P.op("vector", lambda e, pt=pt: e.tensor_copy(out=gcum[:], in_=pt[:, 0:NT]), reads=[b_pt], writes=[b_gcum])
        P.op("scalar", lambda e, pt=pt: e.activation(out=egc[:], in_=pt[:, 0:NT], func=AF.Exp), reads=[b_pt], writes=[b_egc])
        P.op("vector", lambda e, pt=pt: e.tensor_tensor(out=edec[:], in0=pt[:, 16:16 + NT], in1=gcum[:], op=ALU.subtract),
             reads=[b_pt, b_gcum], writes=[b_edec])
        P.op("scalar", lambda e: e.activation(out=edec[:], in_=edec[:], func=AF.Exp), reads=[b_edec], writes=[b_edec])
        P.op("scalar", lambda e, pt=pt: e.activation(out=dec[:], in_=pt[:, 16:16 + NT], func=AF.Exp), reads=[b_pt], writes=[b_dec])
        P.op("vector", lambda e, gsl=gsl: e.tensor_tensor(out=begc[:], in0=beta[:, gsl], in1=egc[:], op=ALU.mult),
             reads=[b_beta, b_egc], writes=[b_begc])

    def prepass_stages(seg, grp):
        cv = cvs[seg % 2]
        qT_, b_qT = cv[0]; kT_, b_kT = cv[1]; vT_, b_vT = cv[2]
        (gcum, b_gcum), (egc, b_egc), (edec, b_edec), (dec, b_dec), (begc, b_begc) = stat[seg % 2]
        gi = (seg * (NT // GT) + grp) % 2
        Ts = [grp * GT + t for t in range(GT)]
        cs = lambda t: slice(Ts[t] * 128, (Ts[t] + 1) * 128)
        Gs = [seg * NT + T for T in Ts]
        kd, b_kd = kdec[gi]; qT2, b_qT2 = qkdT[gi]; u_, b_u = uu[gi]; w_, b_w = wT[gi]
        pk, b_pk = bank(); pv, b_pv = bank()
        for t in range(GT):
            P.op("tensor", lambda e, t=t: e.transpose(pk[:, t * 128:(t + 1) * 128], kT_[:, cs(t)], IDENT), reads=[b_kT, b_C], writes=[b_pk])
        for t in range(GT):
            P.op("tensor", lambda e, t=t: e.transpose(pv[:, t * 128:(t + 1) * 128], vT_[:, cs(t)], IDENT), reads=[b_vT, b_C], writes=[b_pv])
        for t in range(GT):
            P.op("vector", lambda e, t=t: e.tensor_scalar(out=vb[0][:, t, :], in0=pv[:, t * 128:(t + 1) * 128], scalar1=beta[:, Gs[t]:Gs[t] + 1], scalar2=None, op0=ALU.mult),
                 reads=[b_pv, b_beta], writes=[vb[1]])
        for t in range(GT):
            P.op("scalar", lambda e, t=t: e.activation(out=rw[0][:, t, :], in_=pk[:, t * 128:(t + 1) * 128], func=AF.Copy, scale=begc[:, Ts[t]:Ts[t] + 1]),
                 reads=[b_pk, b_begc], writes=[rw[1]])
            P.op("scalar", lambda e, t=t: e.activation(out=kd[:, t, :], in_=pk[:, t * 128:(t + 1) * 128], func=AF.Copy, scale=edec[:, Ts[t]:Ts[t] + 1]),
                 reads=[b_pk, b_edec], writes=[b_kd])
        for t in range(GT):
            P.op("vector", lambda e, t=t: e.tensor_scalar(out=Gm[0][:, t, :], in0=Mm, scalar1=gg[:, Gs[t]:Gs[t] + 1], scalar2=None, op0=ALU.mult),
                 reads=[b_C, b_gg], writes=[Gm[1]])
        yield
        pd, b_pd = bank(); pkk, b_pkk = bank(); pqk, b_pqk = bank()
        for t in range(GT):
            o = slice(t * 128, (t + 1) * 128)
            P.mm(pd[:, o], Gm[0][:, t, :], ONES, True, False, reads=[Gm[1], b_C], writes=[b_pd])
            P.mm(pd[:, o], NEGONES, Gm[0][:, t, :], False, True, reads=[Gm[1], b_C], writes=[b_pd])
        for t in range(GT):
            o = slice(t * 128, (t + 1) * 128)
            P.mm(pkk[:, o], kT_[:, cs(t)], kT_[:, cs(t)], True, True, reads=[b_kT], writes=[b_pkk])
        for t in range(GT):
            o = slice(t * 128, (t + 1) * 128)
            P.mm(pqk[:, o], qT_[:, cs(t)], kT_[:, cs(t)], True, True, reads=[b_kT, b_qT], writes=[b_pqk])
        fl = lambda x: x[:].rearrange("p t n -> p (t n)")
        P.op("vector", lambda e: e.scalar_tensor_tensor(out=fl(dmin[0]), in0=pd[:], scalar=0.0, in1=fl(NEG4), op0=ALU.min, op1=ALU.add),
             reads=[b_pd, b_N4], writes=[dmin[1]])
        P.op("scalar", lambda e: e.activation(out=fl(Dm[0]), in_=fl(dmin[0]), func=AF.Exp), reads=[dmin[1]], writes=[Dm[1]])
        P.op("vector", lambda e: e.scalar_tensor_tensor(out=fl(nGm[0]), in0=pd[:], scalar=0.0, in1=fl(STR4), op0=ALU.min, op1=ALU.add),
             reads=[b_pd, b_S4], writes=[nGm[1]])
        P.op("scalar", lambda e: e.activation(out=fl(Dms[0]), in_=fl(nGm[0]), func=AF.Exp), reads=[nGm[1]], writes=[Dms[1]])
        for t in range(GT):
            P.op("vector", lambda e, t=t: e.scalar_tensor_tensor(out=Am[0][:, t, :], in0=pkk[:, t * 128:(t + 1) * 128], scalar=beta[:, Gs[t]:Gs[t] + 1], in1=Dms[0][:, t, :], op0=ALU.mult, op1=ALU.mult),
                 reads=[b_pkk, b_beta, Dms[1]], writes=[Am[1]])
        P.op("vector", lambda e: e.tensor_tensor(out=fl(qkd[0]), in0=pqk[:], in1=fl(Dm[0]), op=ALU.mult), reads=[b_pqk, Dm[1]], writes=[qkd[1]])
        yield
        pbt, b_pbt = bank(); pqt, b_pqt = bank()
        for t in range(GT):
            P.op("tensor", lambda e, t=t: e.transpose(pbt[:, t * 128:(t + 1) * 128], Am[0][:, t, :], IDENT), reads=[Am[1], b_C], writes=[b_pbt])
        for t in range(GT):
            P.op("tensor", lambda e, t=t: e.transpose(pqt[:, t * 128:(t + 1) * 128], qkd[0][:, t, :], IDENT), reads=[qkd[1], b_C], writes=[b_pqt])
        P.op("scalar", lambda e: e.copy(out=fl(Bm[0]), in_=pbt[:]), reads=[b_pbt], writes=[Bm[1]])
        P.op("vector", lambda e: e.tensor_copy(out=fl(qT2), in_=pqt[:]), reads=[b_pqt], writes=[b_qT2])
        Qc, b_Qc = Qm[0]
        P.op("gpsimd", lambda e: e.scalar_tensor_tensor(out=fl(Qc), in0=fl(Bm[0]), scalar=-1.0, in1=fl(ID4), op0=ALU.mult, op1=ALU.add),
             reads=[Bm[1], b_I4], writes=[b_Qc]) if False else \
            P.op("vector", lambda e: e.scalar_tensor_tensor(out=fl(Qc), in0=fl(Bm[0]), scalar=-1.0, in1=fl(ID4), op0=ALU.mult, op1=ALU.add),
                 reads=[Bm[1], b_I4], writes=[b_Qc])
        yield
        Yc, b_Yc = Bm; YTc, b_YTc = Am
        for lv in range(NLV):
            pyt, b_pyt = bank()
            Yn, b_Yn = Ym[lv % 2]; YTn, b_YTn = YTm[lv % 2]
            for t in range(GT):
                P.mm(pyt[:, t * 128:(t + 1) * 128], Yc[:, t, :], YTc[:, t, :], True, True, reads=[b_Yc, b_YTc], writes=[b_pyt])
            if lv < NLV - 1:
                py, b_py = bank()
                for t in range(GT):
                    P.mm(py[:, t * 128:(t + 1) * 128], YTc[:, t, :], Yc[:, t, :], True, True, reads=[b_Yc, b_YTc], writes=[b_py])
            P.op("scalar", lambda e, YTn=YTn, pyt=pyt: e.copy(out=fl(YTn), in_=pyt[:]), reads=[b_pyt], writes=[b_YTn])
            if lv < NLV - 1:
                P.op("vector", lambda e, Yn=Yn, py=py: e.tensor_copy(out=fl(Yn), in_=py[:]), reads=[b_py], writes=[b_Yn])
            yield
            Qo, b_Qo = Qm[lv % 2]; Qn, b_Qn = Qm[(lv + 1) % 2]
            pq, b_pq = bank()
            for t in range(GT):
                P.mm(pq[:, t * 128:(t + 1) * 128], YTn[:, t, :], Qo[:, t, :], True, True, reads=[b_YTn, b_Qo], writes=[b_pq])
            P.op("vector", lambda e, Qn=Qn, Qo=Qo, pq=pq: e.tensor_tensor(out=fl(Qn), in0=pq[:], in1=fl(Qo), op=ALU.add), reads=[b_pq, b_Qo], writes=[b_Qn])
            Yc, b_Yc = Yn, b_Yn
            YTc, b_YTc = YTn, b_YTn
            yield
        Tt, b_Tt = Qm[NLV % 2]
        pu, b_pu = bank(); pw, b_pw = bank()
        for t in range(GT):
            P.mm(pu[:, t * 128:(t + 1) * 128], Tt[:, t, :], vb[0][:, t, :], True, True, reads=[b_Tt, vb[1]], writes=[b_pu])
        for t in range(GT):
            P.mm(pw[:, t * 128:(t + 1) * 128], rw[0][:, t, :], Tt[:, t, :], True, True, reads=[b_Tt, rw[1]], writes=[b_pw])
        P.op("scalar", lambda e: e.copy(out=fl(u_), in_=pu[:]), reads=[b_pu], writes=[b_u])
        P.op("vector", lambda e: e.tensor_copy(out=fl(w_), in_=pw[:]), reads=[b_pw], writes=[b_w])
        G0 = Gs[0]
        P.dma("sync", d_z[gi], zt[gi][0][:], zd[G0 * 128:(G0 + GT) * 128, :].rearrange("(t p) d -> p t d", p=128), writes=[zt[gi][1]])
        P.op("scalar", lambda e: e.activation(out=fl(szt[gi][0]), in_=fl(zt[gi][0]), func=AF.Silu), reads=[zt[gi][1]], writes=[szt[gi][1]])
        P.op("gpsimd", lambda e: e.tensor_tensor(out=fl(szt[gi][0]), in0=fl(szt[gi][0]), in1=fl(GN4), op=ALU.mult), reads=[szt[gi][1], b_G4], writes=[szt[gi][1]])
        yield

    def scan_steps(seg, grp):
        cv = cvs[seg % 2]
        qT_, b_qT = cv[0]
        (gcum, b_gcum), (egc, b_egc), (edec, b_edec), (dec, b_dec), (begc, b_begc) = stat[seg % 2]
        gi = (seg * (NT // GT) + grp) % 2
        kd, b_kd = kdec[gi]; qT2, b_qT2 = qkdT[gi]; u_, b_u = uu[gi]; w_, b_w = wT[gi]
        for t in range(GT):
            T = grp * GT + t
            G = seg * NT + T
            i2 = G % NB
            cs = slice(T * 128, (T + 1) * 128)
            Sc, b_Sc = St[state["scur"]]; Sn, b_Sn = St[1 - state["scur"]]
            P.mm(pV[:, 0:128], w_[:, t, :], Sc[:], True, True, reads=[b_w, b_Sc], writes=[b_pV])
            P.mm(pV[:, 128:256], qT_[:, cs], Sc[:], True, True, reads=[b_qT, b_Sc], writes=[b_pV])
            P.op("vector", lambda e, i2=i2, t=t: e.tensor_tensor(out=vnew[i2][0][:], in0=u_[:, t, :], in1=pV[:, 0:128], op=ALU.subtract),
                 reads=[b_u, b_pV], writes=[vnew[i2][1]])
            P.op("scalar", lambda e, i2=i2, T=T: e.activation(out=o1[i2][0][:], in_=pV[:, 128:256], func=AF.Copy, scale=egc[:, T:T + 1]),
                 reads=[b_pV, b_egc], writes=[o1[i2][1]])
            P.mm(pSt[:, 0:128], kd[:, t, :], vnew[i2][0][:], True, True, reads=[b_kd, vnew[i2][1]], writes=[b_pSt])
            P.mm(pSt[:, 128:256], qT2[:, t, :], vnew[i2][0][:], True, True, reads=[b_qT2, vnew[i2][1]], writes=[b_pSt])
            P.op("vector", lambda e, Sn=Sn, Sc=Sc, T=T: e.scalar_tensor_tensor(out=Sn[:], in0=Sc[:], scalar=dec[:, T:T + 1], in1=pSt[:, 0:128], op0=ALU.mult, op1=ALU.add),
                 reads=[b_Sc, b_dec, b_pSt], writes=[b_Sn])
            P.op("vector", lambda e, i2=i2: e.tensor_tensor(out=ot[i2][0][:], in0=o1[i2][0][:], in1=pSt[:, 128:256], op=ALU.add),
                 reads=[o1[i2][1], b_pSt], writes=[ot[i2][1]])
            state["scur"] = 1 - state["scur"]
            P.op("scalar", lambda e, i2=i2: e.activation(out=osq[i2][0][:], in_=ot[i2][0][:], func=AF.Square, accum_out=ss[i2][0][:]),
                 reads=[ot[i2][1]], writes=[osq[i2][1], ss[i2][1]])
            P.op("scalar", lambda e, i2=i2: e.activation(out=lss[i2][0][:], in_=ss[i2][0][:], func=AF.Ln, scale=1.0 / 128, bias=EPS),
                 reads=[ss[i2][1]], writes=[lss[i2][1]])
            P.op("scalar", lambda e, i2=i2: e.activation(out=rss[i2][0][:], in_=lss[i2][0][:], func=AF.Exp, scale=-0.5),
                 reads=[lss[i2][1]], writes=[rss[i2][1]])
            P.op("vector", lambda e, i2=i2, t=t: e.scalar_tensor_tensor(out=ofb[gi][0][:, t, :], in0=ot[i2][0][:], scalar=rss[i2][0][:, 0:1], in1=szt[gi][0][:, t, :], op0=ALU.mult, op1=ALU.mult),
                 reads=[ot[i2][1], rss[i2][1], szt[gi][1]], writes=[ofb[gi][1]])
            yield
        G0 = seg * NT + grp * GT
        P.dma("sync", d_o[gi], od[G0 * 128:(G0 + GT) * 128, :].rearrange("(t p) d -> p t d", p=128), ofb[gi][0][:], reads=[ofb[gi][1]])

    groups = [(seg, grp) for seg in range(NSEG) for grp in range(NT // GT)]
    prev_scan = None
    nxt_prep = None
    for _ in seg_prep(0):
        pass
    for n_, (seg, grp) in enumerate(groups):
        if grp == 0 and seg > 0:
            if nxt_prep is not None:
                for _ in nxt_prep:
                    pass
                nxt_prep = None
        if grp == (NT // GT) - 2 and seg + 1 < NSEG:
            nxt_prep = seg_prep(seg + 1)
        pre = prepass_stages(seg, grp)
        rounds = 0
        while True:
            try:
                next(pre)
            except StopIteration:
                break
            rounds += 1
            if nxt_prep is not None:
                try:
                    next(nxt_prep)
                except StopIteration:
                    nxt_prep = None
            if prev_scan is not None and rounds % 3 == 0:
                try:
                    next(prev_scan)
                except StopIteration:
                    prev_scan = None
        if prev_scan is not None:
            for _ in prev_scan:
                pass
        prev_scan = scan_steps(seg, grp)
    for _ in prev_scan:
        pass
    st = P.emit(final_waits=[("sync", d) for d in d_o])
    return nc

T = 2048
D = 1024
EPS = 1e-6
NTILES = 4
O_GATE = 4008


def build_merge():
    nc = bass.Bass("TRN2", target_bir_lowering=False)
    DI = lambda n, s, dt=F32: nc.dram_tensor(n, s, dt, kind="ExternalInput").ap()
    xT = DI("xT", [D, T]); brT = DI("brT", [3, 512, T], BF16); gd = DI("g", [128, 8])
    wgb = DI("wgb", [24, 128, 1536]); wo = DI("wo", [8, 128, 1024]); onesd = DI("ones", [128, 128])
    yT = nc.dram_tensor("yT", [D, T], F32, kind="ExternalOutput").ap()
    P = Prog(nc)
    A = nc.alloc_sbuf_tensor
    PSA = nc.alloc_psum_tensor

    def sb(name, shape, dt=F32):
        return A("s_" + name, shape, dt), P.buf(name)
    ones, b_ones = sb("ones", [128, 128], BF16); P.dma("gpsimd", P.dsem(), ones[:], onesd[:, :], writes=[b_ones])
    g, b_g = sb("g", [128, 8]); P.dma("sync", P.dsem(), g[:], gd[:, :], writes=[b_g])
    xin = [sb(f"xin{i}", [128, 8, 512]) for i in range(2)]; d_xin = [P.dsem() for _ in range(2)]
    br, b_br = sb("br", [128, 3, 4, T], BF16); d_br = P.dsem()
    sq = [sb(f"sq{i}", [128, 512], BF16) for i in range(2)]
    lnb, b_ln = sb("lnb", [128, 512]); rstd, b_rstd = sb("rstd", [128, 512])
    hT = sb("hT", [128, 8, T], BF16); b_hs = [P.buf() for _ in range(NTILES)]
    wc = [sb(f"wc{i}", [128, 1536], BF16) for i in range(3)]; d_wc = [P.dsem() for _ in range(3)]
    woc = [sb(f"woc{i}", [128, 1024], BF16) for i in range(2)]; d_wo = [P.dsem() for _ in range(2)]
    sig = [sb(f"sig{i}", [128, 512]) for i in range(2)]
    acc = [sb(f"acc{i}", [128, 512]) for i in range(NTILES)]
    tmp = [sb(f"tmp{i}", [128, 512]) for i in range(2)]
    mixed = sb("mixed", [128, 8, T], BF16); b_mxs = [P.buf() for _ in range(NTILES)]
    xres = [sb(f"xres{i}", [128, 512]) for i in range(2)]; d_xres = [P.dsem() for _ in range(2)]
    yo = [sb(f"yo{i}", [128, 512]) for i in range(2)]; d_yo = [P.dsem() for _ in range(2)]
    pS = (PSA("pS", [128, 512], F32), P.buf(excl=True))
    pG = [(PSA(f"pG{i}", [128, 512], F32), P.buf(excl=True)) for i in range(2)]
    pU = [(PSA(f"pU{i}", [128, 512], F32), P.buf(excl=True)) for i in range(2)]
    pO = [(PSA(f"pO{i}", [128, 512], F32), P.buf(excl=True)) for i in range(2)]
    xT_v = xT.rearrange("(kc p) n -> p kc n", p=128)
    hT_, mixed_ = hT[0], mixed[0]
    P.dma("sync", d_br, br[:, :, :, 0:NTILES * 512], brT[:, :, 0:NTILES * 512].rearrange("n (kc p) t -> p n kc t", p=128), writes=[b_br])
    for tt in range(NTILES):
        ts = slice(tt * 512, (tt + 1) * 512)
        xi, b_xi = xin[tt % 2]
        P.dma("sync", d_xin[tt % 2], xi[:], xT_v[:, :, ts], writes=[b_xi])
        for kc in range(8):
            s, bs = sq[kc % 2]
            P.op("scalar", lambda e, s=s, kc=kc, xi=xi: e.activation(out=s[:], in_=xi[:, kc, :], func=AF.Square), reads=[b_xi], writes=[bs])
            P.mm(pS[0][:], ones[:], s[:], kc == 0, kc == 7, reads=[b_ones, bs], writes=[pS[1]])
        P.op("scalar", lambda e: e.activation(out=lnb[:], in_=pS[0][:], func=AF.Ln, scale=1.0 / D, bias=EPS), reads=[pS[1]], writes=[b_ln])
        P.op("scalar", lambda e: e.activation(out=rstd[:], in_=lnb[:], func=AF.Exp, scale=-0.5), reads=[b_ln], writes=[b_rstd])
        for kc in range(8):
            P.op("vector", lambda e, kc=kc, xi=xi, ts=ts: e.scalar_tensor_tensor(out=hT_[:, kc, ts], in0=xi[:, kc, :], scalar=g[:, kc:kc + 1], in1=rstd[:], op0=ALU.mult, op1=ALU.mult),
                 reads=[b_xi, b_g, b_rstd], writes=[b_hs[tt]])
    cnt = 0; c2n = 0

    def load_w(i):
        w_, bw_ = wc[i % 3]
        P.dma("gpsimd", d_wc[i % 3], w_[:], wgb[i, :, :], writes=[bw_])
    load_w(0)
    for c in range(8):
        for n in range(3):
            w, bw = wc[cnt % 3]
            if cnt + 1 < 24:
                load_w(cnt + 1)
            cnt += 1
            for tt in range(NTILES):
                ts = slice(tt * 512, (tt + 1) * 512)
                a_, b_a = acc[tt]
                pg, b_pg = pG[c2n % 2]; pu, b_pu = pU[c2n % 2]; sg, b_sg = sig[c2n % 2]; tm, b_tm = tmp[c2n % 2]
                c2n += 1
                for kc in range(8):
                    P.mm(pg[:], w[:, kc * 128:(kc + 1) * 128], hT_[:, kc, ts], kc == 0, kc == 7, reads=[bw, b_hs[tt]], writes=[b_pg])
                for kc in range(4):
                    P.mm(pu[:], w[:, 1024 + kc * 128:1024 + (kc + 1) * 128], br[:, n, kc, ts], kc == 0, kc == 3, reads=[bw, b_br], writes=[b_pu])
                P.op("scalar", lambda e, sg=sg, pg=pg: e.activation(out=sg[:], in_=pg[:], func=AF.Sigmoid), reads=[b_pg], writes=[b_sg])
                if n == 0:
                    P.op("vector", lambda e, a_=a_, sg=sg, pu=pu: e.tensor_tensor(out=a_[:], in0=sg[:], in1=pu[:], op=ALU.mult), reads=[b_sg, b_pu], writes=[b_a])
                else:
                    P.op("vector", lambda e, tm=tm, sg=sg, pu=pu: e.tensor_tensor(out=tm[:], in0=sg[:], in1=pu[:], op=ALU.mult), reads=[b_sg, b_pu], writes=[b_tm])
                    if n == 1:
                        P.op("gpsimd", lambda e, a_=a_, tm=tm: e.tensor_tensor(out=a_[:], in0=a_[:], in1=tm[:], op=ALU.add), reads=[b_a, b_tm], writes=[b_a])
                    else:
                        P.op("gpsimd", lambda e, a_=a_, tm=tm, c=c, ts=ts: e.tensor_tensor(out=mixed_[:, c, ts], in0=a_[:], in1=tm[:], op=ALU.add), reads=[b_a, b_tm], writes=[b_mxs[tt]])
    k2 = 0
    P.dma("gpsimd", d_wo[0], woc[0][0][:], wo[0, :, :], writes=[woc[0][1]])
    for c2 in range(8):
        w, bw = woc[c2 % 2]
        if c2 + 1 < 8:
            P.dma("gpsimd", d_wo[(c2 + 1) % 2], woc[(c2 + 1) % 2][0][:], wo[c2 + 1, :, :], writes=[woc[(c2 + 1) % 2][1]])
        for tt in range(NTILES):
            ts = slice(tt * 512, (tt + 1) * 512)
            po, b_po = pO[k2 % 2]; y_, b_y = yo[k2 % 2]; xr, b_xr = xres[k2 % 2]
            P.dma("sync", d_xres[k2 % 2], xr[:], xT[c2 * 128:(c2 + 1) * 128, ts], writes=[b_xr])
            for kc in range(8):
                P.mm(po[:], w[:, kc * 128:(kc + 1) * 128], mixed_[:, kc, ts], kc == 0, kc == 7, reads=[bw, b_mxs[tt]], writes=[b_po])
            P.op("vector", lambda e, y_=y_, po=po, xr=xr: e.tensor_tensor(out=y_[:], in0=po[:], in1=xr[:], op=ALU.add), reads=[b_po, b_xr], writes=[b_y])
            P.dma("sync", d_yo[k2 % 2], yT[c2 * 128:(c2 + 1) * 128, ts], y_[:], reads=[b_y])
            k2 += 1
    st = P.emit(final_waits=[("sync", d) for d in d_yo])
    return nc


def merge_weights(mix_norm, w_in, w_branch, w_out):
    g = np.ascontiguousarray(mix_norm.reshape(8, 128).T)
    wr = w_in.reshape(8, 128, -1)
    wgb = np.zeros((8, 3, 128, 1536), np.float32)
    for c in range(8):
        for n in range(3):
            c0 = O_GATE + n * 1024 + c * 128
            wgb[c, n, :, 0:1024] = wr[:, :, c0:c0 + 128].transpose(1, 0, 2).reshape(128, 1024)
            wgb[c, n, :, 1024:1536] = w_branch[n].reshape(4, 128, 1024)[:, :, c * 128:(c + 1) * 128].transpose(1, 0, 2).reshape(128, 512)
    wo = np.ascontiguousarray(w_out.reshape(8, 128, 8, 128).transpose(2, 1, 0, 3)).reshape(8, 128, 1024)
    return {"g": g, "wgb": wgb.reshape(24, 128, 1536), "wo": wo, "ones": np.ones((128, 128), np.float32)}

_PROGS = {}


def _prog(name, fn):
    if name not in _PROGS:
        _PROGS[name] = fn()
    return _PROGS[name]


def _run(nc, maps):
    res = run_bass_kernel_spmd(nc, maps, core_ids=list(range(8)))
    return res.results


def _ffn_launch(xT_cores, norm, w_in, w_out):
    g = np.ascontiguousarray(norm.reshape(8, 128).T)
    wi = w_in.reshape(8, 128, 2, NJ, 128)
    w1 = np.ascontiguousarray(wi.transpose(3, 1, 2, 0, 4)).reshape(NJ, 128, 2048)
    wo = w_out.reshape(NJ, 128, 8, 128)
    w2 = np.ascontiguousarray(wo.transpose(2, 1, 0, 3)).reshape(8, 128, NJ * 128)
    ones = np.ones((128, 128), np.float32)
    maps = [{"xT": xT_cores[c], "g": g, "w1": w1, "w2": w2, "ones": ones} for c in range(8)]
    r = _run(_prog("ffn", build_ffn), maps)
    return [np.ascontiguousarray(r[c]["yT"]) for c in range(8)]


def kernel(x, ffa_norm, ffa_w_in, ffa_w_out, mix_norm, w_in, mla_cq_norm, mla_ckv_norm,
           mla_w_uq, mla_w_ukv, mla_q_norm, mla_k_norm, gdn_conv, gdn_a_log, gdn_dt_bias,
           gdn_out_norm, moba_q_norm, moba_k_norm, w_branch, w_out, ffb_norm, ffb_w_in,
           ffb_w_out):
    f = lambda a: np.asarray(a, dtype=np.float32)
    x = f(x)
    B_, S_, D_ = x.shape
    xf = x.reshape(B_ * S_, D_)
    xT = [np.ascontiguousarray(xf[c * T:(c + 1) * T].T) for c in range(8)]
    blkoh, cmask, ident, onesf = attn_consts()
    gcst = gdn_consts()
    for l in range(2):
        xT = _ffn_launch(xT, f(ffa_norm)[l], f(ffa_w_in)[l], f(ffa_w_out)[l])
        W = projb_weights(f(mix_norm)[l], f(w_in)[l], f(mla_cq_norm)[l], f(mla_ckv_norm)[l], f(mla_w_uq)[l],
                          f(mla_w_ukv)[l], f(mla_q_norm)[l], f(mla_k_norm)[l], f(moba_q_norm)[l], f(moba_k_norm)[l])
        maps = []
        for c in range(8):
            m = dict(W)
            m["xT"] = xT[c]
            j = c % 4
            m["rope"] = rope_tables(np.arange(j * T, (j + 1) * T))
            maps.append(m)
        rb = _run(_prog("projb", build_projb), maps)

        def gather(name, b, axis):
            return np.concatenate([rb[b * 4 + j][name] for j in range(4)], axis=axis)
        full = []
        for b in range(2):
            full.append({"mla_qT": gather("mla_qT", b, 2), "mla_kT": gather("mla_kT", b, 2), "mla_v": gather("mla_v", b, 0),
                         "mo_qT": gather("mo_qT", b, 2), "mo_kT": gather("mo_kT", b, 2), "mo_v": gather("mo_v", b, 0),
                         "graw": gather("graw", b, 2), "gba": gather("gba", b, 1), "z": gather("z", b, 0)})
        maps = []
        for c in range(8):
            b, hp = c // 4, c % 4
            F = full[b]
            maps.append({"mq": np.ascontiguousarray(F["mla_qT"][2 * hp:2 * hp + 2]), "mk": np.ascontiguousarray(F["mla_kT"][2 * hp:2 * hp + 2]),
                         "mv": np.ascontiguousarray(F["mla_v"][:, hp * 128:(hp + 1) * 128]),
                         "oq": np.ascontiguousarray(F["mo_qT"][hp]), "ok": np.ascontiguousarray(F["mo_kT"][hp]),
                         "ov": np.ascontiguousarray(F["mo_v"][:, hp * 128:(hp + 1) * 128]),
                         "blkoh": blkoh, "cmask": cmask, "ident": ident, "onesf": onesf})
        ra = _run(_prog("attn", build_attn), maps)
        maps = []
        cw_l = f(gdn_conv)[l]
        for c in range(8):
            b, hd = c // 4, c % 4
            F = full[b]
            cw = np.concatenate([cw_l[:, k0 + hd * 128:k0 + (hd + 1) * 128].T for k0 in (0, 512, 1024)], 1)
            sc = np.stack([np.full(128, f(gdn_a_log)[l][hd], np.float32), np.full(128, f(gdn_dt_bias)[l][hd], np.float32)], 1)
            maps.append({"rq": np.ascontiguousarray(F["graw"][hd]), "rk": np.ascontiguousarray(F["graw"][4 + hd]),
                         "rv": np.ascontiguousarray(F["graw"][8 + hd]),
                         "z": np.ascontiguousarray(F["z"][:, hd * 128:(hd + 1) * 128]),
                         "bl": np.ascontiguousarray(F["gba"][hd].reshape(64, 128).T), "al": np.ascontiguousarray(F["gba"][4 + hd].reshape(64, 128).T),
                         "cw": np.ascontiguousarray(cw), "sc": sc,
                         "gn": np.ascontiguousarray(np.broadcast_to(f(gdn_out_norm)[l][None, :], (128, 128))),
                         "cst": gcst})
        rg = _run(_prog("gdn", build_gdn), maps)
        Wm = merge_weights(f(mix_norm)[l], f(w_in)[l], f(w_branch)[l], f(w_out)[l])
        brT = []
        for b in range(2):
            o_mla = np.concatenate([ra[b * 4 + hp]["oT"][i] for hp in range(4) for i in range(2)], 0)
            o_mo = np.concatenate([ra[b * 4 + hp]["oT"][2 + i] for hp in range(4) for i in range(2)], 0)
            o_gdn = np.concatenate([rg[b * 4 + hd]["o"].T for hd in range(4)], 0)
            brT.append(np.stack([o_mla, o_gdn, o_mo], 0))
        maps = []
        for c in range(8):
            b, j = c // 4, c % 4
            m = dict(Wm)
            m["xT"] = xT[c]
            m["brT"] = np.ascontiguousarray(brT[b][:, :, j * T:(j + 1) * T])
            maps.append(m)
        rm = _run(_prog("merge", build_merge), maps)
        xT = [np.ascontiguousarray(rm[c]["yT"]) for c in range(8)]
        xT = _ffn_launch(xT, f(ffb_norm)[l], f(ffb_w_in)[l], f(ffb_w_out)[l])
    out = np.concatenate([xT[c].T for c in range(8)], 0).reshape(B_, S_, D_)
    return np.ascontiguousarray(out.astype(np.float32))
```
